# Optimizing a Trainium2 kernel written in Bass

```python
import math
import jax, jax.numpy as jnp
from jax import lax
import numpy as np


D_MODEL = 2048
BATCH = 4
SEQ = 2048
DEPTH = 1
DEC_BATCH = 8
DEC_SEQ = 8
PAST_LEN = 16384
PAGE_SIZE = 128

MIX_WIDTH = D_MODEL
A_WIDTH = MIX_WIDTH // 2
B_WIDTH = MIX_WIDTH - A_WIDTH
A_QK_DIM = 64
A_V_DIM = 2 * A_QK_DIM
A_HEADS = A_WIDTH // A_V_DIM
B_K_DIM = 128
B_V_DIM = 128
B_HEADS = B_WIDTH // B_V_DIM
B_CONV_DIM = B_HEADS * (2 * B_K_DIM + B_V_DIM)
CONV_W = 4
CHUNK = 64
Q_BLOCK = 128
IN_WIDTH = 2 * A_HEADS * 2 * A_QK_DIM + A_WIDTH + B_CONV_DIM + B_WIDTH + 2 * B_HEADS
D_FF = ((8 * D_MODEL // 3 + 255) // 256) * 256
EPS = 1e-6

kernel_name = 'hymba_diffattn_gdn_macaron_step'


def rmsnorm(x, w):
    xf = x.astype(jnp.float32)
    y = xf * lax.rsqrt(jnp.mean(xf * xf, axis=-1, keepdims=True) + EPS)
    return (y * w.astype(jnp.float32)).astype(x.dtype)


def l2norm(x):
    return x * lax.rsqrt(jnp.sum(x * x, axis=-1, keepdims=True) + 1e-6)


def half_ffn(x, pre_w, post_w, w_gate, w_up, w_down):
    h = rmsnorm(x, pre_w)
    f = (jax.nn.silu(h @ w_gate) * (h @ w_up)) @ w_down
    return x + 0.5 * rmsnorm(f, post_w)


def alibi_slopes():
    return jnp.exp2(-8.0 * jnp.arange(1, A_HEADS + 1, dtype=jnp.float32) / A_HEADS)


def split_in(h):
    sizes = (A_HEADS * 2 * A_QK_DIM, A_HEADS * 2 * A_QK_DIM, A_WIDTH, B_CONV_DIM, B_WIDTH, B_HEADS, B_HEADS)
    offs = [int(o) for o in np.cumsum(sizes)[:-1]]
    return jnp.split(h, offs, axis=-1)


def diff_attn_block(q, k, v, q_pos, k_pos, lam, slopes):
    bsz, tq = q.shape[:2]
    tk = k.shape[1]
    qf = q.astype(jnp.float32).reshape(bsz, tq, A_HEADS, 2, A_QK_DIM)
    kf = k.astype(jnp.float32).reshape(bsz, tk, A_HEADS, 2, A_QK_DIM)
    s = jnp.einsum('bqhmd,bkhmd->bhmqk', qf, kf) * (A_QK_DIM ** -0.5)
    dist = (q_pos[:, None] - k_pos[None, :]).astype(jnp.float32)
    s = s - (slopes[:, None, None] * dist)[None, :, None]
    s = jnp.where(dist >= 0, s, -jnp.inf)
    p = jax.nn.softmax(s, axis=-1)
    pd = p[:, :, 0] - lam * p[:, :, 1]
    return jnp.einsum('bhqk,bkhd->bqhd', pd, v.astype(jnp.float32))


def diff_attn_prompt(q, k, v, lam, slopes):
    bsz, t = q.shape[:2]
    nb = t // Q_BLOCK
    qb = jnp.swapaxes(q.reshape(bsz, nb, Q_BLOCK, A_HEADS, 2 * A_QK_DIM), 0, 1)
    pos = jnp.arange(t, dtype=jnp.int32)
    pb = pos.reshape(nb, Q_BLOCK)
    ob = lax.map(lambda a: diff_attn_block(a[0], k, v, a[1], pos, lam, slopes), (qb, pb))
    return jnp.swapaxes(ob, 0, 1).reshape(bsz, t, A_HEADS, A_V_DIM)


def gated_delta_chunked(q, k, v, g, beta, s0):
    bsz, t, nh, _ = q.shape
    n = -(-t // CHUNK)
    pad = n * CHUNK - t

    def blocks(x):
        x = jnp.pad(x, [(0, 0), (0, pad)] + [(0, 0)] * (x.ndim - 2))
        x = x.reshape((bsz, n, CHUNK) + x.shape[2:])
        x = jnp.moveaxis(x, 3, 2)
        return jnp.moveaxis(x, 1, 0)

    qc, kc, vc, gc, bc = blocks(q), blocks(k), blocks(v), blocks(g), blocks(beta)
    G = jnp.cumsum(gc, axis=-1)
    idx = jnp.arange(CHUNK)
    causal = idx[:, None] >= idx[None, :]
    strict = idx[:, None] > idx[None, :]
    decay = jnp.exp(jnp.where(causal, G[..., :, None] - G[..., None, :], -jnp.inf))
    kbeta = kc * bc[..., None]
    L = jnp.where(strict, jnp.einsum('nbhid,nbhjd->nbhij', kbeta, kc) * decay, 0.0)
    eye = jnp.eye(CHUNK, dtype=jnp.float32)
    tinv = lax.linalg.triangular_solve(L + eye, jnp.broadcast_to(eye, L.shape), left_side=True,
                                       lower=True, unit_diagonal=True)
    u = tinv @ (vc * bc[..., None])
    w = tinv @ (kbeta * jnp.exp(G)[..., None])
    qk = jnp.einsum('nbhid,nbhjd->nbhij', qc, kc) * decay

    def step(S, xs):
        q_i, k_i, u_i, w_i, qk_i, G_i = xs
        v_new = u_i - w_i @ S
        o_i = (q_i * jnp.exp(G_i)[..., None]) @ S + qk_i @ v_new
        g_last = G_i[..., -1:]
        S = S * jnp.exp(g_last)[..., None] + jnp.einsum('bhck,bhcv->bhkv', k_i * jnp.exp(g_last - G_i)[..., None], v_new)
        return S, o_i

    S, o = lax.scan(step, s0, (qc, kc, u, w, qk, G))
    o = jnp.moveaxis(jnp.moveaxis(o, 0, 1), 2, 3).reshape(bsz, n * CHUNK, nh, -1)[:, :t]
    return o, S


def token_mix(x, attend, conv_buf, s0, pre_w, post_w, w_in, conv_w, a_log, dt_bias, delta_norm_w,
              subln_w, w_out, lam_init):
    bsz, t, _ = x.shape
    h = rmsnorm(x, pre_w) @ w_in
    qa, ka, va, qkv_b, z, b_raw, a_raw = split_in(h)
    ka = ka.reshape(bsz, t, A_HEADS, 2 * A_QK_DIM)
    va = va.reshape(bsz, t, A_HEADS, A_V_DIM)
    oa = attend(qa.reshape(bsz, t, A_HEADS, 2 * A_QK_DIM), ka, va)
    oa = (rmsnorm(oa, subln_w) * (1.0 - lam_init)).astype(x.dtype).reshape(bsz, t, A_WIDTH)
    xp = jnp.concatenate([conv_buf.astype(x.dtype), qkv_b], axis=1)
    conv = xp[:, 0:t] * conv_w[0]
    for j in range(1, CONV_W):
        conv = conv + xp[:, j:j + t] * conv_w[j]
    new_buf = xp[:, t:]
    act = jax.nn.silu(conv).astype(jnp.float32)
    qb, kb, vb = jnp.split(act, [B_HEADS * B_K_DIM, 2 * B_HEADS * B_K_DIM], axis=-1)
    qb = l2norm(qb.reshape(bsz, t, B_HEADS, B_K_DIM)) * (B_K_DIM ** -0.5)
    kb = l2norm(kb.reshape(bsz, t, B_HEADS, B_K_DIM))
    vb = vb.reshape(bsz, t, B_HEADS, B_V_DIM)
    beta = jax.nn.sigmoid(b_raw.astype(jnp.float32))
    g = -jnp.exp(a_log.astype(jnp.float32)) * jax.nn.softplus(a_raw.astype(jnp.float32) + dt_bias.astype(jnp.float32))
    ob, s_new = gated_delta_chunked(qb, kb, vb, g, beta, s0.astype(jnp.float32))
    ob = rmsnorm(ob, delta_norm_w) * jax.nn.silu(z.astype(jnp.float32).reshape(bsz, t, B_HEADS, B_V_DIM))
    ob = ob.astype(x.dtype).reshape(bsz, t, B_WIDTH)
    y = jnp.concatenate([oa, ob], axis=-1) @ w_out
    return x + rmsnorm(y, post_w), ka, va, s_new, new_buf


def setup_inputs(seed: int = 0) -> dict:
    key = jax.random.key(seed)
    ks = jax.random.split(key, 32)
    f32 = jnp.float32
    n_pages = PAST_LEN // PAGE_SIZE
    n_used = DEC_BATCH * n_pages
    n_pool = n_used + max(1, n_used // 4)
    perm = jax.random.permutation(ks[0], n_pool)
    page_table = perm[:n_used].reshape(DEC_BATCH, n_pages).astype(jnp.int32)

    def nrm(k, shape, scale):
        return jax.random.normal(k, shape, f32) * scale

    def gain(k, n):
        return 1.0 + 0.02 * jax.random.normal(k, (DEPTH, n), f32)

    dt = jnp.exp(jax.random.uniform(ks[1], (DEPTH, B_HEADS), f32, math.log(1e-3), math.log(1e-1)))
    return {
        'x_prompt': nrm(ks[2], (BATCH, SEQ, D_MODEL), 1.0),
        'x_sample': nrm(ks[3], (DEC_BATCH, DEC_SEQ, D_MODEL), 1.0),
        'cache_k': nrm(ks[4], (DEPTH, n_pool, PAGE_SIZE, A_HEADS, 2 * A_QK_DIM), 1.0),
        'cache_v': nrm(ks[5], (DEPTH, n_pool, PAGE_SIZE, A_HEADS, A_V_DIM), 1.0),
        'state_ssm': nrm(ks[6], (DEPTH, DEC_BATCH, B_HEADS, B_K_DIM, B_V_DIM), 0.1),
        'state_conv': nrm(ks[7], (DEPTH, DEC_BATCH, CONV_W - 1, B_CONV_DIM), 1.0),
        'page_table': page_table,
        'ffn1_pre_w': gain(ks[8], D_MODEL),
        'ffn1_post_w': gain(ks[9], D_MODEL),
        'ffn1_gate': nrm(ks[10], (DEPTH, D_MODEL, D_FF), D_MODEL ** -0.5),
        'ffn1_up': nrm(ks[11], (DEPTH, D_MODEL, D_FF), D_MODEL ** -0.5),
        'ffn1_down': nrm(ks[12], (DEPTH, D_FF, D_MODEL), D_FF ** -0.5),
        'mix_pre_w': gain(ks[13], D_MODEL),
        'mix_post_w': gain(ks[14], D_MODEL),
        'w_in': nrm(ks[15], (DEPTH, D_MODEL, IN_WIDTH), D_MODEL ** -0.5),
        'conv_w': nrm(ks[16], (DEPTH, CONV_W, B_CONV_DIM), CONV_W ** -0.5),
        'a_log': jnp.log(jax.random.uniform(ks[17], (DEPTH, B_HEADS), f32, 1.0, 16.0)),
        'dt_bias': dt + jnp.log(-jnp.expm1(-dt)),
        'delta_norm_w': gain(ks[18], B_V_DIM),
        'lambda_q1': nrm(ks[19], (DEPTH, A_QK_DIM), 0.1),
        'lambda_k1': nrm(ks[20], (DEPTH, A_QK_DIM), 0.1),
        'lambda_q2': nrm(ks[21], (DEPTH, A_QK_DIM), 0.1),
        'lambda_k2': nrm(ks[22], (DEPTH, A_QK_DIM), 0.1),
        'subln_w': gain(ks[23], A_V_DIM),
        'w_out': nrm(ks[24], (DEPTH, MIX_WIDTH, D_MODEL), MIX_WIDTH ** -0.5),
        'ffn2_pre_w': gain(ks[25], D_MODEL),
        'ffn2_post_w': gain(ks[26], D_MODEL),
        'ffn2_gate': nrm(ks[27], (DEPTH, D_MODEL, D_FF), D_MODEL ** -0.5),
        'ffn2_up': nrm(ks[28], (DEPTH, D_MODEL, D_FF), D_MODEL ** -0.5),
        'ffn2_down': nrm(ks[29], (DEPTH, D_FF, D_MODEL), D_FF ** -0.5),
    }


def reference(x_prompt, x_sample, cache_k, cache_v, state_ssm, state_conv, page_table,
              ffn1_pre_w, ffn1_post_w, ffn1_gate, ffn1_up, ffn1_down,
              mix_pre_w, mix_post_w, w_in, conv_w, a_log, dt_bias, delta_norm_w,
              lambda_q1, lambda_k1, lambda_q2, lambda_k2, subln_w, w_out,
              ffn2_pre_w, ffn2_post_w, ffn2_gate, ffn2_up, ffn2_down):
    f32 = jnp.float32
    slopes = alibi_slopes()
    dbsz, tn = x_sample.shape[:2]
    past = page_table.shape[1] * cache_k.shape[2]
    yp, ys = x_prompt, x_sample
    kp_l, vp_l, sp_l, cp_l, ks_l, vs_l, ss_l, cs_l = [], [], [], [], [], [], [], []
    for l in range(DEPTH):
        lam_init = 0.8 - 0.6 * math.exp(-0.3 * l)
        lam = (jnp.exp(jnp.sum(lambda_q1[l].astype(f32) * lambda_k1[l].astype(f32)))
               - jnp.exp(jnp.sum(lambda_q2[l].astype(f32) * lambda_k2[l].astype(f32))) + lam_init)
        ck, cv = cache_k[l], cache_v[l]

        yp = half_ffn(yp, ffn1_pre_w[l], ffn1_post_w[l], ffn1_gate[l], ffn1_up[l], ffn1_down[l])
        ys = half_ffn(ys, ffn1_pre_w[l], ffn1_post_w[l], ffn1_gate[l], ffn1_up[l], ffn1_down[l])

        def attend_prompt(q, k, v):
            return diff_attn_prompt(q, k, v, lam, slopes)

        zero_buf = jnp.zeros((yp.shape[0], CONV_W - 1, B_CONV_DIM), yp.dtype)
        zero_s = jnp.zeros((yp.shape[0], B_HEADS, B_K_DIM, B_V_DIM), f32)
        yp, kp, vp, sp, cp = token_mix(yp, attend_prompt, zero_buf, zero_s, mix_pre_w[l], mix_post_w[l],
                                       w_in[l], conv_w[l], a_log[l], dt_bias[l], delta_norm_w[l],
                                       subln_w[l], w_out[l], lam_init)

        def attend_sample(q, k, v):
            k_past = ck[page_table].reshape(dbsz, past, A_HEADS, 2 * A_QK_DIM)
            v_past = cv[page_table].reshape(dbsz, past, A_HEADS, A_V_DIM)
            k_all = jnp.concatenate([k_past.astype(k.dtype), k], axis=1)
            v_all = jnp.concatenate([v_past.astype(v.dtype), v], axis=1)
            q_pos = past + jnp.arange(tn, dtype=jnp.int32)
            k_pos = jnp.arange(past + tn, dtype=jnp.int32)
            return diff_attn_block(q, k_all, v_all, q_pos, k_pos, lam, slopes)

        ys, ksm, vsm, ssm, csm = token_mix(ys, attend_sample, state_conv[l], state_ssm[l], mix_pre_w[l],
                                           mix_post_w[l], w_in[l], conv_w[l], a_log[l], dt_bias[l],
                                           delta_norm_w[l], subln_w[l], w_out[l], lam_init)

        yp = half_ffn(yp, ffn2_pre_w[l], ffn2_post_w[l], ffn2_gate[l], ffn2_up[l], ffn2_down[l])
        ys = half_ffn(ys, ffn2_pre_w[l], ffn2_post_w[l], ffn2_gate[l], ffn2_up[l], ffn2_down[l])

        kp_l.append(kp)
        vp_l.append(vp)
        sp_l.append(sp.astype(state_ssm.dtype))
        cp_l.append(cp)
        ks_l.append(ksm)
        vs_l.append(vsm)
        ss_l.append(ssm.astype(state_ssm.dtype))
        cs_l.append(csm)

    return (yp, ys, jnp.stack(kp_l), jnp.stack(vp_l), jnp.stack(sp_l), jnp.stack(cp_l),
            jnp.stack(ks_l), jnp.stack(vs_l), jnp.stack(ss_l), jnp.stack(cs_l))
```

```python
import os
from contextlib import ExitStack
import numpy as np
import concourse.bass as bass
import concourse.mybir as mybir
from concourse.bass_utils import run_bass_kernel_spmd

F32 = mybir.dt.float32
BF16 = mybir.dt.bfloat16
I32 = mybir.dt.int32
AF = mybir.ActivationFunctionType
ALU = mybir.AluOpType

D = 2048
DFF = 5632
SEQ = 2048
NS = 8
NROW = SEQ + NS
TW = 512
NT = SEQ // TW
KC = D // 128
FC = DFF // 128
INW = 7184
EPS = 1e-6
NCORES = 8


class Sched:
    ENG = ("pe", "act", "dve", "pool", "sp")

    def __init__(self):
        self.ops = {e: [] for e in self.ENG}
        self.lastw = {}
        self.readers = {}
        self.dma_cnt = {}
        self.waited = {e: {} for e in self.ENG}

    def add(self, eng, fn, reads=(), writes=(), slot=None):
        op = dict(eng=eng, fn=fn, waits=[], flag=False, slot=slot, idx=len(self.ops[eng]))
        if slot is not None:
            self.dma_cnt[slot] = self.dma_cnt.get(slot, 0) + 1
            ev = ("dma", slot, self.dma_cnt[slot], op)
        else:
            ev = ("eng", eng, op["idx"], op)
        deps = []
        for r in reads:
            if r in self.lastw:
                deps.append(self.lastw[r])
        for w in writes:
            if w in self.lastw:
                deps.append(self.lastw[w])
            deps.extend(self.readers.get(w, []))
        best = {}
        for d in deps:
            k = (d[0], d[1])
            if k not in best or best[k][2] < d[2]:
                best[k] = d
        for d in best.values():
            kind, key, n, dop = d
            if kind == "eng" and key == eng and eng == "pe":
                continue
            k = (kind, key)
            if self.waited[eng].get(k, -1) >= n:
                continue
            self.waited[eng][k] = n
            op["waits"].append(d)
            if kind == "eng":
                dop["flag"] = True
        for r in reads:
            self.readers.setdefault(r, []).append(ev)
        for w in writes:
            self.lastw[w] = ev
            self.readers[w] = []
        self.ops[eng].append(op)
        return ev

    def fence(self):
        evs = []
        for e in self.ENG:
            for op in reversed(self.ops[e]):
                if op["fn"] is not None and op["slot"] is None:
                    evs.append(("eng", e, op["idx"], op))
                    break
        for sl, n in self.dma_cnt.items():
            evs.append(("dma", sl, n, None))
        for e in self.ENG:
            op = dict(eng=e, fn=None, waits=[], flag=False, slot=None, idx=len(self.ops[e]))
            for d in evs:
                kind, key, n, dop = d
                if kind == "eng" and key == e:
                    continue
                k = (kind, key)
                if self.waited[e].get(k, -1) >= n:
                    continue
                self.waited[e][k] = n
                op["waits"].append(d)
                if kind == "eng":
                    dop["flag"] = True
            self.ops[e].append(op)

    def emit(self, nc, stack):
        esem = {e: stack.enter_context(nc.semaphore("sem_" + e)) for e in self.ENG}
        ssem = {s: stack.enter_context(nc.semaphore("slot_" + str(s))) for s in self.dma_cnt}
        for e in self.ENG:
            c = 0
            for op in self.ops[e]:
                if op["flag"]:
                    c += 1
                op["cnt"] = c
        block = stack.enter_context(nc.Block())

        def run(e):
            def body(engine):
                for op in self.ops[e]:
                    for kind, key, n, dop in op["waits"]:
                        if kind == "eng":
                            engine.wait_ge(esem[key], dop["cnt"])
                        else:
                            engine.wait_ge(ssem[key], 16 * n)
                    if op["fn"] is None:
                        continue
                    ins = op["fn"](engine)
                    if op["slot"] is not None:
                        ins.then_inc(ssem[op["slot"]], 16)
                    elif op["flag"]:
                        ins.then_inc(esem[e], 1)
            return body

        def run_sp(engine):
            run("sp")(engine)
            for sl in getattr(self, "final_slots", []):
                if sl in ssem:
                    engine.wait_ge(ssem[sl], 16 * self.dma_cnt[sl])

        block.tensor(run("pe"))
        block.scalar(run("act"))
        block.vector(run("dve"))
        block.gpsimd(run("pool"))
        block.sync(run_sp)


def subtiles(t):
    st = [(t * TW + 128 * s, 128, 128 * s) for s in range(TW // 128)]
    if t == NT - 1:
        st.append((SEQ, NS, TW))
    return st


def passes(t):
    p = [(0, TW)]
    if t == NT - 1:
        p.append((TW, NS))
    return p


def build(gdn_only=False, attn_only=False, skip_p3=False, npool=1280, aheads=8, skip_sample=False):
    nc = bass.Bass("TRN2", target_bir_lowering=False)
    S = Sched()
    stack = ExitStack()

    def din(name, shape, dt=F32):
        return nc.dram_tensor(name, shape, dt, kind="ExternalInput").ap()

    def dout(name, shape, dt=F32):
        return nc.dram_tensor(name, shape, dt, kind="ExternalOutput").ap()

    xin = din("xin", [NROW, D])
    ident_d = din("ident", [128, 128], BF16)
    wbc_d = {k: din("wbc_" + k, [128, D]) for k in ("f1pre", "f1post", "mixpre")}
    GG = 2
    wg1 = din("wg1", [FC // GG, 128, KC * 128 * GG])
    wu1 = din("wu1", [FC // GG, 128, KC * 128 * GG])
    KGD = 11
    wd1 = din("wd1", [4, FC // KGD, 128, KGD * 512])
    NFM = 48
    win_fm = din("win_fm", [NFM // GG, 128, KC * 128 * GG])
    KGV = 4
    win_v = din("win_v", [2, KC // KGV, 128, KGV * 512])

    win_ba = din("win_ba", [128, KC * 16])
    gconst = din("gconst", [128, 6 * 128])
    cw_d = din("cw_d", [128, 96])
    hb_d = din("hb_d", [128, 32])
    ssm0_d = din("ssm0", [8, 128, 128])
    convst_d = din("convst", [24, 128, 3])
    ba_d = (din if gdn_only else dout)("ba_d", [NROW, 16])
    aconst = din("aconst", [128, 2048 + 256 + 128])
    mixT_d = dout("mixT_d", [2048, NROW], BF16)
    ssm_p = dout("ssm_p", [8, 128, 128])
    ssm_s = dout("ssm_s", [8, 128, 128])
    projT = (din if gdn_only else dout)("projT", [NFM * 128, NROW])
    v_out = (din if gdn_only else dout)("v_out", [NROW, 1024])
    x1_d = dout("x1", [NROW, D])
    x2_d = dout("x2", [NROW, D])
    y_d = dout("y", [NROW, D])
    wbc2_d = {k: din("wbc_" + k, [128, D]) for k in ("f2pre", "f2post", "mixpost")}
    wg2 = din("wg2", [FC // GG, 128, KC * 128 * GG])
    wu2 = din("wu2", [FC // GG, 128, KC * 128 * GG])
    wd2 = din("wd2", [4, FC // KGD, 128, KGD * 512])
    wout_d = din("wout", [4, KC // KGV, 128, KGV * 512])
    ck_d = din("cache_k", [npool * 128, 1024])
    cv_d = din("cache_v", [npool * 128, 1024])
    pt_d = din("pt", [128, 1], I32)
    sconst = din("sconst", [128, 512])

    nm = {}

    def sb(name, shape, dt=F32):
        t_ = stack.enter_context(nc.sbuf_tensor(name, shape, dt))
        nm[id(t_)] = name
        return t_

    def ps(name, shape, dt=F32):
        t_ = stack.enter_context(nc.psum_tensor(name, shape, dt))
        nm[id(t_)] = name
        return t_

    def N(t_):
        return nm[id(t_)]

    TWX = TW + NS
    ident = sb("ident_sb", [128, 128], BF16)
    wbc = {k: sb("wbc_sb_" + k, [128, D]) for k in wbc_d}
    X = [sb(f"X{i}", [128, D]) for i in range(2)]
    Y = [sb(f"Y{i}", [128, D]) for i in range(5)]
    XN = sb("XN", [128, D], BF16)
    xnT = sb("xnT", [128, KC * TWX], BF16)
    hT = sb("hT", [128, FC * TWX], BF16)
    sm = sb("small", [128, 64])
    wgS = [sb(f"wgS{i}", [128, KC * 128 * GG], BF16) for i in range(2)]
    wuS = [sb(f"wuS{i}", [128, KC * 128 * GG], BF16) for i in range(2)]
    wdS = [sb(f"wdS{i}", [128, KGD * 512], BF16) for i in range(2)]
    stg = [sb(f"stg{i}", [128, TWX]) for i in range(2)]
    wba = sb("wba", [128, KC * 16], BF16)
    finb = [sb(f"finb{i}", [128, 128], BF16) for i in range(2)]
    ptidx = sb("ptidx", [128, 1], I32)
    ptidx2 = sb("ptidx2", [128, 1], I32)
    tmpg = stg
    bank = [ps(f"bank{i}", [128, 512]) for i in range(7)]
    psT = ps("psT", [128, 1024], BF16)

    xnT3 = xnT[:].rearrange("p (c t) -> p c t", t=TWX)
    hT3 = hT[:].rearrange("p (c t) -> p c t", t=TWX)
    psT3 = psT[:].rearrange("p (c t) -> p c t", t=128)

    cnt = {"x": 0, "g": 0, "d": 0, "stg": 0, "tmp": 0, "evac": 0}

    S.add("pool", lambda e: e.dma_start(out=wba[:], in_=win_ba), writes=["wba"], slot="wba")
    S.add("sp", lambda e: e.dma_start(out=ident[:], in_=ident_d), writes=["ident"], slot="ident")
    for k in wbc:
        S.add("sp", lambda e, k=k: e.dma_start(out=wbc[k][:], in_=wbc_d[k]), writes=["wbc" + k], slot="wbc" + k)

    def load_x(src, row0, n, extra_reads=()):
        i = cnt["x"] % 2
        cnt["x"] += 1
        S.add("sp", lambda e: e.dma_start(out=X[i][:n, :], in_=src[row0:row0 + n, :]),
              reads=list(extra_reads), writes=[f"X{i}"], slot=f"X{i}")
        return i

    def rstd_of(src_ap, n, src_res, col):
        S.add("act", lambda e: e.activation(out=XN[:n, :], in_=src_ap, func=AF.Square,
                                            accum_out=sm[:n, col:col + 1]),
              reads=src_res, writes=["XN", f"sm{col}"])
        S.add("dve", lambda e: e.tensor_scalar(out=sm[:n, col + 1:col + 2], in0=sm[:n, col:col + 1],
                                               scalar1=1.0 / D, scalar2=EPS, op0=ALU.mult, op1=ALU.add),
              reads=[f"sm{col}"], writes=[f"sm{col + 1}"])
        S.add("act", lambda e: e.sqrt(out=sm[:n, col + 2:col + 3], in_=sm[:n, col + 1:col + 2]),
              reads=[f"sm{col + 1}"], writes=[f"sm{col + 2}"])
        S.add("dve", lambda e: e.reciprocal(out=sm[:n, col + 3:col + 4], in_=sm[:n, col + 2:col + 3]),
              reads=[f"sm{col + 2}"], writes=[f"sm{col + 3}"])
        return sm[:n, col + 3:col + 4], f"sm{col + 3}"

    def norm_T(src_ap, src_res, n, col, wkey):
        rs, rres = rstd_of(src_ap, n, src_res, 0)
        S.add("dve", lambda e: e.scalar_tensor_tensor(out=XN[:n, :], in0=src_ap, scalar=rs,
                                                      in1=wbc[wkey][:n, :], op0=ALU.mult, op1=ALU.mult),
              reads=src_res + [rres, "wbc" + wres[wkey]], writes=["XN"])
        for half in range(2):
            for j in range(8):
                c = half * 8 + j
                S.add("pe", lambda e, c=c, j=j: e.transpose(out=psT3[:, j, :n], in_=XN[:n, c * 128:(c + 1) * 128],
                                                            identity=ident[:n, :n]),
                      reads=["XN", "ident"], writes=["psT"])
            eng = "act" if half == 0 else "dve"
            if eng == "act":
                S.add("act", lambda e, half=half: e.copy(out=xnT3[:, half * 8:half * 8 + 8, col:col + n],
                                                         in_=psT3[:, :, :n]),
                      reads=["psT"], writes=["xnT"])
            else:
                S.add("dve", lambda e, half=half: e.tensor_copy(out=xnT3[:, half * 8:half * 8 + 8, col:col + n],
                                                                in_=psT3[:, :, :n]),
                      reads=["psT"], writes=["xnT"])

    def linear_fm(t, wsrc, ngroups, stages, skey, evac):
        for g in range(ngroups):
            i = cnt[skey] % 2
            cnt[skey] += 1
            for (arr, st) in zip(wsrc, stages):
                S.add("pool", lambda e, arr=arr, st=st, g=g, i=i: e.dma_start(out=st[i][:], in_=arr[g]),
                      writes=[N(st[i])], slot=N(st[i]))
            for gi in range(GG):
                oc = g * GG + gi
                for (c0, n) in passes(t):
                    outs = []
                    for wi, st in enumerate(stages):
                        b = bank[(cnt["evac"] % 2) * len(stages) + wi]
                        st3 = st[i][:].rearrange("p (k f) -> p k f", f=128 * GG)
                        for kc in range(KC):
                            S.add("pe", lambda e, b=b, st3=st3, kc=kc, gi=gi, c0=c0, n=n: e.matmul(
                                b[:, :n], lhsT=st3[:, kc, gi * 128:(gi + 1) * 128], rhs=xnT3[:, kc, c0:c0 + n],
                                start=(kc == 0), stop=(kc == KC - 1)),
                                reads=[N(st[i]), "xnT"], writes=[N(b)])
                        outs.append(b)
                    cnt["evac"] += 1
                    evac(oc, c0, n, outs)

    def linear_tm(t, wsrc, nn, nkg, kgsz, stages, skey, lhs3, lres, evac):
        sts = subtiles(t)
        for nt_ in range(nn):
            for kg in range(nkg):
                i = cnt[skey] % 2
                cnt[skey] += 1
                S.add("pool", lambda e, nt_=nt_, kg=kg, i=i: e.dma_start(out=stages[i][:, :kgsz * 512],
                                                                         in_=wsrc[nt_, kg]),
                      writes=[N(stages[i])], slot=N(stages[i]))
                st3 = stages[i][:].rearrange("p (k f) -> p k f", f=512)
                for kk in range(kgsz):
                    k = kg * kgsz + kk
                    for si, (row0, n, col) in enumerate(sts):
                        b = bank[2 + si]
                        S.add("pe", lambda e, b=b, st3=st3, kk=kk, k=k, n=n, col=col: e.matmul(
                            b[:n, :], lhsT=lhs3[:, k, col:col + n], rhs=st3[:, kk, :],
                            start=(k == 0), stop=(k == nkg * kgsz - 1)),
                            reads=[N(stages[i]), lres], writes=[N(b)])
            for si, (row0, n, col) in enumerate(sts):
                evac(nt_, si, row0, n, bank[2 + si])

    wres = {"f1pre": "f1pre", "f1post": "f1post", "mixpre": "mixpre",
            "f2pre": "f1pre", "f2post": "f1post", "mixpost": "mixpre"}
    for k2, k1 in list(wres.items()):
        wbc[k2] = wbc[k1]

    def ffn_block(t, wg, wu, wd, postkey, res_src, after):
        sts = subtiles(t)

        def evac_gu(oc, c0, n, outs):
            j = cnt["stg"] % 2
            cnt["stg"] += 1
            S.add("act", lambda e: e.activation(out=tmpg[j][:, :n], in_=outs[0][:, :n], func=AF.Silu),
                  reads=[N(outs[0])], writes=[f"stg{j}"])
            S.add("dve", lambda e: e.tensor_tensor(out=hT3[:, oc, c0:c0 + n], in0=tmpg[j][:, :n],
                                                   in1=outs[1][:, :n], op=ALU.mult),
                  reads=[f"stg{j}", N(outs[1])], writes=["lhs"])

        linear_fm(t, [wg, wu], FC // GG, [wgS, wuS], "g", evac_gu)

        def evac_d(nt_, si, row0, n, b):
            S.add("act", lambda e: e.copy(out=Y[si][:n, nt_ * 512:(nt_ + 1) * 512], in_=b[:n, :]),
                  reads=[N(b)], writes=[f"Y{si}"])

        linear_tm(t, wd, 4, FC // KGD, KGD, wdS, "d", hT3, "lhs", evac_d)

        for si, (row0, n, col) in enumerate(sts):
            post_res(si, row0, n, postkey, res_src, 0.5)
            after(si, row0, n, col)

    def post_res(si, row0, n, postkey, res_src, alpha):
        rs, rres = rstd_of(Y[si][:n, :], n, [f"Y{si}"], 8)
        S.add("dve", lambda e: e.scalar_tensor_tensor(
            out=Y[si][:n, :], in0=Y[si][:n, :], scalar=rs, in1=wbc[postkey][:n, :],
            op0=ALU.mult, op1=ALU.mult),
            reads=[f"Y{si}", rres, "wbc" + wres[postkey]], writes=[f"Y{si}"])
        xi = load_x(res_src, row0, n, extra_reads=([f"x2dram{si}"] if res_src is x2_d else []))
        S.add("dve", lambda e: e.scalar_tensor_tensor(
            out=Y[si][:n, :], in0=Y[si][:n, :], scalar=alpha, in1=X[xi][:n, :],
            op0=ALU.mult, op1=ALU.add),
            reads=[f"Y{si}", f"X{xi}"], writes=[f"Y{si}"])

    def evac_d_plain(nt_, si, row0, n, b):
        S.add("act", lambda e: e.copy(out=Y[si][:n, nt_ * 512:(nt_ + 1) * 512], in_=b[:n, :]),
              reads=[N(b)], writes=[f"Y{si}"])

    def phase1_tile(t):
        sts = subtiles(t)
        for (row0, n, col) in sts:
            xi = load_x(xin, row0, n)
            norm_T(X[xi][:n, :], [f"X{xi}"], n, col, "f1pre")

        def after1(si, row0, n, col):
            S.add("sp", lambda e: e.dma_start(out=x1_d[row0:row0 + n, :], in_=Y[si][:n, :]),
                  reads=[f"Y{si}"], slot=f"x1o{si}")
            norm_T(Y[si][:n, :], [f"Y{si}"], n, col, "mixpre")

        ffn_block(t, wg1, wu1, wd1, "f1post", xin, after1)

        def evac_in(oc, c0, n, outs):
            j = cnt["stg"] % 2
            cnt["stg"] += 1
            S.add("act", lambda e: e.copy(out=stg[j][:, :n], in_=outs[0][:, :n]),
                  reads=[N(outs[0])], writes=[f"stg{j}"])
            gcol = (t * TW + c0) if c0 < TW else SEQ
            S.add("sp", lambda e: e.dma_start(out=projT[oc * 128:(oc + 1) * 128, gcol:gcol + n], in_=stg[j][:, :n]),
                  reads=[f"stg{j}"], slot=f"stgo{j}")

        linear_fm(t, [win_fm], NFM // GG, [wgS], "g", evac_in)

        def evac_v(nt_, si, row0, n, b):
            j = cnt["stg"] % 2
            cnt["stg"] += 1
            S.add("act", lambda e: e.copy(out=stg[j][:n, :512], in_=b[:n, :]),
                  reads=[N(b)], writes=[f"stg{j}"])
            S.add("sp", lambda e: e.dma_start(out=v_out[row0:row0 + n, nt_ * 512:(nt_ + 1) * 512], in_=stg[j][:n, :512]),
                  reads=[f"stg{j}"], slot=f"stgo{j}")

        linear_tm(t, win_v, 2, KC // KGV, KGV, wdS, "d", xnT3, "xnT", evac_v)

        wba3 = wba[:].rearrange("p (k f) -> p k f", f=16)

        def ba_sub(si, row0, n, col):
            b = bank[2 + si]
            for kc in range(KC):
                S.add("pe", lambda e, kc=kc: e.matmul(
                    b[:n, :16], lhsT=xnT3[:, kc, col:col + n], rhs=wba3[:, kc, :],
                    start=(kc == 0), stop=(kc == KC - 1)),
                    reads=["wba", "xnT"], writes=[N(b)])
            j = cnt["stg"] % 2
            cnt["stg"] += 1
            S.add("act", lambda e: e.copy(out=stg[j][:n, :16], in_=b[:n, :16]),
                  reads=[N(b)], writes=[f"stg{j}"])
            S.add("sp", lambda e: e.dma_start(out=ba_d[row0:row0 + n, :], in_=stg[j][:n, :16]),
                  reads=[f"stg{j}"], slot=f"stgo{j}")

        for si, (row0, n, col) in enumerate(sts):
            ba_sub(si, row0, n, col)

    def phase3_tile(t):
        sts = subtiles(t)
        ncols = TW + (NS if t == NT - 1 else 0)
        S.add("sp", lambda e: e.dma_start(out=xnT3[:, :, 0:ncols],
                                          in_=mixT_d[:, t * TW:t * TW + ncols].rearrange("(k p) n -> p k n", p=128)),
              writes=["xnT"], slot="xnTld")
        linear_tm(t, wout_d, 4, KC // KGV, KGV, wdS, "d", xnT3, "xnT", evac_d_plain)

        def mid(si, row0, n, col):
            post_res(si, row0, n, "mixpost", x1_d, 1.0)
            S.add("sp", lambda e: e.dma_start(out=x2_d[row0:row0 + n, :], in_=Y[si][:n, :]),
                  reads=[f"Y{si}"], writes=[f"x2dram{si}"], slot=f"x2o{si}")
            norm_T(Y[si][:n, :], [f"Y{si}"], n, col, "f2pre")

        for si, (row0, n, col) in enumerate(sts):
            mid(si, row0, n, col)

        def after3(si, row0, n, col):
            S.add("sp", lambda e: e.dma_start(out=y_d[row0:row0 + n, :], in_=Y[si][:n, :]),
                  reads=[f"Y{si}"], slot=f"yo{si}")

        ffn_block(t, wg2, wu2, wd2, "f2post", x2_d, after3)

    for t in range(0 if gdn_only else NT):
        phase1_tile(t)


    S.fence()
    pool_cols = {}
    pool_next = [0]

    def galloc(name, ncols_req):
        ncols = ((ncols_req + 31) // 32) * 32
        off = pool_next[0]
        bi, co = off // D, off % D
        if co + ncols > D:
            bi, co = bi + 1, 0
            off = bi * D
        pool_next[0] = off + ncols
        assert bi < 5, "gdn scratch overflow"
        return Y[bi][:, co:co + ncols_req]

    Sst = [galloc(f"S{h}", 128) for h in range(8)]
    gc = galloc("gconst", 768)
    identF, Umat, maskLn, maskUn, onesF, maskUd = [gc[:, i * 128:(i + 1) * 128] for i in range(6)]
    cw = galloc("cw", 96)
    hb = galloc("hb", 32)
    chk = galloc("chk", 64)
    S.add("sp", lambda e: e.dma_start(out=gc, in_=gconst), writes=["gconst"], slot="gconst")
    S.add("sp", lambda e: e.dma_start(out=cw, in_=cw_d), writes=["cw"], slot="cw")
    S.add("sp", lambda e: e.dma_start(out=hb, in_=hb_d), writes=["hb"], slot="hb")
    S.add("act", lambda e: e.activation(out=hb[:, 24:32], in_=hb[:, 0:8], func=AF.Exp), reads=["hb"], writes=["hbA"])
    S.add("dve", lambda e: e.tensor_scalar(out=hb[:, 24:32], in0=hb[:, 24:32], scalar1=-1.0, scalar2=None, op0=ALU.mult),
          reads=["hbA"], writes=["hbA"])
    for h in range(8):
        S.add("dve", lambda e, h=h: e.memset(Sst[h], 0.0), writes=[f"S{h}"])

    US = []
    for p in range(2):
        d_ = {}
        for nme, w_ in (("kxp", 131), ("vxp", 131), ("kc", 128), ("vc", 128), ("t1", 128), ("t2", 128),
                        ("kn", 128), ("ktm", 128), ("vb", 128), ("Gbc", 128), ("rep", 128), ("kbT", 128),
                        ("QT", 128), ("Q", 128), ("R", 128), ("kg", 128), ("nwT", 128), ("vn", 128),
                        ("kd", 128), ("sc", 16),
                        ("qxp", 131), ("qc", 128), ("qg", 128), ("qkT", 128), ("eG", 128), ("t3", 128),
                        ("zs", 128), ("on", 128)):
            d_[nme] = galloc(f"{nme}{p}", w_)
        US.append(d_)

    def pb(i):
        return bank[i % 7][:, 0:128], N(bank[i % 7])

    unit_no = [0]
    KSTAGE = int(os.environ.get('K_STAGE', '99'))
    KCHUNKS = int(os.environ.get('K_CHUNKS', str(SEQ // 128)))
    KSAMPLE = int(os.environ.get('K_SAMPLE', '1'))
    KCHUNKLVL = int(os.environ.get('K_CHUNKLVL', '1'))

    def gdn_chunk(tok0, nvalid, hist_src, sample):
        if not KCHUNKLVL:
            return
        nt_ = min(128, nvalid)
        if nvalid < 128:
            S.add("dve", lambda e: e.memset(chk[:, 0:16], 0.0), writes=["chkBA"])
        S.add("sp", lambda e: e.dma_start(out=chk[:nt_, 0:16], in_=ba_d[tok0:tok0 + nt_, :]),
              writes=["chkBA"], slot="chkBA")
        S.add("act", lambda e: e.activation(out=chk[:, 16:24], in_=chk[:, 0:8], func=AF.Sigmoid),
              reads=["chkBA"], writes=["chkbeta"])
        S.add("dve", lambda e: e.tensor_tensor(out=chk[:, 24:32], in0=chk[:, 8:16], in1=hb[:, 8:16], op=ALU.add),
              reads=["chkBA", "hb"], writes=["chktmp"])
        S.add("dve", lambda e: e.tensor_scalar(out=chk[:, 24:32], in0=chk[:, 24:32], scalar1=30.0, scalar2=None, op0=ALU.min),
              reads=["chktmp"], writes=["chktmp"])
        S.add("act", lambda e: e.activation(out=chk[:, 24:32], in_=chk[:, 24:32], func=AF.Exp),
              reads=["chktmp"], writes=["chktmp"])
        S.add("act", lambda e: e.activation(out=chk[:, 24:32], in_=chk[:, 24:32], func=AF.Ln, bias=1.0),
              reads=["chktmp"], writes=["chktmp"])
        S.add("dve", lambda e: e.tensor_tensor(out=chk[:, 32:40], in0=chk[:, 24:32], in1=hb[:, 24:32], op=ALU.mult),
              reads=["chktmp", "hbA"], writes=["chkg"])
        if nvalid < 128:
            S.add("dve", lambda e: e.tensor_scalar(out=chk[:, 32:40], in0=chk[:, 32:40], scalar1=hb[:, 16:17], scalar2=None, op0=ALU.mult),
                  reads=["chkg", "hb"], writes=["chkg"])
            S.add("dve", lambda e: e.tensor_scalar(out=chk[:, 16:24], in0=chk[:, 16:24], scalar1=hb[:, 16:17], scalar2=None, op0=ALU.mult),
                  reads=["chkbeta", "hb"], writes=["chkbeta"])
        pg, pgr = pb(27)
        S.add("pe", lambda e: e.matmul(pg[:, :8], lhsT=Umat, rhs=chk[:, 32:40], start=True, stop=True),
              reads=["gconst", "chkg"], writes=[pgr])
        S.add("act", lambda e: e.copy(out=chk[:, 40:48], in_=pg[:, :8]), reads=[pgr], writes=["chkG"])

        def unit(h):
            p = unit_no[0] % 2
            unit_no[0] += 1
            u = US[p]
            R_ = lambda nme: f"{nme}{p}"
            P = lambda i: pb(p * 5 + i)
            krow = 2048 + 1024 + 128 * h
            vrow = 2048 + 2048 + 128 * h
            qrow = 2048 + 128 * h
            for (dst, row, part, nme) in ((u["kxp"], krow, 8 + h, "kxp"), (u["vxp"], vrow, 16 + h, "vxp"), (u["qxp"], qrow, h, "qxp")):
                if hist_src is None:
                    S.add("dve", lambda e, dst=dst: e.memset(dst[:, 0:3], 0.0), writes=[R_(nme)])
                elif hist_src == "state":
                    S.add("dve", lambda e, dst=dst: e.memset(dst[:, :], 0.0), writes=[R_(nme)])
                    S.add("sp", lambda e, dst=dst, part=part: e.dma_start(out=dst[:, 0:3], in_=convst_d[part]),
                          writes=[R_(nme)], slot=R_(nme) + "h")
                if hist_src == "prev":
                    S.add("sp", lambda e, dst=dst, row=row: e.dma_start(out=dst[:, 0:131], in_=projT[row:row + 128, tok0 - 3:tok0 + 128]),
                          writes=[R_(nme)], slot=R_(nme))
                else:
                    S.add("sp", lambda e, dst=dst, row=row: e.dma_start(out=dst[:, 3:3 + nt_], in_=projT[row:row + 128, tok0:tok0 + nt_]),
                          writes=[R_(nme)], slot=R_(nme))
            if KSTAGE < 1:
                return
            for (src, dst, part, sn, dn) in ((u["kxp"], u["kc"], 8 + h, "kxp", "kc"), (u["vxp"], u["vc"], 16 + h, "vxp", "vc"), (u["qxp"], u["qc"], h, "qxp", "qc")):
                S.add("dve", lambda e, src=src, dst=dst, part=part: e.tensor_scalar(
                    out=dst, in0=src[:, 0:128], scalar1=cw[:, part * 4:part * 4 + 1], scalar2=None, op0=ALU.mult),
                    reads=[R_(sn), "cw"], writes=[R_(dn)])
                for j in range(1, 4):
                    S.add("dve", lambda e, src=src, dst=dst, part=part, j=j: e.scalar_tensor_tensor(
                        out=dst, in0=src[:, j:j + 128], scalar=cw[:, part * 4 + j:part * 4 + j + 1], in1=dst,
                        op0=ALU.mult, op1=ALU.add),
                        reads=[R_(sn), "cw", R_(dn)], writes=[R_(dn)])
                S.add("act", lambda e, dst=dst: e.activation(out=dst, in_=dst, func=AF.Silu),
                      reads=[R_(dn)], writes=[R_(dn)])
            if KSTAGE < 2:
                return
            S.add("act", lambda e: e.activation(out=u["t1"], in_=u["kc"], func=AF.Square), reads=[R_("kc")], writes=[R_("t1")])
            p0, p0r = P(0)
            S.add("pe", lambda e: e.matmul(p0, lhsT=onesF, rhs=u["t1"], start=True, stop=True),
                  reads=["gconst", R_("t1")], writes=[p0r])
            S.add("dve", lambda e: e.tensor_scalar(out=u["t2"], in0=p0, scalar1=1e-6, scalar2=None, op0=ALU.add),
                  reads=[p0r], writes=[R_("t2")])
            S.add("act", lambda e: e.sqrt(out=u["t2"], in_=u["t2"]), reads=[R_("t2")], writes=[R_("t2")])
            S.add("dve", lambda e: e.reciprocal(out=u["t2"], in_=u["t2"]), reads=[R_("t2")], writes=[R_("t2")])
            S.add("dve", lambda e: e.tensor_tensor(out=u["kn"], in0=u["kc"], in1=u["t2"], op=ALU.mult),
                  reads=[R_("kc"), R_("t2")], writes=[R_("kn")])
            if KSTAGE < 3:
                return
            KSUB = int(os.environ.get('K_SUB', '9'))
            p1, p1r = P(1)
            S.add("pe", lambda e: e.matmul(p1, lhsT=u["kn"], rhs=identF, start=True, stop=True), reads=[R_("kn"), "gconst"], writes=[p1r])
            if KSUB < 1:
                return
            S.add("act", lambda e: e.copy(out=u["ktm"], in_=p1), reads=[p1r], writes=[R_("ktm")])
            if KSUB < 2:
                return
            p2, p2r = P(2)
            vsrc = "kc" if os.environ.get('K_V1') else "vc"
            S.add("pe", lambda e: e.matmul(p2, lhsT=u[vsrc], rhs=identF, start=True, stop=True), reads=[R_(vsrc), "gconst"], writes=[p2r])
            if KSUB < 3:
                return
            S.add("dve", lambda e: e.tensor_scalar(out=u["vb"], in0=p2, scalar1=chk[:, 16 + h:17 + h], scalar2=None, op0=ALU.mult),
                  reads=[p2r, "chkbeta"], writes=[R_("vb")])
            if KSTAGE < 4:
                return
            S.add("dve", lambda e: e.tensor_scalar(out=u["rep"], in0=onesF, scalar1=chk[:, 32 + h:33 + h], scalar2=None, op0=ALU.mult),
                  reads=["gconst", "chkg"], writes=[R_("rep")])
            p3, p3r = P(3)
            S.add("pe", lambda e: e.matmul(p3, lhsT=u["rep"], rhs=Umat, start=True, stop=True),
                  reads=[R_("rep"), "gconst"], writes=[p3r])
            S.add("act", lambda e: e.copy(out=u["Gbc"], in_=p3), reads=[p3r], writes=[R_("Gbc")])
            S.add("dve", lambda e: e.tensor_scalar(out=u["rep"], in0=onesF, scalar1=chk[:, 16 + h:17 + h], scalar2=None, op0=ALU.mult),
                  reads=["gconst", "chkbeta"], writes=[R_("rep")])
            p4, p4r = P(4)
            S.add("pe", lambda e: e.matmul(p4, lhsT=u["rep"], rhs=identF, start=True, stop=True),
                  reads=[R_("rep"), "gconst"], writes=[p4r])
            S.add("dve", lambda e: e.tensor_tensor(out=u["kbT"], in0=u["kn"], in1=p4, op=ALU.mult),
                  reads=[R_("kn"), p4r], writes=[R_("kbT")])
            if KSTAGE < 5:
                return
            p5, p5r = P(5)
            p6, p6r = P(6)
            S.add("pe", lambda e: e.matmul(p5, lhsT=u["kbT"], rhs=u["kn"], start=True, stop=True),
                  reads=[R_("kbT"), R_("kn")], writes=[p5r])
            S.add("pe", lambda e: e.matmul(p6, lhsT=u["kn"], rhs=u["kbT"], start=True, stop=True),
                  reads=[R_("kbT"), R_("kn")], writes=[p6r])
            gcol = chk[:, 40 + h:41 + h]
            S.add("dve", lambda e: e.tensor_scalar(out=u["t1"], in0=u["Gbc"], scalar1=gcol, scalar2=0.0, op0=ALU.subtract, op1=ALU.max),
                  reads=[R_("Gbc"), "chkG"], writes=[R_("t1")])
            S.add("act", lambda e: e.activation(out=u["t1"], in_=u["t1"], func=AF.Exp, scale=-1.0), reads=[R_("t1")], writes=[R_("t1")])
            S.add("dve", lambda e: e.tensor_tensor(out=u["t1"], in0=u["t1"], in1=maskLn, op=ALU.mult),
                  reads=[R_("t1"), "gconst"], writes=[R_("t1")])
            S.add("dve", lambda e: e.tensor_tensor(out=u["QT"], in0=u["t1"], in1=p5, op=ALU.mult),
                  reads=[R_("t1"), p5r], writes=[R_("QT")])
            S.add("dve", lambda e: e.tensor_scalar(out=u["t2"], in0=u["Gbc"], scalar1=gcol, scalar2=0.0, op0=ALU.subtract, op1=ALU.min),
                  reads=[R_("Gbc"), "chkG"], writes=[R_("t2")])
            S.add("act", lambda e: e.activation(out=u["t2"], in_=u["t2"], func=AF.Exp), reads=[R_("t2")], writes=[R_("t2")])
            S.add("dve", lambda e: e.tensor_tensor(out=u["t3"], in0=u["t2"], in1=maskUd, op=ALU.mult),
                  reads=[R_("t2"), "gconst"], writes=[R_("t3")])
            S.add("dve", lambda e: e.tensor_tensor(out=u["t2"], in0=u["t2"], in1=maskUn, op=ALU.mult),
                  reads=[R_("t2"), "gconst"], writes=[R_("t2")])
            S.add("dve", lambda e: e.tensor_tensor(out=u["Q"], in0=u["t2"], in1=p6, op=ALU.mult),
                  reads=[R_("t2"), p6r], writes=[R_("Q")])
            S.add("dve", lambda e: e.tensor_tensor(out=u["R"], in0=u["Q"], in1=identF, op=ALU.add),
                  reads=[R_("Q"), "gconst"], writes=[R_("R")])
            if KSTAGE < 6:
                return
            for lvl in range(1, 7):
                pa, par = P(7)
                pq, pqr = P(8)
                S.add("pe", lambda e, pa=pa: e.matmul(pa, lhsT=u["Q"], rhs=u["QT"], start=True, stop=True),
                      reads=[R_("Q"), R_("QT")], writes=[par])
                if lvl < 6:
                    S.add("pe", lambda e, pq=pq: e.matmul(pq, lhsT=u["QT"], rhs=u["Q"], start=True, stop=True),
                          reads=[R_("Q"), R_("QT")], writes=[pqr])
                S.add("act", lambda e, pa=pa: e.copy(out=u["QT"], in_=pa), reads=[par], writes=[R_("QT")])
                if lvl < 6:
                    S.add("dve", lambda e, pq=pq: e.tensor_copy(out=u["Q"], in_=pq), reads=[pqr], writes=[R_("Q")])
                pr, prr = P(9)
                S.add("pe", lambda e, pr=pr: e.matmul(pr, lhsT=u["QT"], rhs=u["R"], start=True, stop=True),
                      reads=[R_("QT"), R_("R")], writes=[prr])
                S.add("dve", lambda e, pr=pr: e.tensor_tensor(out=u["R"], in0=u["R"], in1=pr, op=ALU.add),
                      reads=[R_("R"), prr], writes=[R_("R")])
            if KSTAGE < 7:
                return
            S.add("act", lambda e: e.activation(out=u["t1"], in_=u["qc"], func=AF.Square), reads=[R_("qc")], writes=[R_("t1")])
            pq0, pq0r = P(1)
            S.add("pe", lambda e: e.matmul(pq0, lhsT=onesF, rhs=u["t1"], start=True, stop=True),
                  reads=["gconst", R_("t1")], writes=[pq0r])
            S.add("dve", lambda e: e.tensor_scalar(out=u["t1"], in0=pq0, scalar1=1e-6, scalar2=None, op0=ALU.add),
                  reads=[pq0r], writes=[R_("t1")])
            S.add("act", lambda e: e.sqrt(out=u["t1"], in_=u["t1"]), reads=[R_("t1")], writes=[R_("t1")])
            S.add("dve", lambda e: e.reciprocal(out=u["t1"], in_=u["t1"]), reads=[R_("t1")], writes=[R_("t1")])
            S.add("dve", lambda e: e.scalar_tensor_tensor(out=u["qc"], in0=u["qc"], scalar=128.0 ** -0.5, in1=u["t1"],
                                                          op0=ALU.mult, op1=ALU.mult),
                  reads=[R_("qc"), R_("t1")], writes=[R_("qc")])
            S.add("act", lambda e: e.activation(out=u["eG"], in_=u["Gbc"], func=AF.Exp), reads=[R_("Gbc")], writes=[R_("eG")])
            S.add("dve", lambda e: e.tensor_tensor(out=u["qg"], in0=u["qc"], in1=u["eG"], op=ALU.mult),
                  reads=[R_("qc"), R_("eG")], writes=[R_("qg")])
            pqk, pqkr = P(2)
            S.add("pe", lambda e: e.matmul(pqk, lhsT=u["kn"], rhs=u["qc"], start=True, stop=True),
                  reads=[R_("kn"), R_("qc")], writes=[pqkr])
            S.add("dve", lambda e: e.tensor_tensor(out=u["qkT"], in0=u["t3"], in1=pqk, op=ALU.mult),
                  reads=[R_("t3"), pqkr], writes=[R_("qkT")])
            sc = u["sc"]
            S.add("act", lambda e: e.activation(out=sc[:, 0:1], in_=gcol, func=AF.Exp), reads=["chkG"], writes=[R_("sc")])
            S.add("dve", lambda e: e.tensor_tensor(out=sc[:, 0:1], in0=sc[:, 0:1], in1=chk[:, 16 + h:17 + h], op=ALU.mult),
                  reads=[R_("sc"), "chkbeta"], writes=[R_("sc")])
            S.add("dve", lambda e: e.tensor_tensor(out=sc[:, 1:2], in0=u["Gbc"][:, 127:128], in1=gcol, op=ALU.subtract),
                  reads=[R_("Gbc"), "chkG"], writes=[R_("sc")])
            S.add("act", lambda e: e.activation(out=sc[:, 1:2], in_=sc[:, 1:2], func=AF.Exp), reads=[R_("sc")], writes=[R_("sc")])
            S.add("act", lambda e: e.activation(out=sc[:, 2:3], in_=u["Gbc"][:, 127:128], func=AF.Exp),
                  reads=[R_("Gbc")], writes=[R_("sc")])
            S.add("dve", lambda e: e.tensor_scalar(out=u["kg"], in0=u["ktm"], scalar1=sc[:, 0:1], scalar2=None, op0=ALU.mult),
                  reads=[R_("ktm"), R_("sc")], writes=[R_("kg")])
            S.add("dve", lambda e: e.tensor_scalar(out=u["kd"], in0=u["ktm"], scalar1=sc[:, 1:2], scalar2=None, op0=ALU.mult),
                  reads=[R_("ktm"), R_("sc")], writes=[R_("kd")])
            if KSTAGE < 8:
                return
            p10, p10r = P(10)
            S.add("pe", lambda e: e.matmul(p10, lhsT=u["kg"], rhs=u["R"], start=True, stop=True),
                  reads=[R_("kg"), R_("R")], writes=[p10r])
            S.add("act", lambda e: e.mul(out=u["nwT"], in_=p10, mul=-1.0), reads=[p10r], writes=[R_("nwT")])
            p11, p11r = P(11)
            S.add("pe", lambda e: e.matmul(p11, lhsT=u["R"], rhs=u["vb"], start=True, stop=False),
                  reads=[R_("R"), R_("vb")], writes=[p11r])
            S.add("pe", lambda e: e.matmul(p11, lhsT=u["nwT"], rhs=Sst[h], start=False, stop=True),
                  reads=[R_("nwT"), f"S{h}"], writes=[p11r])
            S.add("act", lambda e: e.copy(out=u["vn"], in_=p11), reads=[p11r], writes=[R_("vn")])
            po, por = P(3)
            S.add("pe", lambda e: e.matmul(po, lhsT=Sst[h], rhs=u["qg"], start=True, stop=False),
                  reads=[f"S{h}", R_("qg")], writes=[por])
            S.add("pe", lambda e: e.matmul(po, lhsT=u["vn"], rhs=u["qkT"], start=False, stop=True),
                  reads=[R_("vn"), R_("qkT")], writes=[por])
            S.add("act", lambda e: e.activation(out=u["t1"], in_=po, func=AF.Square), reads=[por], writes=[R_("t1")])
            pss, pssr = P(4)
            S.add("pe", lambda e: e.matmul(pss, lhsT=onesF, rhs=u["t1"], start=True, stop=True),
                  reads=["gconst", R_("t1")], writes=[pssr])
            S.add("dve", lambda e: e.tensor_scalar(out=u["t1"], in0=pss, scalar1=1.0 / 128, scalar2=EPS, op0=ALU.mult, op1=ALU.add),
                  reads=[pssr], writes=[R_("t1")])
            S.add("act", lambda e: e.sqrt(out=u["t1"], in_=u["t1"]), reads=[R_("t1")], writes=[R_("t1")])
            S.add("dve", lambda e: e.reciprocal(out=u["t1"], in_=u["t1"]), reads=[R_("t1")], writes=[R_("t1")])
            S.add("dve", lambda e: e.tensor_tensor(out=u["on"], in0=u["t1"], in1=po, op=ALU.mult),
                  reads=[R_("t1"), por], writes=[R_("on")])
            zrow = 5120 + 128 * h
            S.add("sp", lambda e: e.dma_start(out=u["zs"][:, :nt_], in_=projT[zrow:zrow + 128, tok0:tok0 + nt_]),
                  writes=[R_("zs")], slot=R_("zs"))
            S.add("act", lambda e: e.activation(out=u["zs"][:, :nt_], in_=u["zs"][:, :nt_], func=AF.Silu), reads=[R_("zs")], writes=[R_("zs")])
            S.add("dve", lambda e: e.scalar_tensor_tensor(out=finb[p][:, :nt_], in0=u["on"][:, :nt_], scalar=hb[:, 17:18], in1=u["zs"][:, :nt_],
                                                          op0=ALU.mult, op1=ALU.mult),
                  reads=[R_("on"), "hb", R_("zs")], writes=[f"finb{p}"])
            S.add("sp", lambda e: e.dma_start(out=mixT_d[1024 + 128 * h:1024 + 128 * h + 128, tok0:tok0 + nt_], in_=finb[p][:, :nt_]),
                  reads=[f"finb{p}"], slot=f"finbo{p}")
            S.add("pe", lambda e: e.matmul(p0, lhsT=u["kd"], rhs=u["vn"], start=True, stop=True),
                  reads=[R_("kd"), R_("vn")], writes=[p0r])
            S.add("dve", lambda e: e.scalar_tensor_tensor(out=Sst[h], in0=Sst[h], scalar=sc[:, 2:3], in1=p0,
                                                          op0=ALU.mult, op1=ALU.add),
                  reads=[f"S{h}", R_("sc"), p0r], writes=[f"S{h}"])

        for h in range(8):
            unit(h)

    for c in range(0 if attn_only else KCHUNKS):
        gdn_chunk(c * 128, 128, None if c == 0 else "prev", False)
    for h in range(8):
        S.add("sp", lambda e, h=h: e.dma_start(out=ssm_p[h], in_=Sst[h]), reads=[f"S{h}"], slot=f"ssmo{h}")
        S.add("sp", lambda e, h=h: e.dma_start(out=Sst[h], in_=ssm0_d[h]), writes=[f"S{h}"], slot=f"ssmi{h}")
    if KSAMPLE and not attn_only:
        gdn_chunk(SEQ, NS, "state", True)
    for h in range(8):
        S.add("sp", lambda e, h=h: e.dma_start(out=ssm_s[h], in_=Sst[h]), reads=[f"S{h}"], slot=f"ssmo{h}")


    S.fence()
    hoff = [0]

    def balloc(ncols):
        o = hoff[0]
        hoff[0] += ncols
        assert hoff[0] <= FC * TWX
        return hT[:, o:o + ncols]

    A_pos = Y[0][0:1, :]
    A_ones = Y[1][0:1, :]
    A_kb1 = Y[2][0:1, :]
    A_qb0 = [Y[3][0:1, :], Y[4][0:1, :]]
    A_m = X[0][0:1, :]
    A_sm = X[0][0:1, 0:0]
    xo = [0]

    def xalloc(ncols):
        o = xo[0]
        xo[0] += ncols
        assert xo[0] <= D
        return X[1][:, o:o + ncols]

    a_ident = xalloc(128)
    a_mask = xalloc(128)
    a_O = [[xalloc(128) for s_ in range(4)] for m_ in range(2)]
    a_pd = xalloc(128)
    a_fin = xalloc(128)
    a_sub = xalloc(128)
    a_lam = xalloc(256)
    a_small = xalloc(32)
    a_row = xalloc(0)
    S.add("sp", lambda e: e.dma_start(out=A_pos, in_=aconst[0:1, 0:2048]), writes=["A_pos"], slot="A_pos")
    S.add("sp", lambda e: e.dma_start(out=a_lam, in_=aconst[:, 2048:2304]), writes=["a_lam"], slot="a_lam")
    S.add("sp", lambda e: e.dma_start(out=a_sub, in_=aconst[:, 2304:2432]), writes=["a_sub"], slot="a_sub")
    S.add("sp", lambda e: e.dma_start(out=a_ident, in_=gconst[:, 0:128]), writes=["a_ident"], slot="a_ident")
    S.add("sp", lambda e: e.dma_start(out=a_mask, in_=gconst[:, 640:768]), writes=["a_mask"], slot="a_mask")
    S.add("dve", lambda e: e.memset(A_ones, 1.0), writes=["A_ones"])
    S.add("dve", lambda e: e.tensor_scalar(out=a_sub, in0=a_sub, scalar1=0.8, scalar2=None, op0=ALU.mult),
          reads=["a_sub"], writes=["a_sub"])
    for i_ in range(2):
        S.add("dve", lambda e, i_=i_: e.tensor_tensor(out=a_lam[:, 128 * i_:128 * i_ + 64], in0=a_lam[:, 128 * i_:128 * i_ + 64],
                                                   in1=a_lam[:, 128 * i_ + 64:128 * i_ + 128], op=ALU.mult),
              reads=["a_lam"], writes=["a_lam"])
        S.add("dve", lambda e, i_=i_: e.reduce_sum(out=a_small[:, i_:i_ + 1], in_=a_lam[:, 128 * i_:128 * i_ + 64], axis=mybir.AxisListType.X),
              reads=["a_lam"], writes=["a_small"])
        S.add("act", lambda e, i_=i_: e.activation(out=a_small[:, i_:i_ + 1], in_=a_small[:, i_:i_ + 1], func=AF.Exp),
              reads=["a_small"], writes=["a_small"])
    S.add("dve", lambda e: e.tensor_tensor(out=a_small[:, 2:3], in0=a_small[:, 1:2], in1=a_small[:, 0:1], op=ALU.subtract),
          reads=["a_small"], writes=["a_small"])
    S.add("dve", lambda e: e.tensor_scalar(out=a_small[:, 2:3], in0=a_small[:, 2:3], scalar1=-0.2, scalar2=None, op0=ALU.add),
          reads=["a_small"], writes=["a_small"])
    neglam = a_small[:, 2:3]

    qTb = [balloc(SEQ) for _ in range(2)]
    kTb = [balloc(SEQ) for _ in range(2)]
    sqb = balloc(SEQ)
    Vx = [balloc(16 * 129) for _ in range(2)]
    Eb = [balloc(512) for _ in range(2)]
    onesb = balloc(8)
    finb2 = balloc(512)
    S.add("dve", lambda e: e.memset(onesb, 1.0), writes=["onesb"])

    AX = mybir.AxisListType.X

    def attn_head(h):
        hp = h % 2
        slope = 2.0 ** (-(h + 1))
        q_, k_, vx = qTb[hp], kTb[hp], Vx[hp]
        vx3 = vx.rearrange("p (b c) -> p b c", c=129)
        S.add("pool", lambda e: e.dma_start(out=q_, in_=projT[128 * h:128 * h + 128, 0:SEQ]), writes=[f"qTb{hp}"], slot=f"qTb{hp}")
        S.add("pool", lambda e: e.dma_start(out=k_, in_=projT[1024 + 128 * h:1024 + 128 * h + 128, 0:SEQ]), writes=[f"kTb{hp}"], slot=f"kTb{hp}")
        S.add("dve", lambda e: e.memset(vx, 1.0), writes=[f"Vx{hp}"])
        S.add("pool", lambda e: e.dma_start(out=vx3[:, :, 0:128],
                                            in_=v_out[0:SEQ, 128 * h:128 * h + 128].rearrange("(b p) c -> p b c", p=128)),
              writes=[f"Vx{hp}"], slot=f"Vx{hp}")
        S.add("dve", lambda e: e.tensor_scalar(out=A_kb1, in0=A_pos, scalar1=8.0 * slope, scalar2=None, op0=ALU.mult),
              reads=["A_pos"], writes=["A_kb1"])

        def sumsq(src, lo, cb, srcres):
            S.add("pe", lambda e: e.matmul(bank[6][0:1, :], lhsT=onesb[lo:lo + 64, 0:1], rhs=sqb[lo:lo + 64, cb * 512:(cb + 1) * 512],
                                           start=True, stop=True), reads=["onesb", "sqb"], writes=[N(bank[6])])

        def norms(m):
            lo = 64 * m
            S.add("act", lambda e: e.activation(out=sqb[lo:lo + 64, :], in_=k_[lo:lo + 64, :], func=AF.Square),
                  reads=[f"kTb{hp}"], writes=["sqb"])

            def kmax(cb):
                sumsq(k_, lo, cb, f"kTb{hp}")
                S.add("dve", lambda e: e.reduce_max(out=a_small[0:1, 4 + cb:5 + cb], in_=bank[6][0:1, :], axis=AX),
                      reads=[N(bank[6])], writes=["a_small"])

            for cb in range(4):
                kmax(cb)
            S.add("dve", lambda e: e.reduce_max(out=a_small[0:1, 8:9], in_=a_small[0:1, 4:8], axis=AX),
                  reads=["a_small"], writes=["a_small"])
            S.add("act", lambda e: e.activation(out=sqb[lo:lo + 64, :], in_=q_[lo:lo + 64, :], func=AF.Square),
                  reads=[f"qTb{hp}"], writes=["sqb"])

            def qn(cb):
                sumsq(q_, lo, cb, f"qTb{hp}")
                S.add("dve", lambda e: e.tensor_scalar(out=A_m[:, cb * 512:(cb + 1) * 512], in0=bank[6][0:1, :],
                                                       scalar1=a_small[0:1, 8:9], scalar2=1.21, op0=ALU.mult, op1=ALU.mult),
                      reads=[N(bank[6]), "a_small"], writes=["A_m"])

            for cb in range(4):
                qn(cb)
            S.add("act", lambda e: e.sqrt(out=A_m, in_=A_m), reads=["A_m"], writes=["A_m"])
            S.add("dve", lambda e: e.scalar_tensor_tensor(out=A_qb0[m], in0=A_pos, scalar=-8.0 * slope, in1=A_m,
                                                          op0=ALU.mult, op1=ALU.subtract),
                  reads=["A_pos", "A_m"], writes=[f"A_qb0{m}"])

        for m in range(2):
            norms(m)

        def qk_unit(Q, m, j, accs):
            lo = 64 * m
            eb = cnt["evac"] % 2
            cnt["evac"] += 1
            psb = bank[4 + eb]
            E = Eb[eb]
            S.add("pe", lambda e: e.matmul(psb[:, :], lhsT=k_[lo:lo + 64, 128 * j:128 * j + 128],
                                           rhs=q_[lo:lo + 64, 512 * Q:512 * Q + 512], start=True, stop=False),
                  reads=[f"kTb{hp}", f"qTb{hp}"], writes=[N(psb)])
            S.add("pe", lambda e: e.matmul(psb[:, :], lhsT=A_ones[:, 0:128], rhs=A_qb0[m][:, 512 * Q:512 * Q + 512],
                                           start=False, stop=False),
                  reads=["A_ones", f"A_qb0{m}"], writes=[N(psb)])
            S.add("pe", lambda e: e.matmul(psb[:, :], lhsT=A_kb1[:, 128 * j:128 * j + 128], rhs=A_ones[:, 0:512],
                                           start=False, stop=True),
                  reads=["A_ones", "A_kb1"], writes=[N(psb)])
            s0 = max(0, j - 4 * Q)
            S.add("act", lambda e: e.activation(out=E[:, 128 * s0:512], in_=psb[:, 128 * s0:512], func=AF.Exp, scale=0.125),
                  reads=[N(psb)], writes=[f"Eb{eb}"])

            def pv(s_):
                qi = 4 * Q + s_
                if j > qi:
                    return
                if j == qi:
                    S.add("dve", lambda e: e.tensor_tensor(out=E[:, 128 * s_:128 * s_ + 128], in0=E[:, 128 * s_:128 * s_ + 128],
                                                           in1=a_mask, op=ALU.mult),
                          reads=[f"Eb{eb}", "a_mask"], writes=[f"Eb{eb}"])
                acc, accr = accs[s_]
                S.add("pe", lambda e: e.matmul(acc, lhsT=E[:, 128 * s_:128 * s_ + 128], rhs=vx3[:, j, :],
                                               start=(j == 0), stop=(j == qi)),
                      reads=[f"Eb{eb}", f"Vx{hp}"], writes=[accr])

            for s_ in range(4):
                pv(s_)

        def qblock_map(Q, m):
            accs = [(bank[s_][:, 0:129], N(bank[s_])) for s_ in range(4)]
            for j in range(4 * Q + 4):
                qk_unit(Q, m, j, accs)

            def fin_acc(s_):
                acc, accr = accs[s_]
                S.add("dve", lambda e: e.reciprocal(out=a_small[:, 16 + s_:17 + s_], in_=acc[:, 128:129]),
                      reads=[accr], writes=["a_small"])
                S.add("dve", lambda e: e.tensor_scalar(out=a_O[m][s_], in0=acc[:, 0:128], scalar1=a_small[:, 16 + s_:17 + s_],
                                                       scalar2=None, op0=ALU.mult),
                      reads=[accr, "a_small"], writes=[f"a_O{m}{s_}"])

            for s_ in range(4):
                fin_acc(s_)

        def fin_sub(Q, s_):
            S.add("dve", lambda e: e.scalar_tensor_tensor(out=a_pd, in0=a_O[1][s_], scalar=neglam, in1=a_O[0][s_],
                                                          op0=ALU.mult, op1=ALU.add),
                  reads=[f"a_O0{s_}", f"a_O1{s_}", "a_small"], writes=["a_pd"])
            S.add("act", lambda e: e.activation(out=a_fin, in_=a_pd, func=AF.Square, accum_out=a_small[:, 20:21]),
                  reads=["a_pd"], writes=["a_fin", "a_small"])
            S.add("dve", lambda e: e.tensor_scalar(out=a_small[:, 21:22], in0=a_small[:, 20:21], scalar1=1.0 / 128, scalar2=EPS,
                                                   op0=ALU.mult, op1=ALU.add), reads=["a_small"], writes=["a_small"])
            S.add("act", lambda e: e.sqrt(out=a_small[:, 21:22], in_=a_small[:, 21:22]), reads=["a_small"], writes=["a_small"])
            S.add("dve", lambda e: e.reciprocal(out=a_small[:, 22:23], in_=a_small[:, 21:22]), reads=["a_small"], writes=["a_small"])
            S.add("dve", lambda e: e.scalar_tensor_tensor(out=a_fin, in0=a_pd, scalar=a_small[:, 22:23], in1=a_sub,
                                                          op0=ALU.mult, op1=ALU.mult),
                  reads=["a_pd", "a_small", "a_sub"], writes=["a_fin"])
            S.add("pe", lambda e: e.matmul(bank[6][:, 0:128], lhsT=a_fin, rhs=a_ident, start=True, stop=True),
                  reads=["a_fin", "a_ident"], writes=[N(bank[6])])
            S.add("act", lambda e: e.copy(out=finb2[:, 128 * s_:128 * s_ + 128], in_=bank[6][:, 0:128]),
                  reads=[N(bank[6])], writes=["finb2"])

        def qblock(Q):
            for m in range(2):
                qblock_map(Q, m)
            for s_ in range(4):
                fin_sub(Q, s_)
            S.add("sp", lambda e: e.dma_start(out=mixT_d[128 * h:128 * h + 128, 512 * Q:512 * Q + 512], in_=finb2),
                  reads=["finb2"], slot="finb2o")

        for Q in range(4):
            qblock(Q)

    for h in range(aheads):
        attn_head(h)

    def sample_attn():
        S.fence()
        KV = [Y[0], Y[1]]
        KTs = [Y[2][:, 0:1024], Y[2][:, 1024:2048]]
        C = Y[3]
        posp, slopeRow, maskS = C[:, 0:128], C[:, 128:256], C[:, 256:384]
        pcol, subcol = C[:, 384:385], C[:, 385:386]
        identS, onesS = C[:, 512:640], C[:, 640:768]
        qs, knew = C[:, 768:832], C[:, 832:896]
        Sb = [C[:, 896:1024], C[:, 1024:1152]]
        Es = [C[:, 1152:1280], C[:, 1280:1408]]
        Rr, On, pd, sq, rstd = C[:, 1408:1536], C[:, 1536:1664], C[:, 1664:1728], C[:, 1728:1792], C[:, 1792:1856]
        vnew = Y[4][0:8, 0:1024]
        fins = finb[0][:, 0:64]
        qs3 = qs.rearrange("p (h q) -> p h q", q=8)
        knew3 = knew.rearrange("p (h q) -> p h q", q=8)
        S.add("sp", lambda e: e.dma_start(out=C[:, 0:512], in_=sconst), writes=["sC"], slot="sC")
        S.add("sp", lambda e: e.dma_start(out=identS, in_=gconst[:, 0:128]), writes=["sI"], slot="sI")
        S.add("sp", lambda e: e.dma_start(out=onesS, in_=gconst[:, 512:640]), writes=["sO"], slot="sO")
        S.add("sp", lambda e: e.dma_start(out=ptidx[:], in_=pt_d), writes=["ptidx"], slot="ptidx")
        S.add("dve", lambda e: e.tensor_scalar(out=ptidx2[:], in0=ptidx[:], scalar1=128, scalar2=None, op0=ALU.mult),
              reads=["ptidx"], writes=["ptidx2"])
        S.add("sp", lambda e: e.dma_start(out=qs3, in_=projT[0:1024, SEQ:SEQ + NS].rearrange("(h p) q -> p h q", p=128)),
              writes=["qs"], slot="qs")
        S.add("sp", lambda e: e.dma_start(out=knew3, in_=projT[1024:2048, SEQ:SEQ + NS].rearrange("(h p) q -> p h q", p=128)),
              writes=["knew"], slot="knew")
        S.add("sp", lambda e: e.dma_start(out=vnew, in_=v_out[SEQ:SEQ + NS, :]), writes=["vnew"], slot="vnew")
        SA = int(os.environ.get("SA_STAGE", "9"))
        qm = C[:, 1856:1984]
        qm4 = qm.rearrange("p (h m q) -> p h m q", m=2, q=8)
        S.add("dve", lambda e: e.memset(qm, 0.0), writes=["qm"])
        S.add("dve", lambda e: e.tensor_scalar(out=qm4[0:64, :, 0, :], in0=qs3[0:64, :, :], scalar1=0.125, scalar2=None, op0=ALU.mult),
              reads=["qs"], writes=["qm"])
        S.add("dve", lambda e: e.tensor_scalar(out=qm4[64:128, :, 1, :], in0=qs3[64:128, :, :], scalar1=0.125, scalar2=None, op0=ALU.mult),
              reads=["qs"], writes=["qm"])
        accO, accOr = bank[3][:, 0:128], N(bank[3])
        accR, accRr = bank[4][:, 0:128], N(bank[4])
        scb, scbr = bank[2], N(bank[2])

        def step(tok):
            b = tok % 2
            tb = (bank[0], bank[1]) if b == 0 else (bank[5], bank[6])
            S.add("pool", lambda e: e.indirect_dma_start(
                out=KV[b][:, 0:1024], out_offset=None, in_=ck_d[:, :],
                in_offset=bass.IndirectOffsetOnAxis(ap=ptidx2[:, 0:1], axis=0), element_offset=tok * 1024),
                reads=["ptidx2"], writes=[f"KVk{b}"], slot=f"KVk{b}")
            S.add("pool", lambda e: e.indirect_dma_start(
                out=KV[b][:, 1024:2048], out_offset=None, in_=cv_d[:, :],
                in_offset=bass.IndirectOffsetOnAxis(ap=ptidx2[:, 0:1], axis=0), element_offset=tok * 1024),
                reads=["ptidx2"], writes=[f"KVv{b}"], slot=f"KVv{b}")

            def tr(half):
                for hh in range(4):
                    h = half * 4 + hh
                    S.add("pe", lambda e, h=h, hh=hh: e.matmul(tb[half][:, hh * 128:(hh + 1) * 128], lhsT=KV[b][:, h * 128:(h + 1) * 128],
                                                               rhs=identS, start=True, stop=True),
                          reads=[f"KVk{b}", "sI"], writes=[N(tb[half])])
                if half == 0:
                    S.add("act", lambda e: e.copy(out=KTs[b][:, 0:512], in_=tb[0][:, :]), reads=[N(tb[0])], writes=[f"KT{b}a"])
                else:
                    S.add("dve", lambda e: e.tensor_copy(out=KTs[b][:, 512:1024], in_=tb[1][:, :]), reads=[N(tb[1])], writes=[f"KT{b}b"])

            if SA < 2:
                return
            tr(0)
            tr(1)
            if SA < 3:
                return
            for h in range(8):
                S.add("pe", lambda e, h=h: e.matmul(scb[:, h * 16:(h + 1) * 16],
                                                    lhsT=KTs[b][:, h * 128:(h + 1) * 128],
                                                    rhs=qm[:, h * 16:(h + 1) * 16], start=True, stop=True),
                      reads=[f"KT{b}a" if h < 4 else f"KT{b}b", "qm"], writes=[scbr])
            S.add("dve", lambda e: e.scalar_tensor_tensor(out=Sb[b], in0=slopeRow, scalar=posp[:, tok:tok + 1], in1=scb[:, 0:128],
                                                          op0=ALU.mult, op1=ALU.add),
                  reads=["sC", scbr], writes=[f"Sb{b}"])
            S.add("act", lambda e: e.activation(out=Es[b], in_=Sb[b], func=AF.Exp), reads=[f"Sb{b}"], writes=[f"Es{b}"])
            if SA < 4:
                return
            for h in range(8):
                S.add("pe", lambda e, h=h: e.matmul(accO[:, h * 16:(h + 1) * 16], lhsT=KV[b][:, 1024 + h * 128:1024 + (h + 1) * 128],
                                                    rhs=Es[b][:, h * 16:(h + 1) * 16], start=False, stop=False),
                      reads=[f"KVv{b}", f"Es{b}"], writes=[accOr])
            S.add("pe", lambda e: e.matmul(accR, lhsT=onesS, rhs=Es[b], start=(tok == 0), stop=False),
                  reads=["sO", f"Es{b}"], writes=[accRr])

        zerosS = Y[4][:, 1024:1152]
        S.add("dve", lambda e: e.memset(zerosS, 0.0), writes=["zerosS"])
        if SA >= 4:
            S.add("pe", lambda e: e.matmul(accO, lhsT=onesS, rhs=zerosS, start=True, stop=False),
                  reads=["sO", "zerosS"], writes=[accOr])
        for tok in range(128):
            step(tok)
        if SA < 5:
            return
        for h in range(8):
            S.add("pe", lambda e, h=h: e.matmul(scb[0:NS, h * 16:(h + 1) * 16],
                                                lhsT=knew3[:, h, :],
                                                rhs=qm[:, h * 16:(h + 1) * 16], start=True, stop=True),
                  reads=["knew", "qm"], writes=[scbr])
        S.add("dve", lambda e: e.scalar_tensor_tensor(out=Sb[0][0:NS, :], in0=slopeRow[0:NS, :], scalar=pcol[0:NS, :], in1=scb[0:NS, 0:128],
                                                      op0=ALU.mult, op1=ALU.add),
              reads=["sC", scbr], writes=["Sb0"])
        S.add("act", lambda e: e.activation(out=Es[0][0:NS, :], in_=Sb[0][0:NS, :], func=AF.Exp), reads=["Sb0"], writes=["Es0"])
        S.add("dve", lambda e: e.tensor_tensor(out=Es[0][0:NS, :], in0=Es[0][0:NS, :], in1=maskS[0:NS, :], op=ALU.mult),
              reads=["Es0", "sC"], writes=["Es0"])
        for h in range(8):
            S.add("pe", lambda e, h=h: e.matmul(accO[:, h * 16:(h + 1) * 16], lhsT=vnew[:, h * 128:(h + 1) * 128],
                                                rhs=Es[0][0:NS, h * 16:(h + 1) * 16], start=False, stop=(h == 7)),
                  reads=["vnew", "Es0"], writes=[accOr])
        S.add("pe", lambda e: e.matmul(accR, lhsT=onesS[0:NS, :], rhs=Es[0][0:NS, :], start=False, stop=True),
              reads=["sO", "Es0"], writes=[accRr])
        S.add("dve", lambda e: e.reciprocal(out=Rr, in_=accR), reads=[accRr], writes=["sRr"])
        S.add("dve", lambda e: e.tensor_tensor(out=On, in0=accO, in1=Rr, op=ALU.mult), reads=[accOr, "sRr"], writes=["sOn"])
        On4 = On.rearrange("p (h m q) -> p h m q", m=2, q=8)
        pd3 = pd.rearrange("p (h q) -> p h q", q=8)
        S.add("dve", lambda e: e.scalar_tensor_tensor(out=pd3, in0=On4[:, :, 1, :], scalar=neglam, in1=On4[:, :, 0, :],
                                                      op0=ALU.mult, op1=ALU.add),
              reads=["sOn", "a_small"], writes=["spd"])
        S.add("act", lambda e: e.activation(out=sq, in_=pd, func=AF.Square), reads=["spd"], writes=["ssq"])
        S.add("pe", lambda e: e.matmul(scb[:, 0:64], lhsT=onesS, rhs=sq, start=True, stop=True), reads=["sO", "ssq"], writes=[scbr])
        S.add("dve", lambda e: e.tensor_scalar(out=rstd, in0=scb[:, 0:64], scalar1=1.0 / 128, scalar2=EPS, op0=ALU.mult, op1=ALU.add),
              reads=[scbr], writes=["srstd"])
        S.add("act", lambda e: e.sqrt(out=rstd, in_=rstd), reads=["srstd"], writes=["srstd"])
        S.add("dve", lambda e: e.reciprocal(out=rstd, in_=rstd), reads=["srstd"], writes=["srstd"])
        S.add("dve", lambda e: e.tensor_tensor(out=pd, in0=pd, in1=rstd, op=ALU.mult), reads=["spd", "srstd"], writes=["spd"])
        S.add("dve", lambda e: e.tensor_scalar(out=fins, in0=pd, scalar1=subcol, scalar2=0.8, op0=ALU.mult, op1=ALU.mult),
              reads=["spd", "sC"], writes=["finb0"])
        S.add("sp", lambda e: e.dma_start(out=mixT_d[0:1024, SEQ:SEQ + NS].rearrange("(h p) q -> p h q", p=128),
                                          in_=fins.rearrange("p (h q) -> p h q", q=8)),
              reads=["finb0"], slot="finbo0")

    if not skip_sample:
        sample_attn()

    if not skip_p3:
        S.fence()
        for k2 in ("f2pre", "f2post", "mixpost"):
            S.add("sp", lambda e, k2=k2: e.dma_start(out=wbc[k2][:], in_=wbc2_d[k2]), writes=["wbc" + wres[k2]], slot="wbc" + wres[k2])
        for t in range(NT):
            phase3_tile(t)

    S.final_slots = list(S.dma_cnt.keys())
    S.emit(nc, stack)
    stack.close()
    return nc


def fm_blocks(w, gg):
    K, N = w.shape
    W = 128 * gg
    a = w.reshape(K // 128, 128, N // W, W).transpose(2, 1, 0, 3)
    return np.ascontiguousarray(a).reshape(N // W, 128, (K // 128) * W)


def tm_blocks(w, kgsz):
    K, N = w.shape
    a = w.reshape(K // (128 * kgsz), kgsz, 128, N // 512, 512).transpose(3, 0, 2, 1, 4)
    return np.ascontiguousarray(a).reshape(N // 512, K // (128 * kgsz), 128, kgsz * 512)


def gconst_host():
    i = np.arange(128)
    ident = np.eye(128, dtype=np.float32)
    U = (i[:, None] <= i[None, :]).astype(np.float32)
    mL = -(i[:, None] > i[None, :]).astype(np.float32)
    mU = -(i[None, :] > i[:, None]).astype(np.float32)
    ones = np.ones((128, 128), np.float32)
    mUd = (i[None, :] >= i[:, None]).astype(np.float32)
    return np.ascontiguousarray(np.concatenate([ident, U, mL, mU, ones, mUd], axis=1))


def hb_host(a_log, dt_bias, dnw=None):
    hb = np.zeros((128, 32), np.float32)
    hb[:, 0:8] = a_log[None, :]
    hb[:, 8:16] = dt_bias[None, :]
    hb[:NS, 16] = 1.0
    if dnw is not None:
        hb[:, 17] = dnw
    return hb


def aconst_host(lq1, lk1, lq2, lk2, subln):
    a = np.zeros((128, 2048 + 256 + 128), np.float32)
    a[0, 0:2048] = np.arange(2048, dtype=np.float32)
    a[:, 2048:2112] = lq1[None]
    a[:, 2112:2176] = lk1[None]
    a[:, 2176:2240] = lq2[None]
    a[:, 2240:2304] = lk2[None]
    a[:, 2304:2432] = subln[None]
    return a


def sconst_host(subln):
    a = np.zeros((128, 512), np.float32)
    pp = np.arange(128, dtype=np.float32)
    a[:, 0:128] = 128.0 * pp[:, None] + pp[None, :] - 16384.0
    col = np.arange(128)
    a[:, 128:256] = (2.0 ** (-(col // 16 + 1).astype(np.float32)))[None, :]
    a[:, 256:384] = ((col % 8)[None, :] >= np.arange(128)[:, None]).astype(np.float32)
    a[:, 384] = pp
    a[:, 385] = subln
    return a


_NC = None


def kernel(**inp):
    import ml_dtypes
    global _NC
    if _NC is None:
        _NC = build()
    nc = _NC
    f = lambda k: np.asarray(inp[k], dtype=np.float32)
    xp, xs = f("x_prompt"), f("x_sample")
    w_in = f("w_in")[0]
    fm_cols = np.concatenate([w_in[:, 0:2048], w_in[:, 3072:7168]], axis=1)
    shared = {
        "ident": np.eye(128, dtype=np.float32).astype(ml_dtypes.bfloat16),
        "wbc_f1pre": np.ascontiguousarray(np.broadcast_to(f("ffn1_pre_w")[0], (128, D))),
        "wbc_f1post": np.ascontiguousarray(np.broadcast_to(f("ffn1_post_w")[0], (128, D))),
        "wbc_mixpre": np.ascontiguousarray(np.broadcast_to(f("mix_pre_w")[0], (128, D))),
        "wg1": fm_blocks(f("ffn1_gate")[0], 2),
        "wu1": fm_blocks(f("ffn1_up")[0], 2),
        "wd1": tm_blocks(f("ffn1_down")[0], 11),
        "win_fm": fm_blocks(fm_cols, 2),
        "win_v": tm_blocks(w_in[:, 2048:3072], 4),
        "win_ba": np.ascontiguousarray(w_in[:, 7168:7184].reshape(KC, 128, 16).transpose(1, 0, 2)).reshape(128, KC * 16),
        "gconst": gconst_host(),
        "cw_d": np.ascontiguousarray(f("conv_w")[0].T.reshape(24, 128, 4).transpose(1, 0, 2)).reshape(128, 96),
        "hb_d": hb_host(f("a_log")[0], f("dt_bias")[0], f("delta_norm_w")[0]),
        "aconst": aconst_host(f("lambda_q1")[0], f("lambda_k1")[0], f("lambda_q2")[0], f("lambda_k2")[0], f("subln_w")[0]),
        "wbc_f2pre": np.ascontiguousarray(np.broadcast_to(f("ffn2_pre_w")[0], (128, D))),
        "wbc_f2post": np.ascontiguousarray(np.broadcast_to(f("ffn2_post_w")[0], (128, D))),
        "wbc_mixpost": np.ascontiguousarray(np.broadcast_to(f("mix_post_w")[0], (128, D))),
        "wg2": fm_blocks(f("ffn2_gate")[0], 2),
        "wu2": fm_blocks(f("ffn2_up")[0], 2),
        "wd2": tm_blocks(f("ffn2_down")[0], 11),
        "wout": tm_blocks(f("w_out")[0], 4),
        "cache_k": f("cache_k")[0].reshape(-1, 1024),
        "cache_v": f("cache_v")[0].reshape(-1, 1024),
        "sconst": sconst_host(f("subln_w")[0]),
    }
    page_table = np.asarray(inp["page_table"]).astype(np.int32)
    ssm0 = f("state_ssm")[0]
    convst = f("state_conv")[0]
    in_maps = []
    for c in range(NCORES):
        m = dict(shared)
        m["xin"] = np.ascontiguousarray(np.concatenate([xp[c % 4], xs[c]], axis=0))
        m["ssm0"] = np.ascontiguousarray(ssm0[c])
        m["convst"] = np.ascontiguousarray(convst[c].T.reshape(24, 128, 3))
        m["pt"] = np.ascontiguousarray(page_table[c].reshape(128, 1))
        in_maps.append(m)
    res = run_bass_kernel_spmd(nc, in_maps, core_ids=list(range(NCORES)))
    R = [{k: np.asarray(v) for k, v in r.items()} for r in res.results]
    B = 4
    k_prompt = np.stack([R[b]["projT"][1024:2048, :SEQ].T.reshape(SEQ, 8, 128) for b in range(B)])[None]
    v_prompt = np.stack([R[b]["v_out"][:SEQ].reshape(SEQ, 8, 128) for b in range(B)])[None]
    conv_prompt = np.stack([R[b]["projT"][2048:5120, SEQ - 3:SEQ].T for b in range(B)])[None]
    k_sample = np.stack([R[c]["projT"][1024:2048, SEQ:].T.reshape(NS, 8, 128) for c in range(8)])[None]
    v_sample = np.stack([R[c]["v_out"][SEQ:].reshape(NS, 8, 128) for c in range(8)])[None]
    conv_sample = np.stack([R[c]["projT"][2048:5120, NROW - 3:NROW].T for c in range(8)])[None]
    y_prompt = np.stack([R[b]["y"][:SEQ] for b in range(B)])
    y_sample = np.stack([R[c]["y"][SEQ:] for c in range(8)])
    ssm_prompt = np.stack([R[b]["ssm_p"] for b in range(B)])[None]
    ssm_sample = np.stack([R[c]["ssm_s"] for c in range(8)])[None]
    out = (y_prompt, y_sample, k_prompt, v_prompt, ssm_prompt, conv_prompt,
           k_sample, v_sample, ssm_sample, conv_sample)
    return tuple(np.ascontiguousarray(o, dtype=np.float32) for o in out)
```

```python
import os
from contextlib import ExitStack
import numpy as np
import concourse.bass as bass
import concourse.mybir as mybir
from concourse.bass_utils import run_bass_kernel_spmd

F32 = mybir.dt.float32
BF16 = mybir.dt.bfloat16
I32 = mybir.dt.int32
AF = mybir.ActivationFunctionType
ALU = mybir.AluOpType

D = 2048
DFF = 5632
SEQ = 2048
NS = 8
NROW = SEQ + NS
TW = 512
NT = SEQ // TW
KC = D // 128
FC = DFF // 128
INW = 7184
EPS = 1e-6
NCORES = 8


class Sched:
    ENG = ("pe", "act", "dve", "pool", "sp")

    def __init__(self):
        self.ops = {e: [] for e in self.ENG}
        self.lastw = {}
        self.readers = {}
        self.dma_cnt = {}
        self.waited = {e: {} for e in self.ENG}

    def add(self, eng, fn, reads=(), writes=(), slot=None):
        op = dict(eng=eng, fn=fn, waits=[], flag=False, slot=slot, idx=len(self.ops[eng]))
        if slot is not None:
            self.dma_cnt[slot] = self.dma_cnt.get(slot, 0) + 1
            ev = ("dma", slot, self.dma_cnt[slot], op)
        else:
            ev = ("eng", eng, op["idx"], op)
        deps = []
        for r in reads:
            if r in self.lastw:
                deps.append(self.lastw[r])
        for w in writes:
            if w in self.lastw:
                deps.append(self.lastw[w])
            deps.extend(self.readers.get(w, []))
        best = {}
        for d in deps:
            k = (d[0], d[1])
            if k not in best or best[k][2] < d[2]:
                best[k] = d
        for d in best.values():
            kind, key, n, dop = d
            if kind == "eng" and key == eng and eng == "pe":
                continue
            k = (kind, key)
            if self.waited[eng].get(k, -1) >= n:
                continue
            self.waited[eng][k] = n
            op["waits"].append(d)
            if kind == "eng":
                dop["flag"] = True
        for r in reads:
            self.readers.setdefault(r, []).append(ev)
        for w in writes:
            self.lastw[w] = ev
            self.readers[w] = []
        self.ops[eng].append(op)
        return ev

    def fence(self):
        evs = []
        for e in self.ENG:
            for op in reversed(self.ops[e]):
                if op["fn"] is not None and op["slot"] is None:
                    evs.append(("eng", e, op["idx"], op))
                    break
        for sl, n in self.dma_cnt.items():
            evs.append(("dma", sl, n, None))
        for e in self.ENG:
            op = dict(eng=e, fn=None, waits=[], flag=False, slot=None, idx=len(self.ops[e]))
            for d in evs:
                kind, key, n, dop = d
                if kind == "eng" and key == e:
                    continue
                k = (kind, key)
                if self.waited[e].get(k, -1) >= n:
                    continue
                self.waited[e][k] = n
                op["waits"].append(d)
                if kind == "eng":
                    dop["flag"] = True
            self.ops[e].append(op)

    def emit(self, nc, stack):
        esem = {e: stack.enter_context(nc.semaphore("sem_" + e)) for e in self.ENG}
        ssem = {s: stack.enter_context(nc.semaphore("slot_" + str(s))) for s in self.dma_cnt}
        for e in self.ENG:
            c = 0
            for op in self.ops[e]:
                if op["flag"]:
                    c += 1
                op["cnt"] = c
        block = stack.enter_context(nc.Block())

        def run(e):
            def body(engine):
                for op in self.ops[e]:
                    for kind, key, n, dop in op["waits"]:
                        if kind == "eng":
                            engine.wait_ge(esem[key], dop["cnt"])
                        else:
                            engine.wait_ge(ssem[key], 16 * n)
                    if op["fn"] is None:
                        continue
                    ins = op["fn"](engine)
                    if op["slot"] is not None:
                        ins.then_inc(ssem[op["slot"]], 16)
                    elif op["flag"]:
                        ins.then_inc(esem[e], 1)
            return body

        def run_sp(engine):
            run("sp")(engine)
            for sl in getattr(self, "final_slots", []):
                if sl in ssem:
                    engine.wait_ge(ssem[sl], 16 * self.dma_cnt[sl])

        block.tensor(run("pe"))
        block.scalar(run("act"))
        block.vector(run("dve"))
        block.gpsimd(run("pool"))
        block.sync(run_sp)


def subtiles(t):
    st = [(t * TW + 128 * s, 128, 128 * s) for s in range(TW // 128)]
    if t == NT - 1:
        st.append((SEQ, NS, TW))
    return st


def passes(t):
    p = [(0, TW)]
    if t == NT - 1:
        p.append((TW, NS))
    return p


def build(gdn_only=False, attn_only=False, skip_p3=False, npool=1280, aheads=8, skip_sample=False):
    nc = bass.Bass("TRN2", target_bir_lowering=False)
    S = Sched()
    stack = ExitStack()

    def din(name, shape, dt=F32):
        return nc.dram_tensor(name, shape, dt, kind="ExternalInput").ap()

    def dout(name, shape, dt=F32):
        return nc.dram_tensor(name, shape, dt, kind="ExternalOutput").ap()

    xin = din("xin", [NROW, D])
    ident_d = din("ident", [128, 128], BF16)
    wbc_d = {k: din("wbc_" + k, [128, D]) for k in ("f1pre", "f1post", "mixpre")}
    GG = 2
    wg1 = din("wg1", [FC // GG, 128, KC * 128 * GG])
    wu1 = din("wu1", [FC // GG, 128, KC * 128 * GG])
    KGD = 11
    wd1 = din("wd1", [4, FC // KGD, 128, KGD * 512])
    NFM = 48
    win_fm = din("win_fm", [NFM // GG, 128, KC * 128 * GG])
    KGV = 4
    win_v = din("win_v", [2, KC // KGV, 128, KGV * 512])

    win_ba = din("win_ba", [128, KC * 16])
    gconst = din("gconst", [128, 6 * 128])
    cw_d = din("cw_d", [128, 96])
    hb_d = din("hb_d", [128, 32])
    ssm0_d = din("ssm0", [8, 128, 128])
    convst_d = din("convst", [24, 128, 3])
    ba_d = (din if gdn_only else dout)("ba_d", [NROW, 16])
    aconst = din("aconst", [128, 2048 + 256 + 128])
    mixT_d = dout("mixT_d", [2048, NROW], BF16)
    ssm_p = dout("ssm_p", [8, 128, 128])
    ssm_s = dout("ssm_s", [8, 128, 128])
    projT = (din if gdn_only else dout)("projT", [NFM * 128, NROW])
    v_out = (din if gdn_only else dout)("v_out", [NROW, 1024])
    x1_d = dout("x1", [NROW, D])
    x2_d = dout("x2", [NROW, D])
    y_d = dout("y", [NROW, D])
    wbc2_d = {k: din("wbc_" + k, [128, D]) for k in ("f2pre", "f2post", "mixpost")}
    wg2 = din("wg2", [FC // GG, 128, KC * 128 * GG])
    wu2 = din("wu2", [FC // GG, 128, KC * 128 * GG])
    wd2 = din("wd2", [4, FC // KGD, 128, KGD * 512])
    wout_d = din("wout", [4, KC // KGV, 128, KGV * 512])
    ck_d = din("cache_k", [npool * 128, 1024])
    cv_d = din("cache_v", [npool * 128, 1024])
    pt_d = din("pt", [128, 1], I32)
    sconst = din("sconst", [128, 512])

    nm = {}

    def sb(name, shape, dt=F32):
        t_ = stack.enter_context(nc.sbuf_tensor(name, shape, dt))
        nm[id(t_)] = name
        return t_

    def ps(name, shape, dt=F32):
        t_ = stack.enter_context(nc.psum_tensor(name, shape, dt))
        nm[id(t_)] = name
        return t_

    def N(t_):
        return nm[id(t_)]

    TWX = TW + NS
    ident = sb("ident_sb", [128, 128], BF16)
    wbc = {k: sb("wbc_sb_" + k, [128, D]) for k in wbc_d}
    X = [sb(f"X{i}", [128, D]) for i in range(2)]
    Y = [sb(f"Y{i}", [128, D]) for i in range(5)]
    XN = sb("XN", [128, D], BF16)
    xnT = sb("xnT", [128, KC * TWX], BF16)
    hT = sb("hT", [128, FC * TWX], BF16)
    sm = sb("small", [128, 64])
    wgS = [sb(f"wgS{i}", [128, KC * 128 * GG], BF16) for i in range(2)]
    wuS = [sb(f"wuS{i}", [128, KC * 128 * GG], BF16) for i in range(2)]
    wdS = [sb(f"wdS{i}", [128, KGD * 512], BF16) for i in range(2)]
    stg = [sb(f"stg{i}", [128, TWX]) for i in range(2)]
    wba = sb("wba", [128, KC * 16], BF16)
    finb = [sb(f"finb{i}", [128, 128], BF16) for i in range(2)]
    ptidx = sb("ptidx", [128, 1], I32)
    ptidx2 = sb("ptidx2", [128, 1], I32)
    tmpg = stg
    bank = [ps(f"bank{i}", [128, 512]) for i in range(7)]
    psT = ps("psT", [128, 1024], BF16)

    xnT3 = xnT[:].rearrange("p (c t) -> p c t", t=TWX)
    hT3 = hT[:].rearrange("p (c t) -> p c t", t=TWX)
    psT3 = psT[:].rearrange("p (c t) -> p c t", t=128)

    cnt = {"x": 0, "g": 0, "d": 0, "stg": 0, "tmp": 0, "evac": 0}

    S.add("pool", lambda e: e.dma_start(out=wba[:], in_=win_ba), writes=["wba"], slot="wba")
    S.add("sp", lambda e: e.dma_start(out=ident[:], in_=ident_d), writes=["ident"], slot="ident")
    for k in wbc:
        S.add("sp", lambda e, k=k: e.dma_start(out=wbc[k][:], in_=wbc_d[k]), writes=["wbc" + k], slot="wbc" + k)

    def load_x(src, row0, n, extra_reads=()):
        i = cnt["x"] % 2
        cnt["x"] += 1
        S.add("sp", lambda e: e.dma_start(out=X[i][:n, :], in_=src[row0:row0 + n, :]),
              reads=list(extra_reads), writes=[f"X{i}"], slot=f"X{i}")
        return i

    def rstd_of(src_ap, n, src_res, col):
        S.add("act", lambda e: e.activation(out=XN[:n, :], in_=src_ap, func=AF.Square,
                                            accum_out=sm[:n, col:col + 1]),
              reads=src_res, writes=["XN", f"sm{col}"])
        S.add("dve", lambda e: e.tensor_scalar(out=sm[:n, col + 1:col + 2], in0=sm[:n, col:col + 1],
                                               scalar1=1.0 / D, scalar2=EPS, op0=ALU.mult, op1=ALU.add),
              reads=[f"sm{col}"], writes=[f"sm{col + 1}"])
        S.add("act", lambda e: e.sqrt(out=sm[:n, col + 2:col + 3], in_=sm[:n, col + 1:col + 2]),
              reads=[f"sm{col + 1}"], writes=[f"sm{col + 2}"])
        S.add("dve", lambda e: e.reciprocal(out=sm[:n, col + 3:col + 4], in_=sm[:n, col + 2:col + 3]),
              reads=[f"sm{col + 2}"], writes=[f"sm{col + 3}"])
        return sm[:n, col + 3:col + 4], f"sm{col + 3}"

    def norm_T(src_ap, src_res, n, col, wkey):
        rs, rres = rstd_of(src_ap, n, src_res, 0)
        S.add("dve", lambda e: e.scalar_tensor_tensor(out=XN[:n, :], in0=src_ap, scalar=rs,
                                                      in1=wbc[wkey][:n, :], op0=ALU.mult, op1=ALU.mult),
              reads=src_res + [rres, "wbc" + wres[wkey]], writes=["XN"])
        for half in range(2):
            for j in range(8):
                c = half * 8 + j
                S.add("pe", lambda e, c=c, j=j: e.transpose(out=psT3[:, j, :n], in_=XN[:n, c * 128:(c + 1) * 128],
                                                            identity=ident[:n, :n]),
                      reads=["XN", "ident"], writes=["psT"])
            eng = "act" if half == 0 else "dve"
            if eng == "act":
                S.add("act", lambda e, half=half: e.copy(out=xnT3[:, half * 8:half * 8 + 8, col:col + n],
                                                         in_=psT3[:, :, :n]),
                      reads=["psT"], writes=["xnT"])
            else:
                S.add("dve", lambda e, half=half: e.tensor_copy(out=xnT3[:, half * 8:half * 8 + 8, col:col + n],
                                                                in_=psT3[:, :, :n]),
                      reads=["psT"], writes=["xnT"])

    def linear_fm(t, wsrc, ngroups, stages, skey, evac):
        for g in range(ngroups):
            i = cnt[skey] % 2
            cnt[skey] += 1
            for (arr, st) in zip(wsrc, stages):
                S.add("pool", lambda e, arr=arr, st=st, g=g, i=i: e.dma_start(out=st[i][:], in_=arr[g]),
                      writes=[N(st[i])], slot=N(st[i]))
            for gi in range(GG):
                oc = g * GG + gi
                for (c0, n) in passes(t):
                    outs = []
                    for wi, st in enumerate(stages):
                        b = bank[(cnt["evac"] % 2) * len(stages) + wi]
                        st3 = st[i][:].rearrange("p (k f) -> p k f", f=128 * GG)
                        for kc in range(KC):
                            S.add("pe", lambda e, b=b, st3=st3, kc=kc, gi=gi, c0=c0, n=n: e.matmul(
                                b[:, :n], lhsT=st3[:, kc, gi * 128:(gi + 1) * 128], rhs=xnT3[:, kc, c0:c0 + n],
                                start=(kc == 0), stop=(kc == KC - 1)),
                                reads=[N(st[i]), "xnT"], writes=[N(b)])
                        outs.append(b)
                    cnt["evac"] += 1
                    evac(oc, c0, n, outs)

    def linear_tm(t, wsrc, nn, nkg, kgsz, stages, skey, lhs3, lres, evac):
        sts = subtiles(t)
        for nt_ in range(nn):
            for kg in range(nkg):
                i = cnt[skey] % 2
                cnt[skey] += 1
                S.add("pool", lambda e, nt_=nt_, kg=kg, i=i: e.dma_start(out=stages[i][:, :kgsz * 512],
                                                                         in_=wsrc[nt_, kg]),
                      writes=[N(stages[i])], slot=N(stages[i]))
                st3 = stages[i][:].rearrange("p (k f) -> p k f", f=512)
                for kk in range(kgsz):
                    k = kg * kgsz + kk
                    for si, (row0, n, col) in enumerate(sts):
                        b = bank[2 + si]
                        S.add("pe", lambda e, b=b, st3=st3, kk=kk, k=k, n=n, col=col: e.matmul(
                            b[:n, :], lhsT=lhs3[:, k, col:col + n], rhs=st3[:, kk, :],
                            start=(k == 0), stop=(k == nkg * kgsz - 1)),
                            reads=[N(stages[i]), lres], writes=[N(b)])
            for si, (row0, n, col) in enumerate(sts):
                evac(nt_, si, row0, n, bank[2 + si])

    wres = {"f1pre": "f1pre", "f1post": "f1post", "mixpre": "mixpre",
            "f2pre": "f1pre", "f2post": "f1post", "mixpost": "mixpre"}
    for k2, k1 in list(wres.items()):
        wbc[k2] = wbc[k1]

    def ffn_block(t, wg, wu, wd, postkey, res_src, after):
        sts = subtiles(t)

        def evac_gu(oc, c0, n, outs):
            j = cnt["stg"] % 2
            cnt["stg"] += 1
            S.add("act", lambda e: e.activation(out=tmpg[j][:, :n], in_=outs[0][:, :n], func=AF.Silu),
                  reads=[N(outs[0])], writes=[f"stg{j}"])
            S.add("dve", lambda e: e.tensor_tensor(out=hT3[:, oc, c0:c0 + n], in0=tmpg[j][:, :n],
                                                   in1=outs[1][:, :n], op=ALU.mult),
                  reads=[f"stg{j}", N(outs[1])], writes=["lhs"])

        linear_fm(t, [wg, wu], FC // GG, [wgS, wuS], "g", evac_gu)

        def evac_d(nt_, si, row0, n, b):
            S.add("act", lambda e: e.copy(out=Y[si][:n, nt_ * 512:(nt_ + 1) * 512], in_=b[:n, :]),
                  reads=[N(b)], writes=[f"Y{si}"])

        linear_tm(t, wd, 4, FC // KGD, KGD, wdS, "d", hT3, "lhs", evac_d)

        for si, (row0, n, col) in enumerate(sts):
            post_res(si, row0, n, postkey, res_src, 0.5)
            after(si, row0, n, col)

    def post_res(si, row0, n, postkey, res_src, alpha):
        rs, rres = rstd_of(Y[si][:n, :], n, [f"Y{si}"], 8)
        S.add("dve", lambda e: e.scalar_tensor_tensor(
            out=Y[si][:n, :], in0=Y[si][:n, :], scalar=rs, in1=wbc[postkey][:n, :],
            op0=ALU.mult, op1=ALU.mult),
            reads=[f"Y{si}", rres, "wbc" + wres[postkey]], writes=[f"Y{si}"])
        xi = load_x(res_src, row0, n, extra_reads=([f"x2dram{si}"] if res_src is x2_d else []))
        S.add("dve", lambda e: e.scalar_tensor_tensor(
            out=Y[si][:n, :], in0=Y[si][:n, :], scalar=alpha, in1=X[xi][:n, :],
            op0=ALU.mult, op1=ALU.add),
            reads=[f"Y{si}", f"X{xi}"], writes=[f"Y{si}"])

    def evac_d_plain(nt_, si, row0, n, b):
        S.add("act", lambda e: e.copy(out=Y[si][:n, nt_ * 512:(nt_ + 1) * 512], in_=b[:n, :]),
              reads=[N(b)], writes=[f"Y{si}"])

    def phase1_tile(t):
        sts = subtiles(t)
        for (row0, n, col) in sts:
            xi = load_x(xin, row0, n)
            norm_T(X[xi][:n, :], [f"X{xi}"], n, col, "f1pre")

        def after1(si, row0, n, col):
            S.add("sp", lambda e: e.dma_start(out=x1_d[row0:row0 + n, :], in_=Y[si][:n, :]),
                  reads=[f"Y{si}"], slot=f"x1o{si}")
            norm_T(Y[si][:n, :], [f"Y{si}"], n, col, "mixpre")

        ffn_block(t, wg1, wu1, wd1, "f1post", xin, after1)

        def evac_in(oc, c0, n, outs):
            j = cnt["stg"] % 2
            cnt["stg"] += 1
            S.add("act", lambda e: e.copy(out=stg[j][:, :n], in_=outs[0][:, :n]),
                  reads=[N(outs[0])], writes=[f"stg{j}"])
            gcol = (t * TW + c0) if c0 < TW else SEQ
            S.add("sp", lambda e: e.dma_start(out=projT[oc * 128:(oc + 1) * 128, gcol:gcol + n], in_=stg[j][:, :n]),
                  reads=[f"stg{j}"], slot=f"stgo{j}")

        linear_fm(t, [win_fm], NFM // GG, [wgS], "g", evac_in)

        def evac_v(nt_, si, row0, n, b):
            j = cnt["stg"] % 2
            cnt["stg"] += 1
            S.add("act", lambda e: e.copy(out=stg[j][:n, :512], in_=b[:n, :]),
                  reads=[N(b)], writes=[f"stg{j}"])
            S.add("sp", lambda e: e.dma_start(out=v_out[row0:row0 + n, nt_ * 512:(nt_ + 1) * 512], in_=stg[j][:n, :512]),
                  reads=[f"stg{j}"], slot=f"stgo{j}")

        linear_tm(t, win_v, 2, KC // KGV, KGV, wdS, "d", xnT3, "xnT", evac_v)

        wba3 = wba[:].rearrange("p (k f) -> p k f", f=16)

        def ba_sub(si, row0, n, col):
            b = bank[2 + si]
            for kc in range(KC):
                S.add("pe", lambda e, kc=kc: e.matmul(
                    b[:n, :16], lhsT=xnT3[:, kc, col:col + n], rhs=wba3[:, kc, :],
                    start=(kc == 0), stop=(kc == KC - 1)),
                    reads=["wba", "xnT"], writes=[N(b)])
            j = cnt["stg"] % 2
            cnt["stg"] += 1
            S.add("act", lambda e: e.copy(out=stg[j][:n, :16], in_=b[:n, :16]),
                  reads=[N(b)], writes=[f"stg{j}"])
            S.add("sp", lambda e: e.dma_start(out=ba_d[row0:row0 + n, :], in_=stg[j][:n, :16]),
                  reads=[f"stg{j}"], slot=f"stgo{j}")

        for si, (row0, n, col) in enumerate(sts):
            ba_sub(si, row0, n, col)

    def phase3_tile(t):
        sts = subtiles(t)
        ncols = TW + (NS if t == NT - 1 else 0)
        S.add("sp", lambda e: e.dma_start(out=xnT3[:, :, 0:ncols],
                                          in_=mixT_d[:, t * TW:t * TW + ncols].rearrange("(k p) n -> p k n", p=128)),
              writes=["xnT"], slot="xnTld")
        linear_tm(t, wout_d, 4, KC // KGV, KGV, wdS, "d", xnT3, "xnT", evac_d_plain)

        def mid(si, row0, n, col):
            post_res(si, row0, n, "mixpost", x1_d, 1.0)
            S.add("sp", lambda e: e.dma_start(out=x2_d[row0:row0 + n, :], in_=Y[si][:n, :]),
                  reads=[f"Y{si}"], writes=[f"x2dram{si}"], slot=f"x2o{si}")
            norm_T(Y[si][:n, :], [f"Y{si}"], n, col, "f2pre")

        for si, (row0, n, col) in enumerate(sts):
            mid(si, row0, n, col)

        def after3(si, row0, n, col):
            S.add("sp", lambda e: e.dma_start(out=y_d[row0:row0 + n, :], in_=Y[si][:n, :]),
                  reads=[f"Y{si}"], slot=f"yo{si}")

        ffn_block(t, wg2, wu2, wd2, "f2post", x2_d, after3)

    for t in range(0 if gdn_only else NT):
        phase1_tile(t)


    S.fence()
    pool_cols = {}
    pool_next = [0]

    def galloc(name, ncols_req):
        ncols = ((ncols_req + 31) // 32) * 32
        off = pool_next[0]
        bi, co = off // D, off % D
        if co + ncols > D:
            bi, co = bi + 1, 0
            off = bi * D
        pool_next[0] = off + ncols
        assert bi < 5, "gdn scratch overflow"
        return Y[bi][:, co:co + ncols_req]

    Sst = [galloc(f"S{h}", 128) for h in range(8)]
    gc = galloc("gconst", 768)
    identF, Umat, maskLn, maskUn, onesF, maskUd = [gc[:, i * 128:(i + 1) * 128] for i in range(6)]
    cw = galloc("cw", 96)
    hb = galloc("hb", 32)
    chk = galloc("chk", 64)
    S.add("sp", lambda e: e.dma_start(out=gc, in_=gconst), writes=["gconst"], slot="gconst")
    S.add("sp", lambda e: e.dma_start(out=cw, in_=cw_d), writes=["cw"], slot="cw")
    S.add("sp", lambda e: e.dma_start(out=hb, in_=hb_d), writes=["hb"], slot="hb")
    S.add("act", lambda e: e.activation(out=hb[:, 24:32], in_=hb[:, 0:8], func=AF.Exp), reads=["hb"], writes=["hbA"])
    S.add("dve", lambda e: e.tensor_scalar(out=hb[:, 24:32], in0=hb[:, 24:32], scalar1=-1.0, scalar2=None, op0=ALU.mult),
          reads=["hbA"], writes=["hbA"])
    for h in range(8):
        S.add("dve", lambda e, h=h: e.memset(Sst[h], 0.0), writes=[f"S{h}"])

    US = []
    for p in range(2):
        d_ = {}
        for nme, w_ in (("kxp", 131), ("vxp", 131), ("kc", 128), ("vc", 128), ("t1", 128), ("t2", 128),
                        ("kn", 128), ("ktm", 128), ("vb", 128), ("Gbc", 128), ("rep", 128), ("kbT", 128),
                        ("QT", 128), ("Q", 128), ("R", 128), ("kg", 128), ("nwT", 128), ("vn", 128),
                        ("kd", 128), ("sc", 16),
                        ("qxp", 131), ("qc", 128), ("qg", 128), ("qkT", 128), ("eG", 128), ("t3", 128),
                        ("zs", 128), ("on", 128)):
            d_[nme] = galloc(f"{nme}{p}", w_)
        US.append(d_)

    def pb(i):
        return bank[i % 7][:, 0:128], N(bank[i % 7])

    unit_no = [0]
    KSTAGE = int(os.environ.get('K_STAGE', '99'))
    KCHUNKS = int(os.environ.get('K_CHUNKS', str(SEQ // 128)))
    KSAMPLE = int(os.environ.get('K_SAMPLE', '1'))
    KCHUNKLVL = int(os.environ.get('K_CHUNKLVL', '1'))

    def gdn_chunk(tok0, nvalid, hist_src, sample):
        if not KCHUNKLVL:
            return
        nt_ = min(128, nvalid)
        if nvalid < 128:
            S.add("dve", lambda e: e.memset(chk[:, 0:16], 0.0), writes=["chkBA"])
        S.add("sp", lambda e: e.dma_start(out=chk[:nt_, 0:16], in_=ba_d[tok0:tok0 + nt_, :]),
              writes=["chkBA"], slot="chkBA")
        S.add("act", lambda e: e.activation(out=chk[:, 16:24], in_=chk[:, 0:8], func=AF.Sigmoid),
              reads=["chkBA"], writes=["chkbeta"])
        S.add("dve", lambda e: e.tensor_tensor(out=chk[:, 24:32], in0=chk[:, 8:16], in1=hb[:, 8:16], op=ALU.add),
              reads=["chkBA", "hb"], writes=["chktmp"])
        S.add("dve", lambda e: e.tensor_scalar(out=chk[:, 24:32], in0=chk[:, 24:32], scalar1=30.0, scalar2=None, op0=ALU.min),
              reads=["chktmp"], writes=["chktmp"])
        S.add("act", lambda e: e.activation(out=chk[:, 24:32], in_=chk[:, 24:32], func=AF.Exp),
              reads=["chktmp"], writes=["chktmp"])
        S.add("act", lambda e: e.activation(out=chk[:, 24:32], in_=chk[:, 24:32], func=AF.Ln, bias=1.0),
              reads=["chktmp"], writes=["chktmp"])
        S.add("dve", lambda e: e.tensor_tensor(out=chk[:, 32:40], in0=chk[:, 24:32], in1=hb[:, 24:32], op=ALU.mult),
              reads=["chktmp", "hbA"], writes=["chkg"])
        if nvalid < 128:
            S.add("dve", lambda e: e.tensor_scalar(out=chk[:, 32:40], in0=chk[:, 32:40], scalar1=hb[:, 16:17], scalar2=None, op0=ALU.mult),
                  reads=["chkg", "hb"], writes=["chkg"])
            S.add("dve", lambda e: e.tensor_scalar(out=chk[:, 16:24], in0=chk[:, 16:24], scalar1=hb[:, 16:17], scalar2=None, op0=ALU.mult),
                  reads=["chkbeta", "hb"], writes=["chkbeta"])
        pg, pgr = pb(27)
        S.add("pe", lambda e: e.matmul(pg[:, :8], lhsT=Umat, rhs=chk[:, 32:40], start=True, stop=True),
              reads=["gconst", "chkg"], writes=[pgr])
        S.add("act", lambda e: e.copy(out=chk[:, 40:48], in_=pg[:, :8]), reads=[pgr], writes=["chkG"])

        def unit(h):
            p = unit_no[0] % 2
            unit_no[0] += 1
            u = US[p]
            R_ = lambda nme: f"{nme}{p}"
            def P(i):
                bk = bank[3 * p + i % 3]
                return bk[:, (i // 3) * 128:(i // 3 + 1) * 128], f"{N(bk)}r{i // 3}"
            krow = 2048 + 1024 + 128 * h
            vrow = 2048 + 2048 + 128 * h
            qrow = 2048 + 128 * h
            for (dst, row, part, nme) in ((u["kxp"], krow, 8 + h, "kxp"), (u["vxp"], vrow, 16 + h, "vxp"), (u["qxp"], qrow, h, "qxp")):
                if hist_src is None:
                    S.add("dve", lambda e, dst=dst: e.memset(dst[:, 0:3], 0.0), writes=[R_(nme)])
                elif hist_src == "state":
                    S.add("dve", lambda e, dst=dst: e.memset(dst[:, :], 0.0), writes=[R_(nme)])
                    S.add("sp", lambda e, dst=dst, part=part: e.dma_start(out=dst[:, 0:3], in_=convst_d[part]),
                          writes=[R_(nme)], slot=R_(nme) + "h")
                if hist_src == "prev":
                    S.add("sp", lambda e, dst=dst, row=row: e.dma_start(out=dst[:, 0:131], in_=projT[row:row + 128, tok0 - 3:tok0 + 128]),
                          writes=[R_(nme)], slot=R_(nme))
                else:
                    S.add("sp", lambda e, dst=dst, row=row: e.dma_start(out=dst[:, 3:3 + nt_], in_=projT[row:row + 128, tok0:tok0 + nt_]),
                          writes=[R_(nme)], slot=R_(nme))
            if KSTAGE < 1:
                return
            for (src, dst, part, sn, dn) in ((u["kxp"], u["kc"], 8 + h, "kxp", "kc"), (u["vxp"], u["vc"], 16 + h, "vxp", "vc"), (u["qxp"], u["qc"], h, "qxp", "qc")):
                S.add("dve", lambda e, src=src, dst=dst, part=part: e.tensor_scalar(
                    out=dst, in0=src[:, 0:128], scalar1=cw[:, part * 4:part * 4 + 1], scalar2=None, op0=ALU.mult),
                    reads=[R_(sn), "cw"], writes=[R_(dn)])
                for j in range(1, 4):
                    S.add("dve", lambda e, src=src, dst=dst, part=part, j=j: e.scalar_tensor_tensor(
                        out=dst, in0=src[:, j:j + 128], scalar=cw[:, part * 4 + j:part * 4 + j + 1], in1=dst,
                        op0=ALU.mult, op1=ALU.add),
                        reads=[R_(sn), "cw", R_(dn)], writes=[R_(dn)])
                S.add("act", lambda e, dst=dst: e.activation(out=dst, in_=dst, func=AF.Silu),
                      reads=[R_(dn)], writes=[R_(dn)])
            if KSTAGE < 2:
                return
            S.add("act", lambda e: e.activation(out=u["t1"], in_=u["kc"], func=AF.Square), reads=[R_("kc")], writes=[R_("t1")])
            p0, p0r = P(0)
            S.add("pe", lambda e: e.matmul(p0, lhsT=onesF, rhs=u["t1"], start=True, stop=True),
                  reads=["gconst", R_("t1")], writes=[p0r])
            S.add("dve", lambda e: e.tensor_scalar(out=u["t2"], in0=p0, scalar1=1e-6, scalar2=None, op0=ALU.add),
                  reads=[p0r], writes=[R_("t2")])
            S.add("act", lambda e: e.sqrt(out=u["t2"], in_=u["t2"]), reads=[R_("t2")], writes=[R_("t2")])
            S.add("dve", lambda e: e.reciprocal(out=u["t2"], in_=u["t2"]), reads=[R_("t2")], writes=[R_("t2")])
            S.add("dve", lambda e: e.tensor_tensor(out=u["kn"], in0=u["kc"], in1=u["t2"], op=ALU.mult),
                  reads=[R_("kc"), R_("t2")], writes=[R_("kn")])
            if KSTAGE < 3:
                return
            KSUB = int(os.environ.get('K_SUB', '9'))
            p1, p1r = P(1)
            S.add("pe", lambda e: e.matmul(p1, lhsT=u["kn"], rhs=identF, start=True, stop=True), reads=[R_("kn"), "gconst"], writes=[p1r])
            if KSUB < 1:
                return
            S.add("act", lambda e: e.copy(out=u["ktm"], in_=p1), reads=[p1r], writes=[R_("ktm")])
            if KSUB < 2:
                return
            p2, p2r = P(2)
            vsrc = "kc" if os.environ.get('K_V1') else "vc"
            S.add("pe", lambda e: e.matmul(p2, lhsT=u[vsrc], rhs=identF, start=True, stop=True), reads=[R_(vsrc), "gconst"], writes=[p2r])
            if KSUB < 3:
                return
            S.add("dve", lambda e: e.tensor_scalar(out=u["vb"], in0=p2, scalar1=chk[:, 16 + h:17 + h], scalar2=None, op0=ALU.mult),
                  reads=[p2r, "chkbeta"], writes=[R_("vb")])
            if KSTAGE < 4:
                return
            S.add("dve", lambda e: e.tensor_scalar(out=u["rep"], in0=onesF, scalar1=chk[:, 32 + h:33 + h], scalar2=None, op0=ALU.mult),
                  reads=["gconst", "chkg"], writes=[R_("rep")])
            p3, p3r = P(3)
            S.add("pe", lambda e: e.matmul(p3, lhsT=u["rep"], rhs=Umat, start=True, stop=True),
                  reads=[R_("rep"), "gconst"], writes=[p3r])
            S.add("act", lambda e: e.copy(out=u["Gbc"], in_=p3), reads=[p3r], writes=[R_("Gbc")])
            S.add("dve", lambda e: e.tensor_scalar(out=u["rep"], in0=onesF, scalar1=chk[:, 16 + h:17 + h], scalar2=None, op0=ALU.mult),
                  reads=["gconst", "chkbeta"], writes=[R_("rep")])
            p4, p4r = P(4)
            S.add("pe", lambda e: e.matmul(p4, lhsT=u["rep"], rhs=identF, start=True, stop=True),
                  reads=[R_("rep"), "gconst"], writes=[p4r])
            S.add("dve", lambda e: e.tensor_tensor(out=u["kbT"], in0=u["kn"], in1=p4, op=ALU.mult),
                  reads=[R_("kn"), p4r], writes=[R_("kbT")])
            if KSTAGE < 5:
                return
            p5, p5r = P(5)
            p6, p6r = P(6)
            S.add("pe", lambda e: e.matmul(p5, lhsT=u["kbT"], rhs=u["kn"], start=True, stop=True),
                  reads=[R_("kbT"), R_("kn")], writes=[p5r])
            S.add("pe", lambda e: e.matmul(p6, lhsT=u["kn"], rhs=u["kbT"], start=True, stop=True),
                  reads=[R_("kbT"), R_("kn")], writes=[p6r])
            gcol = chk[:, 40 + h:41 + h]
            S.add("dve", lambda e: e.tensor_scalar(out=u["t1"], in0=u["Gbc"], scalar1=gcol, scalar2=0.0, op0=ALU.subtract, op1=ALU.max),
                  reads=[R_("Gbc"), "chkG"], writes=[R_("t1")])
            S.add("act", lambda e: e.activation(out=u["t1"], in_=u["t1"], func=AF.Exp, scale=-1.0), reads=[R_("t1")], writes=[R_("t1")])
            S.add("dve", lambda e: e.tensor_tensor(out=u["t1"], in0=u["t1"], in1=maskLn, op=ALU.mult),
                  reads=[R_("t1"), "gconst"], writes=[R_("t1")])
            S.add("dve", lambda e: e.tensor_tensor(out=u["QT"], in0=u["t1"], in1=p5, op=ALU.mult),
                  reads=[R_("t1"), p5r], writes=[R_("QT")])
            S.add("dve", lambda e: e.tensor_scalar(out=u["t2"], in0=u["Gbc"], scalar1=gcol, scalar2=0.0, op0=ALU.subtract, op1=ALU.min),
                  reads=[R_("Gbc"), "chkG"], writes=[R_("t2")])
            S.add("act", lambda e: e.activation(out=u["t2"], in_=u["t2"], func=AF.Exp), reads=[R_("t2")], writes=[R_("t2")])
            S.add("dve", lambda e: e.tensor_tensor(out=u["t3"], in0=u["t2"], in1=maskUd, op=ALU.mult),
                  reads=[R_("t2"), "gconst"], writes=[R_("t3")])
            S.add("dve", lambda e: e.tensor_tensor(out=u["t2"], in0=u["t2"], in1=maskUn, op=ALU.mult),
                  reads=[R_("t2"), "gconst"], writes=[R_("t2")])
            S.add("dve", lambda e: e.tensor_tensor(out=u["Q"], in0=u["t2"], in1=p6, op=ALU.mult),
                  reads=[R_("t2"), p6r], writes=[R_("Q")])
            S.add("dve", lambda e: e.tensor_tensor(out=u["R"], in0=u["Q"], in1=identF, op=ALU.add),
                  reads=[R_("Q"), "gconst"], writes=[R_("R")])
            if KSTAGE < 6:
                return
            for lvl in range(1, 7):
                pa, par = P(7)
                pq, pqr = P(8)
                S.add("pe", lambda e, pa=pa: e.matmul(pa, lhsT=u["Q"], rhs=u["QT"], start=True, stop=True),
                      reads=[R_("Q"), R_("QT")], writes=[par])
                if lvl < 6:
                    S.add("pe", lambda e, pq=pq: e.matmul(pq, lhsT=u["QT"], rhs=u["Q"], start=True, stop=True),
                          reads=[R_("Q"), R_("QT")], writes=[pqr])
                S.add("act", lambda e, pa=pa: e.copy(out=u["QT"], in_=pa), reads=[par], writes=[R_("QT")])
                if lvl < 6:
                    S.add("dve", lambda e, pq=pq: e.tensor_copy(out=u["Q"], in_=pq), reads=[pqr], writes=[R_("Q")])
                pr, prr = P(9)
                S.add("pe", lambda e, pr=pr: e.matmul(pr, lhsT=u["QT"], rhs=u["R"], start=True, stop=True),
                      reads=[R_("QT"), R_("R")], writes=[prr])
                S.add("dve", lambda e, pr=pr: e.tensor_tensor(out=u["R"], in0=u["R"], in1=pr, op=ALU.add),
                      reads=[R_("R"), prr], writes=[R_("R")])
            if KSTAGE < 7:
                return
            S.add("act", lambda e: e.activation(out=u["t1"], in_=u["qc"], func=AF.Square), reads=[R_("qc")], writes=[R_("t1")])
            pq0, pq0r = P(1)
            S.add("pe", lambda e: e.matmul(pq0, lhsT=onesF, rhs=u["t1"], start=True, stop=True),
                  reads=["gconst", R_("t1")], writes=[pq0r])
            S.add("dve", lambda e: e.tensor_scalar(out=u["t1"], in0=pq0, scalar1=1e-6, scalar2=None, op0=ALU.add),
                  reads=[pq0r], writes=[R_("t1")])
            S.add("act", lambda e: e.sqrt(out=u["t1"], in_=u["t1"]), reads=[R_("t1")], writes=[R_("t1")])
            S.add("dve", lambda e: e.reciprocal(out=u["t1"], in_=u["t1"]), reads=[R_("t1")], writes=[R_("t1")])
            S.add("dve", lambda e: e.scalar_tensor_tensor(out=u["qc"], in0=u["qc"], scalar=128.0 ** -0.5, in1=u["t1"],
                                                          op0=ALU.mult, op1=ALU.mult),
                  reads=[R_("qc"), R_("t1")], writes=[R_("qc")])
            S.add("act", lambda e: e.activation(out=u["eG"], in_=u["Gbc"], func=AF.Exp), reads=[R_("Gbc")], writes=[R_("eG")])
            S.add("dve", lambda e: e.tensor_tensor(out=u["qg"], in0=u["qc"], in1=u["eG"], op=ALU.mult),
                  reads=[R_("qc"), R_("eG")], writes=[R_("qg")])
            pqk, pqkr = P(2)
            S.add("pe", lambda e: e.matmul(pqk, lhsT=u["kn"], rhs=u["qc"], start=True, stop=True),
                  reads=[R_("kn"), R_("qc")], writes=[pqkr])
            S.add("dve", lambda e: e.tensor_tensor(out=u["qkT"], in0=u["t3"], in1=pqk, op=ALU.mult),
                  reads=[R_("t3"), pqkr], writes=[R_("qkT")])
            sc = u["sc"]
            S.add("act", lambda e: e.activation(out=sc[:, 0:1], in_=gcol, func=AF.Exp), reads=["chkG"], writes=[R_("sc")])
            S.add("dve", lambda e: e.tensor_tensor(out=sc[:, 0:1], in0=sc[:, 0:1], in1=chk[:, 16 + h:17 + h], op=ALU.mult),
                  reads=[R_("sc"), "chkbeta"], writes=[R_("sc")])
            S.add("dve", lambda e: e.tensor_tensor(out=sc[:, 1:2], in0=u["Gbc"][:, 127:128], in1=gcol, op=ALU.subtract),
                  reads=[R_("Gbc"), "chkG"], writes=[R_("sc")])
            S.add("act", lambda e: e.activation(out=sc[:, 1:2], in_=sc[:, 1:2], func=AF.Exp), reads=[R_("sc")], writes=[R_("sc")])
            S.add("act", lambda e: e.activation(out=sc[:, 2:3], in_=u["Gbc"][:, 127:128], func=AF.Exp),
                  reads=[R_("Gbc")], writes=[R_("sc")])
            S.add("dve", lambda e: e.tensor_scalar(out=u["kg"], in0=u["ktm"], scalar1=sc[:, 0:1], scalar2=None, op0=ALU.mult),
                  reads=[R_("ktm"), R_("sc")], writes=[R_("kg")])
            S.add("dve", lambda e: e.tensor_scalar(out=u["kd"], in0=u["ktm"], scalar1=sc[:, 1:2], scalar2=None, op0=ALU.mult),
                  reads=[R_("ktm"), R_("sc")], writes=[R_("kd")])
            if KSTAGE < 8:
                return
            p10, p10r = P(10)
            S.add("pe", lambda e: e.matmul(p10, lhsT=u["kg"], rhs=u["R"], start=True, stop=True),
                  reads=[R_("kg"), R_("R")], writes=[p10r])
            S.add("act", lambda e: e.mul(out=u["nwT"], in_=p10, mul=-1.0), reads=[p10r], writes=[R_("nwT")])
            p11, p11r = P(11)
            S.add("pe", lambda e: e.matmul(p11, lhsT=u["R"], rhs=u["vb"], start=True, stop=False),
                  reads=[R_("R"), R_("vb")], writes=[p11r])
            S.add("pe", lambda e: e.matmul(p11, lhsT=u["nwT"], rhs=Sst[h], start=False, stop=True),
                  reads=[R_("nwT"), f"S{h}"], writes=[p11r])
            S.add("act", lambda e: e.copy(out=u["vn"], in_=p11), reads=[p11r], writes=[R_("vn")])
            po, por = P(3)
            S.add("pe", lambda e: e.matmul(po, lhsT=Sst[h], rhs=u["qg"], start=True, stop=False),
                  reads=[f"S{h}", R_("qg")], writes=[por])
            S.add("pe", lambda e: e.matmul(po, lhsT=u["vn"], rhs=u["qkT"], start=False, stop=True),
                  reads=[R_("vn"), R_("qkT")], writes=[por])
            S.add("act", lambda e: e.activation(out=u["t1"], in_=po, func=AF.Square), reads=[por], writes=[R_("t1")])
            pss, pssr = P(4)
            S.add("pe", lambda e: e.matmul(pss, lhsT=onesF, rhs=u["t1"], start=True, stop=True),
                  reads=["gconst", R_("t1")], writes=[pssr])
            S.add("dve", lambda e: e.tensor_scalar(out=u["t1"], in0=pss, scalar1=1.0 / 128, scalar2=EPS, op0=ALU.mult, op1=ALU.add),
                  reads=[pssr], writes=[R_("t1")])
            S.add("act", lambda e: e.sqrt(out=u["t1"], in_=u["t1"]), reads=[R_("t1")], writes=[R_("t1")])
            S.add("dve", lambda e: e.reciprocal(out=u["t1"], in_=u["t1"]), reads=[R_("t1")], writes=[R_("t1")])
            S.add("dve", lambda e: e.tensor_tensor(out=u["on"], in0=u["t1"], in1=po, op=ALU.mult),
                  reads=[R_("t1"), por], writes=[R_("on")])
            zrow = 5120 + 128 * h
            S.add("sp", lambda e: e.dma_start(out=u["zs"][:, :nt_], in_=projT[zrow:zrow + 128, tok0:tok0 + nt_]),
                  writes=[R_("zs")], slot=R_("zs"))
            S.add("act", lambda e: e.activation(out=u["zs"][:, :nt_], in_=u["zs"][:, :nt_], func=AF.Silu), reads=[R_("zs")], writes=[R_("zs")])
            S.add("dve", lambda e: e.scalar_tensor_tensor(out=finb[p][:, :nt_], in0=u["on"][:, :nt_], scalar=hb[:, 17:18], in1=u["zs"][:, :nt_],
                                                          op0=ALU.mult, op1=ALU.mult),
                  reads=[R_("on"), "hb", R_("zs")], writes=[f"finb{p}"])
            S.add("sp", lambda e: e.dma_start(out=mixT_d[1024 + 128 * h:1024 + 128 * h + 128, tok0:tok0 + nt_], in_=finb[p][:, :nt_]),
                  reads=[f"finb{p}"], slot=f"finbo{p}")
            S.add("pe", lambda e: e.matmul(p0, lhsT=u["kd"], rhs=u["vn"], start=True, stop=True),
                  reads=[R_("kd"), R_("vn")], writes=[p0r])
            S.add("dve", lambda e: e.scalar_tensor_tensor(out=Sst[h], in0=Sst[h], scalar=sc[:, 2:3], in1=p0,
                                                          op0=ALU.mult, op1=ALU.add),
                  reads=[f"S{h}", R_("sc"), p0r], writes=[f"S{h}"])

        def record(h):
            rec = []
            S.add = lambda *a, **k: rec.append((a, k))
            try:
                unit(h)
            finally:
                del S.add
            items = []
            for a, k in rec:
                if items and a[0] == "pe" and items[-1][-1][0][0] == "pe":
                    items[-1].append((a, k))
                else:
                    items.append([(a, k)])
            return items

        for h in range(0, 8, 2):
            ia, ib = record(h), record(h + 1)
            for i in range(max(len(ia), len(ib))):
                for it in (ia, ib):
                    if i < len(it):
                        for a, k in it[i]:
                            S.add(*a, **k)

    for c in range(0 if attn_only else KCHUNKS):
        gdn_chunk(c * 128, 128, None if c == 0 else "prev", False)
    for h in range(8):
        S.add("sp", lambda e, h=h: e.dma_start(out=ssm_p[h], in_=Sst[h]), reads=[f"S{h}"], slot=f"ssmo{h}")
        S.add("sp", lambda e, h=h: e.dma_start(out=Sst[h], in_=ssm0_d[h]), writes=[f"S{h}"], slot=f"ssmi{h}")
    if KSAMPLE and not attn_only:
        gdn_chunk(SEQ, NS, "state", True)
    for h in range(8):
        S.add("sp", lambda e, h=h: e.dma_start(out=ssm_s[h], in_=Sst[h]), reads=[f"S{h}"], slot=f"ssmo{h}")


    S.fence()
    hoff = [0]

    def balloc(ncols):
        o = hoff[0]
        hoff[0] += ncols
        assert hoff[0] <= FC * TWX
        return hT[:, o:o + ncols]

    A_pos = Y[0][0:1, :]
    A_ones = Y[1][0:1, :]
    A_kb1 = Y[2][0:1, :]
    A_qb0 = [Y[3][0:1, :], Y[4][0:1, :]]
    A_m = X[0][0:1, :]
    A_sm = X[0][0:1, 0:0]
    xo = [0]

    def xalloc(ncols):
        o = xo[0]
        xo[0] += ncols
        assert xo[0] <= D
        return X[1][:, o:o + ncols]

    a_ident = xalloc(128)
    a_mask = xalloc(128)
    a_O = [[xalloc(128) for s_ in range(4)] for m_ in range(2)]
    a_pd = xalloc(128)
    a_fin = xalloc(128)
    a_sub = xalloc(128)
    a_lam = xalloc(256)
    a_small = xalloc(32)
    a_row = xalloc(0)
    S.add("sp", lambda e: e.dma_start(out=A_pos, in_=aconst[0:1, 0:2048]), writes=["A_pos"], slot="A_pos")
    S.add("sp", lambda e: e.dma_start(out=a_lam, in_=aconst[:, 2048:2304]), writes=["a_lam"], slot="a_lam")
    S.add("sp", lambda e: e.dma_start(out=a_sub, in_=aconst[:, 2304:2432]), writes=["a_sub"], slot="a_sub")
    S.add("sp", lambda e: e.dma_start(out=a_ident, in_=gconst[:, 0:128]), writes=["a_ident"], slot="a_ident")
    S.add("sp", lambda e: e.dma_start(out=a_mask, in_=gconst[:, 640:768]), writes=["a_mask"], slot="a_mask")
    S.add("dve", lambda e: e.memset(A_ones, 1.0), writes=["A_ones"])
    S.add("dve", lambda e: e.tensor_scalar(out=a_sub, in0=a_sub, scalar1=0.8, scalar2=None, op0=ALU.mult),
          reads=["a_sub"], writes=["a_sub"])
    for i_ in range(2):
        S.add("dve", lambda e, i_=i_: e.tensor_tensor(out=a_lam[:, 128 * i_:128 * i_ + 64], in0=a_lam[:, 128 * i_:128 * i_ + 64],
                                                   in1=a_lam[:, 128 * i_ + 64:128 * i_ + 128], op=ALU.mult),
              reads=["a_lam"], writes=["a_lam"])
        S.add("dve", lambda e, i_=i_: e.reduce_sum(out=a_small[:, i_:i_ + 1], in_=a_lam[:, 128 * i_:128 * i_ + 64], axis=mybir.AxisListType.X),
              reads=["a_lam"], writes=["a_small"])
        S.add("act", lambda e, i_=i_: e.activation(out=a_small[:, i_:i_ + 1], in_=a_small[:, i_:i_ + 1], func=AF.Exp),
              reads=["a_small"], writes=["a_small"])
    S.add("dve", lambda e: e.tensor_tensor(out=a_small[:, 2:3], in0=a_small[:, 1:2], in1=a_small[:, 0:1], op=ALU.subtract),
          reads=["a_small"], writes=["a_small"])
    S.add("dve", lambda e: e.tensor_scalar(out=a_small[:, 2:3], in0=a_small[:, 2:3], scalar1=-0.2, scalar2=None, op0=ALU.add),
          reads=["a_small"], writes=["a_small"])
    neglam = a_small[:, 2:3]

    qTb = [balloc(SEQ) for _ in range(2)]
    kTb = [balloc(SEQ) for _ in range(2)]
    sqb = balloc(SEQ)
    Vx = [balloc(16 * 129) for _ in range(2)]
    Eb = [balloc(512) for _ in range(2)]
    onesb = balloc(8)
    finb2 = balloc(512)
    S.add("dve", lambda e: e.memset(onesb, 1.0), writes=["onesb"])

    AX = mybir.AxisListType.X

    def attn_head(h):
        hp = h % 2
        slope = 2.0 ** (-(h + 1))
        q_, k_, vx = qTb[hp], kTb[hp], Vx[hp]
        vx3 = vx.rearrange("p (b c) -> p b c", c=129)
        S.add("pool", lambda e: e.dma_start(out=q_, in_=projT[128 * h:128 * h + 128, 0:SEQ]), writes=[f"qTb{hp}"], slot=f"qTb{hp}")
        S.add("pool", lambda e: e.dma_start(out=k_, in_=projT[1024 + 128 * h:1024 + 128 * h + 128, 0:SEQ]), writes=[f"kTb{hp}"], slot=f"kTb{hp}")
        S.add("dve", lambda e: e.memset(vx, 1.0), writes=[f"Vx{hp}"])
        S.add("pool", lambda e: e.dma_start(out=vx3[:, :, 0:128],
                                            in_=v_out[0:SEQ, 128 * h:128 * h + 128].rearrange("(b p) c -> p b c", p=128)),
              writes=[f"Vx{hp}"], slot=f"Vx{hp}")
        S.add("dve", lambda e: e.tensor_scalar(out=A_kb1, in0=A_pos, scalar1=8.0 * slope, scalar2=None, op0=ALU.mult),
              reads=["A_pos"], writes=["A_kb1"])

        def sumsq(src, lo, cb, srcres):
            S.add("pe", lambda e: e.matmul(bank[6][0:1, :], lhsT=onesb[lo:lo + 64, 0:1], rhs=sqb[lo:lo + 64, cb * 512:(cb + 1) * 512],
                                           start=True, stop=True), reads=["onesb", "sqb"], writes=[N(bank[6])])

        def norms(m):
            lo = 64 * m
            S.add("act", lambda e: e.activation(out=sqb[lo:lo + 64, :], in_=k_[lo:lo + 64, :], func=AF.Square),
                  reads=[f"kTb{hp}"], writes=["sqb"])

            def kmax(cb):
                sumsq(k_, lo, cb, f"kTb{hp}")
                S.add("dve", lambda e: e.reduce_max(out=a_small[0:1, 4 + cb:5 + cb], in_=bank[6][0:1, :], axis=AX),
                      reads=[N(bank[6])], writes=["a_small"])

            for cb in range(4):
                kmax(cb)
            S.add("dve", lambda e: e.reduce_max(out=a_small[0:1, 8:9], in_=a_small[0:1, 4:8], axis=AX),
                  reads=["a_small"], writes=["a_small"])
            S.add("act", lambda e: e.activation(out=sqb[lo:lo + 64, :], in_=q_[lo:lo + 64, :], func=AF.Square),
                  reads=[f"qTb{hp}"], writes=["sqb"])

            def qn(cb):
                sumsq(q_, lo, cb, f"qTb{hp}")
                S.add("dve", lambda e: e.tensor_scalar(out=A_m[:, cb * 512:(cb + 1) * 512], in0=bank[6][0:1, :],
                                                       scalar1=a_small[0:1, 8:9], scalar2=1.21, op0=ALU.mult, op1=ALU.mult),
                      reads=[N(bank[6]), "a_small"], writes=["A_m"])

            for cb in range(4):
                qn(cb)
            S.add("act", lambda e: e.sqrt(out=A_m, in_=A_m), reads=["A_m"], writes=["A_m"])
            S.add("dve", lambda e: e.scalar_tensor_tensor(out=A_qb0[m], in0=A_pos, scalar=-8.0 * slope, in1=A_m,
                                                          op0=ALU.mult, op1=ALU.subtract),
                  reads=["A_pos", "A_m"], writes=[f"A_qb0{m}"])

        for m in range(2):
            norms(m)

        def qk_unit(Q, m, j, accs):
            lo = 64 * m
            eb = cnt["evac"] % 2
            cnt["evac"] += 1
            psb = bank[4 + eb]
            E = Eb[eb]
            S.add("pe", lambda e: e.matmul(psb[:, :], lhsT=k_[lo:lo + 64, 128 * j:128 * j + 128],
                                           rhs=q_[lo:lo + 64, 512 * Q:512 * Q + 512], start=True, stop=False),
                  reads=[f"kTb{hp}", f"qTb{hp}"], writes=[N(psb)])
            S.add("pe", lambda e: e.matmul(psb[:, :], lhsT=A_ones[:, 0:128], rhs=A_qb0[m][:, 512 * Q:512 * Q + 512],
                                           start=False, stop=False),
                  reads=["A_ones", f"A_qb0{m}"], writes=[N(psb)])
            S.add("pe", lambda e: e.matmul(psb[:, :], lhsT=A_kb1[:, 128 * j:128 * j + 128], rhs=A_ones[:, 0:512],
                                           start=False, stop=True),
                  reads=["A_ones", "A_kb1"], writes=[N(psb)])
            s0 = max(0, j - 4 * Q)
            S.add("act", lambda e: e.activation(out=E[:, 128 * s0:512], in_=psb[:, 128 * s0:512], func=AF.Exp, scale=0.125),
                  reads=[N(psb)], writes=[f"Eb{eb}"])

            def pv(s_):
                qi = 4 * Q + s_
                if j > qi:
                    return
                if j == qi:
                    S.add("dve", lambda e: e.tensor_tensor(out=E[:, 128 * s_:128 * s_ + 128], in0=E[:, 128 * s_:128 * s_ + 128],
                                                           in1=a_mask, op=ALU.mult),
                          reads=[f"Eb{eb}", "a_mask"], writes=[f"Eb{eb}"])
                acc, accr = accs[s_]
                S.add("pe", lambda e: e.matmul(acc, lhsT=E[:, 128 * s_:128 * s_ + 128], rhs=vx3[:, j, :],
                                               start=(j == 0), stop=(j == qi)),
                      reads=[f"Eb{eb}", f"Vx{hp}"], writes=[accr])

            for s_ in range(4):
                pv(s_)

        def qblock_map(Q, m):
            accs = [(bank[s_][:, 0:129], N(bank[s_])) for s_ in range(4)]
            for j in range(4 * Q + 4):
                qk_unit(Q, m, j, accs)

            def fin_acc(s_):
                acc, accr = accs[s_]
                S.add("dve", lambda e: e.reciprocal(out=a_small[:, 16 + s_:17 + s_], in_=acc[:, 128:129]),
                      reads=[accr], writes=["a_small"])
                S.add("dve", lambda e: e.tensor_scalar(out=a_O[m][s_], in0=acc[:, 0:128], scalar1=a_small[:, 16 + s_:17 + s_],
                                                       scalar2=None, op0=ALU.mult),
                      reads=[accr, "a_small"], writes=[f"a_O{m}{s_}"])

            for s_ in range(4):
                fin_acc(s_)

        def fin_sub(Q, s_):
            S.add("dve", lambda e: e.scalar_tensor_tensor(out=a_pd, in0=a_O[1][s_], scalar=neglam, in1=a_O[0][s_],
                                                          op0=ALU.mult, op1=ALU.add),
                  reads=[f"a_O0{s_}", f"a_O1{s_}", "a_small"], writes=["a_pd"])
            S.add("act", lambda e: e.activation(out=a_fin, in_=a_pd, func=AF.Square, accum_out=a_small[:, 20:21]),
                  reads=["a_pd"], writes=["a_fin", "a_small"])
            S.add("dve", lambda e: e.tensor_scalar(out=a_small[:, 21:22], in0=a_small[:, 20:21], scalar1=1.0 / 128, scalar2=EPS,
                                                   op0=ALU.mult, op1=ALU.add), reads=["a_small"], writes=["a_small"])
            S.add("act", lambda e: e.sqrt(out=a_small[:, 21:22], in_=a_small[:, 21:22]), reads=["a_small"], writes=["a_small"])
            S.add("dve", lambda e: e.reciprocal(out=a_small[:, 22:23], in_=a_small[:, 21:22]), reads=["a_small"], writes=["a_small"])
            S.add("dve", lambda e: e.scalar_tensor_tensor(out=a_fin, in0=a_pd, scalar=a_small[:, 22:23], in1=a_sub,
                                                          op0=ALU.mult, op1=ALU.mult),
                  reads=["a_pd", "a_small", "a_sub"], writes=["a_fin"])
            S.add("pe", lambda e: e.matmul(bank[6][:, 0:128], lhsT=a_fin, rhs=a_ident, start=True, stop=True),
                  reads=["a_fin", "a_ident"], writes=[N(bank[6])])
            S.add("act", lambda e: e.copy(out=finb2[:, 128 * s_:128 * s_ + 128], in_=bank[6][:, 0:128]),
                  reads=[N(bank[6])], writes=["finb2"])

        def qblock(Q):
            for m in range(2):
                qblock_map(Q, m)
            for s_ in range(4):
                fin_sub(Q, s_)
            S.add("sp", lambda e: e.dma_start(out=mixT_d[128 * h:128 * h + 128, 512 * Q:512 * Q + 512], in_=finb2),
                  reads=["finb2"], slot="finb2o")

        for Q in range(4):
            qblock(Q)

    for h in range(aheads):
        attn_head(h)

    def sample_attn():
        S.fence()
        KV = [Y[0], Y[1]]
        KTs = [Y[2][:, 0:1024], Y[2][:, 1024:2048]]
        C = Y[3]
        posp, slopeRow, maskS = C[:, 0:128], C[:, 128:256], C[:, 256:384]
        pcol, subcol = C[:, 384:385], C[:, 385:386]
        identS, onesS = C[:, 512:640], C[:, 640:768]
        qs, knew = C[:, 768:832], C[:, 832:896]
        Sb = [C[:, 896:1024], C[:, 1024:1152]]
        Es = [C[:, 1152:1280], C[:, 1280:1408]]
        Rr, On, pd, sq, rstd = C[:, 1408:1536], C[:, 1536:1664], C[:, 1664:1728], C[:, 1728:1792], C[:, 1792:1856]
        vnew = Y[4][0:8, 0:1024]
        fins = finb[0][:, 0:64]
        qs3 = qs.rearrange("p (h q) -> p h q", q=8)
        knew3 = knew.rearrange("p (h q) -> p h q", q=8)
        S.add("sp", lambda e: e.dma_start(out=C[:, 0:512], in_=sconst), writes=["sC"], slot="sC")
        S.add("sp", lambda e: e.dma_start(out=identS, in_=gconst[:, 0:128]), writes=["sI"], slot="sI")
        S.add("sp", lambda e: e.dma_start(out=onesS, in_=gconst[:, 512:640]), writes=["sO"], slot="sO")
        S.add("sp", lambda e: e.dma_start(out=ptidx[:], in_=pt_d), writes=["ptidx"], slot="ptidx")
        S.add("dve", lambda e: e.tensor_scalar(out=ptidx2[:], in0=ptidx[:], scalar1=128, scalar2=None, op0=ALU.mult),
              reads=["ptidx"], writes=["ptidx2"])
        S.add("sp", lambda e: e.dma_start(out=qs3, in_=projT[0:1024, SEQ:SEQ + NS].rearrange("(h p) q -> p h q", p=128)),
              writes=["qs"], slot="qs")
        S.add("sp", lambda e: e.dma_start(out=knew3, in_=projT[1024:2048, SEQ:SEQ + NS].rearrange("(h p) q -> p h q", p=128)),
              writes=["knew"], slot="knew")
        S.add("sp", lambda e: e.dma_start(out=vnew, in_=v_out[SEQ:SEQ + NS, :]), writes=["vnew"], slot="vnew")
        SA = int(os.environ.get("SA_STAGE", "9"))
        qm = C[:, 1856:1984]
        qm4 = qm.rearrange("p (h m q) -> p h m q", m=2, q=8)
        S.add("dve", lambda e: e.memset(qm, 0.0), writes=["qm"])
        S.add("dve", lambda e: e.tensor_scalar(out=qm4[0:64, :, 0, :], in0=qs3[0:64, :, :], scalar1=0.125, scalar2=None, op0=ALU.mult),
              reads=["qs"], writes=["qm"])
        S.add("dve", lambda e: e.tensor_scalar(out=qm4[64:128, :, 1, :], in0=qs3[64:128, :, :], scalar1=0.125, scalar2=None, op0=ALU.mult),
              reads=["qs"], writes=["qm"])
        accO, accOr = bank[3][:, 0:128], N(bank[3])
        accR, accRr = bank[4][:, 0:128], N(bank[4])
        scb, scbr = bank[2], N(bank[2])

        def step(tok):
            b = tok % 2
            tb = (bank[0], bank[1]) if b == 0 else (bank[5], bank[6])
            S.add("pool", lambda e: e.indirect_dma_start(
                out=KV[b][:, 0:1024], out_offset=None, in_=ck_d[:, :],
                in_offset=bass.IndirectOffsetOnAxis(ap=ptidx2[:, 0:1], axis=0), element_offset=tok * 1024),
                reads=["ptidx2"], writes=[f"KVk{b}"], slot=f"KVk{b}")
            S.add("pool", lambda e: e.indirect_dma_start(
                out=KV[b][:, 1024:2048], out_offset=None, in_=cv_d[:, :],
                in_offset=bass.IndirectOffsetOnAxis(ap=ptidx2[:, 0:1], axis=0), element_offset=tok * 1024),
                reads=["ptidx2"], writes=[f"KVv{b}"], slot=f"KVv{b}")

            def tr(half):
                for hh in range(4):
                    h = half * 4 + hh
                    S.add("pe", lambda e, h=h, hh=hh: e.matmul(tb[half][:, hh * 128:(hh + 1) * 128], lhsT=KV[b][:, h * 128:(h + 1) * 128],
                                                               rhs=identS, start=True, stop=True),
                          reads=[f"KVk{b}", "sI"], writes=[N(tb[half])])
                if half == 0:
                    S.add("act", lambda e: e.copy(out=KTs[b][:, 0:512], in_=tb[0][:, :]), reads=[N(tb[0])], writes=[f"KT{b}a"])
                else:
                    S.add("dve", lambda e: e.tensor_copy(out=KTs[b][:, 512:1024], in_=tb[1][:, :]), reads=[N(tb[1])], writes=[f"KT{b}b"])

            if SA < 2:
                return
            tr(0)
            tr(1)
            if SA < 3:
                return
            for h in range(8):
                S.add("pe", lambda e, h=h: e.matmul(scb[:, h * 16:(h + 1) * 16],
                                                    lhsT=KTs[b][:, h * 128:(h + 1) * 128],
                                                    rhs=qm[:, h * 16:(h + 1) * 16], start=True, stop=True),
                      reads=[f"KT{b}a" if h < 4 else f"KT{b}b", "qm"], writes=[scbr])
            S.add("dve", lambda e: e.scalar_tensor_tensor(out=Sb[b], in0=slopeRow, scalar=posp[:, tok:tok + 1], in1=scb[:, 0:128],
                                                          op0=ALU.mult, op1=ALU.add),
                  reads=["sC", scbr], writes=[f"Sb{b}"])
            S.add("act", lambda e: e.activation(out=Es[b], in_=Sb[b], func=AF.Exp), reads=[f"Sb{b}"], writes=[f"Es{b}"])
            if SA < 4:
                return
            for h in range(8):
                S.add("pe", lambda e, h=h: e.matmul(accO[:, h * 16:(h + 1) * 16], lhsT=KV[b][:, 1024 + h * 128:1024 + (h + 1) * 128],
                                                    rhs=Es[b][:, h * 16:(h + 1) * 16], start=False, stop=False),
                      reads=[f"KVv{b}", f"Es{b}"], writes=[accOr])
            S.add("pe", lambda e: e.matmul(accR, lhsT=onesS, rhs=Es[b], start=(tok == 0), stop=False),
                  reads=["sO", f"Es{b}"], writes=[accRr])

        zerosS = Y[4][:, 1024:1152]
        S.add("dve", lambda e: e.memset(zerosS, 0.0), writes=["zerosS"])
        if SA >= 4:
            S.add("pe", lambda e: e.matmul(accO, lhsT=onesS, rhs=zerosS, start=True, stop=False),
                  reads=["sO", "zerosS"], writes=[accOr])
        for tok in range(128):
            step(tok)
        if SA < 5:
            return
        for h in range(8):
            S.add("pe", lambda e, h=h: e.matmul(scb[0:NS, h * 16:(h + 1) * 16],
                                                lhsT=knew3[:, h, :],
                                                rhs=qm[:, h * 16:(h + 1) * 16], start=True, stop=True),
                  reads=["knew", "qm"], writes=[scbr])
        S.add("dve", lambda e: e.scalar_tensor_tensor(out=Sb[0][0:NS, :], in0=slopeRow[0:NS, :], scalar=pcol[0:NS, :], in1=scb[0:NS, 0:128],
                                                      op0=ALU.mult, op1=ALU.add),
              reads=["sC", scbr], writes=["Sb0"])
        S.add("act", lambda e: e.activation(out=Es[0][0:NS, :], in_=Sb[0][0:NS, :], func=AF.Exp), reads=["Sb0"], writes=["Es0"])
        S.add("dve", lambda e: e.tensor_tensor(out=Es[0][0:NS, :], in0=Es[0][0:NS, :], in1=maskS[0:NS, :], op=ALU.mult),
              reads=["Es0", "sC"], writes=["Es0"])
        for h in range(8):
            S.add("pe", lambda e, h=h: e.matmul(accO[:, h * 16:(h + 1) * 16], lhsT=vnew[:, h * 128:(h + 1) * 128],
                                                rhs=Es[0][0:NS, h * 16:(h + 1) * 16], start=False, stop=(h == 7)),
                  reads=["vnew", "Es0"], writes=[accOr])
        S.add("pe", lambda e: e.matmul(accR, lhsT=onesS[0:NS, :], rhs=Es[0][0:NS, :], start=False, stop=True),
              reads=["sO", "Es0"], writes=[accRr])
        S.add("dve", lambda e: e.reciprocal(out=Rr, in_=accR), reads=[accRr], writes=["sRr"])
        S.add("dve", lambda e: e.tensor_tensor(out=On, in0=accO, in1=Rr, op=ALU.mult), reads=[accOr, "sRr"], writes=["sOn"])
        On4 = On.rearrange("p (h m q) -> p h m q", m=2, q=8)
        pd3 = pd.rearrange("p (h q) -> p h q", q=8)
        S.add("dve", lambda e: e.scalar_tensor_tensor(out=pd3, in0=On4[:, :, 1, :], scalar=neglam, in1=On4[:, :, 0, :],
                                                      op0=ALU.mult, op1=ALU.add),
              reads=["sOn", "a_small"], writes=["spd"])
        S.add("act", lambda e: e.activation(out=sq, in_=pd, func=AF.Square), reads=["spd"], writes=["ssq"])
        S.add("pe", lambda e: e.matmul(scb[:, 0:64], lhsT=onesS, rhs=sq, start=True, stop=True), reads=["sO", "ssq"], writes=[scbr])
        S.add("dve", lambda e: e.tensor_scalar(out=rstd, in0=scb[:, 0:64], scalar1=1.0 / 128, scalar2=EPS, op0=ALU.mult, op1=ALU.add),
              reads=[scbr], writes=["srstd"])
        S.add("act", lambda e: e.sqrt(out=rstd, in_=rstd), reads=["srstd"], writes=["srstd"])
        S.add("dve", lambda e: e.reciprocal(out=rstd, in_=rstd), reads=["srstd"], writes=["srstd"])
        S.add("dve", lambda e: e.tensor_tensor(out=pd, in0=pd, in1=rstd, op=ALU.mult), reads=["spd", "srstd"], writes=["spd"])
        S.add("dve", lambda e: e.tensor_scalar(out=fins, in0=pd, scalar1=subcol, scalar2=0.8, op0=ALU.mult, op1=ALU.mult),
              reads=["spd", "sC"], writes=["finb0"])
        S.add("sp", lambda e: e.dma_start(out=mixT_d[0:1024, SEQ:SEQ + NS].rearrange("(h p) q -> p h q", p=128),
                                          in_=fins.rearrange("p (h q) -> p h q", q=8)),
              reads=["finb0"], slot="finbo0")

    if not skip_sample:
        sample_attn()

    if not skip_p3:
        S.fence()
        for k2 in ("f2pre", "f2post", "mixpost"):
            S.add("sp", lambda e, k2=k2: e.dma_start(out=wbc[k2][:], in_=wbc2_d[k2]), writes=["wbc" + wres[k2]], slot="wbc" + wres[k2])
        for t in range(NT):
            phase3_tile(t)

    S.final_slots = list(S.dma_cnt.keys())
    S.emit(nc, stack)
    stack.close()
    return nc


def fm_blocks(w, gg):
    K, N = w.shape
    W = 128 * gg
    a = w.reshape(K // 128, 128, N // W, W).transpose(2, 1, 0, 3)
    return np.ascontiguousarray(a).reshape(N // W, 128, (K // 128) * W)


def tm_blocks(w, kgsz):
    K, N = w.shape
    a = w.reshape(K // (128 * kgsz), kgsz, 128, N // 512, 512).transpose(3, 0, 2, 1, 4)
    return np.ascontiguousarray(a).reshape(N // 512, K // (128 * kgsz), 128, kgsz * 512)


def gconst_host():
    i = np.arange(128)
    ident = np.eye(128, dtype=np.float32)
    U = (i[:, None] <= i[None, :]).astype(np.float32)
    mL = -(i[:, None] > i[None, :]).astype(np.float32)
    mU = -(i[None, :] > i[:, None]).astype(np.float32)
    ones = np.ones((128, 128), np.float32)
    mUd = (i[None, :] >= i[:, None]).astype(np.float32)
    return np.ascontiguousarray(np.concatenate([ident, U, mL, mU, ones, mUd], axis=1))


def hb_host(a_log, dt_bias, dnw=None):
    hb = np.zeros((128, 32), np.float32)
    hb[:, 0:8] = a_log[None, :]
    hb[:, 8:16] = dt_bias[None, :]
    hb[:NS, 16] = 1.0
    if dnw is not None:
        hb[:, 17] = dnw
    return hb


def aconst_host(lq1, lk1, lq2, lk2, subln):
    a = np.zeros((128, 2048 + 256 + 128), np.float32)
    a[0, 0:2048] = np.arange(2048, dtype=np.float32)
    a[:, 2048:2112] = lq1[None]
    a[:, 2112:2176] = lk1[None]
    a[:, 2176:2240] = lq2[None]
    a[:, 2240:2304] = lk2[None]
    a[:, 2304:2432] = subln[None]
    return a


def sconst_host(subln):
    a = np.zeros((128, 512), np.float32)
    pp = np.arange(128, dtype=np.float32)
    a[:, 0:128] = 128.0 * pp[:, None] + pp[None, :] - 16384.0
    col = np.arange(128)
    a[:, 128:256] = (2.0 ** (-(col // 16 + 1).astype(np.float32)))[None, :]
    a[:, 256:384] = ((col % 8)[None, :] >= np.arange(128)[:, None]).astype(np.float32)
    a[:, 384] = pp
    a[:, 385] = subln
    return a


_NC = None


def kernel(**inp):
    import ml_dtypes
    global _NC
    if _NC is None:
        _NC = build()
    nc = _NC
    f = lambda k: np.asarray(inp[k], dtype=np.float32)
    xp, xs = f("x_prompt"), f("x_sample")
    w_in = f("w_in")[0]
    fm_cols = np.concatenate([w_in[:, 0:2048], w_in[:, 3072:7168]], axis=1)
    shared = {
        "ident": np.eye(128, dtype=np.float32).astype(ml_dtypes.bfloat16),
        "wbc_f1pre": np.ascontiguousarray(np.broadcast_to(f("ffn1_pre_w")[0], (128, D))),
        "wbc_f1post": np.ascontiguousarray(np.broadcast_to(f("ffn1_post_w")[0], (128, D))),
        "wbc_mixpre": np.ascontiguousarray(np.broadcast_to(f("mix_pre_w")[0], (128, D))),
        "wg1": fm_blocks(f("ffn1_gate")[0], 2),
        "wu1": fm_blocks(f("ffn1_up")[0], 2),
        "wd1": tm_blocks(f("ffn1_down")[0], 11),
        "win_fm": fm_blocks(fm_cols, 2),
        "win_v": tm_blocks(w_in[:, 2048:3072], 4),
        "win_ba": np.ascontiguousarray(w_in[:, 7168:7184].reshape(KC, 128, 16).transpose(1, 0, 2)).reshape(128, KC * 16),
        "gconst": gconst_host(),
        "cw_d": np.ascontiguousarray(f("conv_w")[0].T.reshape(24, 128, 4).transpose(1, 0, 2)).reshape(128, 96),
        "hb_d": hb_host(f("a_log")[0], f("dt_bias")[0], f("delta_norm_w")[0]),
        "aconst": aconst_host(f("lambda_q1")[0], f("lambda_k1")[0], f("lambda_q2")[0], f("lambda_k2")[0], f("subln_w")[0]),
        "wbc_f2pre": np.ascontiguousarray(np.broadcast_to(f("ffn2_pre_w")[0], (128, D))),
        "wbc_f2post": np.ascontiguousarray(np.broadcast_to(f("ffn2_post_w")[0], (128, D))),
        "wbc_mixpost": np.ascontiguousarray(np.broadcast_to(f("mix_post_w")[0], (128, D))),
        "wg2": fm_blocks(f("ffn2_gate")[0], 2),
        "wu2": fm_blocks(f("ffn2_up")[0], 2),
        "wd2": tm_blocks(f("ffn2_down")[0], 11),
        "wout": tm_blocks(f("w_out")[0], 4),
        "cache_k": f("cache_k")[0].reshape(-1, 1024),
        "cache_v": f("cache_v")[0].reshape(-1, 1024),
        "sconst": sconst_host(f("subln_w")[0]),
    }
    page_table = np.asarray(inp["page_table"]).astype(np.int32)
    ssm0 = f("state_ssm")[0]
    convst = f("state_conv")[0]
    in_maps = []
    for c in range(NCORES):
        m = dict(shared)
        m["xin"] = np.ascontiguousarray(np.concatenate([xp[c % 4], xs[c]], axis=0))
        m["ssm0"] = np.ascontiguousarray(ssm0[c])
        m["convst"] = np.ascontiguousarray(convst[c].T.reshape(24, 128, 3))
        m["pt"] = np.ascontiguousarray(page_table[c].reshape(128, 1))
        in_maps.append(m)
    res = run_bass_kernel_spmd(nc, in_maps, core_ids=list(range(NCORES)))
    R = [{k: np.asarray(v) for k, v in r.items()} for r in res.results]
    B = 4
    k_prompt = np.stack([R[b]["projT"][1024:2048, :SEQ].T.reshape(SEQ, 8, 128) for b in range(B)])[None]
    v_prompt = np.stack([R[b]["v_out"][:SEQ].reshape(SEQ, 8, 128) for b in range(B)])[None]
    conv_prompt = np.stack([R[b]["projT"][2048:5120, SEQ - 3:SEQ].T for b in range(B)])[None]
    k_sample = np.stack([R[c]["projT"][1024:2048, SEQ:].T.reshape(NS, 8, 128) for c in range(8)])[None]
    v_sample = np.stack([R[c]["v_out"][SEQ:].reshape(NS, 8, 128) for c in range(8)])[None]
    conv_sample = np.stack([R[c]["projT"][2048:5120, NROW - 3:NROW].T for c in range(8)])[None]
    y_prompt = np.stack([R[b]["y"][:SEQ] for b in range(B)])
    y_sample = np.stack([R[c]["y"][SEQ:] for c in range(8)])
    ssm_prompt = np.stack([R[b]["ssm_p"] for b in range(B)])[None]
    ssm_sample = np.stack([R[c]["ssm_s"] for c in range(8)])[None]
    out = (y_prompt, y_sample, k_prompt, v_prompt, ssm_prompt, conv_prompt,
           k_sample, v_sample, ssm_sample, conv_sample)
    return tuple(np.ascontiguousarray(o, dtype=np.float32) for o in out)
```

```python
import os
from contextlib import ExitStack
import numpy as np
import concourse.bass as bass
import concourse.mybir as mybir
from concourse.bass_utils import run_bass_kernel_spmd

F32 = mybir.dt.float32
BF16 = mybir.dt.bfloat16
I32 = mybir.dt.int32
AF = mybir.ActivationFunctionType
ALU = mybir.AluOpType

D = 2048
DFF = 5632
SEQ = 2048
NS = 8
NROW = SEQ + NS
TW = 512
NT = SEQ // TW
KC = D // 128
FC = DFF // 128
INW = 7184
EPS = 1e-6
NCORES = 8


class Sched:
    ENG = ("pe", "act", "dve", "pool", "sp")

    def __init__(self):
        self.ops = {e: [] for e in self.ENG}
        self.lastw = {}
        self.readers = {}
        self.dma_cnt = {}
        self.waited = {e: {} for e in self.ENG}

    def add(self, eng, fn, reads=(), writes=(), slot=None):
        op = dict(eng=eng, fn=fn, waits=[], flag=False, slot=slot, idx=len(self.ops[eng]))
        if slot is not None:
            self.dma_cnt[slot] = self.dma_cnt.get(slot, 0) + 1
            ev = ("dma", slot, self.dma_cnt[slot], op)
        else:
            ev = ("eng", eng, op["idx"], op)
        deps = []
        for r in reads:
            if r in self.lastw:
                deps.append(self.lastw[r])
        for w in writes:
            if w in self.lastw:
                deps.append(self.lastw[w])
            deps.extend(self.readers.get(w, []))
        best = {}
        for d in deps:
            k = (d[0], d[1])
            if k not in best or best[k][2] < d[2]:
                best[k] = d
        for d in best.values():
            kind, key, n, dop = d
            if kind == "eng" and key == eng and eng == "pe":
                continue
            k = (kind, key)
            if self.waited[eng].get(k, -1) >= n:
                continue
            self.waited[eng][k] = n
            op["waits"].append(d)
            if kind == "eng":
                dop["flag"] = True
        for r in reads:
            self.readers.setdefault(r, []).append(ev)
        for w in writes:
            self.lastw[w] = ev
            self.readers[w] = []
        self.ops[eng].append(op)
        return ev

    def fence(self):
        evs = []
        for e in self.ENG:
            for op in reversed(self.ops[e]):
                if op["fn"] is not None and op["slot"] is None:
                    evs.append(("eng", e, op["idx"], op))
                    break
        for sl, n in self.dma_cnt.items():
            evs.append(("dma", sl, n, None))
        for e in self.ENG:
            op = dict(eng=e, fn=None, waits=[], flag=False, slot=None, idx=len(self.ops[e]))
            for d in evs:
                kind, key, n, dop = d
                if kind == "eng" and key == e:
                    continue
                k = (kind, key)
                if self.waited[e].get(k, -1) >= n:
                    continue
                self.waited[e][k] = n
                op["waits"].append(d)
                if kind == "eng":
                    dop["flag"] = True
            self.ops[e].append(op)

    def emit(self, nc, stack):
        esem = {e: stack.enter_context(nc.semaphore("sem_" + e)) for e in self.ENG}
        ssem = {s: stack.enter_context(nc.semaphore("slot_" + str(s))) for s in self.dma_cnt}
        for e in self.ENG:
            c = 0
            for op in self.ops[e]:
                if op["flag"]:
                    c += 1
                op["cnt"] = c
        block = stack.enter_context(nc.Block())

        def run(e):
            def body(engine):
                for op in self.ops[e]:
                    for kind, key, n, dop in op["waits"]:
                        if kind == "eng":
                            engine.wait_ge(esem[key], dop["cnt"])
                        else:
                            engine.wait_ge(ssem[key], 16 * n)
                    if op["fn"] is None:
                        continue
                    ins = op["fn"](engine)
                    if op["slot"] is not None:
                        ins.then_inc(ssem[op["slot"]], 16)
                    elif op["flag"]:
                        ins.then_inc(esem[e], 1)
            return body

        def run_sp(engine):
            run("sp")(engine)
            for sl in getattr(self, "final_slots", []):
                if sl in ssem:
                    engine.wait_ge(ssem[sl], 16 * self.dma_cnt[sl])

        block.tensor(run("pe"))
        block.scalar(run("act"))
        block.vector(run("dve"))
        block.gpsimd(run("pool"))
        block.sync(run_sp)


def subtiles(t):
    st = [(t * TW + 128 * s, 128, 128 * s) for s in range(TW // 128)]
    if t == NT - 1:
        st.append((SEQ, NS, TW))
    return st


def passes(t):
    p = [(0, TW)]
    if t == NT - 1:
        p.append((TW, NS))
    return p


def build(gdn_only=False, attn_only=False, skip_p3=False, npool=1280, aheads=8, skip_sample=False):
    nc = bass.Bass("TRN2", target_bir_lowering=False)
    S = Sched()
    stack = ExitStack()

    def din(name, shape, dt=F32):
        return nc.dram_tensor(name, shape, dt, kind="ExternalInput").ap()

    def dout(name, shape, dt=F32):
        return nc.dram_tensor(name, shape, dt, kind="ExternalOutput").ap()

    xin = din("xin", [NROW, D])
    ident_d = din("ident", [128, 128], BF16)
    wbc_d = {k: din("wbc_" + k, [128, D]) for k in ("f1pre", "f1post", "mixpre")}
    GG = 2
    wg1 = din("wg1", [FC // GG, 128, KC * 128 * GG])
    wu1 = din("wu1", [FC // GG, 128, KC * 128 * GG])
    KGD = 11
    wd1 = din("wd1", [4, FC // KGD, 128, KGD * 512])
    NFM = 48
    win_fm = din("win_fm", [NFM // GG, 128, KC * 128 * GG])
    KGV = 4
    win_v = din("win_v", [2, KC // KGV, 128, KGV * 512])

    win_ba = din("win_ba", [128, KC * 16])
    gconst = din("gconst", [128, 6 * 128])
    cw_d = din("cw_d", [128, 96])
    hb_d = din("hb_d", [128, 32])
    ssm0_d = din("ssm0", [8, 128, 128])
    convst_d = din("convst", [24, 128, 3])
    ba_d = (din if gdn_only else dout)("ba_d", [NROW, 16])
    aconst = din("aconst", [128, 2048 + 256 + 128 + 16])
    mixT_d = dout("mixT_d", [2048, NROW], BF16)
    ssm_p = dout("ssm_p", [8, 128, 128])
    ssm_s = dout("ssm_s", [8, 128, 128])
    projT = (din if gdn_only else dout)("projT", [NFM * 128, NROW])
    v_out = (din if gdn_only else dout)("v_out", [NROW, 1024])
    x1_d = dout("x1", [NROW, D])
    x2_d = dout("x2", [NROW, D])
    y_d = dout("y", [NROW, D])
    wbc2_d = {k: din("wbc_" + k, [128, D]) for k in ("f2pre", "f2post", "mixpost")}
    wg2 = din("wg2", [FC // GG, 128, KC * 128 * GG])
    wu2 = din("wu2", [FC // GG, 128, KC * 128 * GG])
    wd2 = din("wd2", [4, FC // KGD, 128, KGD * 512])
    wout_d = din("wout", [4, KC // KGV, 128, KGV * 512])
    ck_d = din("cache_k", [npool * 128, 1024])
    cv_d = din("cache_v", [npool * 128, 1024])
    pt_d = din("pt", [128, 1], I32)
    sconst = din("sconst", [128, 512])

    nm = {}

    def sb(name, shape, dt=F32):
        t_ = stack.enter_context(nc.sbuf_tensor(name, shape, dt))
        nm[id(t_)] = name
        return t_

    def ps(name, shape, dt=F32):
        t_ = stack.enter_context(nc.psum_tensor(name, shape, dt))
        nm[id(t_)] = name
        return t_

    def N(t_):
        return nm[id(t_)]

    TWX = TW + NS
    ident = sb("ident_sb", [128, 128], BF16)
    wbc = {k: sb("wbc_sb_" + k, [128, D]) for k in wbc_d}
    X = [sb(f"X{i}", [128, D]) for i in range(2)]
    Y = [sb(f"Y{i}", [128, D]) for i in range(5)]
    XN = sb("XN", [128, D], BF16)
    xnT = sb("xnT", [128, KC * TWX], BF16)
    hT = sb("hT", [128, FC * TWX], BF16)
    sm = sb("small", [128, 64])
    wgS = [sb(f"wgS{i}", [128, KC * 128 * GG], BF16) for i in range(2)]
    wuS = [sb(f"wuS{i}", [128, KC * 128 * GG], BF16) for i in range(2)]
    wdS = [sb(f"wdS{i}", [128, KGD * 512], BF16) for i in range(2)]
    stg = [sb(f"stg{i}", [128, TWX]) for i in range(2)]
    wba = sb("wba", [128, KC * 16], BF16)
    finb = [sb(f"finb{i}", [128, 128], BF16) for i in range(2)]
    ptidx = sb("ptidx", [128, 1], I32)
    ptidx2 = sb("ptidx2", [128, 1], I32)
    tmpg = stg
    bank = [ps(f"bank{i}", [128, 512]) for i in range(7)]
    psT = ps("psT", [128, 1024], BF16)

    xnT3 = xnT[:].rearrange("p (c t) -> p c t", t=TWX)
    hT3 = hT[:].rearrange("p (c t) -> p c t", t=TWX)
    psT3 = psT[:].rearrange("p (c t) -> p c t", t=128)

    cnt = {"x": 0, "g": 0, "d": 0, "stg": 0, "tmp": 0, "evac": 0}

    S.add("pool", lambda e: e.dma_start(out=wba[:], in_=win_ba), writes=["wba"], slot="wba")
    S.add("sp", lambda e: e.dma_start(out=ident[:], in_=ident_d), writes=["ident"], slot="ident")
    for k in wbc:
        S.add("sp", lambda e, k=k: e.dma_start(out=wbc[k][:], in_=wbc_d[k]), writes=["wbc" + k], slot="wbc" + k)

    def load_x(src, row0, n, extra_reads=()):
        i = cnt["x"] % 2
        cnt["x"] += 1
        S.add("sp", lambda e: e.dma_start(out=X[i][:n, :], in_=src[row0:row0 + n, :]),
              reads=list(extra_reads), writes=[f"X{i}"], slot=f"X{i}")
        return i

    def rstd_of(src_ap, n, src_res, col):
        S.add("act", lambda e: e.activation(out=XN[:n, :], in_=src_ap, func=AF.Square,
                                            accum_out=sm[:n, col:col + 1]),
              reads=src_res, writes=["XN", f"sm{col}"])
        S.add("dve", lambda e: e.tensor_scalar(out=sm[:n, col + 1:col + 2], in0=sm[:n, col:col + 1],
                                               scalar1=1.0 / D, scalar2=EPS, op0=ALU.mult, op1=ALU.add),
              reads=[f"sm{col}"], writes=[f"sm{col + 1}"])
        S.add("act", lambda e: e.sqrt(out=sm[:n, col + 2:col + 3], in_=sm[:n, col + 1:col + 2]),
              reads=[f"sm{col + 1}"], writes=[f"sm{col + 2}"])
        S.add("dve", lambda e: e.reciprocal(out=sm[:n, col + 3:col + 4], in_=sm[:n, col + 2:col + 3]),
              reads=[f"sm{col + 2}"], writes=[f"sm{col + 3}"])
        return sm[:n, col + 3:col + 4], f"sm{col + 3}"

    def norm_T(src_ap, src_res, n, col, wkey):
        rs, rres = rstd_of(src_ap, n, src_res, 0)
        S.add("dve", lambda e: e.scalar_tensor_tensor(out=XN[:n, :], in0=src_ap, scalar=rs,
                                                      in1=wbc[wkey][:n, :], op0=ALU.mult, op1=ALU.mult),
              reads=src_res + [rres, "wbc" + wres[wkey]], writes=["XN"])
        for half in range(2):
            for j in range(8):
                c = half * 8 + j
                S.add("pe", lambda e, c=c, j=j: e.transpose(out=psT3[:, j, :n], in_=XN[:n, c * 128:(c + 1) * 128],
                                                            identity=ident[:n, :n]),
                      reads=["XN", "ident"], writes=["psT"])
            eng = "act" if half == 0 else "dve"
            if eng == "act":
                S.add("act", lambda e, half=half: e.copy(out=xnT3[:, half * 8:half * 8 + 8, col:col + n],
                                                         in_=psT3[:, :, :n]),
                      reads=["psT"], writes=["xnT"])
            else:
                S.add("dve", lambda e, half=half: e.tensor_copy(out=xnT3[:, half * 8:half * 8 + 8, col:col + n],
                                                                in_=psT3[:, :, :n]),
                      reads=["psT"], writes=["xnT"])

    def linear_fm(t, wsrc, ngroups, stages, skey, evac):
        for g in range(ngroups):
            i = cnt[skey] % 2
            cnt[skey] += 1
            for (arr, st) in zip(wsrc, stages):
                S.add("pool", lambda e, arr=arr, st=st, g=g, i=i: e.dma_start(out=st[i][:], in_=arr[g]),
                      writes=[N(st[i])], slot=N(st[i]))
            for gi in range(GG):
                oc = g * GG + gi
                for (c0, n) in passes(t):
                    outs = []
                    for wi, st in enumerate(stages):
                        b = bank[(cnt["evac"] % 2) * len(stages) + wi]
                        st3 = st[i][:].rearrange("p (k f) -> p k f", f=128 * GG)
                        for kc in range(KC):
                            S.add("pe", lambda e, b=b, st3=st3, kc=kc, gi=gi, c0=c0, n=n: e.matmul(
                                b[:, :n], lhsT=st3[:, kc, gi * 128:(gi + 1) * 128], rhs=xnT3[:, kc, c0:c0 + n],
                                start=(kc == 0), stop=(kc == KC - 1)),
                                reads=[N(st[i]), "xnT"], writes=[N(b)])
                        outs.append(b)
                    cnt["evac"] += 1
                    evac(oc, c0, n, outs)

    def linear_tm(t, wsrc, nn, nkg, kgsz, stages, skey, lhs3, lres, evac):
        sts = subtiles(t)
        for nt_ in range(nn):
            for kg in range(nkg):
                i = cnt[skey] % 2
                cnt[skey] += 1
                S.add("pool", lambda e, nt_=nt_, kg=kg, i=i: e.dma_start(out=stages[i][:, :kgsz * 512],
                                                                         in_=wsrc[nt_, kg]),
                      writes=[N(stages[i])], slot=N(stages[i]))
                st3 = stages[i][:].rearrange("p (k f) -> p k f", f=512)
                for kk in range(kgsz):
                    k = kg * kgsz + kk
                    for si, (row0, n, col) in enumerate(sts):
                        b = bank[2 + si]
                        S.add("pe", lambda e, b=b, st3=st3, kk=kk, k=k, n=n, col=col: e.matmul(
                            b[:n, :], lhsT=lhs3[:, k, col:col + n], rhs=st3[:, kk, :],
                            start=(k == 0), stop=(k == nkg * kgsz - 1)),
                            reads=[N(stages[i]), lres], writes=[N(b)])
            for si, (row0, n, col) in enumerate(sts):
                evac(nt_, si, row0, n, bank[2 + si])

    wres = {"f1pre": "f1pre", "f1post": "f1post", "mixpre": "mixpre",
            "f2pre": "f1pre", "f2post": "f1post", "mixpost": "mixpre"}
    for k2, k1 in list(wres.items()):
        wbc[k2] = wbc[k1]

    def ffn_block(t, wg, wu, wd, postkey, res_src, after):
        sts = subtiles(t)

        def evac_gu(oc, c0, n, outs):
            j = cnt["stg"] % 2
            cnt["stg"] += 1
            S.add("act", lambda e: e.activation(out=tmpg[j][:, :n], in_=outs[0][:, :n], func=AF.Silu),
                  reads=[N(outs[0])], writes=[f"stg{j}"])
            S.add("dve", lambda e: e.tensor_tensor(out=hT3[:, oc, c0:c0 + n], in0=tmpg[j][:, :n],
                                                   in1=outs[1][:, :n], op=ALU.mult),
                  reads=[f"stg{j}", N(outs[1])], writes=["lhs"])

        linear_fm(t, [wg, wu], FC // GG, [wgS, wuS], "g", evac_gu)

        def evac_d(nt_, si, row0, n, b):
            S.add("act", lambda e: e.copy(out=Y[si][:n, nt_ * 512:(nt_ + 1) * 512], in_=b[:n, :]),
                  reads=[N(b)], writes=[f"Y{si}"])

        linear_tm(t, wd, 4, FC // KGD, KGD, wdS, "d", hT3, "lhs", evac_d)

        for si, (row0, n, col) in enumerate(sts):
            post_res(si, row0, n, postkey, res_src, 0.5)
            after(si, row0, n, col)

    def post_res(si, row0, n, postkey, res_src, alpha):
        rs, rres = rstd_of(Y[si][:n, :], n, [f"Y{si}"], 8)
        S.add("dve", lambda e: e.scalar_tensor_tensor(
            out=Y[si][:n, :], in0=Y[si][:n, :], scalar=rs, in1=wbc[postkey][:n, :],
            op0=ALU.mult, op1=ALU.mult),
            reads=[f"Y{si}", rres, "wbc" + wres[postkey]], writes=[f"Y{si}"])
        xi = load_x(res_src, row0, n, extra_reads=([f"x2dram{si}"] if res_src is x2_d else []))
        S.add("dve", lambda e: e.scalar_tensor_tensor(
            out=Y[si][:n, :], in0=Y[si][:n, :], scalar=alpha, in1=X[xi][:n, :],
            op0=ALU.mult, op1=ALU.add),
            reads=[f"Y{si}", f"X{xi}"], writes=[f"Y{si}"])

    def evac_d_plain(nt_, si, row0, n, b):
        S.add("act", lambda e: e.copy(out=Y[si][:n, nt_ * 512:(nt_ + 1) * 512], in_=b[:n, :]),
              reads=[N(b)], writes=[f"Y{si}"])

    def phase1_tile(t):
        sts = subtiles(t)
        for (row0, n, col) in sts:
            xi = load_x(xin, row0, n)
            norm_T(X[xi][:n, :], [f"X{xi}"], n, col, "f1pre")

        def after1(si, row0, n, col):
            S.add("sp", lambda e: e.dma_start(out=x1_d[row0:row0 + n, :], in_=Y[si][:n, :]),
                  reads=[f"Y{si}"], slot=f"x1o{si}")
            norm_T(Y[si][:n, :], [f"Y{si}"], n, col, "mixpre")

        ffn_block(t, wg1, wu1, wd1, "f1post", xin, after1)

        def evac_in(oc, c0, n, outs):
            j = cnt["stg"] % 2
            cnt["stg"] += 1
            S.add("act", lambda e: e.copy(out=stg[j][:, :n], in_=outs[0][:, :n]),
                  reads=[N(outs[0])], writes=[f"stg{j}"])
            gcol = (t * TW + c0) if c0 < TW else SEQ
            S.add("sp", lambda e: e.dma_start(out=projT[oc * 128:(oc + 1) * 128, gcol:gcol + n], in_=stg[j][:, :n]),
                  reads=[f"stg{j}"], slot=f"stgo{j}")

        linear_fm(t, [win_fm], NFM // GG, [wgS], "g", evac_in)

        def evac_v(nt_, si, row0, n, b):
            j = cnt["stg"] % 2
            cnt["stg"] += 1
            S.add("act", lambda e: e.copy(out=stg[j][:n, :512], in_=b[:n, :]),
                  reads=[N(b)], writes=[f"stg{j}"])
            S.add("sp", lambda e: e.dma_start(out=v_out[row0:row0 + n, nt_ * 512:(nt_ + 1) * 512], in_=stg[j][:n, :512]),
                  reads=[f"stg{j}"], slot=f"stgo{j}")

        linear_tm(t, win_v, 2, KC // KGV, KGV, wdS, "d", xnT3, "xnT", evac_v)

        wba3 = wba[:].rearrange("p (k f) -> p k f", f=16)

        def ba_sub(si, row0, n, col):
            b = bank[2 + si]
            for kc in range(KC):
                S.add("pe", lambda e, kc=kc: e.matmul(
                    b[:n, :16], lhsT=xnT3[:, kc, col:col + n], rhs=wba3[:, kc, :],
                    start=(kc == 0), stop=(kc == KC - 1)),
                    reads=["wba", "xnT"], writes=[N(b)])
            j = cnt["stg"] % 2
            cnt["stg"] += 1
            S.add("act", lambda e: e.copy(out=stg[j][:n, :16], in_=b[:n, :16]),
                  reads=[N(b)], writes=[f"stg{j}"])
            S.add("sp", lambda e: e.dma_start(out=ba_d[row0:row0 + n, :], in_=stg[j][:n, :16]),
                  reads=[f"stg{j}"], slot=f"stgo{j}")

        for si, (row0, n, col) in enumerate(sts):
            ba_sub(si, row0, n, col)

    def phase3_tile(t):
        sts = subtiles(t)
        ncols = TW + (NS if t == NT - 1 else 0)
        S.add("sp", lambda e: e.dma_start(out=xnT3[:, :, 0:ncols],
                                          in_=mixT_d[:, t * TW:t * TW + ncols].rearrange("(k p) n -> p k n", p=128)),
              writes=["xnT"], slot="xnTld")
        linear_tm(t, wout_d, 4, KC // KGV, KGV, wdS, "d", xnT3, "xnT", evac_d_plain)

        def mid(si, row0, n, col):
            post_res(si, row0, n, "mixpost", x1_d, 1.0)
            S.add("sp", lambda e: e.dma_start(out=x2_d[row0:row0 + n, :], in_=Y[si][:n, :]),
                  reads=[f"Y{si}"], writes=[f"x2dram{si}"], slot=f"x2o{si}")
            norm_T(Y[si][:n, :], [f"Y{si}"], n, col, "f2pre")

        for si, (row0, n, col) in enumerate(sts):
            mid(si, row0, n, col)

        def after3(si, row0, n, col):
            S.add("sp", lambda e: e.dma_start(out=y_d[row0:row0 + n, :], in_=Y[si][:n, :]),
                  reads=[f"Y{si}"], slot=f"yo{si}")

        ffn_block(t, wg2, wu2, wd2, "f2post", x2_d, after3)

    for t in range(0 if gdn_only else NT):
        phase1_tile(t)


    S.fence()
    pool_cols = {}
    pool_next = [0]

    def galloc(name, ncols_req):
        ncols = ((ncols_req + 31) // 32) * 32
        off = pool_next[0]
        bi, co = off // D, off % D
        if co + ncols > D:
            bi, co = bi + 1, 0
            off = bi * D
        pool_next[0] = off + ncols
        assert bi < 5, "gdn scratch overflow"
        return Y[bi][:, co:co + ncols_req]

    Sst = [galloc(f"S{h}", 128) for h in range(8)]
    gc = galloc("gconst", 768)
    identF, Umat, maskLn, maskUn, onesF, maskUd = [gc[:, i * 128:(i + 1) * 128] for i in range(6)]
    cw = galloc("cw", 96)
    hb = galloc("hb", 32)
    chk = galloc("chk", 64)
    S.add("sp", lambda e: e.dma_start(out=gc, in_=gconst), writes=["gconst"], slot="gconst")
    S.add("sp", lambda e: e.dma_start(out=cw, in_=cw_d), writes=["cw"], slot="cw")
    S.add("sp", lambda e: e.dma_start(out=hb, in_=hb_d), writes=["hb"], slot="hb")
    S.add("act", lambda e: e.activation(out=hb[:, 24:32], in_=hb[:, 0:8], func=AF.Exp), reads=["hb"], writes=["hbA"])
    S.add("dve", lambda e: e.tensor_scalar(out=hb[:, 24:32], in0=hb[:, 24:32], scalar1=-1.0, scalar2=None, op0=ALU.mult),
          reads=["hbA"], writes=["hbA"])
    for h in range(8):
        S.add("dve", lambda e, h=h: e.memset(Sst[h], 0.0), writes=[f"S{h}"])

    US = []
    for p in range(2):
        d_ = {}
        for nme, w_ in (("kxp", 131), ("vxp", 131), ("kc", 128), ("vc", 128), ("t1", 128), ("t2", 128),
                        ("kn", 128), ("ktm", 128), ("vb", 128), ("Gbc", 128), ("rep", 128), ("kbT", 128),
                        ("QT", 128), ("Q", 128), ("R", 128), ("kg", 128), ("nwT", 128), ("vn", 128),
                        ("kd", 128), ("sc", 16),
                        ("qxp", 131), ("qc", 128), ("qg", 128), ("qkT", 128), ("eG", 128), ("t3", 128),
                        ("zs", 128), ("on", 128)):
            d_[nme] = galloc(f"{nme}{p}", w_)
        US.append(d_)

    def pb(i):
        return bank[i % 7][:, 0:128], N(bank[i % 7])

    unit_no = [0]
    KSTAGE = int(os.environ.get('K_STAGE', '99'))
    KCHUNKS = int(os.environ.get('K_CHUNKS', str(SEQ // 128)))
    KSAMPLE = int(os.environ.get('K_SAMPLE', '1'))
    KCHUNKLVL = int(os.environ.get('K_CHUNKLVL', '1'))

    def gdn_chunk(tok0, nvalid, hist_src, sample):
        if not KCHUNKLVL:
            return
        nt_ = min(128, nvalid)
        if nvalid < 128:
            S.add("dve", lambda e: e.memset(chk[:, 0:16], 0.0), writes=["chkBA"])
        S.add("sp", lambda e: e.dma_start(out=chk[:nt_, 0:16], in_=ba_d[tok0:tok0 + nt_, :]),
              writes=["chkBA"], slot="chkBA")
        S.add("act", lambda e: e.activation(out=chk[:, 16:24], in_=chk[:, 0:8], func=AF.Sigmoid),
              reads=["chkBA"], writes=["chkbeta"])
        S.add("dve", lambda e: e.tensor_tensor(out=chk[:, 24:32], in0=chk[:, 8:16], in1=hb[:, 8:16], op=ALU.add),
              reads=["chkBA", "hb"], writes=["chktmp"])
        S.add("dve", lambda e: e.tensor_scalar(out=chk[:, 24:32], in0=chk[:, 24:32], scalar1=30.0, scalar2=None, op0=ALU.min),
              reads=["chktmp"], writes=["chktmp"])
        S.add("act", lambda e: e.activation(out=chk[:, 24:32], in_=chk[:, 24:32], func=AF.Exp),
              reads=["chktmp"], writes=["chktmp"])
        S.add("act", lambda e: e.activation(out=chk[:, 24:32], in_=chk[:, 24:32], func=AF.Ln, bias=1.0),
              reads=["chktmp"], writes=["chktmp"])
        S.add("dve", lambda e: e.tensor_tensor(out=chk[:, 32:40], in0=chk[:, 24:32], in1=hb[:, 24:32], op=ALU.mult),
              reads=["chktmp", "hbA"], writes=["chkg"])
        if nvalid < 128:
            S.add("dve", lambda e: e.tensor_scalar(out=chk[:, 32:40], in0=chk[:, 32:40], scalar1=hb[:, 16:17], scalar2=None, op0=ALU.mult),
                  reads=["chkg", "hb"], writes=["chkg"])
            S.add("dve", lambda e: e.tensor_scalar(out=chk[:, 16:24], in0=chk[:, 16:24], scalar1=hb[:, 16:17], scalar2=None, op0=ALU.mult),
                  reads=["chkbeta", "hb"], writes=["chkbeta"])
        pg, pgr = pb(27)
        S.add("pe", lambda e: e.matmul(pg[:, :8], lhsT=Umat, rhs=chk[:, 32:40], start=True, stop=True),
              reads=["gconst", "chkg"], writes=[pgr])
        S.add("act", lambda e: e.copy(out=chk[:, 40:48], in_=pg[:, :8]), reads=[pgr], writes=["chkG"])

        def unit(h):
            p = unit_no[0] % 2
            unit_no[0] += 1
            u = US[p]
            R_ = lambda nme: f"{nme}{p}"
            def P(i):
                bk = bank[3 * p + i % 3]
                return bk[:, (i // 3) * 128:(i // 3 + 1) * 128], f"{N(bk)}r{i // 3}"
            krow = 2048 + 1024 + 128 * h
            vrow = 2048 + 2048 + 128 * h
            qrow = 2048 + 128 * h
            for (dst, row, part, nme) in ((u["kxp"], krow, 8 + h, "kxp"), (u["vxp"], vrow, 16 + h, "vxp"), (u["qxp"], qrow, h, "qxp")):
                if hist_src is None:
                    S.add("dve", lambda e, dst=dst: e.memset(dst[:, 0:3], 0.0), writes=[R_(nme)])
                elif hist_src == "state":
                    S.add("dve", lambda e, dst=dst: e.memset(dst[:, :], 0.0), writes=[R_(nme)])
                    S.add("sp", lambda e, dst=dst, part=part: e.dma_start(out=dst[:, 0:3], in_=convst_d[part]),
                          writes=[R_(nme)], slot=R_(nme) + "h")
                if hist_src == "prev":
                    S.add("sp", lambda e, dst=dst, row=row: e.dma_start(out=dst[:, 0:131], in_=projT[row:row + 128, tok0 - 3:tok0 + 128]),
                          writes=[R_(nme)], slot=R_(nme))
                else:
                    S.add("sp", lambda e, dst=dst, row=row: e.dma_start(out=dst[:, 3:3 + nt_], in_=projT[row:row + 128, tok0:tok0 + nt_]),
                          writes=[R_(nme)], slot=R_(nme))
            if KSTAGE < 1:
                return
            for (src, dst, part, sn, dn) in ((u["kxp"], u["kc"], 8 + h, "kxp", "kc"), (u["vxp"], u["vc"], 16 + h, "vxp", "vc"), (u["qxp"], u["qc"], h, "qxp", "qc")):
                S.add("dve", lambda e, src=src, dst=dst, part=part: e.tensor_scalar(
                    out=dst, in0=src[:, 0:128], scalar1=cw[:, part * 4:part * 4 + 1], scalar2=None, op0=ALU.mult),
                    reads=[R_(sn), "cw"], writes=[R_(dn)])
                for j in range(1, 4):
                    S.add("dve", lambda e, src=src, dst=dst, part=part, j=j: e.scalar_tensor_tensor(
                        out=dst, in0=src[:, j:j + 128], scalar=cw[:, part * 4 + j:part * 4 + j + 1], in1=dst,
                        op0=ALU.mult, op1=ALU.add),
                        reads=[R_(sn), "cw", R_(dn)], writes=[R_(dn)])
                S.add("act", lambda e, dst=dst: e.activation(out=dst, in_=dst, func=AF.Silu),
                      reads=[R_(dn)], writes=[R_(dn)])
            if KSTAGE < 2:
                return
            S.add("act", lambda e: e.activation(out=u["t1"], in_=u["kc"], func=AF.Square), reads=[R_("kc")], writes=[R_("t1")])
            p0, p0r = P(0)
            S.add("pe", lambda e: e.matmul(p0, lhsT=onesF, rhs=u["t1"], start=True, stop=True),
                  reads=["gconst", R_("t1")], writes=[p0r])
            S.add("dve", lambda e: e.tensor_scalar(out=u["t2"], in0=p0, scalar1=1e-6, scalar2=None, op0=ALU.add),
                  reads=[p0r], writes=[R_("t2")])
            S.add("act", lambda e: e.sqrt(out=u["t2"], in_=u["t2"]), reads=[R_("t2")], writes=[R_("t2")])
            S.add("dve", lambda e: e.reciprocal(out=u["t2"], in_=u["t2"]), reads=[R_("t2")], writes=[R_("t2")])
            S.add("dve", lambda e: e.tensor_tensor(out=u["kn"], in0=u["kc"], in1=u["t2"], op=ALU.mult),
                  reads=[R_("kc"), R_("t2")], writes=[R_("kn")])
            if KSTAGE < 3:
                return
            KSUB = int(os.environ.get('K_SUB', '9'))
            p1, p1r = P(1)
            S.add("pe", lambda e: e.matmul(p1, lhsT=u["kn"], rhs=identF, start=True, stop=True), reads=[R_("kn"), "gconst"], writes=[p1r])
            if KSUB < 1:
                return
            S.add("act", lambda e: e.copy(out=u["ktm"], in_=p1), reads=[p1r], writes=[R_("ktm")])
            if KSUB < 2:
                return
            p2, p2r = P(2)
            vsrc = "kc" if os.environ.get('K_V1') else "vc"
            S.add("pe", lambda e: e.matmul(p2, lhsT=u[vsrc], rhs=identF, start=True, stop=True), reads=[R_(vsrc), "gconst"], writes=[p2r])
            if KSUB < 3:
                return
            S.add("dve", lambda e: e.tensor_scalar(out=u["vb"], in0=p2, scalar1=chk[:, 16 + h:17 + h], scalar2=None, op0=ALU.mult),
                  reads=[p2r, "chkbeta"], writes=[R_("vb")])
            if KSTAGE < 4:
                return
            S.add("dve", lambda e: e.tensor_scalar(out=u["rep"], in0=onesF, scalar1=chk[:, 32 + h:33 + h], scalar2=None, op0=ALU.mult),
                  reads=["gconst", "chkg"], writes=[R_("rep")])
            p3, p3r = P(3)
            S.add("pe", lambda e: e.matmul(p3, lhsT=u["rep"], rhs=Umat, start=True, stop=True),
                  reads=[R_("rep"), "gconst"], writes=[p3r])
            S.add("act", lambda e: e.copy(out=u["Gbc"], in_=p3), reads=[p3r], writes=[R_("Gbc")])
            S.add("dve", lambda e: e.tensor_scalar(out=u["rep"], in0=onesF, scalar1=chk[:, 16 + h:17 + h], scalar2=None, op0=ALU.mult),
                  reads=["gconst", "chkbeta"], writes=[R_("rep")])
            p4, p4r = P(4)
            S.add("pe", lambda e: e.matmul(p4, lhsT=u["rep"], rhs=identF, start=True, stop=True),
                  reads=[R_("rep"), "gconst"], writes=[p4r])
            S.add("dve", lambda e: e.tensor_tensor(out=u["kbT"], in0=u["kn"], in1=p4, op=ALU.mult),
                  reads=[R_("kn"), p4r], writes=[R_("kbT")])
            if KSTAGE < 5:
                return
            p5, p5r = P(5)
            p6, p6r = P(6)
            S.add("pe", lambda e: e.matmul(p5, lhsT=u["kbT"], rhs=u["kn"], start=True, stop=True),
                  reads=[R_("kbT"), R_("kn")], writes=[p5r])
            S.add("pe", lambda e: e.matmul(p6, lhsT=u["kn"], rhs=u["kbT"], start=True, stop=True),
                  reads=[R_("kbT"), R_("kn")], writes=[p6r])
            gcol = chk[:, 40 + h:41 + h]
            S.add("dve", lambda e: e.tensor_scalar(out=u["t1"], in0=u["Gbc"], scalar1=gcol, scalar2=0.0, op0=ALU.subtract, op1=ALU.max),
                  reads=[R_("Gbc"), "chkG"], writes=[R_("t1")])
            S.add("act", lambda e: e.activation(out=u["t1"], in_=u["t1"], func=AF.Exp, scale=-1.0), reads=[R_("t1")], writes=[R_("t1")])
            S.add("dve", lambda e: e.tensor_tensor(out=u["t1"], in0=u["t1"], in1=maskLn, op=ALU.mult),
                  reads=[R_("t1"), "gconst"], writes=[R_("t1")])
            S.add("dve", lambda e: e.tensor_tensor(out=u["QT"], in0=u["t1"], in1=p5, op=ALU.mult),
                  reads=[R_("t1"), p5r], writes=[R_("QT")])
            S.add("dve", lambda e: e.tensor_scalar(out=u["t2"], in0=u["Gbc"], scalar1=gcol, scalar2=0.0, op0=ALU.subtract, op1=ALU.min),
                  reads=[R_("Gbc"), "chkG"], writes=[R_("t2")])
            S.add("act", lambda e: e.activation(out=u["t2"], in_=u["t2"], func=AF.Exp), reads=[R_("t2")], writes=[R_("t2")])
            S.add("dve", lambda e: e.tensor_tensor(out=u["t3"], in0=u["t2"], in1=maskUd, op=ALU.mult),
                  reads=[R_("t2"), "gconst"], writes=[R_("t3")])
            S.add("dve", lambda e: e.tensor_tensor(out=u["t2"], in0=u["t2"], in1=maskUn, op=ALU.mult),
                  reads=[R_("t2"), "gconst"], writes=[R_("t2")])
            S.add("dve", lambda e: e.tensor_tensor(out=u["Q"], in0=u["t2"], in1=p6, op=ALU.mult),
                  reads=[R_("t2"), p6r], writes=[R_("Q")])
            S.add("dve", lambda e: e.tensor_tensor(out=u["R"], in0=u["Q"], in1=identF, op=ALU.add),
                  reads=[R_("Q"), "gconst"], writes=[R_("R")])
            if KSTAGE < 6:
                return
            for lvl in range(1, 7):
                pa, par = P(7)
                pq, pqr = P(8)
                S.add("pe", lambda e, pa=pa: e.matmul(pa, lhsT=u["Q"], rhs=u["QT"], start=True, stop=True),
                      reads=[R_("Q"), R_("QT")], writes=[par])
                if lvl < 6:
                    S.add("pe", lambda e, pq=pq: e.matmul(pq, lhsT=u["QT"], rhs=u["Q"], start=True, stop=True),
                          reads=[R_("Q"), R_("QT")], writes=[pqr])
                S.add("act", lambda e, pa=pa: e.copy(out=u["QT"], in_=pa), reads=[par], writes=[R_("QT")])
                if lvl < 6:
                    S.add("dve", lambda e, pq=pq: e.tensor_copy(out=u["Q"], in_=pq), reads=[pqr], writes=[R_("Q")])
                pr, prr = P(9)
                S.add("pe", lambda e, pr=pr: e.matmul(pr, lhsT=u["QT"], rhs=u["R"], start=True, stop=True),
                      reads=[R_("QT"), R_("R")], writes=[prr])
                S.add("dve", lambda e, pr=pr: e.tensor_tensor(out=u["R"], in0=u["R"], in1=pr, op=ALU.add),
                      reads=[R_("R"), prr], writes=[R_("R")])
            if KSTAGE < 7:
                return
            S.add("act", lambda e: e.activation(out=u["t1"], in_=u["qc"], func=AF.Square), reads=[R_("qc")], writes=[R_("t1")])
            pq0, pq0r = P(1)
            S.add("pe", lambda e: e.matmul(pq0, lhsT=onesF, rhs=u["t1"], start=True, stop=True),
                  reads=["gconst", R_("t1")], writes=[pq0r])
            S.add("dve", lambda e: e.tensor_scalar(out=u["t1"], in0=pq0, scalar1=1e-6, scalar2=None, op0=ALU.add),
                  reads=[pq0r], writes=[R_("t1")])
            S.add("act", lambda e: e.sqrt(out=u["t1"], in_=u["t1"]), reads=[R_("t1")], writes=[R_("t1")])
            S.add("dve", lambda e: e.reciprocal(out=u["t1"], in_=u["t1"]), reads=[R_("t1")], writes=[R_("t1")])
            S.add("dve", lambda e: e.scalar_tensor_tensor(out=u["qc"], in0=u["qc"], scalar=128.0 ** -0.5, in1=u["t1"],
                                                          op0=ALU.mult, op1=ALU.mult),
                  reads=[R_("qc"), R_("t1")], writes=[R_("qc")])
            S.add("act", lambda e: e.activation(out=u["eG"], in_=u["Gbc"], func=AF.Exp), reads=[R_("Gbc")], writes=[R_("eG")])
            S.add("dve", lambda e: e.tensor_tensor(out=u["qg"], in0=u["qc"], in1=u["eG"], op=ALU.mult),
                  reads=[R_("qc"), R_("eG")], writes=[R_("qg")])
            pqk, pqkr = P(2)
            S.add("pe", lambda e: e.matmul(pqk, lhsT=u["kn"], rhs=u["qc"], start=True, stop=True),
                  reads=[R_("kn"), R_("qc")], writes=[pqkr])
            S.add("dve", lambda e: e.tensor_tensor(out=u["qkT"], in0=u["t3"], in1=pqk, op=ALU.mult),
                  reads=[R_("t3"), pqkr], writes=[R_("qkT")])
            sc = u["sc"]
            S.add("act", lambda e: e.activation(out=sc[:, 0:1], in_=gcol, func=AF.Exp), reads=["chkG"], writes=[R_("sc")])
            S.add("dve", lambda e: e.tensor_tensor(out=sc[:, 0:1], in0=sc[:, 0:1], in1=chk[:, 16 + h:17 + h], op=ALU.mult),
                  reads=[R_("sc"), "chkbeta"], writes=[R_("sc")])
            S.add("dve", lambda e: e.tensor_tensor(out=sc[:, 1:2], in0=u["Gbc"][:, 127:128], in1=gcol, op=ALU.subtract),
                  reads=[R_("Gbc"), "chkG"], writes=[R_("sc")])
            S.add("act", lambda e: e.activation(out=sc[:, 1:2], in_=sc[:, 1:2], func=AF.Exp), reads=[R_("sc")], writes=[R_("sc")])
            S.add("act", lambda e: e.activation(out=sc[:, 2:3], in_=u["Gbc"][:, 127:128], func=AF.Exp),
                  reads=[R_("Gbc")], writes=[R_("sc")])
            S.add("dve", lambda e: e.tensor_scalar(out=u["kg"], in0=u["ktm"], scalar1=sc[:, 0:1], scalar2=None, op0=ALU.mult),
                  reads=[R_("ktm"), R_("sc")], writes=[R_("kg")])
            S.add("dve", lambda e: e.tensor_scalar(out=u["kd"], in0=u["ktm"], scalar1=sc[:, 1:2], scalar2=None, op0=ALU.mult),
                  reads=[R_("ktm"), R_("sc")], writes=[R_("kd")])
            if KSTAGE < 8:
                return
            p10, p10r = P(10)
            S.add("pe", lambda e: e.matmul(p10, lhsT=u["kg"], rhs=u["R"], start=True, stop=True),
                  reads=[R_("kg"), R_("R")], writes=[p10r])
            S.add("act", lambda e: e.mul(out=u["nwT"], in_=p10, mul=-1.0), reads=[p10r], writes=[R_("nwT")])
            p11, p11r = P(11)
            S.add("pe", lambda e: e.matmul(p11, lhsT=u["R"], rhs=u["vb"], start=True, stop=False),
                  reads=[R_("R"), R_("vb")], writes=[p11r])
            S.add("pe", lambda e: e.matmul(p11, lhsT=u["nwT"], rhs=Sst[h], start=False, stop=True),
                  reads=[R_("nwT"), f"S{h}"], writes=[p11r])
            S.add("act", lambda e: e.copy(out=u["vn"], in_=p11), reads=[p11r], writes=[R_("vn")])
            po, por = P(3)
            S.add("pe", lambda e: e.matmul(po, lhsT=Sst[h], rhs=u["qg"], start=True, stop=False),
                  reads=[f"S{h}", R_("qg")], writes=[por])
            S.add("pe", lambda e: e.matmul(po, lhsT=u["vn"], rhs=u["qkT"], start=False, stop=True),
                  reads=[R_("vn"), R_("qkT")], writes=[por])
            S.add("act", lambda e: e.activation(out=u["t1"], in_=po, func=AF.Square), reads=[por], writes=[R_("t1")])
            pss, pssr = P(4)
            S.add("pe", lambda e: e.matmul(pss, lhsT=onesF, rhs=u["t1"], start=True, stop=True),
                  reads=["gconst", R_("t1")], writes=[pssr])
            S.add("dve", lambda e: e.tensor_scalar(out=u["t1"], in0=pss, scalar1=1.0 / 128, scalar2=EPS, op0=ALU.mult, op1=ALU.add),
                  reads=[pssr], writes=[R_("t1")])
            S.add("act", lambda e: e.sqrt(out=u["t1"], in_=u["t1"]), reads=[R_("t1")], writes=[R_("t1")])
            S.add("dve", lambda e: e.reciprocal(out=u["t1"], in_=u["t1"]), reads=[R_("t1")], writes=[R_("t1")])
            S.add("dve", lambda e: e.tensor_tensor(out=u["on"], in0=u["t1"], in1=po, op=ALU.mult),
                  reads=[R_("t1"), por], writes=[R_("on")])
            zrow = 5120 + 128 * h
            S.add("sp", lambda e: e.dma_start(out=u["zs"][:, :nt_], in_=projT[zrow:zrow + 128, tok0:tok0 + nt_]),
                  writes=[R_("zs")], slot=R_("zs"))
            S.add("act", lambda e: e.activation(out=u["zs"][:, :nt_], in_=u["zs"][:, :nt_], func=AF.Silu), reads=[R_("zs")], writes=[R_("zs")])
            S.add("dve", lambda e: e.scalar_tensor_tensor(out=finb[p][:, :nt_], in0=u["on"][:, :nt_], scalar=hb[:, 17:18], in1=u["zs"][:, :nt_],
                                                          op0=ALU.mult, op1=ALU.mult),
                  reads=[R_("on"), "hb", R_("zs")], writes=[f"finb{p}"])
            S.add("sp", lambda e: e.dma_start(out=mixT_d[1024 + 128 * h:1024 + 128 * h + 128, tok0:tok0 + nt_], in_=finb[p][:, :nt_]),
                  reads=[f"finb{p}"], slot=f"finbo{p}")
            S.add("pe", lambda e: e.matmul(p0, lhsT=u["kd"], rhs=u["vn"], start=True, stop=True),
                  reads=[R_("kd"), R_("vn")], writes=[p0r])
            S.add("dve", lambda e: e.scalar_tensor_tensor(out=Sst[h], in0=Sst[h], scalar=sc[:, 2:3], in1=p0,
                                                          op0=ALU.mult, op1=ALU.add),
                  reads=[f"S{h}", R_("sc"), p0r], writes=[f"S{h}"])

        def record(h):
            rec = []
            S.add = lambda *a, **k: rec.append((a, k))
            try:
                unit(h)
            finally:
                del S.add
            items = []
            for a, k in rec:
                if items and a[0] == "pe" and items[-1][-1][0][0] == "pe":
                    items[-1].append((a, k))
                else:
                    items.append([(a, k)])
            return items

        for h in range(0, 8, 2):
            ia, ib = record(h), record(h + 1)
            for i in range(max(len(ia), len(ib))):
                for it in (ia, ib):
                    if i < len(it):
                        for a, k in it[i]:
                            S.add(*a, **k)

    for c in range(0 if attn_only else KCHUNKS):
        gdn_chunk(c * 128, 128, None if c == 0 else "prev", False)
    for h in range(8):
        S.add("sp", lambda e, h=h: e.dma_start(out=ssm_p[h], in_=Sst[h]), reads=[f"S{h}"], slot=f"ssmo{h}")
        S.add("sp", lambda e, h=h: e.dma_start(out=Sst[h], in_=ssm0_d[h]), writes=[f"S{h}"], slot=f"ssmi{h}")
    if KSAMPLE and not attn_only:
        gdn_chunk(SEQ, NS, "state", True)
    for h in range(8):
        S.add("sp", lambda e, h=h: e.dma_start(out=ssm_s[h], in_=Sst[h]), reads=[f"S{h}"], slot=f"ssmo{h}")


    S.fence()
    hoff = [0]

    def balloc(ncols):
        o = hoff[0]
        hoff[0] += ncols
        assert hoff[0] <= FC * TWX
        return hT[:, o:o + ncols]

    A_pos = Y[0][64:65, :]
    A_posl = Y[1][64:65, :]
    A_m = X[0][64:65, :]
    xo = [0]

    def xalloc(ncols):
        o = xo[0]
        xo[0] += ncols
        assert xo[0] <= D
        return X[1][:, o:o + ncols]

    a_ident = xalloc(128)
    a_mask = xalloc(128)
    a_O = [[xalloc(128) for s_ in range(4)] for m_ in range(2)]
    a_pd = xalloc(128)
    a_fin = xalloc(128)
    a_sub = xalloc(128)
    a_lam = xalloc(256)
    a_small = xalloc(32)
    a_kb = xalloc(64)
    a_posT = xalloc(16)
    S.add("sp", lambda e: e.dma_start(out=A_pos, in_=aconst[0:1, 0:2048]), writes=["A_pos"], slot="A_pos")
    S.add("sp", lambda e: e.dma_start(out=a_lam, in_=aconst[:, 2048:2304]), writes=["a_lam"], slot="a_lam")
    S.add("sp", lambda e: e.dma_start(out=a_sub, in_=aconst[:, 2304:2432]), writes=["a_sub"], slot="a_sub")
    S.add("sp", lambda e: e.dma_start(out=a_ident, in_=gconst[:, 0:128]), writes=["a_ident"], slot="a_ident")
    S.add("sp", lambda e: e.dma_start(out=a_mask, in_=gconst[:, 640:768]), writes=["a_mask"], slot="a_mask")
    for Q_ in range(4):
        S.add("dve", lambda e, Q_=Q_: e.tensor_scalar(out=A_posl[:, 512 * Q_:512 * Q_ + 512], in0=A_pos[:, 512 * Q_:512 * Q_ + 512],
                                                      scalar1=-512.0 * Q_, scalar2=None, op0=ALU.add),
              reads=["A_pos"], writes=["A_posl"])
    S.add("sp", lambda e: e.dma_start(out=a_posT, in_=aconst[:, 2432:2448]), writes=["a_posT"], slot="a_posT")
    S.add("dve", lambda e: e.tensor_scalar(out=a_sub, in0=a_sub, scalar1=0.8, scalar2=None, op0=ALU.mult),
          reads=["a_sub"], writes=["a_sub"])
    for i_ in range(2):
        S.add("dve", lambda e, i_=i_: e.tensor_tensor(out=a_lam[:, 128 * i_:128 * i_ + 64], in0=a_lam[:, 128 * i_:128 * i_ + 64],
                                                   in1=a_lam[:, 128 * i_ + 64:128 * i_ + 128], op=ALU.mult),
              reads=["a_lam"], writes=["a_lam"])
        S.add("dve", lambda e, i_=i_: e.reduce_sum(out=a_small[:, i_:i_ + 1], in_=a_lam[:, 128 * i_:128 * i_ + 64], axis=mybir.AxisListType.X),
              reads=["a_lam"], writes=["a_small"])
        S.add("act", lambda e, i_=i_: e.activation(out=a_small[:, i_:i_ + 1], in_=a_small[:, i_:i_ + 1], func=AF.Exp),
              reads=["a_small"], writes=["a_small"])
    S.add("dve", lambda e: e.tensor_tensor(out=a_small[:, 2:3], in0=a_small[:, 1:2], in1=a_small[:, 0:1], op=ALU.subtract),
          reads=["a_small"], writes=["a_small"])
    S.add("dve", lambda e: e.tensor_scalar(out=a_small[:, 2:3], in0=a_small[:, 2:3], scalar1=-0.2, scalar2=None, op0=ALU.add),
          reads=["a_small"], writes=["a_small"])
    neglam = a_small[:, 2:3]

    qA = [[balloc(SEQ) for m_ in range(2)] for hp_ in range(2)]
    kA = [[balloc(SEQ) for m_ in range(2)] for hp_ in range(2)]
    sqb = XN[:, :]
    Vx = [balloc(16 * 129) for _ in range(2)]
    Eb = [balloc(512) for _ in range(2)]
    onesb = balloc(72)
    finb2 = balloc(512)
    S.add("dve", lambda e: e.memset(onesb, 1.0), writes=["onesb"])
    for hp_ in range(2):
        for m_ in range(2):
            S.add("dve", lambda e, hp_=hp_, m_=m_: e.memset(kA[hp_][m_][64:65, :], 1.0), writes=[f"kA{hp_}{m_}"])

    AX = mybir.AxisListType.X

    def attn_head(h):
        hp = h % 2
        slope = 2.0 ** (-(h + 1))
        vx = Vx[hp]
        vx3 = vx.rearrange("p (b c) -> p b c", c=129)
        for m_ in range(2):
            S.add("pool", lambda e, m_=m_: e.dma_start(out=qA[hp][m_][0:64, :], in_=projT[128 * h + 64 * m_:128 * h + 64 * m_ + 64, 0:SEQ]),
                  writes=[f"qA{hp}{m_}"], slot=f"qA{hp}{m_}")
            S.add("pool", lambda e, m_=m_: e.dma_start(out=kA[hp][m_][0:64, :], in_=projT[1024 + 128 * h + 64 * m_:1024 + 128 * h + 64 * m_ + 64, 0:SEQ]),
                  writes=[f"kA{hp}{m_}"], slot=f"kA{hp}{m_}")
        S.add("dve", lambda e: e.memset(vx, 1.0), writes=[f"Vx{hp}"])
        S.add("pool", lambda e: e.dma_start(out=vx3[:, :, 0:128],
                                            in_=v_out[0:SEQ, 128 * h:128 * h + 128].rearrange("(b p) c -> p b c", p=128)),
              writes=[f"Vx{hp}"], slot=f"Vx{hp}")
        for Q_ in range(4):
            S.add("dve", lambda e, Q_=Q_: e.tensor_scalar(out=a_kb[:, 16 * Q_:16 * Q_ + 16], in0=a_posT, scalar1=slope, scalar2=-512.0 * Q_ * slope,
                                                          op0=ALU.mult, op1=ALU.add),
                  reads=["a_posT"], writes=["a_kb"])

        def sumsq(cb):
            S.add("pe", lambda e: e.matmul(bank[6][0:65, :], lhsT=onesb[0:64, 0:65], rhs=sqb[0:64, cb * 512:(cb + 1) * 512],
                                           start=True, stop=True), reads=["onesb", "sqb"], writes=[N(bank[6])])

        def norms(m):
            q_, k_ = qA[hp][m], kA[hp][m]
            S.add("act", lambda e: e.activation(out=sqb[0:64, :], in_=k_[0:64, :], func=AF.Square),
                  reads=[f"kA{hp}{m}"], writes=["sqb"])

            def kmax(cb):
                sumsq(cb)
                S.add("dve", lambda e: e.reduce_max(out=a_small[64:65, 4 + cb:5 + cb], in_=bank[6][64:65, :], axis=AX),
                      reads=[N(bank[6])], writes=["a_small"])

            for cb in range(4):
                kmax(cb)
            S.add("dve", lambda e: e.reduce_max(out=a_small[64:65, 8:9], in_=a_small[64:65, 4:8], axis=AX),
                  reads=["a_small"], writes=["a_small"])
            S.add("act", lambda e: e.activation(out=sqb[0:64, :], in_=q_[0:64, :], func=AF.Square),
                  reads=[f"qA{hp}{m}"], writes=["sqb"])

            def qn(cb):
                sumsq(cb)
                S.add("dve", lambda e: e.tensor_scalar(out=A_m[:, cb * 512:(cb + 1) * 512], in0=bank[6][64:65, :],
                                                       scalar1=a_small[64:65, 8:9], scalar2=1.21, op0=ALU.mult, op1=ALU.mult),
                      reads=[N(bank[6]), "a_small"], writes=["A_m"])

            for cb in range(4):
                qn(cb)
            S.add("act", lambda e: e.sqrt(out=A_m, in_=A_m), reads=["A_m"], writes=["A_m"])
            S.add("dve", lambda e: e.scalar_tensor_tensor(out=q_[64:65, :], in0=A_posl, scalar=-8.0 * slope, in1=A_m,
                                                          op0=ALU.mult, op1=ALU.subtract),
                  reads=["A_posl", "A_m"], writes=[f"qA{hp}{m}"])

        for m in range(2):
            norms(m)

        def qk_unit(Q, m, j, accs):
            lo = 64 * m
            eb = cnt["evac"] % 2
            cnt["evac"] += 1
            psb = bank[4 + eb]
            E = Eb[eb]
            S.add("pe", lambda e: e.matmul(psb[:, :], lhsT=kA[hp][m][0:65, 128 * j:128 * j + 128],
                                           rhs=qA[hp][m][0:65, 512 * Q:512 * Q + 512], start=True, stop=True),
                  reads=[f"kA{hp}{m}", f"qA{hp}{m}"], writes=[N(psb)])
            s0 = max(0, j - 4 * Q)
            S.add("act", lambda e: e.activation(out=E[:, 128 * s0:512], in_=psb[:, 128 * s0:512], func=AF.Exp, scale=0.125,
                                                bias=a_kb[:, 16 * Q + j:16 * Q + j + 1]),
                  reads=[N(psb), "a_kb"], writes=[f"Eb{eb}"])

            def pv(s_):
                qi = 4 * Q + s_
                if j > qi:
                    return
                if j == qi:
                    S.add("dve", lambda e: e.tensor_tensor(out=E[:, 128 * s_:128 * s_ + 128], in0=E[:, 128 * s_:128 * s_ + 128],
                                                           in1=a_mask, op=ALU.mult),
                          reads=[f"Eb{eb}", "a_mask"], writes=[f"Eb{eb}"])
                acc, accr = accs[s_]
                S.add("pe", lambda e: e.matmul(acc, lhsT=E[:, 128 * s_:128 * s_ + 128], rhs=vx3[:, j, :],
                                               start=(j == 0), stop=(j == qi)),
                      reads=[f"Eb{eb}", f"Vx{hp}"], writes=[accr])

            for s_ in range(4):
                pv(s_)

        def qblock_map(Q, m):
            accs = [(bank[s_][:, 0:129], N(bank[s_])) for s_ in range(4)]
            for j in range(4 * Q + 4):
                qk_unit(Q, m, j, accs)

            def fin_acc(s_):
                acc, accr = accs[s_]
                S.add("dve", lambda e: e.reciprocal(out=a_small[:, 16 + s_:17 + s_], in_=acc[:, 128:129]),
                      reads=[accr], writes=["a_small"])
                S.add("dve", lambda e: e.tensor_scalar(out=a_O[m][s_], in0=acc[:, 0:128], scalar1=a_small[:, 16 + s_:17 + s_],
                                                       scalar2=None, op0=ALU.mult),
                      reads=[accr, "a_small"], writes=[f"a_O{m}{s_}"])

            for s_ in range(4):
                fin_acc(s_)

        def fin_sub(Q, s_):
            S.add("dve", lambda e: e.scalar_tensor_tensor(out=a_pd, in0=a_O[1][s_], scalar=neglam, in1=a_O[0][s_],
                                                          op0=ALU.mult, op1=ALU.add),
                  reads=[f"a_O0{s_}", f"a_O1{s_}", "a_small"], writes=["a_pd"])
            S.add("act", lambda e: e.activation(out=a_fin, in_=a_pd, func=AF.Square, accum_out=a_small[:, 20:21]),
                  reads=["a_pd"], writes=["a_fin", "a_small"])
            S.add("dve", lambda e: e.tensor_scalar(out=a_small[:, 21:22], in0=a_small[:, 20:21], scalar1=1.0 / 128, scalar2=EPS,
                                                   op0=ALU.mult, op1=ALU.add), reads=["a_small"], writes=["a_small"])
            S.add("act", lambda e: e.sqrt(out=a_small[:, 21:22], in_=a_small[:, 21:22]), reads=["a_small"], writes=["a_small"])
            S.add("dve", lambda e: e.reciprocal(out=a_small[:, 22:23], in_=a_small[:, 21:22]), reads=["a_small"], writes=["a_small"])
            S.add("dve", lambda e: e.scalar_tensor_tensor(out=a_fin, in0=a_pd, scalar=a_small[:, 22:23], in1=a_sub,
                                                          op0=ALU.mult, op1=ALU.mult),
                  reads=["a_pd", "a_small", "a_sub"], writes=["a_fin"])
            S.add("pe", lambda e: e.matmul(bank[6][:, 0:128], lhsT=a_fin, rhs=a_ident, start=True, stop=True),
                  reads=["a_fin", "a_ident"], writes=[N(bank[6])])
            S.add("act", lambda e: e.copy(out=finb2[:, 128 * s_:128 * s_ + 128], in_=bank[6][:, 0:128]),
                  reads=[N(bank[6])], writes=["finb2"])

        def qblock(Q):
            for m in range(2):
                qblock_map(Q, m)
            for s_ in range(4):
                fin_sub(Q, s_)
            S.add("sp", lambda e: e.dma_start(out=mixT_d[128 * h:128 * h + 128, 512 * Q:512 * Q + 512], in_=finb2),
                  reads=["finb2"], slot="finb2o")

        for Q in range(4):
            qblock(Q)

    for h in range(aheads):
        attn_head(h)

    def sample_attn():
        S.fence()
        KV = [Y[0], Y[1]]
        KTs = [Y[2][:, 0:1024], Y[2][:, 1024:2048]]
        C = Y[3]
        posp, slopeRow, maskS = C[:, 0:128], C[:, 128:256], C[:, 256:384]
        pcol, subcol = C[:, 384:385], C[:, 385:386]
        identS, onesS = C[:, 512:640], C[:, 640:768]
        qs, knew = C[:, 768:832], C[:, 832:896]
        Sb = [C[:, 896:1024], C[:, 1024:1152]]
        Es = [C[:, 1152:1280], C[:, 1280:1408]]
        Rr, On, pd, sq, rstd = C[:, 1408:1536], C[:, 1536:1664], C[:, 1664:1728], C[:, 1728:1792], C[:, 1792:1856]
        vnew = Y[4][0:8, 0:1024]
        fins = finb[0][:, 0:64]
        qs3 = qs.rearrange("p (h q) -> p h q", q=8)
        knew3 = knew.rearrange("p (h q) -> p h q", q=8)
        S.add("sp", lambda e: e.dma_start(out=C[:, 0:512], in_=sconst), writes=["sC"], slot="sC")
        S.add("sp", lambda e: e.dma_start(out=identS, in_=gconst[:, 0:128]), writes=["sI"], slot="sI")
        S.add("sp", lambda e: e.dma_start(out=onesS, in_=gconst[:, 512:640]), writes=["sO"], slot="sO")
        S.add("sp", lambda e: e.dma_start(out=ptidx[:], in_=pt_d), writes=["ptidx"], slot="ptidx")
        S.add("dve", lambda e: e.tensor_scalar(out=ptidx2[:], in0=ptidx[:], scalar1=128, scalar2=None, op0=ALU.mult),
              reads=["ptidx"], writes=["ptidx2"])
        S.add("sp", lambda e: e.dma_start(out=qs3, in_=projT[0:1024, SEQ:SEQ + NS].rearrange("(h p) q -> p h q", p=128)),
              writes=["qs"], slot="qs")
        S.add("sp", lambda e: e.dma_start(out=knew3, in_=projT[1024:2048, SEQ:SEQ + NS].rearrange("(h p) q -> p h q", p=128)),
              writes=["knew"], slot="knew")
        S.add("sp", lambda e: e.dma_start(out=vnew, in_=v_out[SEQ:SEQ + NS, :]), writes=["vnew"], slot="vnew")
        SA = int(os.environ.get("SA_STAGE", "9"))
        qm = C[:, 1856:1984]
        qm4 = qm.rearrange("p (h m q) -> p h m q", m=2, q=8)
        S.add("dve", lambda e: e.memset(qm, 0.0), writes=["qm"])
        S.add("dve", lambda e: e.tensor_scalar(out=qm4[0:64, :, 0, :], in0=qs3[0:64, :, :], scalar1=0.125, scalar2=None, op0=ALU.mult),
              reads=["qs"], writes=["qm"])
        S.add("dve", lambda e: e.tensor_scalar(out=qm4[64:128, :, 1, :], in0=qs3[64:128, :, :], scalar1=0.125, scalar2=None, op0=ALU.mult),
              reads=["qs"], writes=["qm"])
        accO, accOr = bank[3][:, 0:128], N(bank[3])
        accR, accRr = bank[4][:, 0:128], N(bank[4])
        scb, scbr = bank[2], N(bank[2])

        def step(tok):
            b = tok % 2
            tb = (bank[0], bank[1]) if b == 0 else (bank[5], bank[6])
            S.add("pool", lambda e: e.indirect_dma_start(
                out=KV[b][:, 0:1024], out_offset=None, in_=ck_d[:, :],
                in_offset=bass.IndirectOffsetOnAxis(ap=ptidx2[:, 0:1], axis=0), element_offset=tok * 1024),
                reads=["ptidx2"], writes=[f"KVk{b}"], slot=f"KVk{b}")
            S.add("pool", lambda e: e.indirect_dma_start(
                out=KV[b][:, 1024:2048], out_offset=None, in_=cv_d[:, :],
                in_offset=bass.IndirectOffsetOnAxis(ap=ptidx2[:, 0:1], axis=0), element_offset=tok * 1024),
                reads=["ptidx2"], writes=[f"KVv{b}"], slot=f"KVv{b}")

            def tr(half):
                for hh in range(4):
                    h = half * 4 + hh
                    S.add("pe", lambda e, h=h, hh=hh: e.matmul(tb[half][:, hh * 128:(hh + 1) * 128], lhsT=KV[b][:, h * 128:(h + 1) * 128],
                                                               rhs=identS, start=True, stop=True),
                          reads=[f"KVk{b}", "sI"], writes=[N(tb[half])])
                if half == 0:
                    S.add("act", lambda e: e.copy(out=KTs[b][:, 0:512], in_=tb[0][:, :]), reads=[N(tb[0])], writes=[f"KT{b}a"])
                else:
                    S.add("dve", lambda e: e.tensor_copy(out=KTs[b][:, 512:1024], in_=tb[1][:, :]), reads=[N(tb[1])], writes=[f"KT{b}b"])

            if SA < 2:
                return
            tr(0)
            tr(1)
            if SA < 3:
                return
            for h in range(8):
                S.add("pe", lambda e, h=h: e.matmul(scb[:, h * 16:(h + 1) * 16],
                                                    lhsT=KTs[b][:, h * 128:(h + 1) * 128],
                                                    rhs=qm[:, h * 16:(h + 1) * 16], start=True, stop=True),
                      reads=[f"KT{b}a" if h < 4 else f"KT{b}b", "qm"], writes=[scbr])
            S.add("dve", lambda e: e.scalar_tensor_tensor(out=Sb[b], in0=slopeRow, scalar=posp[:, tok:tok + 1], in1=scb[:, 0:128],
                                                          op0=ALU.mult, op1=ALU.add),
                  reads=["sC", scbr], writes=[f"Sb{b}"])
            S.add("act", lambda e: e.activation(out=Es[b], in_=Sb[b], func=AF.Exp), reads=[f"Sb{b}"], writes=[f"Es{b}"])
            if SA < 4:
                return
            for h in range(8):
                S.add("pe", lambda e, h=h: e.matmul(accO[:, h * 16:(h + 1) * 16], lhsT=KV[b][:, 1024 + h * 128:1024 + (h + 1) * 128],
                                                    rhs=Es[b][:, h * 16:(h + 1) * 16], start=False, stop=False),
                      reads=[f"KVv{b}", f"Es{b}"], writes=[accOr])
            S.add("pe", lambda e: e.matmul(accR, lhsT=onesS, rhs=Es[b], start=(tok == 0), stop=False),
                  reads=["sO", f"Es{b}"], writes=[accRr])

        zerosS = Y[4][:, 1024:1152]
        S.add("dve", lambda e: e.memset(zerosS, 0.0), writes=["zerosS"])
        if SA >= 4:
            S.add("pe", lambda e: e.matmul(accO, lhsT=onesS, rhs=zerosS, start=True, stop=False),
                  reads=["sO", "zerosS"], writes=[accOr])
        for tok in range(128):
            step(tok)
        if SA < 5:
            return
        for h in range(8):
            S.add("pe", lambda e, h=h: e.matmul(scb[0:NS, h * 16:(h + 1) * 16],
                                                lhsT=knew3[:, h, :],
                                                rhs=qm[:, h * 16:(h + 1) * 16], start=True, stop=True),
                  reads=["knew", "qm"], writes=[scbr])
        S.add("dve", lambda e: e.scalar_tensor_tensor(out=Sb[0][0:NS, :], in0=slopeRow[0:NS, :], scalar=pcol[0:NS, :], in1=scb[0:NS, 0:128],
                                                      op0=ALU.mult, op1=ALU.add),
              reads=["sC", scbr], writes=["Sb0"])
        S.add("act", lambda e: e.activation(out=Es[0][0:NS, :], in_=Sb[0][0:NS, :], func=AF.Exp), reads=["Sb0"], writes=["Es0"])
        S.add("dve", lambda e: e.tensor_tensor(out=Es[0][0:NS, :], in0=Es[0][0:NS, :], in1=maskS[0:NS, :], op=ALU.mult),
              reads=["Es0", "sC"], writes=["Es0"])
        for h in range(8):
            S.add("pe", lambda e, h=h: e.matmul(accO[:, h * 16:(h + 1) * 16], lhsT=vnew[:, h * 128:(h + 1) * 128],
                                                rhs=Es[0][0:NS, h * 16:(h + 1) * 16], start=False, stop=(h == 7)),
                  reads=["vnew", "Es0"], writes=[accOr])
        S.add("pe", lambda e: e.matmul(accR, lhsT=onesS[0:NS, :], rhs=Es[0][0:NS, :], start=False, stop=True),
              reads=["sO", "Es0"], writes=[accRr])
        S.add("dve", lambda e: e.reciprocal(out=Rr, in_=accR), reads=[accRr], writes=["sRr"])
        S.add("dve", lambda e: e.tensor_tensor(out=On, in0=accO, in1=Rr, op=ALU.mult), reads=[accOr, "sRr"], writes=["sOn"])
        On4 = On.rearrange("p (h m q) -> p h m q", m=2, q=8)
        pd3 = pd.rearrange("p (h q) -> p h q", q=8)
        S.add("dve", lambda e: e.scalar_tensor_tensor(out=pd3, in0=On4[:, :, 1, :], scalar=neglam, in1=On4[:, :, 0, :],
                                                      op0=ALU.mult, op1=ALU.add),
              reads=["sOn", "a_small"], writes=["spd"])
        S.add("act", lambda e: e.activation(out=sq, in_=pd, func=AF.Square), reads=["spd"], writes=["ssq"])
        S.add("pe", lambda e: e.matmul(scb[:, 0:64], lhsT=onesS, rhs=sq, start=True, stop=True), reads=["sO", "ssq"], writes=[scbr])
        S.add("dve", lambda e: e.tensor_scalar(out=rstd, in0=scb[:, 0:64], scalar1=1.0 / 128, scalar2=EPS, op0=ALU.mult, op1=ALU.add),
              reads=[scbr], writes=["srstd"])
        S.add("act", lambda e: e.sqrt(out=rstd, in_=rstd), reads=["srstd"], writes=["srstd"])
        S.add("dve", lambda e: e.reciprocal(out=rstd, in_=rstd), reads=["srstd"], writes=["srstd"])
        S.add("dve", lambda e: e.tensor_tensor(out=pd, in0=pd, in1=rstd, op=ALU.mult), reads=["spd", "srstd"], writes=["spd"])
        S.add("dve", lambda e: e.tensor_scalar(out=fins, in0=pd, scalar1=subcol, scalar2=0.8, op0=ALU.mult, op1=ALU.mult),
              reads=["spd", "sC"], writes=["finb0"])
        S.add("sp", lambda e: e.dma_start(out=mixT_d[0:1024, SEQ:SEQ + NS].rearrange("(h p) q -> p h q", p=128),
                                          in_=fins.rearrange("p (h q) -> p h q", q=8)),
              reads=["finb0"], slot="finbo0")

    if not skip_sample:
        sample_attn()

    if not skip_p3:
        S.fence()
        for k2 in ("f2pre", "f2post", "mixpost"):
            S.add("sp", lambda e, k2=k2: e.dma_start(out=wbc[k2][:], in_=wbc2_d[k2]), writes=["wbc" + wres[k2]], slot="wbc" + wres[k2])
        for t in range(NT):
            phase3_tile(t)

    S.final_slots = list(S.dma_cnt.keys())
    S.emit(nc, stack)
    stack.close()
    return nc


def fm_blocks(w, gg):
    K, N = w.shape
    W = 128 * gg
    a = w.reshape(K // 128, 128, N // W, W).transpose(2, 1, 0, 3)
    return np.ascontiguousarray(a).reshape(N // W, 128, (K // 128) * W)


def tm_blocks(w, kgsz):
    K, N = w.shape
    a = w.reshape(K // (128 * kgsz), kgsz, 128, N // 512, 512).transpose(3, 0, 2, 1, 4)
    return np.ascontiguousarray(a).reshape(N // 512, K // (128 * kgsz), 128, kgsz * 512)


def gconst_host():
    i = np.arange(128)
    ident = np.eye(128, dtype=np.float32)
    U = (i[:, None] <= i[None, :]).astype(np.float32)
    mL = -(i[:, None] > i[None, :]).astype(np.float32)
    mU = -(i[None, :] > i[:, None]).astype(np.float32)
    ones = np.ones((128, 128), np.float32)
    mUd = (i[None, :] >= i[:, None]).astype(np.float32)
    return np.ascontiguousarray(np.concatenate([ident, U, mL, mU, ones, mUd], axis=1))


def hb_host(a_log, dt_bias, dnw=None):
    hb = np.zeros((128, 32), np.float32)
    hb[:, 0:8] = a_log[None, :]
    hb[:, 8:16] = dt_bias[None, :]
    hb[:NS, 16] = 1.0
    if dnw is not None:
        hb[:, 17] = dnw
    return hb


def aconst_host(lq1, lk1, lq2, lk2, subln):
    a = np.zeros((128, 2048 + 256 + 128 + 16), np.float32)
    a[0, 0:2048] = np.arange(2048, dtype=np.float32)
    a[:, 2432:2448] = 128.0 * np.arange(16, dtype=np.float32)[None, :] + np.arange(128, dtype=np.float32)[:, None]
    a[:, 2048:2112] = lq1[None]
    a[:, 2112:2176] = lk1[None]
    a[:, 2176:2240] = lq2[None]
    a[:, 2240:2304] = lk2[None]
    a[:, 2304:2432] = subln[None]
    return a


def sconst_host(subln):
    a = np.zeros((128, 512), np.float32)
    pp = np.arange(128, dtype=np.float32)
    a[:, 0:128] = 128.0 * pp[:, None] + pp[None, :] - 16384.0
    col = np.arange(128)
    a[:, 128:256] = (2.0 ** (-(col // 16 + 1).astype(np.float32)))[None, :]
    a[:, 256:384] = ((col % 8)[None, :] >= np.arange(128)[:, None]).astype(np.float32)
    a[:, 384] = pp
    a[:, 385] = subln
    return a


_NC = None


def kernel(**inp):
    import ml_dtypes
    global _NC
    if _NC is None:
        _NC = build()
    nc = _NC
    f = lambda k: np.asarray(inp[k], dtype=np.float32)
    xp, xs = f("x_prompt"), f("x_sample")
    w_in = f("w_in")[0]
    fm_cols = np.concatenate([w_in[:, 0:2048], w_in[:, 3072:7168]], axis=1)
    shared = {
        "ident": np.eye(128, dtype=np.float32).astype(ml_dtypes.bfloat16),
        "wbc_f1pre": np.ascontiguousarray(np.broadcast_to(f("ffn1_pre_w")[0], (128, D))),
        "wbc_f1post": np.ascontiguousarray(np.broadcast_to(f("ffn1_post_w")[0], (128, D))),
        "wbc_mixpre": np.ascontiguousarray(np.broadcast_to(f("mix_pre_w")[0], (128, D))),
        "wg1": fm_blocks(f("ffn1_gate")[0], 2),
        "wu1": fm_blocks(f("ffn1_up")[0], 2),
        "wd1": tm_blocks(f("ffn1_down")[0], 11),
        "win_fm": fm_blocks(fm_cols, 2),
        "win_v": tm_blocks(w_in[:, 2048:3072], 4),
        "win_ba": np.ascontiguousarray(w_in[:, 7168:7184].reshape(KC, 128, 16).transpose(1, 0, 2)).reshape(128, KC * 16),
        "gconst": gconst_host(),
        "cw_d": np.ascontiguousarray(f("conv_w")[0].T.reshape(24, 128, 4).transpose(1, 0, 2)).reshape(128, 96),
        "hb_d": hb_host(f("a_log")[0], f("dt_bias")[0], f("delta_norm_w")[0]),
        "aconst": aconst_host(f("lambda_q1")[0], f("lambda_k1")[0], f("lambda_q2")[0], f("lambda_k2")[0], f("subln_w")[0]),
        "wbc_f2pre": np.ascontiguousarray(np.broadcast_to(f("ffn2_pre_w")[0], (128, D))),
        "wbc_f2post": np.ascontiguousarray(np.broadcast_to(f("ffn2_post_w")[0], (128, D))),
        "wbc_mixpost": np.ascontiguousarray(np.broadcast_to(f("mix_post_w")[0], (128, D))),
        "wg2": fm_blocks(f("ffn2_gate")[0], 2),
        "wu2": fm_blocks(f("ffn2_up")[0], 2),
        "wd2": tm_blocks(f("ffn2_down")[0], 11),
        "wout": tm_blocks(f("w_out")[0], 4),
        "cache_k": f("cache_k")[0].reshape(-1, 1024),
        "cache_v": f("cache_v")[0].reshape(-1, 1024),
        "sconst": sconst_host(f("subln_w")[0]),
    }
    page_table = np.asarray(inp["page_table"]).astype(np.int32)
    ssm0 = f("state_ssm")[0]
    convst = f("state_conv")[0]
    in_maps = []
    for c in range(NCORES):
        m = dict(shared)
        m["xin"] = np.ascontiguousarray(np.concatenate([xp[c % 4], xs[c]], axis=0))
        m["ssm0"] = np.ascontiguousarray(ssm0[c])
        m["convst"] = np.ascontiguousarray(convst[c].T.reshape(24, 128, 3))
        m["pt"] = np.ascontiguousarray(page_table[c].reshape(128, 1))
        in_maps.append(m)
    res = run_bass_kernel_spmd(nc, in_maps, core_ids=list(range(NCORES)))
    R = [{k: np.asarray(v) for k, v in r.items()} for r in res.results]
    B = 4
    k_prompt = np.stack([R[b]["projT"][1024:2048, :SEQ].T.reshape(SEQ, 8, 128) for b in range(B)])[None]
    v_prompt = np.stack([R[b]["v_out"][:SEQ].reshape(SEQ, 8, 128) for b in range(B)])[None]
    conv_prompt = np.stack([R[b]["projT"][2048:5120, SEQ - 3:SEQ].T for b in range(B)])[None]
    k_sample = np.stack([R[c]["projT"][1024:2048, SEQ:].T.reshape(NS, 8, 128) for c in range(8)])[None]
    v_sample = np.stack([R[c]["v_out"][SEQ:].reshape(NS, 8, 128) for c in range(8)])[None]
    conv_sample = np.stack([R[c]["projT"][2048:5120, NROW - 3:NROW].T for c in range(8)])[None]
    y_prompt = np.stack([R[b]["y"][:SEQ] for b in range(B)])
    y_sample = np.stack([R[c]["y"][SEQ:] for c in range(8)])
    ssm_prompt = np.stack([R[b]["ssm_p"] for b in range(B)])[None]
    ssm_sample = np.stack([R[c]["ssm_s"] for c in range(8)])[None]
    out = (y_prompt, y_sample, k_prompt, v_prompt, ssm_prompt, conv_prompt,
           k_sample, v_sample, ssm_sample, conv_sample)
    return tuple(np.ascontiguousarray(o, dtype=np.float32) for o in out)
```

```python
import os
from contextlib import ExitStack
import numpy as np
import concourse.bass as bass
import concourse.mybir as mybir
from concourse.bass_utils import run_bass_kernel_spmd

F32 = mybir.dt.float32
BF16 = mybir.dt.bfloat16
I32 = mybir.dt.int32
AF = mybir.ActivationFunctionType
ALU = mybir.AluOpType

D = 2048
DFF = 5632
SEQ = 2048
NS = 8
NROW = SEQ + NS
TW = 512
NT = SEQ // TW
KC = D // 128
FC = DFF // 128
INW = 7184
EPS = 1e-6
NCORES = 8


class Sched:
    ENG = ("pe", "act", "dve", "pool", "sp")

    def __init__(self):
        self.ops = {e: [] for e in self.ENG}
        self.lastw = {}
        self.readers = {}
        self.dma_cnt = {}
        self.waited = {e: {} for e in self.ENG}

    def add(self, eng, fn, reads=(), writes=(), slot=None):
        op = dict(eng=eng, fn=fn, waits=[], flag=False, slot=slot, idx=len(self.ops[eng]))
        if slot is not None:
            self.dma_cnt[slot] = self.dma_cnt.get(slot, 0) + 1
            ev = ("dma", slot, self.dma_cnt[slot], op)
        else:
            ev = ("eng", eng, op["idx"], op)
        deps = []
        for r in reads:
            if r in self.lastw:
                deps.append(self.lastw[r])
        for w in writes:
            if w in self.lastw:
                deps.append(self.lastw[w])
            deps.extend(self.readers.get(w, []))
        best = {}
        for d in deps:
            k = (d[0], d[1])
            if k not in best or best[k][2] < d[2]:
                best[k] = d
        for d in best.values():
            kind, key, n, dop = d
            if kind == "eng" and key == eng and eng == "pe":
                continue
            k = (kind, key)
            if self.waited[eng].get(k, -1) >= n:
                continue
            self.waited[eng][k] = n
            op["waits"].append(d)
            if kind == "eng":
                dop["flag"] = True
        for r in reads:
            self.readers.setdefault(r, []).append(ev)
        for w in writes:
            self.lastw[w] = ev
            self.readers[w] = []
        self.ops[eng].append(op)
        return ev

    def fence(self):
        evs = []
        for e in self.ENG:
            for op in reversed(self.ops[e]):
                if op["fn"] is not None and op["slot"] is None:
                    evs.append(("eng", e, op["idx"], op))
                    break
        for sl, n in self.dma_cnt.items():
            evs.append(("dma", sl, n, None))
        for e in self.ENG:
            op = dict(eng=e, fn=None, waits=[], flag=False, slot=None, idx=len(self.ops[e]))
            for d in evs:
                kind, key, n, dop = d
                if kind == "eng" and key == e:
                    continue
                k = (kind, key)
                if self.waited[e].get(k, -1) >= n:
                    continue
                self.waited[e][k] = n
                op["waits"].append(d)
                if kind == "eng":
                    dop["flag"] = True
            self.ops[e].append(op)

    def emit(self, nc, stack):
        esem = {e: stack.enter_context(nc.semaphore("sem_" + e)) for e in self.ENG}
        ssem = {s: stack.enter_context(nc.semaphore("slot_" + str(s))) for s in self.dma_cnt}
        for e in self.ENG:
            c = 0
            for op in self.ops[e]:
                if op["flag"]:
                    c += 1
                op["cnt"] = c
        block = stack.enter_context(nc.Block())

        def run(e):
            def body(engine):
                for op in self.ops[e]:
                    for kind, key, n, dop in op["waits"]:
                        if kind == "eng":
                            engine.wait_ge(esem[key], dop["cnt"])
                        else:
                            engine.wait_ge(ssem[key], 16 * n)
                    if op["fn"] is None:
                        continue
                    ins = op["fn"](engine)
                    if op["slot"] is not None:
                        ins.then_inc(ssem[op["slot"]], 16)
                    elif op["flag"]:
                        ins.then_inc(esem[e], 1)
            return body

        def run_sp(engine):
            run("sp")(engine)
            for sl in getattr(self, "final_slots", []):
                if sl in ssem:
                    engine.wait_ge(ssem[sl], 16 * self.dma_cnt[sl])

        block.tensor(run("pe"))
        block.scalar(run("act"))
        block.vector(run("dve"))
        block.gpsimd(run("pool"))
        block.sync(run_sp)


def subtiles(t):
    st = [(t * TW + 128 * s, 128, 128 * s) for s in range(TW // 128)]
    if t == NT - 1:
        st.append((SEQ, NS, TW))
    return st


def passes(t):
    p = [(0, TW)]
    if t == NT - 1:
        p.append((TW, NS))
    return p


def build(gdn_only=False, attn_only=False, skip_p3=False, npool=1280, aheads=8, skip_sample=False):
    nc = bass.Bass("TRN2", target_bir_lowering=False)
    S = Sched()
    stack = ExitStack()

    def din(name, shape, dt=F32):
        return nc.dram_tensor(name, shape, dt, kind="ExternalInput").ap()

    def dout(name, shape, dt=F32):
        return nc.dram_tensor(name, shape, dt, kind="ExternalOutput").ap()

    xin = din("xin", [NROW, D])
    ident_d = din("ident", [128, 128], BF16)
    wbc_d = {k: din("wbc_" + k, [128, D]) for k in ("f1pre", "f1post", "mixpre")}
    GG = 2
    wg1 = din("wg1", [FC // GG, 128, KC * 128 * GG])
    wu1 = din("wu1", [FC // GG, 128, KC * 128 * GG])
    KGD = 11
    wd1 = din("wd1", [4, FC // KGD, 128, KGD * 512])
    NFM = 48
    win_fm = din("win_fm", [NFM // GG, 128, KC * 128 * GG])
    KGV = 4
    win_v = din("win_v", [2, KC // KGV, 128, KGV * 512])

    win_ba = din("win_ba", [128, KC * 16])
    gconst = din("gconst", [128, 6 * 128])
    cw_d = din("cw_d", [128, 96])
    hb_d = din("hb_d", [128, 32])
    ssm0_d = din("ssm0", [8, 128, 128])
    convst_d = din("convst", [24, 128, 3])
    ba_d = (din if gdn_only else dout)("ba_d", [NROW, 16])
    aconst = din("aconst", [128, 2048 + 256 + 128 + 16])
    mixT_d = dout("mixT_d", [2048, NROW], BF16)
    ssm_p = dout("ssm_p", [8, 128, 128])
    ssm_s = dout("ssm_s", [8, 128, 128])
    projT = (din if gdn_only else dout)("projT", [NFM * 128, NROW])
    v_out = (din if gdn_only else dout)("v_out", [NROW, 1024])
    x1_d = dout("x1", [NROW, D])
    x2_d = dout("x2", [NROW, D])
    y_d = dout("y", [NROW, D])
    wbc2_d = {k: din("wbc_" + k, [128, D]) for k in ("f2pre", "f2post", "mixpost")}
    wg2 = din("wg2", [FC // GG, 128, KC * 128 * GG])
    wu2 = din("wu2", [FC // GG, 128, KC * 128 * GG])
    wd2 = din("wd2", [4, FC // KGD, 128, KGD * 512])
    wout_d = din("wout", [4, KC // KGV, 128, KGV * 512])
    ck_d = din("cache_k", [npool * 128, 1024])
    cv_d = din("cache_v", [npool * 128, 1024])
    pt_d = din("pt", [128, 1], I32)
    sconst = din("sconst", [128, 512])

    nm = {}

    def sb(name, shape, dt=F32):
        t_ = stack.enter_context(nc.sbuf_tensor(name, shape, dt))
        nm[id(t_)] = name
        return t_

    def ps(name, shape, dt=F32):
        t_ = stack.enter_context(nc.psum_tensor(name, shape, dt))
        nm[id(t_)] = name
        return t_

    def N(t_):
        return nm[id(t_)]

    TWX = TW + NS
    ident = sb("ident_sb", [128, 128], BF16)
    wbc = {k: sb("wbc_sb_" + k, [128, D]) for k in wbc_d}
    X = [sb(f"X{i}", [128, D]) for i in range(2)]
    Y = [sb(f"Y{i}", [128, D]) for i in range(5)]
    XN = sb("XN", [128, D], BF16)
    xnT = sb("xnT", [128, KC * TWX], BF16)
    hT = sb("hT", [128, FC * TWX], BF16)
    sm = sb("small", [128, 64])
    wgS = [sb(f"wgS{i}", [128, KC * 128 * GG], BF16) for i in range(2)]
    wuS = [sb(f"wuS{i}", [128, KC * 128 * GG], BF16) for i in range(2)]
    wdS = [sb(f"wdS{i}", [128, KGD * 512], BF16) for i in range(2)]
    stg = [sb(f"stg{i}", [128, TWX]) for i in range(2)]
    wba = sb("wba", [128, KC * 16], BF16)
    finb = [sb(f"finb{i}", [128, 128], BF16) for i in range(2)]
    ptidx = sb("ptidx", [128, 1], I32)
    ptidx2 = sb("ptidx2", [128, 1], I32)
    tmpg = stg
    bank = [ps(f"bank{i}", [128, 512]) for i in range(7)]
    psT = ps("psT", [128, 1024], BF16)

    xnT3 = xnT[:].rearrange("p (c t) -> p c t", t=TWX)
    hT3 = hT[:].rearrange("p (c t) -> p c t", t=TWX)
    psT3 = psT[:].rearrange("p (c t) -> p c t", t=128)

    cnt = {"x": 0, "g": 0, "d": 0, "stg": 0, "tmp": 0, "evac": 0}

    S.add("pool", lambda e: e.dma_start(out=wba[:], in_=win_ba), writes=["wba"], slot="wba")
    S.add("sp", lambda e: e.dma_start(out=ident[:], in_=ident_d), writes=["ident"], slot="ident")
    for k in wbc:
        S.add("sp", lambda e, k=k: e.dma_start(out=wbc[k][:], in_=wbc_d[k]), writes=["wbc" + k], slot="wbc" + k)

    def load_x(src, row0, n, extra_reads=()):
        i = cnt["x"] % 2
        cnt["x"] += 1
        S.add("sp", lambda e: e.dma_start(out=X[i][:n, :], in_=src[row0:row0 + n, :]),
              reads=list(extra_reads), writes=[f"X{i}"], slot=f"X{i}")
        return i

    def rstd_of(src_ap, n, src_res, col):
        S.add("act", lambda e: e.activation(out=XN[:n, :], in_=src_ap, func=AF.Square,
                                            accum_out=sm[:n, col:col + 1]),
              reads=src_res, writes=["XN", f"sm{col}"])
        S.add("dve", lambda e: e.tensor_scalar(out=sm[:n, col + 1:col + 2], in0=sm[:n, col:col + 1],
                                               scalar1=1.0 / D, scalar2=EPS, op0=ALU.mult, op1=ALU.add),
              reads=[f"sm{col}"], writes=[f"sm{col + 1}"])
        S.add("act", lambda e: e.sqrt(out=sm[:n, col + 2:col + 3], in_=sm[:n, col + 1:col + 2]),
              reads=[f"sm{col + 1}"], writes=[f"sm{col + 2}"])
        S.add("dve", lambda e: e.reciprocal(out=sm[:n, col + 3:col + 4], in_=sm[:n, col + 2:col + 3]),
              reads=[f"sm{col + 2}"], writes=[f"sm{col + 3}"])
        return sm[:n, col + 3:col + 4], f"sm{col + 3}"

    def norm_T(src_ap, src_res, n, col, wkey):
        rs, rres = rstd_of(src_ap, n, src_res, 0)
        S.add("dve", lambda e: e.scalar_tensor_tensor(out=XN[:n, :], in0=src_ap, scalar=rs,
                                                      in1=wbc[wkey][:n, :], op0=ALU.mult, op1=ALU.mult),
              reads=src_res + [rres, "wbc" + wres[wkey]], writes=["XN"])
        for half in range(2):
            for j in range(8):
                c = half * 8 + j
                S.add("pe", lambda e, c=c, j=j: e.transpose(out=psT3[:, j, :n], in_=XN[:n, c * 128:(c + 1) * 128],
                                                            identity=ident[:n, :n]),
                      reads=["XN", "ident"], writes=["psT"])
            eng = "act" if half == 0 else "dve"
            if eng == "act":
                S.add("act", lambda e, half=half: e.copy(out=xnT3[:, half * 8:half * 8 + 8, col:col + n],
                                                         in_=psT3[:, :, :n]),
                      reads=["psT"], writes=["xnT"])
            else:
                S.add("dve", lambda e, half=half: e.tensor_copy(out=xnT3[:, half * 8:half * 8 + 8, col:col + n],
                                                                in_=psT3[:, :, :n]),
                      reads=["psT"], writes=["xnT"])

    def linear_fm(t, wsrc, ngroups, stages, skey, evac):
        for g in range(ngroups):
            i = cnt[skey] % 2
            cnt[skey] += 1
            for (arr, st) in zip(wsrc, stages):
                S.add("pool", lambda e, arr=arr, st=st, g=g, i=i: e.dma_start(out=st[i][:], in_=arr[g]),
                      writes=[N(st[i])], slot=N(st[i]))
            for gi in range(GG):
                oc = g * GG + gi
                for (c0, n) in passes(t):
                    outs = []
                    for wi, st in enumerate(stages):
                        b = bank[(cnt["evac"] % 2) * len(stages) + wi]
                        st3 = st[i][:].rearrange("p (k f) -> p k f", f=128 * GG)
                        for kc in range(KC):
                            S.add("pe", lambda e, b=b, st3=st3, kc=kc, gi=gi, c0=c0, n=n: e.matmul(
                                b[:, :n], lhsT=st3[:, kc, gi * 128:(gi + 1) * 128], rhs=xnT3[:, kc, c0:c0 + n],
                                start=(kc == 0), stop=(kc == KC - 1)),
                                reads=[N(st[i]), "xnT"], writes=[N(b)])
                        outs.append(b)
                    cnt["evac"] += 1
                    evac(oc, c0, n, outs)

    def linear_tm(t, wsrc, nn, nkg, kgsz, stages, skey, lhs3, lres, evac):
        sts = subtiles(t)
        for nt_ in range(nn):
            for kg in range(nkg):
                i = cnt[skey] % 2
                cnt[skey] += 1
                S.add("pool", lambda e, nt_=nt_, kg=kg, i=i: e.dma_start(out=stages[i][:, :kgsz * 512],
                                                                         in_=wsrc[nt_, kg]),
                      writes=[N(stages[i])], slot=N(stages[i]))
                st3 = stages[i][:].rearrange("p (k f) -> p k f", f=512)
                for kk in range(kgsz):
                    k = kg * kgsz + kk
                    for si, (row0, n, col) in enumerate(sts):
                        b = bank[2 + si]
                        S.add("pe", lambda e, b=b, st3=st3, kk=kk, k=k, n=n, col=col: e.matmul(
                            b[:n, :], lhsT=lhs3[:, k, col:col + n], rhs=st3[:, kk, :],
                            start=(k == 0), stop=(k == nkg * kgsz - 1)),
                            reads=[N(stages[i]), lres], writes=[N(b)])
            for si, (row0, n, col) in enumerate(sts):
                evac(nt_, si, row0, n, bank[2 + si])

    wres = {"f1pre": "f1pre", "f1post": "f1post", "mixpre": "mixpre",
            "f2pre": "f1pre", "f2post": "f1post", "mixpost": "mixpre"}
    for k2, k1 in list(wres.items()):
        wbc[k2] = wbc[k1]

    def ffn_block(t, wg, wu, wd, postkey, res_src, after):
        sts = subtiles(t)

        def evac_gu(oc, c0, n, outs):
            j = cnt["stg"] % 2
            cnt["stg"] += 1
            S.add("act", lambda e: e.activation(out=tmpg[j][:, :n], in_=outs[0][:, :n], func=AF.Silu),
                  reads=[N(outs[0])], writes=[f"stg{j}"])
            S.add("dve", lambda e: e.tensor_tensor(out=hT3[:, oc, c0:c0 + n], in0=tmpg[j][:, :n],
                                                   in1=outs[1][:, :n], op=ALU.mult),
                  reads=[f"stg{j}", N(outs[1])], writes=["lhs"])

        linear_fm(t, [wg, wu], FC // GG, [wgS, wuS], "g", evac_gu)

        def evac_d(nt_, si, row0, n, b):
            S.add("act", lambda e: e.copy(out=Y[si][:n, nt_ * 512:(nt_ + 1) * 512], in_=b[:n, :]),
                  reads=[N(b)], writes=[f"Y{si}"])

        linear_tm(t, wd, 4, FC // KGD, KGD, wdS, "d", hT3, "lhs", evac_d)

        for si, (row0, n, col) in enumerate(sts):
            post_res(si, row0, n, postkey, res_src, 0.5)
            after(si, row0, n, col)

    def post_res(si, row0, n, postkey, res_src, alpha):
        rs, rres = rstd_of(Y[si][:n, :], n, [f"Y{si}"], 8)
        S.add("dve", lambda e: e.scalar_tensor_tensor(
            out=Y[si][:n, :], in0=Y[si][:n, :], scalar=rs, in1=wbc[postkey][:n, :],
            op0=ALU.mult, op1=ALU.mult),
            reads=[f"Y{si}", rres, "wbc" + wres[postkey]], writes=[f"Y{si}"])
        xi = load_x(res_src, row0, n, extra_reads=([f"x2dram{si}"] if res_src is x2_d else []))
        S.add("dve", lambda e: e.scalar_tensor_tensor(
            out=Y[si][:n, :], in0=Y[si][:n, :], scalar=alpha, in1=X[xi][:n, :],
            op0=ALU.mult, op1=ALU.add),
            reads=[f"Y{si}", f"X{xi}"], writes=[f"Y{si}"])

    def evac_d_plain(nt_, si, row0, n, b):
        S.add("act", lambda e: e.copy(out=Y[si][:n, nt_ * 512:(nt_ + 1) * 512], in_=b[:n, :]),
              reads=[N(b)], writes=[f"Y{si}"])

    def phase1_tile(t):
        sts = subtiles(t)
        for (row0, n, col) in sts:
            xi = load_x(xin, row0, n)
            norm_T(X[xi][:n, :], [f"X{xi}"], n, col, "f1pre")

        def after1(si, row0, n, col):
            S.add("sp", lambda e: e.dma_start(out=x1_d[row0:row0 + n, :], in_=Y[si][:n, :]),
                  reads=[f"Y{si}"], slot=f"x1o{si}")
            norm_T(Y[si][:n, :], [f"Y{si}"], n, col, "mixpre")

        ffn_block(t, wg1, wu1, wd1, "f1post", xin, after1)

        def evac_in(oc, c0, n, outs):
            j = cnt["stg"] % 2
            cnt["stg"] += 1
            S.add("act", lambda e: e.copy(out=stg[j][:, :n], in_=outs[0][:, :n]),
                  reads=[N(outs[0])], writes=[f"stg{j}"])
            gcol = (t * TW + c0) if c0 < TW else SEQ
            S.add("sp", lambda e: e.dma_start(out=projT[oc * 128:(oc + 1) * 128, gcol:gcol + n], in_=stg[j][:, :n]),
                  reads=[f"stg{j}"], slot=f"stgo{j}")

        linear_fm(t, [win_fm], NFM // GG, [wgS], "g", evac_in)

        def evac_v(nt_, si, row0, n, b):
            j = cnt["stg"] % 2
            cnt["stg"] += 1
            S.add("act", lambda e: e.copy(out=stg[j][:n, :512], in_=b[:n, :]),
                  reads=[N(b)], writes=[f"stg{j}"])
            S.add("sp", lambda e: e.dma_start(out=v_out[row0:row0 + n, nt_ * 512:(nt_ + 1) * 512], in_=stg[j][:n, :512]),
                  reads=[f"stg{j}"], slot=f"stgo{j}")

        linear_tm(t, win_v, 2, KC // KGV, KGV, wdS, "d", xnT3, "xnT", evac_v)

        wba3 = wba[:].rearrange("p (k f) -> p k f", f=16)

        def ba_sub(si, row0, n, col):
            b = bank[2 + si]
            for kc in range(KC):
                S.add("pe", lambda e, kc=kc: e.matmul(
                    b[:n, :16], lhsT=xnT3[:, kc, col:col + n], rhs=wba3[:, kc, :],
                    start=(kc == 0), stop=(kc == KC - 1)),
                    reads=["wba", "xnT"], writes=[N(b)])
            j = cnt["stg"] % 2
            cnt["stg"] += 1
            S.add("act", lambda e: e.copy(out=stg[j][:n, :16], in_=b[:n, :16]),
                  reads=[N(b)], writes=[f"stg{j}"])
            S.add("sp", lambda e: e.dma_start(out=ba_d[row0:row0 + n, :], in_=stg[j][:n, :16]),
                  reads=[f"stg{j}"], slot=f"stgo{j}")

        for si, (row0, n, col) in enumerate(sts):
            ba_sub(si, row0, n, col)

    def phase3_tile(t):
        sts = subtiles(t)
        ncols = TW + (NS if t == NT - 1 else 0)
        S.add("sp", lambda e: e.dma_start(out=xnT3[:, :, 0:ncols],
                                          in_=mixT_d[:, t * TW:t * TW + ncols].rearrange("(k p) n -> p k n", p=128)),
              writes=["xnT"], slot="xnTld")
        linear_tm(t, wout_d, 4, KC // KGV, KGV, wdS, "d", xnT3, "xnT", evac_d_plain)

        def mid(si, row0, n, col):
            post_res(si, row0, n, "mixpost", x1_d, 1.0)
            S.add("sp", lambda e: e.dma_start(out=x2_d[row0:row0 + n, :], in_=Y[si][:n, :]),
                  reads=[f"Y{si}"], writes=[f"x2dram{si}"], slot=f"x2o{si}")
            norm_T(Y[si][:n, :], [f"Y{si}"], n, col, "f2pre")

        for si, (row0, n, col) in enumerate(sts):
            mid(si, row0, n, col)

        def after3(si, row0, n, col):
            S.add("sp", lambda e: e.dma_start(out=y_d[row0:row0 + n, :], in_=Y[si][:n, :]),
                  reads=[f"Y{si}"], slot=f"yo{si}")

        ffn_block(t, wg2, wu2, wd2, "f2post", x2_d, after3)

    for t in range(0 if gdn_only else NT):
        phase1_tile(t)


    S.fence()
    pool_cols = {}
    pool_next = [0]

    def galloc(name, ncols_req):
        ncols = ((ncols_req + 31) // 32) * 32
        off = pool_next[0]
        bi, co = off // D, off % D
        if co + ncols > D:
            bi, co = bi + 1, 0
            off = bi * D
        pool_next[0] = off + ncols
        assert bi < 5, "gdn scratch overflow"
        return Y[bi][:, co:co + ncols_req]

    Sst = [galloc(f"S{h}", 128) for h in range(8)]
    gc = galloc("gconst", 768)
    identF, Umat, maskLn, maskUn, onesF, maskUd = [gc[:, i * 128:(i + 1) * 128] for i in range(6)]
    cw = galloc("cw", 96)
    hb = galloc("hb", 32)
    chk = galloc("chk", 64)
    S.add("sp", lambda e: e.dma_start(out=gc, in_=gconst), writes=["gconst"], slot="gconst")
    S.add("sp", lambda e: e.dma_start(out=cw, in_=cw_d), writes=["cw"], slot="cw")
    S.add("sp", lambda e: e.dma_start(out=hb, in_=hb_d), writes=["hb"], slot="hb")
    S.add("act", lambda e: e.activation(out=hb[:, 24:32], in_=hb[:, 0:8], func=AF.Exp), reads=["hb"], writes=["hbA"])
    S.add("dve", lambda e: e.tensor_scalar(out=hb[:, 24:32], in0=hb[:, 24:32], scalar1=-1.0, scalar2=None, op0=ALU.mult),
          reads=["hbA"], writes=["hbA"])
    for h in range(8):
        S.add("dve", lambda e, h=h: e.memset(Sst[h], 0.0), writes=[f"S{h}"])

    US = []
    for p in range(2):
        d_ = {}
        for nme, w_ in (("kxp", 131), ("vxp", 131), ("kc", 128), ("vc", 128), ("t1", 128), ("t2", 128),
                        ("kn", 128), ("ktm", 128), ("vb", 128), ("Gbc", 128), ("rep", 128), ("kbT", 128),
                        ("QT", 128), ("Q", 128), ("R", 128), ("kg", 128), ("nwT", 128), ("vn", 128),
                        ("kd", 128), ("sc", 16),
                        ("qxp", 131), ("qc", 128), ("qg", 128), ("qkT", 128), ("eG", 128), ("t3", 128),
                        ("zs", 128), ("on", 128)):
            d_[nme] = galloc(f"{nme}{p}", w_)
        US.append(d_)

    def pb(i):
        return bank[i % 7][:, 0:128], N(bank[i % 7])

    unit_no = [0]
    KSTAGE = int(os.environ.get('K_STAGE', '99'))
    KCHUNKS = int(os.environ.get('K_CHUNKS', str(SEQ // 128)))
    KSAMPLE = int(os.environ.get('K_SAMPLE', '1'))
    KCHUNKLVL = int(os.environ.get('K_CHUNKLVL', '1'))

    def gdn_chunk(tok0, nvalid, hist_src, sample):
        if not KCHUNKLVL:
            return
        nt_ = min(128, nvalid)
        if nvalid < 128:
            S.add("dve", lambda e: e.memset(chk[:, 0:16], 0.0), writes=["chkBA"])
        S.add("sp", lambda e: e.dma_start(out=chk[:nt_, 0:16], in_=ba_d[tok0:tok0 + nt_, :]),
              writes=["chkBA"], slot="chkBA")
        S.add("act", lambda e: e.activation(out=chk[:, 16:24], in_=chk[:, 0:8], func=AF.Sigmoid),
              reads=["chkBA"], writes=["chkbeta"])
        S.add("dve", lambda e: e.tensor_tensor(out=chk[:, 24:32], in0=chk[:, 8:16], in1=hb[:, 8:16], op=ALU.add),
              reads=["chkBA", "hb"], writes=["chktmp"])
        S.add("dve", lambda e: e.tensor_scalar(out=chk[:, 24:32], in0=chk[:, 24:32], scalar1=30.0, scalar2=None, op0=ALU.min),
              reads=["chktmp"], writes=["chktmp"])
        S.add("act", lambda e: e.activation(out=chk[:, 24:32], in_=chk[:, 24:32], func=AF.Exp),
              reads=["chktmp"], writes=["chktmp"])
        S.add("act", lambda e: e.activation(out=chk[:, 24:32], in_=chk[:, 24:32], func=AF.Ln, bias=1.0),
              reads=["chktmp"], writes=["chktmp"])
        S.add("dve", lambda e: e.tensor_tensor(out=chk[:, 32:40], in0=chk[:, 24:32], in1=hb[:, 24:32], op=ALU.mult),
              reads=["chktmp", "hbA"], writes=["chkg"])
        if nvalid < 128:
            S.add("dve", lambda e: e.tensor_scalar(out=chk[:, 32:40], in0=chk[:, 32:40], scalar1=hb[:, 16:17], scalar2=None, op0=ALU.mult),
                  reads=["chkg", "hb"], writes=["chkg"])
            S.add("dve", lambda e: e.tensor_scalar(out=chk[:, 16:24], in0=chk[:, 16:24], scalar1=hb[:, 16:17], scalar2=None, op0=ALU.mult),
                  reads=["chkbeta", "hb"], writes=["chkbeta"])
        pg, pgr = pb(27)
        S.add("pe", lambda e: e.matmul(pg[:, :8], lhsT=Umat, rhs=chk[:, 32:40], start=True, stop=True),
              reads=["gconst", "chkg"], writes=[pgr])
        S.add("act", lambda e: e.copy(out=chk[:, 40:48], in_=pg[:, :8]), reads=[pgr], writes=["chkG"])

        def unit(h):
            p = unit_no[0] % 2
            unit_no[0] += 1
            u = US[p]
            R_ = lambda nme: f"{nme}{p}"
            def P(i):
                bk = bank[3 * p + i % 3]
                return bk[:, (i // 3) * 128:(i // 3 + 1) * 128], f"{N(bk)}r{i // 3}"
            krow = 2048 + 1024 + 128 * h
            vrow = 2048 + 2048 + 128 * h
            qrow = 2048 + 128 * h
            for (dst, row, part, nme) in ((u["kxp"], krow, 8 + h, "kxp"), (u["vxp"], vrow, 16 + h, "vxp"), (u["qxp"], qrow, h, "qxp")):
                if hist_src is None:
                    S.add("dve", lambda e, dst=dst: e.memset(dst[:, 0:3], 0.0), writes=[R_(nme)])
                elif hist_src == "state":
                    S.add("dve", lambda e, dst=dst: e.memset(dst[:, :], 0.0), writes=[R_(nme)])
                    S.add("sp", lambda e, dst=dst, part=part: e.dma_start(out=dst[:, 0:3], in_=convst_d[part]),
                          writes=[R_(nme)], slot=R_(nme) + "h")
                if hist_src == "prev":
                    S.add("sp", lambda e, dst=dst, row=row: e.dma_start(out=dst[:, 0:131], in_=projT[row:row + 128, tok0 - 3:tok0 + 128]),
                          writes=[R_(nme)], slot=R_(nme))
                else:
                    S.add("sp", lambda e, dst=dst, row=row: e.dma_start(out=dst[:, 3:3 + nt_], in_=projT[row:row + 128, tok0:tok0 + nt_]),
                          writes=[R_(nme)], slot=R_(nme))
            if KSTAGE < 1:
                return
            for (src, dst, part, sn, dn) in ((u["kxp"], u["kc"], 8 + h, "kxp", "kc"), (u["vxp"], u["vc"], 16 + h, "vxp", "vc"), (u["qxp"], u["qc"], h, "qxp", "qc")):
                S.add("dve", lambda e, src=src, dst=dst, part=part: e.tensor_scalar(
                    out=dst, in0=src[:, 0:128], scalar1=cw[:, part * 4:part * 4 + 1], scalar2=None, op0=ALU.mult),
                    reads=[R_(sn), "cw"], writes=[R_(dn)])
                for j in range(1, 4):
                    S.add("dve", lambda e, src=src, dst=dst, part=part, j=j: e.scalar_tensor_tensor(
                        out=dst, in0=src[:, j:j + 128], scalar=cw[:, part * 4 + j:part * 4 + j + 1], in1=dst,
                        op0=ALU.mult, op1=ALU.add),
                        reads=[R_(sn), "cw", R_(dn)], writes=[R_(dn)])
                S.add("act", lambda e, dst=dst: e.activation(out=dst, in_=dst, func=AF.Silu),
                      reads=[R_(dn)], writes=[R_(dn)])
            if KSTAGE < 2:
                return
            S.add("act", lambda e: e.activation(out=u["t1"], in_=u["kc"], func=AF.Square), reads=[R_("kc")], writes=[R_("t1")])
            p0, p0r = P(0)
            S.add("pe", lambda e: e.matmul(p0, lhsT=onesF, rhs=u["t1"], start=True, stop=True),
                  reads=["gconst", R_("t1")], writes=[p0r])
            S.add("dve", lambda e: e.tensor_scalar(out=u["t2"], in0=p0, scalar1=1e-6, scalar2=None, op0=ALU.add),
                  reads=[p0r], writes=[R_("t2")])
            S.add("act", lambda e: e.sqrt(out=u["t2"], in_=u["t2"]), reads=[R_("t2")], writes=[R_("t2")])
            S.add("dve", lambda e: e.reciprocal(out=u["t2"], in_=u["t2"]), reads=[R_("t2")], writes=[R_("t2")])
            S.add("dve", lambda e: e.tensor_tensor(out=u["kn"], in0=u["kc"], in1=u["t2"], op=ALU.mult),
                  reads=[R_("kc"), R_("t2")], writes=[R_("kn")])
            if KSTAGE < 3:
                return
            KSUB = int(os.environ.get('K_SUB', '9'))
            p1, p1r = P(1)
            S.add("pe", lambda e: e.matmul(p1, lhsT=u["kn"], rhs=identF, start=True, stop=True), reads=[R_("kn"), "gconst"], writes=[p1r])
            if KSUB < 1:
                return
            S.add("act", lambda e: e.copy(out=u["ktm"], in_=p1), reads=[p1r], writes=[R_("ktm")])
            if KSUB < 2:
                return
            p2, p2r = P(2)
            vsrc = "kc" if os.environ.get('K_V1') else "vc"
            S.add("pe", lambda e: e.matmul(p2, lhsT=u[vsrc], rhs=identF, start=True, stop=True), reads=[R_(vsrc), "gconst"], writes=[p2r])
            if KSUB < 3:
                return
            S.add("dve", lambda e: e.tensor_scalar(out=u["vb"], in0=p2, scalar1=chk[:, 16 + h:17 + h], scalar2=None, op0=ALU.mult),
                  reads=[p2r, "chkbeta"], writes=[R_("vb")])
            if KSTAGE < 4:
                return
            S.add("dve", lambda e: e.tensor_scalar(out=u["rep"], in0=onesF, scalar1=chk[:, 32 + h:33 + h], scalar2=None, op0=ALU.mult),
                  reads=["gconst", "chkg"], writes=[R_("rep")])
            p3, p3r = P(3)
            S.add("pe", lambda e: e.matmul(p3, lhsT=u["rep"], rhs=Umat, start=True, stop=True),
                  reads=[R_("rep"), "gconst"], writes=[p3r])
            S.add("act", lambda e: e.copy(out=u["Gbc"], in_=p3), reads=[p3r], writes=[R_("Gbc")])
            S.add("dve", lambda e: e.tensor_scalar(out=u["rep"], in0=onesF, scalar1=chk[:, 16 + h:17 + h], scalar2=None, op0=ALU.mult),
                  reads=["gconst", "chkbeta"], writes=[R_("rep")])
            p4, p4r = P(4)
            S.add("pe", lambda e: e.matmul(p4, lhsT=u["rep"], rhs=identF, start=True, stop=True),
                  reads=[R_("rep"), "gconst"], writes=[p4r])
            S.add("dve", lambda e: e.tensor_tensor(out=u["kbT"], in0=u["kn"], in1=p4, op=ALU.mult),
                  reads=[R_("kn"), p4r], writes=[R_("kbT")])
            if KSTAGE < 5:
                return
            p5, p5r = P(5)
            p6, p6r = P(6)
            S.add("pe", lambda e: e.matmul(p5, lhsT=u["kbT"], rhs=u["kn"], start=True, stop=True),
                  reads=[R_("kbT"), R_("kn")], writes=[p5r])
            S.add("pe", lambda e: e.matmul(p6, lhsT=u["kn"], rhs=u["kbT"], start=True, stop=True),
                  reads=[R_("kbT"), R_("kn")], writes=[p6r])
            gcol = chk[:, 40 + h:41 + h]
            S.add("dve", lambda e: e.tensor_scalar(out=u["t1"], in0=u["Gbc"], scalar1=gcol, scalar2=0.0, op0=ALU.subtract, op1=ALU.max),
                  reads=[R_("Gbc"), "chkG"], writes=[R_("t1")])
            S.add("act", lambda e: e.activation(out=u["t1"], in_=u["t1"], func=AF.Exp, scale=-1.0), reads=[R_("t1")], writes=[R_("t1")])
            S.add("dve", lambda e: e.tensor_tensor(out=u["t1"], in0=u["t1"], in1=maskLn, op=ALU.mult),
                  reads=[R_("t1"), "gconst"], writes=[R_("t1")])
            S.add("dve", lambda e: e.tensor_tensor(out=u["QT"], in0=u["t1"], in1=p5, op=ALU.mult),
                  reads=[R_("t1"), p5r], writes=[R_("QT")])
            S.add("dve", lambda e: e.tensor_scalar(out=u["t2"], in0=u["Gbc"], scalar1=gcol, scalar2=0.0, op0=ALU.subtract, op1=ALU.min),
                  reads=[R_("Gbc"), "chkG"], writes=[R_("t2")])
            S.add("act", lambda e: e.activation(out=u["t2"], in_=u["t2"], func=AF.Exp), reads=[R_("t2")], writes=[R_("t2")])
            S.add("dve", lambda e: e.tensor_tensor(out=u["t3"], in0=u["t2"], in1=maskUd, op=ALU.mult),
                  reads=[R_("t2"), "gconst"], writes=[R_("t3")])
            S.add("dve", lambda e: e.tensor_tensor(out=u["t2"], in0=u["t2"], in1=maskUn, op=ALU.mult),
                  reads=[R_("t2"), "gconst"], writes=[R_("t2")])
            S.add("dve", lambda e: e.tensor_tensor(out=u["Q"], in0=u["t2"], in1=p6, op=ALU.mult),
                  reads=[R_("t2"), p6r], writes=[R_("Q")])
            S.add("dve", lambda e: e.tensor_tensor(out=u["R"], in0=u["Q"], in1=identF, op=ALU.add),
                  reads=[R_("Q"), "gconst"], writes=[R_("R")])
            if KSTAGE < 6:
                return
            for lvl in range(1, 7):
                pa, par = P(7)
                pq, pqr = P(8)
                S.add("pe", lambda e, pa=pa: e.matmul(pa, lhsT=u["Q"], rhs=u["QT"], start=True, stop=True),
                      reads=[R_("Q"), R_("QT")], writes=[par])
                if lvl < 6:
                    S.add("pe", lambda e, pq=pq: e.matmul(pq, lhsT=u["QT"], rhs=u["Q"], start=True, stop=True),
                          reads=[R_("Q"), R_("QT")], writes=[pqr])
                S.add("act", lambda e, pa=pa: e.copy(out=u["QT"], in_=pa), reads=[par], writes=[R_("QT")])
                if lvl < 6:
                    S.add("dve", lambda e, pq=pq: e.tensor_copy(out=u["Q"], in_=pq), reads=[pqr], writes=[R_("Q")])
                pr, prr = P(9)
                S.add("pe", lambda e, pr=pr: e.matmul(pr, lhsT=u["QT"], rhs=u["R"], start=True, stop=True),
                      reads=[R_("QT"), R_("R")], writes=[prr])
                S.add("dve", lambda e, pr=pr: e.tensor_tensor(out=u["R"], in0=u["R"], in1=pr, op=ALU.add),
                      reads=[R_("R"), prr], writes=[R_("R")])
            if KSTAGE < 7:
                return
            S.add("act", lambda e: e.activation(out=u["t1"], in_=u["qc"], func=AF.Square), reads=[R_("qc")], writes=[R_("t1")])
            pq0, pq0r = P(1)
            S.add("pe", lambda e: e.matmul(pq0, lhsT=onesF, rhs=u["t1"], start=True, stop=True),
                  reads=["gconst", R_("t1")], writes=[pq0r])
            S.add("dve", lambda e: e.tensor_scalar(out=u["t1"], in0=pq0, scalar1=1e-6, scalar2=None, op0=ALU.add),
                  reads=[pq0r], writes=[R_("t1")])
            S.add("act", lambda e: e.sqrt(out=u["t1"], in_=u["t1"]), reads=[R_("t1")], writes=[R_("t1")])
            S.add("dve", lambda e: e.reciprocal(out=u["t1"], in_=u["t1"]), reads=[R_("t1")], writes=[R_("t1")])
            S.add("dve", lambda e: e.scalar_tensor_tensor(out=u["qc"], in0=u["qc"], scalar=128.0 ** -0.5, in1=u["t1"],
                                                          op0=ALU.mult, op1=ALU.mult),
                  reads=[R_("qc"), R_("t1")], writes=[R_("qc")])
            S.add("act", lambda e: e.activation(out=u["eG"], in_=u["Gbc"], func=AF.Exp), reads=[R_("Gbc")], writes=[R_("eG")])
            S.add("dve", lambda e: e.tensor_tensor(out=u["qg"], in0=u["qc"], in1=u["eG"], op=ALU.mult),
                  reads=[R_("qc"), R_("eG")], writes=[R_("qg")])
            pqk, pqkr = P(2)
            S.add("pe", lambda e: e.matmul(pqk, lhsT=u["kn"], rhs=u["qc"], start=True, stop=True),
                  reads=[R_("kn"), R_("qc")], writes=[pqkr])
            S.add("dve", lambda e: e.tensor_tensor(out=u["qkT"], in0=u["t3"], in1=pqk, op=ALU.mult),
                  reads=[R_("t3"), pqkr], writes=[R_("qkT")])
            sc = u["sc"]
            S.add("act", lambda e: e.activation(out=sc[:, 0:1], in_=gcol, func=AF.Exp), reads=["chkG"], writes=[R_("sc")])
            S.add("dve", lambda e: e.tensor_tensor(out=sc[:, 0:1], in0=sc[:, 0:1], in1=chk[:, 16 + h:17 + h], op=ALU.mult),
                  reads=[R_("sc"), "chkbeta"], writes=[R_("sc")])
            S.add("dve", lambda e: e.tensor_tensor(out=sc[:, 1:2], in0=u["Gbc"][:, 127:128], in1=gcol, op=ALU.subtract),
                  reads=[R_("Gbc"), "chkG"], writes=[R_("sc")])
            S.add("act", lambda e: e.activation(out=sc[:, 1:2], in_=sc[:, 1:2], func=AF.Exp), reads=[R_("sc")], writes=[R_("sc")])
            S.add("act", lambda e: e.activation(out=sc[:, 2:3], in_=u["Gbc"][:, 127:128], func=AF.Exp),
                  reads=[R_("Gbc")], writes=[R_("sc")])
            S.add("dve", lambda e: e.tensor_scalar(out=u["kg"], in0=u["ktm"], scalar1=sc[:, 0:1], scalar2=None, op0=ALU.mult),
                  reads=[R_("ktm"), R_("sc")], writes=[R_("kg")])
            S.add("dve", lambda e: e.tensor_scalar(out=u["kd"], in0=u["ktm"], scalar1=sc[:, 1:2], scalar2=None, op0=ALU.mult),
                  reads=[R_("ktm"), R_("sc")], writes=[R_("kd")])
            if KSTAGE < 8:
                return
            p10, p10r = P(10)
            S.add("pe", lambda e: e.matmul(p10, lhsT=u["kg"], rhs=u["R"], start=True, stop=True),
                  reads=[R_("kg"), R_("R")], writes=[p10r])
            S.add("act", lambda e: e.mul(out=u["nwT"], in_=p10, mul=-1.0), reads=[p10r], writes=[R_("nwT")])
            p11, p11r = P(11)
            S.add("pe", lambda e: e.matmul(p11, lhsT=u["R"], rhs=u["vb"], start=True, stop=False),
                  reads=[R_("R"), R_("vb")], writes=[p11r])
            S.add("pe", lambda e: e.matmul(p11, lhsT=u["nwT"], rhs=Sst[h], start=False, stop=True),
                  reads=[R_("nwT"), f"S{h}"], writes=[p11r])
            S.add("act", lambda e: e.copy(out=u["vn"], in_=p11), reads=[p11r], writes=[R_("vn")])
            po, por = P(3)
            S.add("pe", lambda e: e.matmul(po, lhsT=Sst[h], rhs=u["qg"], start=True, stop=False),
                  reads=[f"S{h}", R_("qg")], writes=[por])
            S.add("pe", lambda e: e.matmul(po, lhsT=u["vn"], rhs=u["qkT"], start=False, stop=True),
                  reads=[R_("vn"), R_("qkT")], writes=[por])
            S.add("act", lambda e: e.activation(out=u["t1"], in_=po, func=AF.Square), reads=[por], writes=[R_("t1")])
            pss, pssr = P(4)
            S.add("pe", lambda e: e.matmul(pss, lhsT=onesF, rhs=u["t1"], start=True, stop=True),
                  reads=["gconst", R_("t1")], writes=[pssr])
            S.add("dve", lambda e: e.tensor_scalar(out=u["t1"], in0=pss, scalar1=1.0 / 128, scalar2=EPS, op0=ALU.mult, op1=ALU.add),
                  reads=[pssr], writes=[R_("t1")])
            S.add("act", lambda e: e.sqrt(out=u["t1"], in_=u["t1"]), reads=[R_("t1")], writes=[R_("t1")])
            S.add("dve", lambda e: e.reciprocal(out=u["t1"], in_=u["t1"]), reads=[R_("t1")], writes=[R_("t1")])
            S.add("dve", lambda e: e.tensor_tensor(out=u["on"], in0=u["t1"], in1=po, op=ALU.mult),
                  reads=[R_("t1"), por], writes=[R_("on")])
            zrow = 5120 + 128 * h
            S.add("sp", lambda e: e.dma_start(out=u["zs"][:, :nt_], in_=projT[zrow:zrow + 128, tok0:tok0 + nt_]),
                  writes=[R_("zs")], slot=R_("zs"))
            S.add("act", lambda e: e.activation(out=u["zs"][:, :nt_], in_=u["zs"][:, :nt_], func=AF.Silu), reads=[R_("zs")], writes=[R_("zs")])
            S.add("dve", lambda e: e.scalar_tensor_tensor(out=finb[p][:, :nt_], in0=u["on"][:, :nt_], scalar=hb[:, 17:18], in1=u["zs"][:, :nt_],
                                                          op0=ALU.mult, op1=ALU.mult),
                  reads=[R_("on"), "hb", R_("zs")], writes=[f"finb{p}"])
            S.add("sp", lambda e: e.dma_start(out=mixT_d[1024 + 128 * h:1024 + 128 * h + 128, tok0:tok0 + nt_], in_=finb[p][:, :nt_]),
                  reads=[f"finb{p}"], slot=f"finbo{p}")
            S.add("pe", lambda e: e.matmul(p0, lhsT=u["kd"], rhs=u["vn"], start=True, stop=True),
                  reads=[R_("kd"), R_("vn")], writes=[p0r])
            S.add("dve", lambda e: e.scalar_tensor_tensor(out=Sst[h], in0=Sst[h], scalar=sc[:, 2:3], in1=p0,
                                                          op0=ALU.mult, op1=ALU.add),
                  reads=[f"S{h}", R_("sc"), p0r], writes=[f"S{h}"])

        def record(h):
            rec = []
            S.add = lambda *a, **k: rec.append((a, k))
            try:
                unit(h)
            finally:
                del S.add
            items = []
            for a, k in rec:
                if items and a[0] == "pe" and items[-1][-1][0][0] == "pe":
                    items[-1].append((a, k))
                else:
                    items.append([(a, k)])
            return items

        for h in range(0, 8, 2):
            ia, ib = record(h), record(h + 1)
            for i in range(max(len(ia), len(ib))):
                for it in (ia, ib):
                    if i < len(it):
                        for a, k in it[i]:
                            S.add(*a, **k)

    for c in range(0 if attn_only else KCHUNKS):
        gdn_chunk(c * 128, 128, None if c == 0 else "prev", False)
    for h in range(8):
        S.add("sp", lambda e, h=h: e.dma_start(out=ssm_p[h], in_=Sst[h]), reads=[f"S{h}"], writes=["ssmo_all"], slot="ssmo")
    for h in range(8):
        S.add("sp", lambda e, h=h: e.dma_start(out=Sst[h], in_=ssm0_d[h]), reads=["ssmo_all"], writes=[f"S{h}"], slot=f"ssmi{h}")
    if KSAMPLE and not attn_only:
        gdn_chunk(SEQ, NS, "state", True)
    for h in range(8):
        S.add("sp", lambda e, h=h: e.dma_start(out=ssm_s[h], in_=Sst[h]), reads=[f"S{h}"], slot="ssmo")


    S.fence()
    hoff = [0]

    def balloc(ncols):
        o = hoff[0]
        hoff[0] += ncols
        assert hoff[0] <= FC * TWX
        return hT[:, o:o + ncols]

    A_pos = Y[0][64:65, :]
    A_posl = Y[1][64:65, :]
    A_m = X[0][64:65, :]
    xo = [0]

    def xalloc(ncols):
        o = xo[0]
        xo[0] += ncols
        assert xo[0] <= D
        return X[1][:, o:o + ncols]

    a_ident = xalloc(128)
    a_mask = xalloc(128)
    a_O = [[xalloc(128) for s_ in range(4)] for m_ in range(2)]
    a_pd = xalloc(128)
    a_fin = xalloc(128)
    a_sub = xalloc(128)
    a_lam = xalloc(256)
    a_small = xalloc(32)
    a_kb = xalloc(64)
    a_posT = xalloc(16)
    S.add("sp", lambda e: e.dma_start(out=A_pos, in_=aconst[0:1, 0:2048]), writes=["A_pos"], slot="A_pos")
    S.add("sp", lambda e: e.dma_start(out=a_lam, in_=aconst[:, 2048:2304]), writes=["a_lam"], slot="a_lam")
    S.add("sp", lambda e: e.dma_start(out=a_sub, in_=aconst[:, 2304:2432]), writes=["a_sub"], slot="a_sub")
    S.add("sp", lambda e: e.dma_start(out=a_ident, in_=gconst[:, 0:128]), writes=["a_ident"], slot="a_ident")
    S.add("sp", lambda e: e.dma_start(out=a_mask, in_=gconst[:, 640:768]), writes=["a_mask"], slot="a_mask")
    for Q_ in range(4):
        S.add("dve", lambda e, Q_=Q_: e.tensor_scalar(out=A_posl[:, 512 * Q_:512 * Q_ + 512], in0=A_pos[:, 512 * Q_:512 * Q_ + 512],
                                                      scalar1=-512.0 * Q_, scalar2=None, op0=ALU.add),
              reads=["A_pos"], writes=["A_posl"])
    S.add("sp", lambda e: e.dma_start(out=a_posT, in_=aconst[:, 2432:2448]), writes=["a_posT"], slot="a_posT")
    S.add("dve", lambda e: e.tensor_scalar(out=a_sub, in0=a_sub, scalar1=0.8, scalar2=None, op0=ALU.mult),
          reads=["a_sub"], writes=["a_sub"])
    for i_ in range(2):
        S.add("dve", lambda e, i_=i_: e.tensor_tensor(out=a_lam[:, 128 * i_:128 * i_ + 64], in0=a_lam[:, 128 * i_:128 * i_ + 64],
                                                   in1=a_lam[:, 128 * i_ + 64:128 * i_ + 128], op=ALU.mult),
              reads=["a_lam"], writes=["a_lam"])
        S.add("dve", lambda e, i_=i_: e.reduce_sum(out=a_small[:, i_:i_ + 1], in_=a_lam[:, 128 * i_:128 * i_ + 64], axis=mybir.AxisListType.X),
              reads=["a_lam"], writes=["a_small"])
        S.add("act", lambda e, i_=i_: e.activation(out=a_small[:, i_:i_ + 1], in_=a_small[:, i_:i_ + 1], func=AF.Exp),
              reads=["a_small"], writes=["a_small"])
    S.add("dve", lambda e: e.tensor_tensor(out=a_small[:, 2:3], in0=a_small[:, 1:2], in1=a_small[:, 0:1], op=ALU.subtract),
          reads=["a_small"], writes=["a_small"])
    S.add("dve", lambda e: e.tensor_scalar(out=a_small[:, 2:3], in0=a_small[:, 2:3], scalar1=-0.2, scalar2=None, op0=ALU.add),
          reads=["a_small"], writes=["a_small"])
    neglam = a_small[:, 2:3]

    qA = [[balloc(SEQ) for m_ in range(2)] for hp_ in range(2)]
    kA = [[balloc(SEQ) for m_ in range(2)] for hp_ in range(2)]
    sqb = XN[:, :]
    Vx = [balloc(16 * 129) for _ in range(2)]
    Eb = [balloc(512) for _ in range(2)]
    onesb = balloc(72)
    finb2 = balloc(512)
    S.add("dve", lambda e: e.memset(onesb, 1.0), writes=["onesb"])
    for hp_ in range(2):
        for m_ in range(2):
            S.add("dve", lambda e, hp_=hp_, m_=m_: e.memset(kA[hp_][m_][64:65, :], 1.0), writes=[f"kA{hp_}{m_}"])

    AX = mybir.AxisListType.X

    def attn_head(h):
        hp = h % 2
        slope = 2.0 ** (-(h + 1))
        vx = Vx[hp]
        vx3 = vx.rearrange("p (b c) -> p b c", c=129)
        for m_ in range(2):
            S.add("pool", lambda e, m_=m_: e.dma_start(out=qA[hp][m_][0:64, :], in_=projT[128 * h + 64 * m_:128 * h + 64 * m_ + 64, 0:SEQ]),
                  writes=[f"qA{hp}{m_}"], slot=f"qA{hp}{m_}")
            S.add("pool", lambda e, m_=m_: e.dma_start(out=kA[hp][m_][0:64, :], in_=projT[1024 + 128 * h + 64 * m_:1024 + 128 * h + 64 * m_ + 64, 0:SEQ]),
                  writes=[f"kA{hp}{m_}"], slot=f"kA{hp}{m_}")
        S.add("dve", lambda e: e.memset(vx, 1.0), writes=[f"Vx{hp}"])
        S.add("pool", lambda e: e.dma_start(out=vx3[:, :, 0:128],
                                            in_=v_out[0:SEQ, 128 * h:128 * h + 128].rearrange("(b p) c -> p b c", p=128)),
              writes=[f"Vx{hp}"], slot=f"Vx{hp}")
        for Q_ in range(4):
            S.add("dve", lambda e, Q_=Q_: e.tensor_scalar(out=a_kb[:, 16 * Q_:16 * Q_ + 16], in0=a_posT, scalar1=slope, scalar2=-512.0 * Q_ * slope,
                                                          op0=ALU.mult, op1=ALU.add),
                  reads=["a_posT"], writes=["a_kb"])

        def sumsq(cb):
            S.add("pe", lambda e: e.matmul(bank[6][0:65, :], lhsT=onesb[0:64, 0:65], rhs=sqb[0:64, cb * 512:(cb + 1) * 512],
                                           start=True, stop=True), reads=["onesb", "sqb"], writes=[N(bank[6])])

        def norms(m):
            q_, k_ = qA[hp][m], kA[hp][m]
            S.add("act", lambda e: e.activation(out=sqb[0:64, :], in_=k_[0:64, :], func=AF.Square),
                  reads=[f"kA{hp}{m}"], writes=["sqb"])

            def kmax(cb):
                sumsq(cb)
                S.add("dve", lambda e: e.reduce_max(out=a_small[64:65, 4 + cb:5 + cb], in_=bank[6][64:65, :], axis=AX),
                      reads=[N(bank[6])], writes=["a_small"])

            for cb in range(4):
                kmax(cb)
            S.add("dve", lambda e: e.reduce_max(out=a_small[64:65, 8:9], in_=a_small[64:65, 4:8], axis=AX),
                  reads=["a_small"], writes=["a_small"])
            S.add("act", lambda e: e.activation(out=sqb[0:64, :], in_=q_[0:64, :], func=AF.Square),
                  reads=[f"qA{hp}{m}"], writes=["sqb"])

            def qn(cb):
                sumsq(cb)
                S.add("dve", lambda e: e.tensor_scalar(out=A_m[:, cb * 512:(cb + 1) * 512], in0=bank[6][64:65, :],
                                                       scalar1=a_small[64:65, 8:9], scalar2=1.21, op0=ALU.mult, op1=ALU.mult),
                      reads=[N(bank[6]), "a_small"], writes=["A_m"])

            for cb in range(4):
                qn(cb)
            S.add("act", lambda e: e.sqrt(out=A_m, in_=A_m), reads=["A_m"], writes=["A_m"])
            S.add("dve", lambda e: e.scalar_tensor_tensor(out=q_[64:65, :], in0=A_posl, scalar=-8.0 * slope, in1=A_m,
                                                          op0=ALU.mult, op1=ALU.subtract),
                  reads=["A_posl", "A_m"], writes=[f"qA{hp}{m}"])

        for m in range(2):
            norms(m)

        def qk_A(Q, m, j):
            eb = cnt["evac"] % 2
            cnt["evac"] += 1
            psb = bank[4 + eb]
            E = Eb[eb]
            S.add("pe", lambda e: e.matmul(psb[:, :], lhsT=kA[hp][m][0:65, 128 * j:128 * j + 128],
                                           rhs=qA[hp][m][0:65, 512 * Q:512 * Q + 512], start=True, stop=True),
                  reads=[f"kA{hp}{m}", f"qA{hp}{m}"], writes=[N(psb)])
            s0 = max(0, j - 4 * Q)
            S.add("act", lambda e: e.activation(out=E[:, 128 * s0:512], in_=psb[:, 128 * s0:512], func=AF.Exp, scale=0.125,
                                                bias=a_kb[:, 16 * Q + j:16 * Q + j + 1]),
                  reads=[N(psb), "a_kb"], writes=[f"Eb{eb}"])
            if j >= 4 * Q:
                S.add("dve", lambda e: e.tensor_tensor(out=E[:, 128 * s0:128 * s0 + 128], in0=E[:, 128 * s0:128 * s0 + 128],
                                                       in1=a_mask, op=ALU.mult),
                      reads=[f"Eb{eb}", "a_mask"], writes=[f"Eb{eb}"])
            return eb, E

        def qk_B(Q, j, accs, eb, E):
            def pv(s_):
                qi = 4 * Q + s_
                if j > qi:
                    return
                acc, accr = accs[s_]
                S.add("pe", lambda e: e.matmul(acc, lhsT=E[:, 128 * s_:128 * s_ + 128], rhs=vx3[:, j, :],
                                               start=(j == 0), stop=(j == qi)),
                      reads=[f"Eb{eb}", f"Vx{hp}"], writes=[accr])

            for s_ in range(4):
                pv(s_)

        def qblock_map(Q, m):
            accs = [(bank[s_][:, 0:129], N(bank[s_])) for s_ in range(4)]
            nj = 4 * Q + 4
            st = qk_A(Q, m, 0)
            for j in range(nj):
                nxt = qk_A(Q, m, j + 1) if j + 1 < nj else None
                qk_B(Q, j, accs, *st)
                st = nxt

            def fin_acc(s_):
                acc, accr = accs[s_]
                S.add("dve", lambda e: e.reciprocal(out=a_small[:, 16 + s_:17 + s_], in_=acc[:, 128:129]),
                      reads=[accr], writes=["a_small"])
                S.add("dve", lambda e: e.tensor_scalar(out=a_O[m][s_], in0=acc[:, 0:128], scalar1=a_small[:, 16 + s_:17 + s_],
                                                       scalar2=None, op0=ALU.mult),
                      reads=[accr, "a_small"], writes=[f"a_O{m}{s_}"])

            for s_ in range(4):
                fin_acc(s_)

        def fin_sub(Q, s_):
            S.add("dve", lambda e: e.scalar_tensor_tensor(out=a_pd, in0=a_O[1][s_], scalar=neglam, in1=a_O[0][s_],
                                                          op0=ALU.mult, op1=ALU.add),
                  reads=[f"a_O0{s_}", f"a_O1{s_}", "a_small"], writes=["a_pd"])
            S.add("act", lambda e: e.activation(out=a_fin, in_=a_pd, func=AF.Square, accum_out=a_small[:, 20:21]),
                  reads=["a_pd"], writes=["a_fin", "a_small"])
            S.add("dve", lambda e: e.tensor_scalar(out=a_small[:, 21:22], in0=a_small[:, 20:21], scalar1=1.0 / 128, scalar2=EPS,
                                                   op0=ALU.mult, op1=ALU.add), reads=["a_small"], writes=["a_small"])
            S.add("act", lambda e: e.sqrt(out=a_small[:, 21:22], in_=a_small[:, 21:22]), reads=["a_small"], writes=["a_small"])
            S.add("dve", lambda e: e.reciprocal(out=a_small[:, 22:23], in_=a_small[:, 21:22]), reads=["a_small"], writes=["a_small"])
            S.add("dve", lambda e: e.scalar_tensor_tensor(out=a_fin, in0=a_pd, scalar=a_small[:, 22:23], in1=a_sub,
                                                          op0=ALU.mult, op1=ALU.mult),
                  reads=["a_pd", "a_small", "a_sub"], writes=["a_fin"])
            S.add("pe", lambda e: e.matmul(bank[6][:, 0:128], lhsT=a_fin, rhs=a_ident, start=True, stop=True),
                  reads=["a_fin", "a_ident"], writes=[N(bank[6])])
            S.add("act", lambda e: e.copy(out=finb2[:, 128 * s_:128 * s_ + 128], in_=bank[6][:, 0:128]),
                  reads=[N(bank[6])], writes=["finb2"])

        def qblock(Q):
            for m in range(2):
                qblock_map(Q, m)
            for s_ in range(4):
                fin_sub(Q, s_)
            S.add("sp", lambda e: e.dma_start(out=mixT_d[128 * h:128 * h + 128, 512 * Q:512 * Q + 512], in_=finb2),
                  reads=["finb2"], slot="finb2o")

        for Q in range(4):
            qblock(Q)

    for h in range(aheads):
        attn_head(h)

    def sample_attn():
        S.fence()
        KV = [Y[0], Y[1]]
        KTs = [Y[2][:, 0:1024], Y[2][:, 1024:2048]]
        C = Y[3]
        posp, slopeRow, maskS = C[:, 0:128], C[:, 128:256], C[:, 256:384]
        pcol, subcol = C[:, 384:385], C[:, 385:386]
        identS, onesS = C[:, 512:640], C[:, 640:768]
        qs, knew = C[:, 768:832], C[:, 832:896]
        Sb = [C[:, 896:1024], C[:, 1024:1152]]
        Es = [C[:, 1152:1280], C[:, 1280:1408]]
        Rr, On, pd, sq, rstd = C[:, 1408:1536], C[:, 1536:1664], C[:, 1664:1728], C[:, 1728:1792], C[:, 1792:1856]
        vnew = Y[4][0:8, 0:1024]
        fins = finb[0][:, 0:64]
        qs3 = qs.rearrange("p (h q) -> p h q", q=8)
        knew3 = knew.rearrange("p (h q) -> p h q", q=8)
        S.add("sp", lambda e: e.dma_start(out=C[:, 0:512], in_=sconst), writes=["sC"], slot="sC")
        S.add("sp", lambda e: e.dma_start(out=identS, in_=gconst[:, 0:128]), writes=["sI"], slot="sI")
        S.add("sp", lambda e: e.dma_start(out=onesS, in_=gconst[:, 512:640]), writes=["sO"], slot="sO")
        S.add("sp", lambda e: e.dma_start(out=ptidx[:], in_=pt_d), writes=["ptidx"], slot="ptidx")
        S.add("dve", lambda e: e.tensor_scalar(out=ptidx2[:], in0=ptidx[:], scalar1=128, scalar2=None, op0=ALU.mult),
              reads=["ptidx"], writes=["ptidx2"])
        S.add("sp", lambda e: e.dma_start(out=qs3, in_=projT[0:1024, SEQ:SEQ + NS].rearrange("(h p) q -> p h q", p=128)),
              writes=["qs"], slot="qs")
        S.add("sp", lambda e: e.dma_start(out=knew3, in_=projT[1024:2048, SEQ:SEQ + NS].rearrange("(h p) q -> p h q", p=128)),
              writes=["knew"], slot="knew")
        S.add("sp", lambda e: e.dma_start(out=vnew, in_=v_out[SEQ:SEQ + NS, :]), writes=["vnew"], slot="vnew")
        SA = int(os.environ.get("SA_STAGE", "9"))
        qm = C[:, 1856:1984]
        qm4 = qm.rearrange("p (h m q) -> p h m q", m=2, q=8)
        S.add("dve", lambda e: e.memset(qm, 0.0), writes=["qm"])
        S.add("dve", lambda e: e.tensor_scalar(out=qm4[0:64, :, 0, :], in0=qs3[0:64, :, :], scalar1=0.125, scalar2=None, op0=ALU.mult),
              reads=["qs"], writes=["qm"])
        S.add("dve", lambda e: e.tensor_scalar(out=qm4[64:128, :, 1, :], in0=qs3[64:128, :, :], scalar1=0.125, scalar2=None, op0=ALU.mult),
              reads=["qs"], writes=["qm"])
        accO, accOr = bank[3][:, 0:128], N(bank[3])
        accR, accRr = bank[4][:, 0:128], N(bank[4])
        scb, scbr = bank[2], N(bank[2])

        KV3 = [Y[0], Y[1], X[0]]

        def stA(tok):
            kb_ = tok % 3
            b = tok % 2
            tb = (bank[0], bank[1]) if b == 0 else (bank[5], bank[6])
            S.add("pool", lambda e: e.indirect_dma_start(
                out=KV3[kb_][:, 0:1024], out_offset=None, in_=ck_d[:, :],
                in_offset=bass.IndirectOffsetOnAxis(ap=ptidx2[:, 0:1], axis=0), element_offset=tok * 1024),
                reads=["ptidx2"], writes=[f"KVk{kb_}"], slot=f"KVk{kb_}")
            S.add("pool", lambda e: e.indirect_dma_start(
                out=KV3[kb_][:, 1024:2048], out_offset=None, in_=cv_d[:, :],
                in_offset=bass.IndirectOffsetOnAxis(ap=ptidx2[:, 0:1], axis=0), element_offset=tok * 1024),
                reads=["ptidx2"], writes=[f"KVv{kb_}"], slot=f"KVv{kb_}")
            if SA < 2:
                return

            def tr(half):
                for hh in range(4):
                    h = half * 4 + hh
                    S.add("pe", lambda e, h=h, hh=hh: e.matmul(tb[half][:, hh * 128:(hh + 1) * 128], lhsT=KV3[kb_][:, h * 128:(h + 1) * 128],
                                                               rhs=identS, start=True, stop=True),
                          reads=[f"KVk{kb_}", "sI"], writes=[N(tb[half])])
                if half == 0:
                    S.add("act", lambda e: e.copy(out=KTs[b][:, 0:512], in_=tb[0][:, :]), reads=[N(tb[0])], writes=[f"KT{b}a"])
                else:
                    S.add("dve", lambda e: e.tensor_copy(out=KTs[b][:, 512:1024], in_=tb[1][:, :]), reads=[N(tb[1])], writes=[f"KT{b}b"])

            tr(0)
            tr(1)

        def stB(tok):
            b = tok % 2
            if SA < 3:
                return
            for h in range(8):
                S.add("pe", lambda e, h=h: e.matmul(scb[:, h * 16:(h + 1) * 16],
                                                    lhsT=KTs[b][:, h * 128:(h + 1) * 128],
                                                    rhs=qm[:, h * 16:(h + 1) * 16], start=True, stop=True),
                      reads=[f"KT{b}a" if h < 4 else f"KT{b}b", "qm"], writes=[scbr])
            S.add("dve", lambda e: e.scalar_tensor_tensor(out=Sb[b], in0=slopeRow, scalar=posp[:, tok:tok + 1], in1=scb[:, 0:128],
                                                          op0=ALU.mult, op1=ALU.add),
                  reads=["sC", scbr], writes=[f"Sb{b}"])
            S.add("act", lambda e: e.activation(out=Es[b], in_=Sb[b], func=AF.Exp), reads=[f"Sb{b}"], writes=[f"Es{b}"])

        def stC(tok):
            kb_ = tok % 3
            b = tok % 2
            if SA < 4:
                return
            for h in range(8):
                S.add("pe", lambda e, h=h: e.matmul(accO[:, h * 16:(h + 1) * 16], lhsT=KV3[kb_][:, 1024 + h * 128:1024 + (h + 1) * 128],
                                                    rhs=Es[b][:, h * 16:(h + 1) * 16], start=False, stop=False),
                      reads=[f"KVv{kb_}", f"Es{b}"], writes=[accOr])
            S.add("pe", lambda e: e.matmul(accR, lhsT=onesS, rhs=Es[b], start=(tok == 0), stop=False),
                  reads=["sO", f"Es{b}"], writes=[accRr])

        zerosS = Y[4][:, 1024:1152]
        S.add("dve", lambda e: e.memset(zerosS, 0.0), writes=["zerosS"])
        if SA >= 4:
            S.add("pe", lambda e: e.matmul(accO, lhsT=onesS, rhs=zerosS, start=True, stop=False),
                  reads=["sO", "zerosS"], writes=[accOr])
        stA(0)
        stA(1)
        stB(0)
        for tok in range(128):
            if tok + 2 < 128:
                stA(tok + 2)
            if tok + 1 < 128:
                stB(tok + 1)
            stC(tok)
        if SA < 5:
            return
        for h in range(8):
            S.add("pe", lambda e, h=h: e.matmul(scb[0:NS, h * 16:(h + 1) * 16],
                                                lhsT=knew3[:, h, :],
                                                rhs=qm[:, h * 16:(h + 1) * 16], start=True, stop=True),
                  reads=["knew", "qm"], writes=[scbr])
        S.add("dve", lambda e: e.scalar_tensor_tensor(out=Sb[0][0:NS, :], in0=slopeRow[0:NS, :], scalar=pcol[0:NS, :], in1=scb[0:NS, 0:128],
                                                      op0=ALU.mult, op1=ALU.add),
              reads=["sC", scbr], writes=["Sb0"])
        S.add("act", lambda e: e.activation(out=Es[0][0:NS, :], in_=Sb[0][0:NS, :], func=AF.Exp), reads=["Sb0"], writes=["Es0"])
        S.add("dve", lambda e: e.tensor_tensor(out=Es[0][0:NS, :], in0=Es[0][0:NS, :], in1=maskS[0:NS, :], op=ALU.mult),
              reads=["Es0", "sC"], writes=["Es0"])
        for h in range(8):
            S.add("pe", lambda e, h=h: e.matmul(accO[:, h * 16:(h + 1) * 16], lhsT=vnew[:, h * 128:(h + 1) * 128],
                                                rhs=Es[0][0:NS, h * 16:(h + 1) * 16], start=False, stop=(h == 7)),
                  reads=["vnew", "Es0"], writes=[accOr])
        S.add("pe", lambda e: e.matmul(accR, lhsT=onesS[0:NS, :], rhs=Es[0][0:NS, :], start=False, stop=True),
              reads=["sO", "Es0"], writes=[accRr])
        S.add("dve", lambda e: e.reciprocal(out=Rr, in_=accR), reads=[accRr], writes=["sRr"])
        S.add("dve", lambda e: e.tensor_tensor(out=On, in0=accO, in1=Rr, op=ALU.mult), reads=[accOr, "sRr"], writes=["sOn"])
        On4 = On.rearrange("p (h m q) -> p h m q", m=2, q=8)
        pd3 = pd.rearrange("p (h q) -> p h q", q=8)
        S.add("dve", lambda e: e.scalar_tensor_tensor(out=pd3, in0=On4[:, :, 1, :], scalar=neglam, in1=On4[:, :, 0, :],
                                                      op0=ALU.mult, op1=ALU.add),
              reads=["sOn", "a_small"], writes=["spd"])
        S.add("act", lambda e: e.activation(out=sq, in_=pd, func=AF.Square), reads=["spd"], writes=["ssq"])
        S.add("pe", lambda e: e.matmul(scb[:, 0:64], lhsT=onesS, rhs=sq, start=True, stop=True), reads=["sO", "ssq"], writes=[scbr])
        S.add("dve", lambda e: e.tensor_scalar(out=rstd, in0=scb[:, 0:64], scalar1=1.0 / 128, scalar2=EPS, op0=ALU.mult, op1=ALU.add),
              reads=[scbr], writes=["srstd"])
        S.add("act", lambda e: e.sqrt(out=rstd, in_=rstd), reads=["srstd"], writes=["srstd"])
        S.add("dve", lambda e: e.reciprocal(out=rstd, in_=rstd), reads=["srstd"], writes=["srstd"])
        S.add("dve", lambda e: e.tensor_tensor(out=pd, in0=pd, in1=rstd, op=ALU.mult), reads=["spd", "srstd"], writes=["spd"])
        S.add("dve", lambda e: e.tensor_scalar(out=fins, in0=pd, scalar1=subcol, scalar2=0.8, op0=ALU.mult, op1=ALU.mult),
              reads=["spd", "sC"], writes=["finb0"])
        S.add("sp", lambda e: e.dma_start(out=mixT_d[0:1024, SEQ:SEQ + NS].rearrange("(h p) q -> p h q", p=128),
                                          in_=fins.rearrange("p (h q) -> p h q", q=8)),
              reads=["finb0"], slot="finbo0")

    if not skip_sample:
        sample_attn()

    if not skip_p3:
        S.fence()
        for k2 in ("f2pre", "f2post", "mixpost"):
            S.add("sp", lambda e, k2=k2: e.dma_start(out=wbc[k2][:], in_=wbc2_d[k2]), writes=["wbc" + wres[k2]], slot="wbc" + wres[k2])
        for t in range(NT):
            phase3_tile(t)

    S.final_slots = list(S.dma_cnt.keys())
    S.emit(nc, stack)
    stack.close()
    return nc


def fm_blocks(w, gg):
    K, N = w.shape
    W = 128 * gg
    a = w.reshape(K // 128, 128, N // W, W).transpose(2, 1, 0, 3)
    return np.ascontiguousarray(a).reshape(N // W, 128, (K // 128) * W)


def tm_blocks(w, kgsz):
    K, N = w.shape
    a = w.reshape(K // (128 * kgsz), kgsz, 128, N // 512, 512).transpose(3, 0, 2, 1, 4)
    return np.ascontiguousarray(a).reshape(N // 512, K // (128 * kgsz), 128, kgsz * 512)


def gconst_host():
    i = np.arange(128)
    ident = np.eye(128, dtype=np.float32)
    U = (i[:, None] <= i[None, :]).astype(np.float32)
    mL = -(i[:, None] > i[None, :]).astype(np.float32)
    mU = -(i[None, :] > i[:, None]).astype(np.float32)
    ones = np.ones((128, 128), np.float32)
    mUd = (i[None, :] >= i[:, None]).astype(np.float32)
    return np.ascontiguousarray(np.concatenate([ident, U, mL, mU, ones, mUd], axis=1))


def hb_host(a_log, dt_bias, dnw=None):
    hb = np.zeros((128, 32), np.float32)
    hb[:, 0:8] = a_log[None, :]
    hb[:, 8:16] = dt_bias[None, :]
    hb[:NS, 16] = 1.0
    if dnw is not None:
        hb[:, 17] = dnw
    return hb


def aconst_host(lq1, lk1, lq2, lk2, subln):
    a = np.zeros((128, 2048 + 256 + 128 + 16), np.float32)
    a[0, 0:2048] = np.arange(2048, dtype=np.float32)
    a[:, 2432:2448] = 128.0 * np.arange(16, dtype=np.float32)[None, :] + np.arange(128, dtype=np.float32)[:, None]
    a[:, 2048:2112] = lq1[None]
    a[:, 2112:2176] = lk1[None]
    a[:, 2176:2240] = lq2[None]
    a[:, 2240:2304] = lk2[None]
    a[:, 2304:2432] = subln[None]
    return a


def sconst_host(subln):
    a = np.zeros((128, 512), np.float32)
    pp = np.arange(128, dtype=np.float32)
    a[:, 0:128] = 128.0 * pp[:, None] + pp[None, :] - 16384.0
    col = np.arange(128)
    a[:, 128:256] = (2.0 ** (-(col // 16 + 1).astype(np.float32)))[None, :]
    a[:, 256:384] = ((col % 8)[None, :] >= np.arange(128)[:, None]).astype(np.float32)
    a[:, 384] = pp
    a[:, 385] = subln
    return a


_NC = None


def kernel(**inp):
    import ml_dtypes
    global _NC
    if _NC is None:
        _NC = build()
    nc = _NC
    f = lambda k: np.asarray(inp[k], dtype=np.float32)
    xp, xs = f("x_prompt"), f("x_sample")
    w_in = f("w_in")[0]
    fm_cols = np.concatenate([w_in[:, 0:2048], w_in[:, 3072:7168]], axis=1)
    shared = {
        "ident": np.eye(128, dtype=np.float32).astype(ml_dtypes.bfloat16),
        "wbc_f1pre": np.ascontiguousarray(np.broadcast_to(f("ffn1_pre_w")[0], (128, D))),
        "wbc_f1post": np.ascontiguousarray(np.broadcast_to(f("ffn1_post_w")[0], (128, D))),
        "wbc_mixpre": np.ascontiguousarray(np.broadcast_to(f("mix_pre_w")[0], (128, D))),
        "wg1": fm_blocks(f("ffn1_gate")[0], 2),
        "wu1": fm_blocks(f("ffn1_up")[0], 2),
        "wd1": tm_blocks(f("ffn1_down")[0], 11),
        "win_fm": fm_blocks(fm_cols, 2),
        "win_v": tm_blocks(w_in[:, 2048:3072], 4),
        "win_ba": np.ascontiguousarray(w_in[:, 7168:7184].reshape(KC, 128, 16).transpose(1, 0, 2)).reshape(128, KC * 16),
        "gconst": gconst_host(),
        "cw_d": np.ascontiguousarray(f("conv_w")[0].T.reshape(24, 128, 4).transpose(1, 0, 2)).reshape(128, 96),
        "hb_d": hb_host(f("a_log")[0], f("dt_bias")[0], f("delta_norm_w")[0]),
        "aconst": aconst_host(f("lambda_q1")[0], f("lambda_k1")[0], f("lambda_q2")[0], f("lambda_k2")[0], f("subln_w")[0]),
        "wbc_f2pre": np.ascontiguousarray(np.broadcast_to(f("ffn2_pre_w")[0], (128, D))),
        "wbc_f2post": np.ascontiguousarray(np.broadcast_to(f("ffn2_post_w")[0], (128, D))),
        "wbc_mixpost": np.ascontiguousarray(np.broadcast_to(f("mix_post_w")[0], (128, D))),
        "wg2": fm_blocks(f("ffn2_gate")[0], 2),
        "wu2": fm_blocks(f("ffn2_up")[0], 2),
        "wd2": tm_blocks(f("ffn2_down")[0], 11),
        "wout": tm_blocks(f("w_out")[0], 4),
        "cache_k": f("cache_k")[0].reshape(-1, 1024),
        "cache_v": f("cache_v")[0].reshape(-1, 1024),
        "sconst": sconst_host(f("subln_w")[0]),
    }
    page_table = np.asarray(inp["page_table"]).astype(np.int32)
    ssm0 = f("state_ssm")[0]
    convst = f("state_conv")[0]
    in_maps = []
    for c in range(NCORES):
        m = dict(shared)
        m["xin"] = np.ascontiguousarray(np.concatenate([xp[c % 4], xs[c]], axis=0))
        m["ssm0"] = np.ascontiguousarray(ssm0[c])
        m["convst"] = np.ascontiguousarray(convst[c].T.reshape(24, 128, 3))
        m["pt"] = np.ascontiguousarray(page_table[c].reshape(128, 1))
        in_maps.append(m)
    res = run_bass_kernel_spmd(nc, in_maps, core_ids=list(range(NCORES)))
    R = [{k: np.asarray(v) for k, v in r.items()} for r in res.results]
    B = 4
    k_prompt = np.stack([R[b]["projT"][1024:2048, :SEQ].T.reshape(SEQ, 8, 128) for b in range(B)])[None]
    v_prompt = np.stack([R[b]["v_out"][:SEQ].reshape(SEQ, 8, 128) for b in range(B)])[None]
    conv_prompt = np.stack([R[b]["projT"][2048:5120, SEQ - 3:SEQ].T for b in range(B)])[None]
    k_sample = np.stack([R[c]["projT"][1024:2048, SEQ:].T.reshape(NS, 8, 128) for c in range(8)])[None]
    v_sample = np.stack([R[c]["v_out"][SEQ:].reshape(NS, 8, 128) for c in range(8)])[None]
    conv_sample = np.stack([R[c]["projT"][2048:5120, NROW - 3:NROW].T for c in range(8)])[None]
    y_prompt = np.stack([R[b]["y"][:SEQ] for b in range(B)])
    y_sample = np.stack([R[c]["y"][SEQ:] for c in range(8)])
    ssm_prompt = np.stack([R[b]["ssm_p"] for b in range(B)])[None]
    ssm_sample = np.stack([R[c]["ssm_s"] for c in range(8)])[None]
    out = (y_prompt, y_sample, k_prompt, v_prompt, ssm_prompt, conv_prompt,
           k_sample, v_sample, ssm_sample, conv_sample)
    return tuple(np.ascontiguousarray(o, dtype=np.float32) for o in out)
```

```python
import os
from contextlib import ExitStack
import numpy as np
import concourse.bass as bass
import concourse.mybir as mybir
from concourse.bass_utils import run_bass_kernel_spmd

F32 = mybir.dt.float32
BF16 = mybir.dt.bfloat16
I32 = mybir.dt.int32
AF = mybir.ActivationFunctionType
ALU = mybir.AluOpType

D = 2048
DFF = 5632
SEQ = 2048
NS = 8
NROW = SEQ + NS
TW = 512
NT = SEQ // TW
KC = D // 128
FC = DFF // 128
INW = 7184
EPS = 1e-6
NCORES = 8


class Sched:
    ENG = ("pe", "act", "dve", "pool", "sp")

    def __init__(self):
        self.ops = {e: [] for e in self.ENG}
        self.lastw = {}
        self.readers = {}
        self.dma_cnt = {}
        self.waited = {e: {} for e in self.ENG}

    def add(self, eng, fn, reads=(), writes=(), slot=None):
        op = dict(eng=eng, fn=fn, waits=[], flag=False, slot=slot, idx=len(self.ops[eng]))
        if slot is not None:
            self.dma_cnt[slot] = self.dma_cnt.get(slot, 0) + 1
            ev = ("dma", slot, self.dma_cnt[slot], op)
        else:
            ev = ("eng", eng, op["idx"], op)
        deps = []
        for r in reads:
            if r in self.lastw:
                deps.append(self.lastw[r])
        for w in writes:
            if w in self.lastw:
                deps.append(self.lastw[w])
            deps.extend(self.readers.get(w, []))
        best = {}
        for d in deps:
            k = (d[0], d[1])
            if k not in best or best[k][2] < d[2]:
                best[k] = d
        for d in best.values():
            kind, key, n, dop = d
            if kind == "eng" and key == eng and eng == "pe":
                continue
            k = (kind, key)
            if self.waited[eng].get(k, -1) >= n:
                continue
            self.waited[eng][k] = n
            op["waits"].append(d)
            if kind == "eng":
                dop["flag"] = True
        for r in reads:
            self.readers.setdefault(r, []).append(ev)
        for w in writes:
            self.lastw[w] = ev
            self.readers[w] = []
        self.ops[eng].append(op)
        return ev

    def fence(self):
        evs = []
        for e in self.ENG:
            for op in reversed(self.ops[e]):
                if op["fn"] is not None and op["slot"] is None:
                    evs.append(("eng", e, op["idx"], op))
                    break
        for sl, n in self.dma_cnt.items():
            evs.append(("dma", sl, n, None))
        for e in self.ENG:
            op = dict(eng=e, fn=None, waits=[], flag=False, slot=None, idx=len(self.ops[e]))
            for d in evs:
                kind, key, n, dop = d
                if kind == "eng" and key == e:
                    continue
                k = (kind, key)
                if self.waited[e].get(k, -1) >= n:
                    continue
                self.waited[e][k] = n
                op["waits"].append(d)
                if kind == "eng":
                    dop["flag"] = True
            self.ops[e].append(op)

    def emit(self, nc, stack):
        esem = {e: stack.enter_context(nc.semaphore("sem_" + e)) for e in self.ENG}
        ssem = {s: stack.enter_context(nc.semaphore("slot_" + str(s))) for s in self.dma_cnt}
        for e in self.ENG:
            c = 0
            for op in self.ops[e]:
                if op["flag"]:
                    c += 1
                op["cnt"] = c
        block = stack.enter_context(nc.Block())

        def run(e):
            def body(engine):
                for op in self.ops[e]:
                    for kind, key, n, dop in op["waits"]:
                        if kind == "eng":
                            engine.wait_ge(esem[key], dop["cnt"])
                        else:
                            engine.wait_ge(ssem[key], 16 * n)
                    if op["fn"] is None:
                        continue
                    ins = op["fn"](engine)
                    if op["slot"] is not None:
                        ins.then_inc(ssem[op["slot"]], 16)
                    elif op["flag"]:
                        ins.then_inc(esem[e], 1)
            return body

        def run_sp(engine):
            run("sp")(engine)
            for sl in getattr(self, "final_slots", []):
                if sl in ssem:
                    engine.wait_ge(ssem[sl], 16 * self.dma_cnt[sl])

        block.tensor(run("pe"))
        block.scalar(run("act"))
        block.vector(run("dve"))
        block.gpsimd(run("pool"))
        block.sync(run_sp)


def subtiles(t):
    st = [(t * TW + 128 * s, 128, 128 * s) for s in range(TW // 128)]
    if t == NT - 1:
        st.append((SEQ, NS, TW))
    return st


def passes(t):
    p = [(0, TW)]
    if t == NT - 1:
        p.append((TW, NS))
    return p


def build(gdn_only=False, attn_only=False, skip_p3=False, npool=1280, aheads=8, skip_sample=False):
    nc = bass.Bass("TRN2", target_bir_lowering=False)
    S = Sched()
    stack = ExitStack()

    def din(name, shape, dt=F32):
        return nc.dram_tensor(name, shape, dt, kind="ExternalInput").ap()

    def dout(name, shape, dt=F32):
        return nc.dram_tensor(name, shape, dt, kind="ExternalOutput").ap()

    xin = din("xin", [NROW, D])
    ident_d = din("ident", [128, 128], BF16)
    wbc_d = {k: din("wbc_" + k, [128, D]) for k in ("f1pre", "f1post", "mixpre")}
    GG = 2
    wg1 = din("wg1", [FC // GG, 128, KC * 128 * GG])
    wu1 = din("wu1", [FC // GG, 128, KC * 128 * GG])
    KGD = 11
    wd1 = din("wd1", [4, FC // KGD, 128, KGD * 512])
    NFM = 48
    win_fm = din("win_fm", [NFM // GG, 128, KC * 128 * GG])
    KGV = 4
    win_v = din("win_v", [2, KC // KGV, 128, KGV * 512])

    win_ba = din("win_ba", [128, KC * 16])
    gconst = din("gconst", [128, 6 * 128])
    cw_d = din("cw_d", [128, 96])
    hb_d = din("hb_d", [128, 32])
    ssm0_d = din("ssm0", [8, 128, 128])
    convst_d = din("convst", [24, 128, 3])
    ba_d = (din if gdn_only else dout)("ba_d", [NROW, 16])
    aconst = din("aconst", [128, 2048 + 256 + 128 + 16])
    mixT_d = dout("mixT_d", [2048, NROW], BF16)
    ssm_p = dout("ssm_p", [8, 128, 128])
    ssm_s = dout("ssm_s", [8, 128, 128])
    projT = (din if gdn_only else dout)("projT", [NFM * 128, NROW])
    v_out = (din if gdn_only else dout)("v_out", [NROW, 1024])
    x1_d = dout("x1", [NROW, D])
    x2_d = dout("x2", [NROW, D])
    y_d = dout("y", [NROW, D])
    wbc2_d = {k: din("wbc_" + k, [128, D]) for k in ("f2pre", "f2post", "mixpost")}
    wg2 = din("wg2", [FC // GG, 128, KC * 128 * GG])
    wu2 = din("wu2", [FC // GG, 128, KC * 128 * GG])
    wd2 = din("wd2", [4, FC // KGD, 128, KGD * 512])
    wout_d = din("wout", [4, KC // KGV, 128, KGV * 512])
    ck_d = din("cache_k", [npool * 128, 1024])
    cv_d = din("cache_v", [npool * 128, 1024])
    pt_d = din("pt", [128, 1], I32)
    sconst = din("sconst", [128, 512])

    nm = {}

    def sb(name, shape, dt=F32):
        t_ = stack.enter_context(nc.sbuf_tensor(name, shape, dt))
        nm[id(t_)] = name
        return t_

    def ps(name, shape, dt=F32):
        t_ = stack.enter_context(nc.psum_tensor(name, shape, dt))
        nm[id(t_)] = name
        return t_

    def N(t_):
        return nm[id(t_)]

    TWX = TW + NS
    ident = sb("ident_sb", [128, 128], BF16)
    wbc = {k: sb("wbc_sb_" + k, [128, D]) for k in wbc_d}
    X = [sb(f"X{i}", [128, D]) for i in range(2)]
    Y = [sb(f"Y{i}", [128, D]) for i in range(5)]
    XN = sb("XN", [128, D], BF16)
    xnT = sb("xnT", [128, KC * TWX], BF16)
    hT = sb("hT", [128, FC * TWX], BF16)
    sm = sb("small", [128, 64])
    wgS = [sb(f"wgS{i}", [128, KC * 128 * GG], BF16) for i in range(2)]
    wuS = [sb(f"wuS{i}", [128, KC * 128 * GG], BF16) for i in range(2)]
    wdS = [sb(f"wdS{i}", [128, KGD * 512], BF16) for i in range(2)]
    stg = [sb(f"stg{i}", [128, TWX]) for i in range(2)]
    wba = sb("wba", [128, KC * 16], BF16)
    finb = [sb(f"finb{i}", [128, 128], BF16) for i in range(2)]
    ptidx = sb("ptidx", [128, 1], I32)
    ptidx2 = sb("ptidx2", [128, 1], I32)
    tmpg = stg
    bank = [ps(f"bank{i}", [128, 512]) for i in range(7)]
    psT = ps("psT", [128, 1024], BF16)

    xnT3 = xnT[:].rearrange("p (c t) -> p c t", t=TWX)
    hT3 = hT[:].rearrange("p (c t) -> p c t", t=TWX)
    psT3 = psT[:].rearrange("p (c t) -> p c t", t=128)

    cnt = {"x": 0, "g": 0, "d": 0, "stg": 0, "tmp": 0, "evac": 0}

    S.add("pool", lambda e: e.dma_start(out=wba[:], in_=win_ba), writes=["wba"], slot="wba")
    S.add("sp", lambda e: e.dma_start(out=ident[:], in_=ident_d), writes=["ident"], slot="ident")
    for k in wbc:
        S.add("sp", lambda e, k=k: e.dma_start(out=wbc[k][:], in_=wbc_d[k]), writes=["wbc" + k], slot="wbc" + k)

    def load_x(src, row0, n, extra_reads=()):
        i = cnt["x"] % 2
        cnt["x"] += 1
        S.add("sp", lambda e: e.dma_start(out=X[i][:n, :], in_=src[row0:row0 + n, :]),
              reads=list(extra_reads), writes=[f"X{i}"], slot=f"X{i}")
        return i

    def rstd_of(src_ap, n, src_res, col):
        S.add("act", lambda e: e.activation(out=XN[:n, :], in_=src_ap, func=AF.Square,
                                            accum_out=sm[:n, col:col + 1]),
              reads=src_res, writes=["XN", f"sm{col}"])
        S.add("dve", lambda e: e.tensor_scalar(out=sm[:n, col + 1:col + 2], in0=sm[:n, col:col + 1],
                                               scalar1=1.0 / D, scalar2=EPS, op0=ALU.mult, op1=ALU.add),
              reads=[f"sm{col}"], writes=[f"sm{col + 1}"])
        S.add("act", lambda e: e.sqrt(out=sm[:n, col + 2:col + 3], in_=sm[:n, col + 1:col + 2]),
              reads=[f"sm{col + 1}"], writes=[f"sm{col + 2}"])
        S.add("dve", lambda e: e.reciprocal(out=sm[:n, col + 3:col + 4], in_=sm[:n, col + 2:col + 3]),
              reads=[f"sm{col + 2}"], writes=[f"sm{col + 3}"])
        return sm[:n, col + 3:col + 4], f"sm{col + 3}"

    def norm_T(src_ap, src_res, n, col, wkey):
        rs, rres = rstd_of(src_ap, n, src_res, 0)
        S.add("dve", lambda e: e.scalar_tensor_tensor(out=XN[:n, :], in0=src_ap, scalar=rs,
                                                      in1=wbc[wkey][:n, :], op0=ALU.mult, op1=ALU.mult),
              reads=src_res + [rres, "wbc" + wres[wkey]], writes=["XN"])
        for half in range(2):
            for j in range(8):
                c = half * 8 + j
                S.add("pe", lambda e, c=c, j=j: e.transpose(out=psT3[:, j, :n], in_=XN[:n, c * 128:(c + 1) * 128],
                                                            identity=ident[:n, :n]),
                      reads=["XN", "ident"], writes=["psT"])
            eng = "act" if half == 0 else "dve"
            if eng == "act":
                S.add("act", lambda e, half=half: e.copy(out=xnT3[:, half * 8:half * 8 + 8, col:col + n],
                                                         in_=psT3[:, :, :n]),
                      reads=["psT"], writes=["xnT"])
            else:
                S.add("dve", lambda e, half=half: e.tensor_copy(out=xnT3[:, half * 8:half * 8 + 8, col:col + n],
                                                                in_=psT3[:, :, :n]),
                      reads=["psT"], writes=["xnT"])

    def linear_fm(t, wsrc, ngroups, stages, skey, evac):
        for g in range(ngroups):
            i = cnt[skey] % 2
            cnt[skey] += 1
            for (arr, st) in zip(wsrc, stages):
                S.add("pool", lambda e, arr=arr, st=st, g=g, i=i: e.dma_start(out=st[i][:], in_=arr[g]),
                      writes=[N(st[i])], slot=N(st[i]))
            for gi in range(GG):
                oc = g * GG + gi
                for (c0, n) in passes(t):
                    outs = []
                    for wi, st in enumerate(stages):
                        b = bank[(cnt["evac"] % 2) * len(stages) + wi]
                        st3 = st[i][:].rearrange("p (k f) -> p k f", f=128 * GG)
                        for kc in range(KC):
                            S.add("pe", lambda e, b=b, st3=st3, kc=kc, gi=gi, c0=c0, n=n: e.matmul(
                                b[:, :n], lhsT=st3[:, kc, gi * 128:(gi + 1) * 128], rhs=xnT3[:, kc, c0:c0 + n],
                                start=(kc == 0), stop=(kc == KC - 1)),
                                reads=[N(st[i]), "xnT"], writes=[N(b)])
                        outs.append(b)
                    cnt["evac"] += 1
                    evac(oc, c0, n, outs)

    def linear_tm(t, wsrc, nn, nkg, kgsz, stages, skey, lhs3, lres, evac):
        sts = subtiles(t)
        for nt_ in range(nn):
            for kg in range(nkg):
                i = cnt[skey] % 2
                cnt[skey] += 1
                S.add("pool", lambda e, nt_=nt_, kg=kg, i=i: e.dma_start(out=stages[i][:, :kgsz * 512],
                                                                         in_=wsrc[nt_, kg]),
                      writes=[N(stages[i])], slot=N(stages[i]))
                st3 = stages[i][:].rearrange("p (k f) -> p k f", f=512)
                for kk in range(kgsz):
                    k = kg * kgsz + kk
                    for si, (row0, n, col) in enumerate(sts):
                        b = bank[2 + si]
                        S.add("pe", lambda e, b=b, st3=st3, kk=kk, k=k, n=n, col=col: e.matmul(
                            b[:n, :], lhsT=lhs3[:, k, col:col + n], rhs=st3[:, kk, :],
                            start=(k == 0), stop=(k == nkg * kgsz - 1)),
                            reads=[N(stages[i]), lres], writes=[N(b)])
            for si, (row0, n, col) in enumerate(sts):
                evac(nt_, si, row0, n, bank[2 + si])

    wres = {"f1pre": "f1pre", "f1post": "f1post", "mixpre": "mixpre",
            "f2pre": "f1pre", "f2post": "f1post", "mixpost": "mixpre"}
    for k2, k1 in list(wres.items()):
        wbc[k2] = wbc[k1]

    def ffn_block(t, wg, wu, wd, postkey, res_src, after):
        sts = subtiles(t)

        def evac_gu(oc, c0, n, outs):
            j = cnt["stg"] % 2
            cnt["stg"] += 1
            S.add("act", lambda e: e.activation(out=tmpg[j][:, :n], in_=outs[0][:, :n], func=AF.Silu),
                  reads=[N(outs[0])], writes=[f"stg{j}"])
            S.add("dve", lambda e: e.tensor_tensor(out=hT3[:, oc, c0:c0 + n], in0=tmpg[j][:, :n],
                                                   in1=outs[1][:, :n], op=ALU.mult),
                  reads=[f"stg{j}", N(outs[1])], writes=["lhs"])

        linear_fm(t, [wg, wu], FC // GG, [wgS, wuS], "g", evac_gu)

        def evac_d(nt_, si, row0, n, b):
            S.add("act", lambda e: e.copy(out=Y[si][:n, nt_ * 512:(nt_ + 1) * 512], in_=b[:n, :]),
                  reads=[N(b)], writes=[f"Y{si}"])

        linear_tm(t, wd, 4, FC // KGD, KGD, wdS, "d", hT3, "lhs", evac_d)

        for si, (row0, n, col) in enumerate(sts):
            post_res(si, row0, n, postkey, res_src, 0.5)
            after(si, row0, n, col)

    def post_res(si, row0, n, postkey, res_src, alpha):
        rs, rres = rstd_of(Y[si][:n, :], n, [f"Y{si}"], 8)
        S.add("dve", lambda e: e.scalar_tensor_tensor(
            out=Y[si][:n, :], in0=Y[si][:n, :], scalar=rs, in1=wbc[postkey][:n, :],
            op0=ALU.mult, op1=ALU.mult),
            reads=[f"Y{si}", rres, "wbc" + wres[postkey]], writes=[f"Y{si}"])
        xi = load_x(res_src, row0, n, extra_reads=([f"x2dram{si}"] if res_src is x2_d else []))
        S.add("dve", lambda e: e.scalar_tensor_tensor(
            out=Y[si][:n, :], in0=Y[si][:n, :], scalar=alpha, in1=X[xi][:n, :],
            op0=ALU.mult, op1=ALU.add),
            reads=[f"Y{si}", f"X{xi}"], writes=[f"Y{si}"])

    def evac_d_plain(nt_, si, row0, n, b):
        S.add("act", lambda e: e.copy(out=Y[si][:n, nt_ * 512:(nt_ + 1) * 512], in_=b[:n, :]),
              reads=[N(b)], writes=[f"Y{si}"])

    def phase1_tile(t):
        sts = subtiles(t)
        for (row0, n, col) in sts:
            xi = load_x(xin, row0, n)
            norm_T(X[xi][:n, :], [f"X{xi}"], n, col, "f1pre")

        def after1(si, row0, n, col):
            S.add("sp", lambda e: e.dma_start(out=x1_d[row0:row0 + n, :], in_=Y[si][:n, :]),
                  reads=[f"Y{si}"], slot=f"x1o{si}")
            norm_T(Y[si][:n, :], [f"Y{si}"], n, col, "mixpre")

        ffn_block(t, wg1, wu1, wd1, "f1post", xin, after1)

        def evac_in(oc, c0, n, outs):
            j = cnt["stg"] % 2
            cnt["stg"] += 1
            S.add("act", lambda e: e.copy(out=stg[j][:, :n], in_=outs[0][:, :n]),
                  reads=[N(outs[0])], writes=[f"stg{j}"])
            gcol = (t * TW + c0) if c0 < TW else SEQ
            S.add("sp", lambda e: e.dma_start(out=projT[oc * 128:(oc + 1) * 128, gcol:gcol + n], in_=stg[j][:, :n]),
                  reads=[f"stg{j}"], slot=f"stgo{j}")

        linear_fm(t, [win_fm], NFM // GG, [wgS], "g", evac_in)

        def evac_v(nt_, si, row0, n, b):
            j = cnt["stg"] % 2
            cnt["stg"] += 1
            S.add("act", lambda e: e.copy(out=stg[j][:n, :512], in_=b[:n, :]),
                  reads=[N(b)], writes=[f"stg{j}"])
            S.add("sp", lambda e: e.dma_start(out=v_out[row0:row0 + n, nt_ * 512:(nt_ + 1) * 512], in_=stg[j][:n, :512]),
                  reads=[f"stg{j}"], slot=f"stgo{j}")

        linear_tm(t, win_v, 2, KC // KGV, KGV, wdS, "d", xnT3, "xnT", evac_v)

        wba3 = wba[:].rearrange("p (k f) -> p k f", f=16)

        def ba_sub(si, row0, n, col):
            b = bank[2 + si]
            for kc in range(KC):
                S.add("pe", lambda e, kc=kc: e.matmul(
                    b[:n, :16], lhsT=xnT3[:, kc, col:col + n], rhs=wba3[:, kc, :],
                    start=(kc == 0), stop=(kc == KC - 1)),
                    reads=["wba", "xnT"], writes=[N(b)])
            j = cnt["stg"] % 2
            cnt["stg"] += 1
            S.add("act", lambda e: e.copy(out=stg[j][:n, :16], in_=b[:n, :16]),
                  reads=[N(b)], writes=[f"stg{j}"])
            S.add("sp", lambda e: e.dma_start(out=ba_d[row0:row0 + n, :], in_=stg[j][:n, :16]),
                  reads=[f"stg{j}"], slot=f"stgo{j}")

        for si, (row0, n, col) in enumerate(sts):
            ba_sub(si, row0, n, col)

    def phase3_tile(t):
        sts = subtiles(t)
        ncols = TW + (NS if t == NT - 1 else 0)
        S.add("sp", lambda e: e.dma_start(out=xnT3[:, :, 0:ncols],
                                          in_=mixT_d[:, t * TW:t * TW + ncols].rearrange("(k p) n -> p k n", p=128)),
              writes=["xnT"], slot="xnTld")
        linear_tm(t, wout_d, 4, KC // KGV, KGV, wdS, "d", xnT3, "xnT", evac_d_plain)

        def mid(si, row0, n, col):
            post_res(si, row0, n, "mixpost", x1_d, 1.0)
            S.add("sp", lambda e: e.dma_start(out=x2_d[row0:row0 + n, :], in_=Y[si][:n, :]),
                  reads=[f"Y{si}"], writes=[f"x2dram{si}"], slot=f"x2o{si}")
            norm_T(Y[si][:n, :], [f"Y{si}"], n, col, "f2pre")

        for si, (row0, n, col) in enumerate(sts):
            mid(si, row0, n, col)

        def after3(si, row0, n, col):
            S.add("sp", lambda e: e.dma_start(out=y_d[row0:row0 + n, :], in_=Y[si][:n, :]),
                  reads=[f"Y{si}"], slot=f"yo{si}")

        ffn_block(t, wg2, wu2, wd2, "f2post", x2_d, after3)

    for t in range(0 if gdn_only else NT):
        phase1_tile(t)


    S.fence()
    pool_cols = {}
    pool_next = [0]

    def galloc(name, ncols_req):
        ncols = ((ncols_req + 31) // 32) * 32
        off = pool_next[0]
        bi, co = off // D, off % D
        if co + ncols > D:
            bi, co = bi + 1, 0
            off = bi * D
        pool_next[0] = off + ncols
        assert bi < 5, "gdn scratch overflow"
        return Y[bi][:, co:co + ncols_req]

    Sst = [galloc(f"S{h}", 128) for h in range(8)]
    gc = galloc("gconst", 768)
    identF, Umat, maskLn, maskUn, onesF, maskUd = [gc[:, i * 128:(i + 1) * 128] for i in range(6)]
    cw = galloc("cw", 96)
    hb = galloc("hb", 32)
    chk = galloc("chk", 64)
    S.add("sp", lambda e: e.dma_start(out=gc, in_=gconst), writes=["gconst"], slot="gconst")
    S.add("sp", lambda e: e.dma_start(out=cw, in_=cw_d), writes=["cw"], slot="cw")
    S.add("sp", lambda e: e.dma_start(out=hb, in_=hb_d), writes=["hb"], slot="hb")
    S.add("act", lambda e: e.activation(out=hb[:, 24:32], in_=hb[:, 0:8], func=AF.Exp), reads=["hb"], writes=["hbA"])
    S.add("dve", lambda e: e.tensor_scalar(out=hb[:, 24:32], in0=hb[:, 24:32], scalar1=-1.0, scalar2=None, op0=ALU.mult),
          reads=["hbA"], writes=["hbA"])
    for h in range(8):
        S.add("dve", lambda e, h=h: e.memset(Sst[h], 0.0), writes=[f"S{h}"])

    US = []
    for p in range(2):
        d_ = {}
        for nme, w_ in (("kxp", 131), ("vxp", 131), ("kc", 128), ("vc", 128), ("t1", 128), ("t2", 128),
                        ("kn", 128), ("ktm", 128), ("vb", 128), ("Gbc", 128), ("rep", 128), ("kbT", 128),
                        ("QT", 128), ("Q", 128), ("R", 128), ("kg", 128), ("nwT", 128), ("vn", 128),
                        ("kd", 128), ("sc", 16),
                        ("qxp", 131), ("qc", 128), ("qg", 128), ("qkT", 128), ("eG", 128), ("t3", 128),
                        ("zs", 128), ("on", 128)):
            d_[nme] = galloc(f"{nme}{p}", w_)
        US.append(d_)

    def pb(i):
        return bank[i % 7][:, 0:128], N(bank[i % 7])

    unit_no = [0]
    KSTAGE = int(os.environ.get('K_STAGE', '99'))
    KCHUNKS = int(os.environ.get('K_CHUNKS', str(SEQ // 128)))
    KSAMPLE = int(os.environ.get('K_SAMPLE', '1'))
    KCHUNKLVL = int(os.environ.get('K_CHUNKLVL', '1'))

    def gdn_chunk(tok0, nvalid, hist_src, sample):
        if not KCHUNKLVL:
            return
        nt_ = min(128, nvalid)
        if nvalid < 128:
            S.add("dve", lambda e: e.memset(chk[:, 0:16], 0.0), writes=["chkBA"])
        S.add("sp", lambda e: e.dma_start(out=chk[:nt_, 0:16], in_=ba_d[tok0:tok0 + nt_, :]),
              writes=["chkBA"], slot="chkBA")
        S.add("act", lambda e: e.activation(out=chk[:, 16:24], in_=chk[:, 0:8], func=AF.Sigmoid),
              reads=["chkBA"], writes=["chkbeta"])
        S.add("dve", lambda e: e.tensor_tensor(out=chk[:, 24:32], in0=chk[:, 8:16], in1=hb[:, 8:16], op=ALU.add),
              reads=["chkBA", "hb"], writes=["chktmp"])
        S.add("dve", lambda e: e.tensor_scalar(out=chk[:, 24:32], in0=chk[:, 24:32], scalar1=30.0, scalar2=None, op0=ALU.min),
              reads=["chktmp"], writes=["chktmp"])
        S.add("act", lambda e: e.activation(out=chk[:, 24:32], in_=chk[:, 24:32], func=AF.Exp),
              reads=["chktmp"], writes=["chktmp"])
        S.add("act", lambda e: e.activation(out=chk[:, 24:32], in_=chk[:, 24:32], func=AF.Ln, bias=1.0),
              reads=["chktmp"], writes=["chktmp"])
        S.add("dve", lambda e: e.tensor_tensor(out=chk[:, 32:40], in0=chk[:, 24:32], in1=hb[:, 24:32], op=ALU.mult),
              reads=["chktmp", "hbA"], writes=["chkg"])
        if nvalid < 128:
            S.add("dve", lambda e: e.tensor_scalar(out=chk[:, 32:40], in0=chk[:, 32:40], scalar1=hb[:, 16:17], scalar2=None, op0=ALU.mult),
                  reads=["chkg", "hb"], writes=["chkg"])
            S.add("dve", lambda e: e.tensor_scalar(out=chk[:, 16:24], in0=chk[:, 16:24], scalar1=hb[:, 16:17], scalar2=None, op0=ALU.mult),
                  reads=["chkbeta", "hb"], writes=["chkbeta"])
        pg, pgr = pb(27)
        S.add("pe", lambda e: e.matmul(pg[:, :8], lhsT=Umat, rhs=chk[:, 32:40], start=True, stop=True),
              reads=["gconst", "chkg"], writes=[pgr])
        S.add("act", lambda e: e.copy(out=chk[:, 40:48], in_=pg[:, :8]), reads=[pgr], writes=["chkG"])

        def unit(h):
            p = unit_no[0] % 2
            unit_no[0] += 1
            u = US[p]
            R_ = lambda nme: f"{nme}{p}"
            def P(i):
                bk = bank[3 * p + i % 3]
                return bk[:, (i // 3) * 128:(i // 3 + 1) * 128], f"{N(bk)}r{i // 3}"
            krow = 2048 + 1024 + 128 * h
            vrow = 2048 + 2048 + 128 * h
            qrow = 2048 + 128 * h
            for (dst, row, part, nme) in ((u["kxp"], krow, 8 + h, "kxp"), (u["vxp"], vrow, 16 + h, "vxp"), (u["qxp"], qrow, h, "qxp")):
                if hist_src is None:
                    S.add("dve", lambda e, dst=dst: e.memset(dst[:, 0:3], 0.0), writes=[R_(nme)])
                elif hist_src == "state":
                    S.add("dve", lambda e, dst=dst: e.memset(dst[:, :], 0.0), writes=[R_(nme)])
                    S.add("sp", lambda e, dst=dst, part=part: e.dma_start(out=dst[:, 0:3], in_=convst_d[part]),
                          writes=[R_(nme)], slot=R_(nme) + "h")
                if hist_src == "prev":
                    S.add("sp", lambda e, dst=dst, row=row: e.dma_start(out=dst[:, 0:131], in_=projT[row:row + 128, tok0 - 3:tok0 + 128]),
                          writes=[R_(nme)], slot=R_(nme))
                else:
                    S.add("sp", lambda e, dst=dst, row=row: e.dma_start(out=dst[:, 3:3 + nt_], in_=projT[row:row + 128, tok0:tok0 + nt_]),
                          writes=[R_(nme)], slot=R_(nme))
            if KSTAGE < 1:
                return
            for (src, dst, part, sn, dn) in ((u["kxp"], u["kc"], 8 + h, "kxp", "kc"), (u["vxp"], u["vc"], 16 + h, "vxp", "vc"), (u["qxp"], u["qc"], h, "qxp", "qc")):
                S.add("dve", lambda e, src=src, dst=dst, part=part: e.tensor_scalar(
                    out=dst, in0=src[:, 0:128], scalar1=cw[:, part * 4:part * 4 + 1], scalar2=None, op0=ALU.mult),
                    reads=[R_(sn), "cw"], writes=[R_(dn)])
                for j in range(1, 4):
                    S.add("dve", lambda e, src=src, dst=dst, part=part, j=j: e.scalar_tensor_tensor(
                        out=dst, in0=src[:, j:j + 128], scalar=cw[:, part * 4 + j:part * 4 + j + 1], in1=dst,
                        op0=ALU.mult, op1=ALU.add),
                        reads=[R_(sn), "cw", R_(dn)], writes=[R_(dn)])
                S.add("act", lambda e, dst=dst: e.activation(out=dst, in_=dst, func=AF.Silu),
                      reads=[R_(dn)], writes=[R_(dn)])
            if KSTAGE < 2:
                return
            S.add("act", lambda e: e.activation(out=u["t1"], in_=u["kc"], func=AF.Square), reads=[R_("kc")], writes=[R_("t1")])
            p0, p0r = P(0)
            S.add("pe", lambda e: e.matmul(p0, lhsT=onesF, rhs=u["t1"], start=True, stop=True),
                  reads=["gconst", R_("t1")], writes=[p0r])
            S.add("dve", lambda e: e.tensor_scalar(out=u["t2"], in0=p0, scalar1=1e-6, scalar2=None, op0=ALU.add),
                  reads=[p0r], writes=[R_("t2")])
            S.add("act", lambda e: e.sqrt(out=u["t2"], in_=u["t2"]), reads=[R_("t2")], writes=[R_("t2")])
            S.add("dve", lambda e: e.reciprocal(out=u["t2"], in_=u["t2"]), reads=[R_("t2")], writes=[R_("t2")])
            S.add("dve", lambda e: e.tensor_tensor(out=u["kn"], in0=u["kc"], in1=u["t2"], op=ALU.mult),
                  reads=[R_("kc"), R_("t2")], writes=[R_("kn")])
            if KSTAGE < 3:
                return
            KSUB = int(os.environ.get('K_SUB', '9'))
            p1, p1r = P(1)
            S.add("pe", lambda e: e.matmul(p1, lhsT=u["kn"], rhs=identF, start=True, stop=True), reads=[R_("kn"), "gconst"], writes=[p1r])
            if KSUB < 1:
                return
            S.add("act", lambda e: e.copy(out=u["ktm"], in_=p1), reads=[p1r], writes=[R_("ktm")])
            if KSUB < 2:
                return
            p2, p2r = P(2)
            vsrc = "kc" if os.environ.get('K_V1') else "vc"
            S.add("pe", lambda e: e.matmul(p2, lhsT=u[vsrc], rhs=identF, start=True, stop=True), reads=[R_(vsrc), "gconst"], writes=[p2r])
            if KSUB < 3:
                return
            S.add("dve", lambda e: e.tensor_scalar(out=u["vb"], in0=p2, scalar1=chk[:, 16 + h:17 + h], scalar2=None, op0=ALU.mult),
                  reads=[p2r, "chkbeta"], writes=[R_("vb")])
            if KSTAGE < 4:
                return
            S.add("dve", lambda e: e.tensor_scalar(out=u["rep"], in0=onesF, scalar1=chk[:, 32 + h:33 + h], scalar2=None, op0=ALU.mult),
                  reads=["gconst", "chkg"], writes=[R_("rep")])
            p3, p3r = P(3)
            S.add("pe", lambda e: e.matmul(p3, lhsT=u["rep"], rhs=Umat, start=True, stop=True),
                  reads=[R_("rep"), "gconst"], writes=[p3r])
            S.add("act", lambda e: e.copy(out=u["Gbc"], in_=p3), reads=[p3r], writes=[R_("Gbc")])
            S.add("dve", lambda e: e.tensor_scalar(out=u["rep"], in0=onesF, scalar1=chk[:, 16 + h:17 + h], scalar2=None, op0=ALU.mult),
                  reads=["gconst", "chkbeta"], writes=[R_("rep")])
            p4, p4r = P(4)
            S.add("pe", lambda e: e.matmul(p4, lhsT=u["rep"], rhs=identF, start=True, stop=True),
                  reads=[R_("rep"), "gconst"], writes=[p4r])
            S.add("dve", lambda e: e.tensor_tensor(out=u["kbT"], in0=u["kn"], in1=p4, op=ALU.mult),
                  reads=[R_("kn"), p4r], writes=[R_("kbT")])
            if KSTAGE < 5:
                return
            p5, p5r = P(5)
            p6, p6r = P(6)
            S.add("pe", lambda e: e.matmul(p5, lhsT=u["kbT"], rhs=u["kn"], start=True, stop=True),
                  reads=[R_("kbT"), R_("kn")], writes=[p5r])
            S.add("pe", lambda e: e.matmul(p6, lhsT=u["kn"], rhs=u["kbT"], start=True, stop=True),
                  reads=[R_("kbT"), R_("kn")], writes=[p6r])
            gcol = chk[:, 40 + h:41 + h]
            S.add("dve", lambda e: e.tensor_scalar(out=u["t1"], in0=u["Gbc"], scalar1=gcol, scalar2=0.0, op0=ALU.subtract, op1=ALU.max),
                  reads=[R_("Gbc"), "chkG"], writes=[R_("t1")])
            S.add("act", lambda e: e.activation(out=u["t1"], in_=u["t1"], func=AF.Exp, scale=-1.0), reads=[R_("t1")], writes=[R_("t1")])
            S.add("dve", lambda e: e.tensor_tensor(out=u["t1"], in0=u["t1"], in1=maskLn, op=ALU.mult),
                  reads=[R_("t1"), "gconst"], writes=[R_("t1")])
            S.add("dve", lambda e: e.tensor_tensor(out=u["QT"], in0=u["t1"], in1=p5, op=ALU.mult),
                  reads=[R_("t1"), p5r], writes=[R_("QT")])
            S.add("dve", lambda e: e.tensor_scalar(out=u["t2"], in0=u["Gbc"], scalar1=gcol, scalar2=0.0, op0=ALU.subtract, op1=ALU.min),
                  reads=[R_("Gbc"), "chkG"], writes=[R_("t2")])
            S.add("act", lambda e: e.activation(out=u["t2"], in_=u["t2"], func=AF.Exp), reads=[R_("t2")], writes=[R_("t2")])
            S.add("dve", lambda e: e.tensor_tensor(out=u["t3"], in0=u["t2"], in1=maskUd, op=ALU.mult),
                  reads=[R_("t2"), "gconst"], writes=[R_("t3")])
            S.add("dve", lambda e: e.tensor_tensor(out=u["t2"], in0=u["t2"], in1=maskUn, op=ALU.mult),
                  reads=[R_("t2"), "gconst"], writes=[R_("t2")])
            S.add("dve", lambda e: e.tensor_tensor(out=u["Q"], in0=u["t2"], in1=p6, op=ALU.mult),
                  reads=[R_("t2"), p6r], writes=[R_("Q")])
            S.add("dve", lambda e: e.tensor_tensor(out=u["R"], in0=u["Q"], in1=identF, op=ALU.add),
                  reads=[R_("Q"), "gconst"], writes=[R_("R")])
            if KSTAGE < 6:
                return
            for lvl in range(1, 7):
                pa, par = P(7)
                pq, pqr = P(8)
                S.add("pe", lambda e, pa=pa: e.matmul(pa, lhsT=u["Q"], rhs=u["QT"], start=True, stop=True),
                      reads=[R_("Q"), R_("QT")], writes=[par])
                if lvl < 6:
                    S.add("pe", lambda e, pq=pq: e.matmul(pq, lhsT=u["QT"], rhs=u["Q"], start=True, stop=True),
                          reads=[R_("Q"), R_("QT")], writes=[pqr])
                S.add("act", lambda e, pa=pa: e.copy(out=u["QT"], in_=pa), reads=[par], writes=[R_("QT")])
                if lvl < 6:
                    S.add("dve", lambda e, pq=pq: e.tensor_copy(out=u["Q"], in_=pq), reads=[pqr], writes=[R_("Q")])
                pr, prr = P(9)
                S.add("pe", lambda e, pr=pr: e.matmul(pr, lhsT=u["QT"], rhs=u["R"], start=True, stop=True),
                      reads=[R_("QT"), R_("R")], writes=[prr])
                S.add("dve", lambda e, pr=pr: e.tensor_tensor(out=u["R"], in0=u["R"], in1=pr, op=ALU.add),
                      reads=[R_("R"), prr], writes=[R_("R")])
            if KSTAGE < 7:
                return
            S.add("act", lambda e: e.activation(out=u["t1"], in_=u["qc"], func=AF.Square), reads=[R_("qc")], writes=[R_("t1")])
            pq0, pq0r = P(1)
            S.add("pe", lambda e: e.matmul(pq0, lhsT=onesF, rhs=u["t1"], start=True, stop=True),
                  reads=["gconst", R_("t1")], writes=[pq0r])
            S.add("dve", lambda e: e.tensor_scalar(out=u["t1"], in0=pq0, scalar1=1e-6, scalar2=None, op0=ALU.add),
                  reads=[pq0r], writes=[R_("t1")])
            S.add("act", lambda e: e.sqrt(out=u["t1"], in_=u["t1"]), reads=[R_("t1")], writes=[R_("t1")])
            S.add("dve", lambda e: e.reciprocal(out=u["t1"], in_=u["t1"]), reads=[R_("t1")], writes=[R_("t1")])
            S.add("dve", lambda e: e.scalar_tensor_tensor(out=u["qc"], in0=u["qc"], scalar=128.0 ** -0.5, in1=u["t1"],
                                                          op0=ALU.mult, op1=ALU.mult),
                  reads=[R_("qc"), R_("t1")], writes=[R_("qc")])
            S.add("act", lambda e: e.activation(out=u["eG"], in_=u["Gbc"], func=AF.Exp), reads=[R_("Gbc")], writes=[R_("eG")])
            S.add("dve", lambda e: e.tensor_tensor(out=u["qg"], in0=u["qc"], in1=u["eG"], op=ALU.mult),
                  reads=[R_("qc"), R_("eG")], writes=[R_("qg")])
            pqk, pqkr = P(2)
            S.add("pe", lambda e: e.matmul(pqk, lhsT=u["kn"], rhs=u["qc"], start=True, stop=True),
                  reads=[R_("kn"), R_("qc")], writes=[pqkr])
            S.add("dve", lambda e: e.tensor_tensor(out=u["qkT"], in0=u["t3"], in1=pqk, op=ALU.mult),
                  reads=[R_("t3"), pqkr], writes=[R_("qkT")])
            sc = u["sc"]
            S.add("act", lambda e: e.activation(out=sc[:, 0:1], in_=gcol, func=AF.Exp), reads=["chkG"], writes=[R_("sc")])
            S.add("dve", lambda e: e.tensor_tensor(out=sc[:, 0:1], in0=sc[:, 0:1], in1=chk[:, 16 + h:17 + h], op=ALU.mult),
                  reads=[R_("sc"), "chkbeta"], writes=[R_("sc")])
            S.add("dve", lambda e: e.tensor_tensor(out=sc[:, 1:2], in0=u["Gbc"][:, 127:128], in1=gcol, op=ALU.subtract),
                  reads=[R_("Gbc"), "chkG"], writes=[R_("sc")])
            S.add("act", lambda e: e.activation(out=sc[:, 1:2], in_=sc[:, 1:2], func=AF.Exp), reads=[R_("sc")], writes=[R_("sc")])
            S.add("act", lambda e: e.activation(out=sc[:, 2:3], in_=u["Gbc"][:, 127:128], func=AF.Exp),
                  reads=[R_("Gbc")], writes=[R_("sc")])
            S.add("dve", lambda e: e.tensor_scalar(out=u["kg"], in0=u["ktm"], scalar1=sc[:, 0:1], scalar2=None, op0=ALU.mult),
                  reads=[R_("ktm"), R_("sc")], writes=[R_("kg")])
            S.add("dve", lambda e: e.tensor_scalar(out=u["kd"], in0=u["ktm"], scalar1=sc[:, 1:2], scalar2=None, op0=ALU.mult),
                  reads=[R_("ktm"), R_("sc")], writes=[R_("kd")])
            if KSTAGE < 8:
                return
            p10, p10r = P(10)
            S.add("pe", lambda e: e.matmul(p10, lhsT=u["kg"], rhs=u["R"], start=True, stop=True),
                  reads=[R_("kg"), R_("R")], writes=[p10r])
            S.add("act", lambda e: e.mul(out=u["nwT"], in_=p10, mul=-1.0), reads=[p10r], writes=[R_("nwT")])
            p11, p11r = P(11)
            S.add("pe", lambda e: e.matmul(p11, lhsT=u["R"], rhs=u["vb"], start=True, stop=False),
                  reads=[R_("R"), R_("vb")], writes=[p11r])
            S.add("pe", lambda e: e.matmul(p11, lhsT=u["nwT"], rhs=Sst[h], start=False, stop=True),
                  reads=[R_("nwT"), f"S{h}"], writes=[p11r])
            S.add("act", lambda e: e.copy(out=u["vn"], in_=p11), reads=[p11r], writes=[R_("vn")])
            po, por = P(3)
            S.add("pe", lambda e: e.matmul(po, lhsT=Sst[h], rhs=u["qg"], start=True, stop=False),
                  reads=[f"S{h}", R_("qg")], writes=[por])
            S.add("pe", lambda e: e.matmul(po, lhsT=u["vn"], rhs=u["qkT"], start=False, stop=True),
                  reads=[R_("vn"), R_("qkT")], writes=[por])
            S.add("act", lambda e: e.activation(out=u["t1"], in_=po, func=AF.Square), reads=[por], writes=[R_("t1")])
            pss, pssr = P(4)
            S.add("pe", lambda e: e.matmul(pss, lhsT=onesF, rhs=u["t1"], start=True, stop=True),
                  reads=["gconst", R_("t1")], writes=[pssr])
            S.add("dve", lambda e: e.tensor_scalar(out=u["t1"], in0=pss, scalar1=1.0 / 128, scalar2=EPS, op0=ALU.mult, op1=ALU.add),
                  reads=[pssr], writes=[R_("t1")])
            S.add("act", lambda e: e.sqrt(out=u["t1"], in_=u["t1"]), reads=[R_("t1")], writes=[R_("t1")])
            S.add("dve", lambda e: e.reciprocal(out=u["t1"], in_=u["t1"]), reads=[R_("t1")], writes=[R_("t1")])
            S.add("dve", lambda e: e.tensor_tensor(out=u["on"], in0=u["t1"], in1=po, op=ALU.mult),
                  reads=[R_("t1"), por], writes=[R_("on")])
            zrow = 5120 + 128 * h
            S.add("sp", lambda e: e.dma_start(out=u["zs"][:, :nt_], in_=projT[zrow:zrow + 128, tok0:tok0 + nt_]),
                  writes=[R_("zs")], slot=R_("zs"))
            S.add("act", lambda e: e.activation(out=u["zs"][:, :nt_], in_=u["zs"][:, :nt_], func=AF.Silu), reads=[R_("zs")], writes=[R_("zs")])
            S.add("dve", lambda e: e.scalar_tensor_tensor(out=finb[p][:, :nt_], in0=u["on"][:, :nt_], scalar=hb[:, 17:18], in1=u["zs"][:, :nt_],
                                                          op0=ALU.mult, op1=ALU.mult),
                  reads=[R_("on"), "hb", R_("zs")], writes=[f"finb{p}"])
            S.add("sp", lambda e: e.dma_start(out=mixT_d[1024 + 128 * h:1024 + 128 * h + 128, tok0:tok0 + nt_], in_=finb[p][:, :nt_]),
                  reads=[f"finb{p}"], slot=f"finbo{p}")
            S.add("pe", lambda e: e.matmul(p0, lhsT=u["kd"], rhs=u["vn"], start=True, stop=True),
                  reads=[R_("kd"), R_("vn")], writes=[p0r])
            S.add("dve", lambda e: e.scalar_tensor_tensor(out=Sst[h], in0=Sst[h], scalar=sc[:, 2:3], in1=p0,
                                                          op0=ALU.mult, op1=ALU.add),
                  reads=[f"S{h}", R_("sc"), p0r], writes=[f"S{h}"])

        def record(h):
            rec = []
            S.add = lambda *a, **k: rec.append((a, k))
            try:
                unit(h)
            finally:
                del S.add
            items = []
            for a, k in rec:
                if items and a[0] == "pe" and items[-1][-1][0][0] == "pe":
                    items[-1].append((a, k))
                else:
                    items.append([(a, k)])
            return items

        pending = list(range(8))

        def refill(sl):
            for h in pending:
                if h % 2 == sl:
                    pending.remove(h)
                    unit_no[0] = h
                    return record(h)
            return None

        slots = [refill(0), refill(1)]
        while slots[0] is not None or slots[1] is not None:
            for sl in range(2):
                it = slots[sl]
                if it is None:
                    continue
                for a, k in it.pop(0):
                    S.add(*a, **k)
                if not it:
                    slots[sl] = refill(sl)

    for c in range(0 if attn_only else KCHUNKS):
        gdn_chunk(c * 128, 128, None if c == 0 else "prev", False)
    for h in range(8):
        S.add("sp", lambda e, h=h: e.dma_start(out=ssm_p[h], in_=Sst[h]), reads=[f"S{h}"], writes=["ssmo_all"], slot="ssmo")
    for h in range(8):
        S.add("sp", lambda e, h=h: e.dma_start(out=Sst[h], in_=ssm0_d[h]), reads=["ssmo_all"], writes=[f"S{h}"], slot=f"ssmi{h}")
    if KSAMPLE and not attn_only:
        gdn_chunk(SEQ, NS, "state", True)
    for h in range(8):
        S.add("sp", lambda e, h=h: e.dma_start(out=ssm_s[h], in_=Sst[h]), reads=[f"S{h}"], slot="ssmo")


    S.fence()
    hoff = [0]

    def balloc(ncols):
        o = hoff[0]
        hoff[0] += ncols
        assert hoff[0] <= FC * TWX
        return hT[:, o:o + ncols]

    A_pos = Y[0][64:65, :]
    A_posl = Y[1][64:65, :]
    A_m = X[0][64:65, :]
    xo = [0]

    def xalloc(ncols):
        o = xo[0]
        xo[0] += ncols
        assert xo[0] <= D
        return X[1][:, o:o + ncols]

    a_ident = xalloc(128)
    a_mask = xalloc(128)
    a_O = [[xalloc(128) for s_ in range(4)] for m_ in range(2)]
    a_pd = xalloc(128)
    a_fin = xalloc(128)
    a_sub = xalloc(128)
    a_lam = xalloc(256)
    a_small = xalloc(32)
    a_kb = xalloc(64)
    a_posT = xalloc(16)
    S.add("sp", lambda e: e.dma_start(out=A_pos, in_=aconst[0:1, 0:2048]), writes=["A_pos"], slot="A_pos")
    S.add("sp", lambda e: e.dma_start(out=a_lam, in_=aconst[:, 2048:2304]), writes=["a_lam"], slot="a_lam")
    S.add("sp", lambda e: e.dma_start(out=a_sub, in_=aconst[:, 2304:2432]), writes=["a_sub"], slot="a_sub")
    S.add("sp", lambda e: e.dma_start(out=a_ident, in_=gconst[:, 0:128]), writes=["a_ident"], slot="a_ident")
    S.add("sp", lambda e: e.dma_start(out=a_mask, in_=gconst[:, 640:768]), writes=["a_mask"], slot="a_mask")
    for Q_ in range(4):
        S.add("dve", lambda e, Q_=Q_: e.tensor_scalar(out=A_posl[:, 512 * Q_:512 * Q_ + 512], in0=A_pos[:, 512 * Q_:512 * Q_ + 512],
                                                      scalar1=-512.0 * Q_, scalar2=None, op0=ALU.add),
              reads=["A_pos"], writes=["A_posl"])
    S.add("sp", lambda e: e.dma_start(out=a_posT, in_=aconst[:, 2432:2448]), writes=["a_posT"], slot="a_posT")
    S.add("dve", lambda e: e.tensor_scalar(out=a_sub, in0=a_sub, scalar1=0.8, scalar2=None, op0=ALU.mult),
          reads=["a_sub"], writes=["a_sub"])
    for i_ in range(2):
        S.add("dve", lambda e, i_=i_: e.tensor_tensor(out=a_lam[:, 128 * i_:128 * i_ + 64], in0=a_lam[:, 128 * i_:128 * i_ + 64],
                                                   in1=a_lam[:, 128 * i_ + 64:128 * i_ + 128], op=ALU.mult),
              reads=["a_lam"], writes=["a_lam"])
        S.add("dve", lambda e, i_=i_: e.reduce_sum(out=a_small[:, i_:i_ + 1], in_=a_lam[:, 128 * i_:128 * i_ + 64], axis=mybir.AxisListType.X),
              reads=["a_lam"], writes=["a_small"])
        S.add("act", lambda e, i_=i_: e.activation(out=a_small[:, i_:i_ + 1], in_=a_small[:, i_:i_ + 1], func=AF.Exp),
              reads=["a_small"], writes=["a_small"])
    S.add("dve", lambda e: e.tensor_tensor(out=a_small[:, 2:3], in0=a_small[:, 1:2], in1=a_small[:, 0:1], op=ALU.subtract),
          reads=["a_small"], writes=["a_small"])
    S.add("dve", lambda e: e.tensor_scalar(out=a_small[:, 2:3], in0=a_small[:, 2:3], scalar1=-0.2, scalar2=None, op0=ALU.add),
          reads=["a_small"], writes=["a_small"])
    neglam = a_small[:, 2:3]

    qA = [[balloc(SEQ) for m_ in range(2)] for hp_ in range(2)]
    kA = [[balloc(SEQ) for m_ in range(2)] for hp_ in range(2)]
    sqb = XN[:, :]
    Vx = [balloc(16 * 129) for _ in range(2)]
    Eb = [balloc(512) for _ in range(2)]
    onesb = balloc(72)
    finb2 = balloc(512)
    S.add("dve", lambda e: e.memset(onesb, 1.0), writes=["onesb"])
    for hp_ in range(2):
        for m_ in range(2):
            S.add("dve", lambda e, hp_=hp_, m_=m_: e.memset(kA[hp_][m_][64:65, :], 1.0), writes=[f"kA{hp_}{m_}"])

    AX = mybir.AxisListType.X

    def attn_head(h):
        hp = h % 2
        slope = 2.0 ** (-(h + 1))
        vx = Vx[hp]
        vx3 = vx.rearrange("p (b c) -> p b c", c=129)
        for m_ in range(2):
            S.add("pool", lambda e, m_=m_: e.dma_start(out=qA[hp][m_][0:64, :], in_=projT[128 * h + 64 * m_:128 * h + 64 * m_ + 64, 0:SEQ]),
                  writes=[f"qA{hp}{m_}"], slot=f"qA{hp}{m_}")
            S.add("pool", lambda e, m_=m_: e.dma_start(out=kA[hp][m_][0:64, :], in_=projT[1024 + 128 * h + 64 * m_:1024 + 128 * h + 64 * m_ + 64, 0:SEQ]),
                  writes=[f"kA{hp}{m_}"], slot=f"kA{hp}{m_}")
        S.add("dve", lambda e: e.memset(vx, 1.0), writes=[f"Vx{hp}"])
        S.add("pool", lambda e: e.dma_start(out=vx3[:, :, 0:128],
                                            in_=v_out[0:SEQ, 128 * h:128 * h + 128].rearrange("(b p) c -> p b c", p=128)),
              writes=[f"Vx{hp}"], slot=f"Vx{hp}")
        for Q_ in range(4):
            S.add("dve", lambda e, Q_=Q_: e.tensor_scalar(out=a_kb[:, 16 * Q_:16 * Q_ + 16], in0=a_posT, scalar1=slope, scalar2=-512.0 * Q_ * slope,
                                                          op0=ALU.mult, op1=ALU.add),
                  reads=["a_posT"], writes=["a_kb"])

        def sumsq(cb):
            S.add("pe", lambda e: e.matmul(bank[6][0:65, :], lhsT=onesb[0:64, 0:65], rhs=sqb[0:64, cb * 512:(cb + 1) * 512],
                                           start=True, stop=True), reads=["onesb", "sqb"], writes=[N(bank[6])])

        def norms(m):
            q_, k_ = qA[hp][m], kA[hp][m]
            S.add("act", lambda e: e.activation(out=sqb[0:64, :], in_=k_[0:64, :], func=AF.Square),
                  reads=[f"kA{hp}{m}"], writes=["sqb"])

            def kmax(cb):
                sumsq(cb)
                S.add("dve", lambda e: e.reduce_max(out=a_small[64:65, 4 + cb:5 + cb], in_=bank[6][64:65, :], axis=AX),
                      reads=[N(bank[6])], writes=["a_small"])

            for cb in range(4):
                kmax(cb)
            S.add("dve", lambda e: e.reduce_max(out=a_small[64:65, 8:9], in_=a_small[64:65, 4:8], axis=AX),
                  reads=["a_small"], writes=["a_small"])
            S.add("act", lambda e: e.activation(out=sqb[0:64, :], in_=q_[0:64, :], func=AF.Square),
                  reads=[f"qA{hp}{m}"], writes=["sqb"])

            def qn(cb):
                sumsq(cb)
                S.add("dve", lambda e: e.tensor_scalar(out=A_m[:, cb * 512:(cb + 1) * 512], in0=bank[6][64:65, :],
                                                       scalar1=a_small[64:65, 8:9], scalar2=1.21, op0=ALU.mult, op1=ALU.mult),
                      reads=[N(bank[6]), "a_small"], writes=["A_m"])

            for cb in range(4):
                qn(cb)
            S.add("act", lambda e: e.sqrt(out=A_m, in_=A_m), reads=["A_m"], writes=["A_m"])
            S.add("dve", lambda e: e.scalar_tensor_tensor(out=q_[64:65, :], in0=A_posl, scalar=-8.0 * slope, in1=A_m,
                                                          op0=ALU.mult, op1=ALU.subtract),
                  reads=["A_posl", "A_m"], writes=[f"qA{hp}{m}"])

        for m in range(2):
            norms(m)

        def qk_A(Q, m, j):
            eb = cnt["evac"] % 2
            cnt["evac"] += 1
            psb = bank[4 + eb]
            E = Eb[eb]
            S.add("pe", lambda e: e.matmul(psb[:, :], lhsT=kA[hp][m][0:65, 128 * j:128 * j + 128],
                                           rhs=qA[hp][m][0:65, 512 * Q:512 * Q + 512], start=True, stop=True),
                  reads=[f"kA{hp}{m}", f"qA{hp}{m}"], writes=[N(psb)])
            s0 = max(0, j - 4 * Q)
            S.add("act", lambda e: e.activation(out=E[:, 128 * s0:512], in_=psb[:, 128 * s0:512], func=AF.Exp, scale=0.125,
                                                bias=a_kb[:, 16 * Q + j:16 * Q + j + 1]),
                  reads=[N(psb), "a_kb"], writes=[f"Eb{eb}"])
            if j >= 4 * Q:
                S.add("dve", lambda e: e.tensor_tensor(out=E[:, 128 * s0:128 * s0 + 128], in0=E[:, 128 * s0:128 * s0 + 128],
                                                       in1=a_mask, op=ALU.mult),
                      reads=[f"Eb{eb}", "a_mask"], writes=[f"Eb{eb}"])
            return eb, E

        def qk_B(Q, j, accs, eb, E):
            def pv(s_):
                qi = 4 * Q + s_
                if j > qi:
                    return
                acc, accr = accs[s_]
                S.add("pe", lambda e: e.matmul(acc, lhsT=E[:, 128 * s_:128 * s_ + 128], rhs=vx3[:, j, :],
                                               start=(j == 0), stop=(j == qi)),
                      reads=[f"Eb{eb}", f"Vx{hp}"], writes=[accr])

            for s_ in range(4):
                pv(s_)

        def qblock_map(Q, m):
            accs = [(bank[s_][:, 0:129], N(bank[s_])) for s_ in range(4)]
            nj = 4 * Q + 4
            st = qk_A(Q, m, 0)
            for j in range(nj):
                nxt = qk_A(Q, m, j + 1) if j + 1 < nj else None
                qk_B(Q, j, accs, *st)
                st = nxt

            def fin_acc(s_):
                acc, accr = accs[s_]
                S.add("dve", lambda e: e.reciprocal(out=a_small[:, 16 + s_:17 + s_], in_=acc[:, 128:129]),
                      reads=[accr], writes=["a_small"])
                S.add("dve", lambda e: e.tensor_scalar(out=a_O[m][s_], in0=acc[:, 0:128], scalar1=a_small[:, 16 + s_:17 + s_],
                                                       scalar2=None, op0=ALU.mult),
                      reads=[accr, "a_small"], writes=[f"a_O{m}{s_}"])

            for s_ in range(4):
                fin_acc(s_)

        def fin_sub(Q, s_):
            S.add("dve", lambda e: e.scalar_tensor_tensor(out=a_pd, in0=a_O[1][s_], scalar=neglam, in1=a_O[0][s_],
                                                          op0=ALU.mult, op1=ALU.add),
                  reads=[f"a_O0{s_}", f"a_O1{s_}", "a_small"], writes=["a_pd"])
            S.add("act", lambda e: e.activation(out=a_fin, in_=a_pd, func=AF.Square, accum_out=a_small[:, 20:21]),
                  reads=["a_pd"], writes=["a_fin", "a_small"])
            S.add("dve", lambda e: e.tensor_scalar(out=a_small[:, 21:22], in0=a_small[:, 20:21], scalar1=1.0 / 128, scalar2=EPS,
                                                   op0=ALU.mult, op1=ALU.add), reads=["a_small"], writes=["a_small"])
            S.add("act", lambda e: e.sqrt(out=a_small[:, 21:22], in_=a_small[:, 21:22]), reads=["a_small"], writes=["a_small"])
            S.add("dve", lambda e: e.reciprocal(out=a_small[:, 22:23], in_=a_small[:, 21:22]), reads=["a_small"], writes=["a_small"])
            S.add("dve", lambda e: e.scalar_tensor_tensor(out=a_fin, in0=a_pd, scalar=a_small[:, 22:23], in1=a_sub,
                                                          op0=ALU.mult, op1=ALU.mult),
                  reads=["a_pd", "a_small", "a_sub"], writes=["a_fin"])
            S.add("pe", lambda e: e.matmul(bank[6][:, 0:128], lhsT=a_fin, rhs=a_ident, start=True, stop=True),
                  reads=["a_fin", "a_ident"], writes=[N(bank[6])])
            S.add("act", lambda e: e.copy(out=finb2[:, 128 * s_:128 * s_ + 128], in_=bank[6][:, 0:128]),
                  reads=[N(bank[6])], writes=["finb2"])

        def qblock(Q):
            for m in range(2):
                qblock_map(Q, m)
            for s_ in range(4):
                fin_sub(Q, s_)
            S.add("sp", lambda e: e.dma_start(out=mixT_d[128 * h:128 * h + 128, 512 * Q:512 * Q + 512], in_=finb2),
                  reads=["finb2"], slot="finb2o")

        for Q in range(4):
            qblock(Q)

    for h in range(aheads):
        attn_head(h)

    def sample_attn():
        S.fence()
        KV = [Y[0], Y[1]]
        KTs = [Y[2][:, 0:1024], Y[2][:, 1024:2048]]
        C = Y[3]
        posp, slopeRow, maskS = C[:, 0:128], C[:, 128:256], C[:, 256:384]
        pcol, subcol = C[:, 384:385], C[:, 385:386]
        identS, onesS = C[:, 512:640], C[:, 640:768]
        qs, knew = C[:, 768:832], C[:, 832:896]
        Sb = [C[:, 896:1024], C[:, 1024:1152]]
        Es = [C[:, 1152:1280], C[:, 1280:1408]]
        Rr, On, pd, sq, rstd = C[:, 1408:1536], C[:, 1536:1664], C[:, 1664:1728], C[:, 1728:1792], C[:, 1792:1856]
        vnew = Y[4][0:8, 0:1024]
        fins = finb[0][:, 0:64]
        qs3 = qs.rearrange("p (h q) -> p h q", q=8)
        knew3 = knew.rearrange("p (h q) -> p h q", q=8)
        S.add("sp", lambda e: e.dma_start(out=C[:, 0:512], in_=sconst), writes=["sC"], slot="sC")
        S.add("sp", lambda e: e.dma_start(out=identS, in_=gconst[:, 0:128]), writes=["sI"], slot="sI")
        S.add("sp", lambda e: e.dma_start(out=onesS, in_=gconst[:, 512:640]), writes=["sO"], slot="sO")
        S.add("sp", lambda e: e.dma_start(out=ptidx[:], in_=pt_d), writes=["ptidx"], slot="ptidx")
        S.add("dve", lambda e: e.tensor_scalar(out=ptidx2[:], in0=ptidx[:], scalar1=128, scalar2=None, op0=ALU.mult),
              reads=["ptidx"], writes=["ptidx2"])
        S.add("sp", lambda e: e.dma_start(out=qs3, in_=projT[0:1024, SEQ:SEQ + NS].rearrange("(h p) q -> p h q", p=128)),
              writes=["qs"], slot="qs")
        S.add("sp", lambda e: e.dma_start(out=knew3, in_=projT[1024:2048, SEQ:SEQ + NS].rearrange("(h p) q -> p h q", p=128)),
              writes=["knew"], slot="knew")
        S.add("sp", lambda e: e.dma_start(out=vnew, in_=v_out[SEQ:SEQ + NS, :]), writes=["vnew"], slot="vnew")
        SA = int(os.environ.get("SA_STAGE", "9"))
        qm = C[:, 1856:1984]
        qm4 = qm.rearrange("p (h m q) -> p h m q", m=2, q=8)
        S.add("dve", lambda e: e.memset(qm, 0.0), writes=["qm"])
        S.add("dve", lambda e: e.tensor_scalar(out=qm4[0:64, :, 0, :], in0=qs3[0:64, :, :], scalar1=0.125, scalar2=None, op0=ALU.mult),
              reads=["qs"], writes=["qm"])
        S.add("dve", lambda e: e.tensor_scalar(out=qm4[64:128, :, 1, :], in0=qs3[64:128, :, :], scalar1=0.125, scalar2=None, op0=ALU.mult),
              reads=["qs"], writes=["qm"])
        accO, accOr = bank[3][:, 0:128], N(bank[3])
        accR, accRr = bank[4][:, 0:128], N(bank[4])
        scb, scbr = bank[2], N(bank[2])

        KV3 = [Y[0], Y[1], X[0]]

        def stA(tok):
            kb_ = tok % 3
            b = tok % 2
            tb = (bank[0], bank[1]) if b == 0 else (bank[5], bank[6])
            S.add("pool", lambda e: e.indirect_dma_start(
                out=KV3[kb_][:, 0:1024], out_offset=None, in_=ck_d[:, :],
                in_offset=bass.IndirectOffsetOnAxis(ap=ptidx2[:, 0:1], axis=0), element_offset=tok * 1024),
                reads=["ptidx2"], writes=[f"KVk{kb_}"], slot=f"KVk{kb_}")
            S.add("pool", lambda e: e.indirect_dma_start(
                out=KV3[kb_][:, 1024:2048], out_offset=None, in_=cv_d[:, :],
                in_offset=bass.IndirectOffsetOnAxis(ap=ptidx2[:, 0:1], axis=0), element_offset=tok * 1024),
                reads=["ptidx2"], writes=[f"KVv{kb_}"], slot=f"KVv{kb_}")
            if SA < 2:
                return

            def tr(half):
                for hh in range(4):
                    h = half * 4 + hh
                    S.add("pe", lambda e, h=h, hh=hh: e.matmul(tb[half][:, hh * 128:(hh + 1) * 128], lhsT=KV3[kb_][:, h * 128:(h + 1) * 128],
                                                               rhs=identS, start=True, stop=True),
                          reads=[f"KVk{kb_}", "sI"], writes=[N(tb[half])])
                if half == 0:
                    S.add("act", lambda e: e.copy(out=KTs[b][:, 0:512], in_=tb[0][:, :]), reads=[N(tb[0])], writes=[f"KT{b}a"])
                else:
                    S.add("dve", lambda e: e.tensor_copy(out=KTs[b][:, 512:1024], in_=tb[1][:, :]), reads=[N(tb[1])], writes=[f"KT{b}b"])

            tr(0)
            tr(1)

        def stB(tok):
            b = tok % 2
            if SA < 3:
                return
            for h in range(8):
                S.add("pe", lambda e, h=h: e.matmul(scb[:, h * 16:(h + 1) * 16],
                                                    lhsT=KTs[b][:, h * 128:(h + 1) * 128],
                                                    rhs=qm[:, h * 16:(h + 1) * 16], start=True, stop=True),
                      reads=[f"KT{b}a" if h < 4 else f"KT{b}b", "qm"], writes=[scbr])
            S.add("dve", lambda e: e.scalar_tensor_tensor(out=Sb[b], in0=slopeRow, scalar=posp[:, tok:tok + 1], in1=scb[:, 0:128],
                                                          op0=ALU.mult, op1=ALU.add),
                  reads=["sC", scbr], writes=[f"Sb{b}"])
            S.add("act", lambda e: e.activation(out=Es[b], in_=Sb[b], func=AF.Exp), reads=[f"Sb{b}"], writes=[f"Es{b}"])

        def stC(tok):
            kb_ = tok % 3
            b = tok % 2
            if SA < 4:
                return
            for h in range(8):
                S.add("pe", lambda e, h=h: e.matmul(accO[:, h * 16:(h + 1) * 16], lhsT=KV3[kb_][:, 1024 + h * 128:1024 + (h + 1) * 128],
                                                    rhs=Es[b][:, h * 16:(h + 1) * 16], start=False, stop=False),
                      reads=[f"KVv{kb_}", f"Es{b}"], writes=[accOr])
            S.add("pe", lambda e: e.matmul(accR, lhsT=onesS, rhs=Es[b], start=(tok == 0), stop=False),
                  reads=["sO", f"Es{b}"], writes=[accRr])

        zerosS = Y[4][:, 1024:1152]
        S.add("dve", lambda e: e.memset(zerosS, 0.0), writes=["zerosS"])
        if SA >= 4:
            S.add("pe", lambda e: e.matmul(accO, lhsT=onesS, rhs=zerosS, start=True, stop=False),
                  reads=["sO", "zerosS"], writes=[accOr])
        stA(0)
        stA(1)
        stB(0)
        for tok in range(128):
            if tok + 2 < 128:
                stA(tok + 2)
            if tok + 1 < 128:
                stB(tok + 1)
            stC(tok)
        if SA < 5:
            return
        for h in range(8):
            S.add("pe", lambda e, h=h: e.matmul(scb[0:NS, h * 16:(h + 1) * 16],
                                                lhsT=knew3[:, h, :],
                                                rhs=qm[:, h * 16:(h + 1) * 16], start=True, stop=True),
                  reads=["knew", "qm"], writes=[scbr])
        S.add("dve", lambda e: e.scalar_tensor_tensor(out=Sb[0][0:NS, :], in0=slopeRow[0:NS, :], scalar=pcol[0:NS, :], in1=scb[0:NS, 0:128],
                                                      op0=ALU.mult, op1=ALU.add),
              reads=["sC", scbr], writes=["Sb0"])
        S.add("act", lambda e: e.activation(out=Es[0][0:NS, :], in_=Sb[0][0:NS, :], func=AF.Exp), reads=["Sb0"], writes=["Es0"])
        S.add("dve", lambda e: e.tensor_tensor(out=Es[0][0:NS, :], in0=Es[0][0:NS, :], in1=maskS[0:NS, :], op=ALU.mult),
              reads=["Es0", "sC"], writes=["Es0"])
        for h in range(8):
            S.add("pe", lambda e, h=h: e.matmul(accO[:, h * 16:(h + 1) * 16], lhsT=vnew[:, h * 128:(h + 1) * 128],
                                                rhs=Es[0][0:NS, h * 16:(h + 1) * 16], start=False, stop=(h == 7)),
                  reads=["vnew", "Es0"], writes=[accOr])
        S.add("pe", lambda e: e.matmul(accR, lhsT=onesS[0:NS, :], rhs=Es[0][0:NS, :], start=False, stop=True),
              reads=["sO", "Es0"], writes=[accRr])
        S.add("dve", lambda e: e.reciprocal(out=Rr, in_=accR), reads=[accRr], writes=["sRr"])
        S.add("dve", lambda e: e.tensor_tensor(out=On, in0=accO, in1=Rr, op=ALU.mult), reads=[accOr, "sRr"], writes=["sOn"])
        On4 = On.rearrange("p (h m q) -> p h m q", m=2, q=8)
        pd3 = pd.rearrange("p (h q) -> p h q", q=8)
        S.add("dve", lambda e: e.scalar_tensor_tensor(out=pd3, in0=On4[:, :, 1, :], scalar=neglam, in1=On4[:, :, 0, :],
                                                      op0=ALU.mult, op1=ALU.add),
              reads=["sOn", "a_small"], writes=["spd"])
        S.add("act", lambda e: e.activation(out=sq, in_=pd, func=AF.Square), reads=["spd"], writes=["ssq"])
        S.add("pe", lambda e: e.matmul(scb[:, 0:64], lhsT=onesS, rhs=sq, start=True, stop=True), reads=["sO", "ssq"], writes=[scbr])
        S.add("dve", lambda e: e.tensor_scalar(out=rstd, in0=scb[:, 0:64], scalar1=1.0 / 128, scalar2=EPS, op0=ALU.mult, op1=ALU.add),
              reads=[scbr], writes=["srstd"])
        S.add("act", lambda e: e.sqrt(out=rstd, in_=rstd), reads=["srstd"], writes=["srstd"])
        S.add("dve", lambda e: e.reciprocal(out=rstd, in_=rstd), reads=["srstd"], writes=["srstd"])
        S.add("dve", lambda e: e.tensor_tensor(out=pd, in0=pd, in1=rstd, op=ALU.mult), reads=["spd", "srstd"], writes=["spd"])
        S.add("dve", lambda e: e.tensor_scalar(out=fins, in0=pd, scalar1=subcol, scalar2=0.8, op0=ALU.mult, op1=ALU.mult),
              reads=["spd", "sC"], writes=["finb0"])
        S.add("sp", lambda e: e.dma_start(out=mixT_d[0:1024, SEQ:SEQ + NS].rearrange("(h p) q -> p h q", p=128),
                                          in_=fins.rearrange("p (h q) -> p h q", q=8)),
              reads=["finb0"], slot="finbo0")

    if not skip_sample:
        sample_attn()

    if not skip_p3:
        S.fence()
        for k2 in ("f2pre", "f2post", "mixpost"):
            S.add("sp", lambda e, k2=k2: e.dma_start(out=wbc[k2][:], in_=wbc2_d[k2]), writes=["wbc" + wres[k2]], slot="wbc" + wres[k2])
        for t in range(NT):
            phase3_tile(t)

    S.final_slots = list(S.dma_cnt.keys())
    S.emit(nc, stack)
    stack.close()
    return nc


def fm_blocks(w, gg):
    K, N = w.shape
    W = 128 * gg
    a = w.reshape(K // 128, 128, N // W, W).transpose(2, 1, 0, 3)
    return np.ascontiguousarray(a).reshape(N // W, 128, (K // 128) * W)


def tm_blocks(w, kgsz):
    K, N = w.shape
    a = w.reshape(K // (128 * kgsz), kgsz, 128, N // 512, 512).transpose(3, 0, 2, 1, 4)
    return np.ascontiguousarray(a).reshape(N // 512, K // (128 * kgsz), 128, kgsz * 512)


def gconst_host():
    i = np.arange(128)
    ident = np.eye(128, dtype=np.float32)
    U = (i[:, None] <= i[None, :]).astype(np.float32)
    mL = -(i[:, None] > i[None, :]).astype(np.float32)
    mU = -(i[None, :] > i[:, None]).astype(np.float32)
    ones = np.ones((128, 128), np.float32)
    mUd = (i[None, :] >= i[:, None]).astype(np.float32)
    return np.ascontiguousarray(np.concatenate([ident, U, mL, mU, ones, mUd], axis=1))


def hb_host(a_log, dt_bias, dnw=None):
    hb = np.zeros((128, 32), np.float32)
    hb[:, 0:8] = a_log[None, :]
    hb[:, 8:16] = dt_bias[None, :]
    hb[:NS, 16] = 1.0
    if dnw is not None:
        hb[:, 17] = dnw
    return hb


def aconst_host(lq1, lk1, lq2, lk2, subln):
    a = np.zeros((128, 2048 + 256 + 128 + 16), np.float32)
    a[0, 0:2048] = np.arange(2048, dtype=np.float32)
    a[:, 2432:2448] = 128.0 * np.arange(16, dtype=np.float32)[None, :] + np.arange(128, dtype=np.float32)[:, None]
    a[:, 2048:2112] = lq1[None]
    a[:, 2112:2176] = lk1[None]
    a[:, 2176:2240] = lq2[None]
    a[:, 2240:2304] = lk2[None]
    a[:, 2304:2432] = subln[None]
    return a


def sconst_host(subln):
    a = np.zeros((128, 512), np.float32)
    pp = np.arange(128, dtype=np.float32)
    a[:, 0:128] = 128.0 * pp[:, None] + pp[None, :] - 16384.0
    col = np.arange(128)
    a[:, 128:256] = (2.0 ** (-(col // 16 + 1).astype(np.float32)))[None, :]
    a[:, 256:384] = ((col % 8)[None, :] >= np.arange(128)[:, None]).astype(np.float32)
    a[:, 384] = pp
    a[:, 385] = subln
    return a


_NC = None


def kernel(**inp):
    import ml_dtypes
    global _NC
    if _NC is None:
        _NC = build()
    nc = _NC
    f = lambda k: np.asarray(inp[k], dtype=np.float32)
    xp, xs = f("x_prompt"), f("x_sample")
    w_in = f("w_in")[0]
    fm_cols = np.concatenate([w_in[:, 0:2048], w_in[:, 3072:7168]], axis=1)
    shared = {
        "ident": np.eye(128, dtype=np.float32).astype(ml_dtypes.bfloat16),
        "wbc_f1pre": np.ascontiguousarray(np.broadcast_to(f("ffn1_pre_w")[0], (128, D))),
        "wbc_f1post": np.ascontiguousarray(np.broadcast_to(f("ffn1_post_w")[0], (128, D))),
        "wbc_mixpre": np.ascontiguousarray(np.broadcast_to(f("mix_pre_w")[0], (128, D))),
        "wg1": fm_blocks(f("ffn1_gate")[0], 2),
        "wu1": fm_blocks(f("ffn1_up")[0], 2),
        "wd1": tm_blocks(f("ffn1_down")[0], 11),
        "win_fm": fm_blocks(fm_cols, 2),
        "win_v": tm_blocks(w_in[:, 2048:3072], 4),
        "win_ba": np.ascontiguousarray(w_in[:, 7168:7184].reshape(KC, 128, 16).transpose(1, 0, 2)).reshape(128, KC * 16),
        "gconst": gconst_host(),
        "cw_d": np.ascontiguousarray(f("conv_w")[0].T.reshape(24, 128, 4).transpose(1, 0, 2)).reshape(128, 96),
        "hb_d": hb_host(f("a_log")[0], f("dt_bias")[0], f("delta_norm_w")[0]),
        "aconst": aconst_host(f("lambda_q1")[0], f("lambda_k1")[0], f("lambda_q2")[0], f("lambda_k2")[0], f("subln_w")[0]),
        "wbc_f2pre": np.ascontiguousarray(np.broadcast_to(f("ffn2_pre_w")[0], (128, D))),
        "wbc_f2post": np.ascontiguousarray(np.broadcast_to(f("ffn2_post_w")[0], (128, D))),
        "wbc_mixpost": np.ascontiguousarray(np.broadcast_to(f("mix_post_w")[0], (128, D))),
        "wg2": fm_blocks(f("ffn2_gate")[0], 2),
        "wu2": fm_blocks(f("ffn2_up")[0], 2),
        "wd2": tm_blocks(f("ffn2_down")[0], 11),
        "wout": tm_blocks(f("w_out")[0], 4),
        "cache_k": f("cache_k")[0].reshape(-1, 1024),
        "cache_v": f("cache_v")[0].reshape(-1, 1024),
        "sconst": sconst_host(f("subln_w")[0]),
    }
    page_table = np.asarray(inp["page_table"]).astype(np.int32)
    ssm0 = f("state_ssm")[0]
    convst = f("state_conv")[0]
    in_maps = []
    for c in range(NCORES):
        m = dict(shared)
        m["xin"] = np.ascontiguousarray(np.concatenate([xp[c % 4], xs[c]], axis=0))
        m["ssm0"] = np.ascontiguousarray(ssm0[c])
        m["convst"] = np.ascontiguousarray(convst[c].T.reshape(24, 128, 3))
        m["pt"] = np.ascontiguousarray(page_table[c].reshape(128, 1))
        in_maps.append(m)
    res = run_bass_kernel_spmd(nc, in_maps, core_ids=list(range(NCORES)))
    R = [{k: np.asarray(v) for k, v in r.items()} for r in res.results]
    B = 4
    k_prompt = np.stack([R[b]["projT"][1024:2048, :SEQ].T.reshape(SEQ, 8, 128) for b in range(B)])[None]
    v_prompt = np.stack([R[b]["v_out"][:SEQ].reshape(SEQ, 8, 128) for b in range(B)])[None]
    conv_prompt = np.stack([R[b]["projT"][2048:5120, SEQ - 3:SEQ].T for b in range(B)])[None]
    k_sample = np.stack([R[c]["projT"][1024:2048, SEQ:].T.reshape(NS, 8, 128) for c in range(8)])[None]
    v_sample = np.stack([R[c]["v_out"][SEQ:].reshape(NS, 8, 128) for c in range(8)])[None]
    conv_sample = np.stack([R[c]["projT"][2048:5120, NROW - 3:NROW].T for c in range(8)])[None]
    y_prompt = np.stack([R[b]["y"][:SEQ] for b in range(B)])
    y_sample = np.stack([R[c]["y"][SEQ:] for c in range(8)])
    ssm_prompt = np.stack([R[b]["ssm_p"] for b in range(B)])[None]
    ssm_sample = np.stack([R[c]["ssm_s"] for c in range(8)])[None]
    out = (y_prompt, y_sample, k_prompt, v_prompt, ssm_prompt, conv_prompt,
           k_sample, v_sample, ssm_sample, conv_sample)
    return tuple(np.ascontiguousarray(o, dtype=np.float32) for o in out)
```
